# Optimizing a Trainium2 kernel written in Bass

```python
import math
import jax, jax.numpy as jnp
from jax import lax
import numpy as np

D_MODEL = 1024
BATCH = 32
SEQ = 256
DEPTH = 2
DEC_BATCH = 2
DEC_SEQ = 2048
PAST_LEN = 512

GRID_W = 64
EPS = 1e-6
H_A = 8
DK = 64
DV = 64
W_A = H_A * DV
QKV_W = 2 * H_A * DK + H_A * DV
CONV_K = 3
CHUNK = 64
W_B = 512
HY_ORDER = 2
HY_EMB = 33
HY_HID = 64
HY_DECAY_TARGET = 1e-2
HY_FAST_PCT = 0.3
HY_SLOW_PCT = 1.5
G_C = 8
DC = 64
W_C = G_C * DC
N_BRANCH = 3
D_FF = ((8 * D_MODEL + 3 * 256 - 1) // (3 * 256)) * 256
OFF_Z = QKV_W
OFF_B = OFF_Z + W_A
OFF_A = OFF_B + 2 * H_A
OFF_HY = OFF_A + 2 * H_A
OFF_FN = OFF_HY + (HY_ORDER + 1) * W_B
OFF_GATE = OFF_FN + W_C
D_IN = OFF_GATE + N_BRANCH * D_MODEL

kernel_name = "hybrid_deltanet_hyena_fnet_diffusion_step"


def rmsnorm(x, g):
    xf = x.astype(jnp.float32)
    y = xf * lax.rsqrt(jnp.mean(xf * xf, axis=-1, keepdims=True) + EPS)
    return (y * g.astype(jnp.float32)).astype(x.dtype)


def l2norm(t):
    return t * lax.rsqrt(jnp.sum(t * t, axis=-1, keepdims=True) + EPS)


def centred_dwconv(x, w):
    k = w.shape[0]
    p = k // 2
    L = x.shape[1]
    xp = jnp.pad(x, ((0, 0), (p, p), (0, 0)))
    out = xp[:, 0:L] * w[0]
    for i in range(1, k):
        out = out + xp[:, i:i + L] * w[i]
    return out


def grid_pos_embed(n_tokens):
    rows = n_tokens // GRID_W
    r = jnp.repeat(jnp.arange(rows), GRID_W).astype(jnp.float32)
    col = jnp.tile(jnp.arange(GRID_W), rows).astype(jnp.float32)
    quarter = D_MODEL // 4
    omega = 1.0 / (10000.0 ** (jnp.arange(quarter, dtype=jnp.float32) / quarter))

    def emb(pos):
        a = pos[:, None] * omega[None, :]
        return jnp.concatenate([jnp.sin(a), jnp.cos(a)], axis=-1)

    return jnp.concatenate([emb(r), emb(col)], axis=-1)


def gated_delta_chunked(q, k, v, g, beta, s0):
    bn, L, H, _ = q.shape
    n = L // CHUNK

    def chunks(t):
        t = t.reshape((bn, n, CHUNK, H) + t.shape[3:])
        return jnp.moveaxis(t, 3, 1)

    q, k, v, g, beta = chunks(q), chunks(k), chunks(v), chunks(g), chunks(beta)
    gc = jnp.cumsum(g, axis=-1)
    idx = jnp.arange(CHUNK)
    incl = idx[:, None] >= idx[None, :]
    strict = idx[:, None] > idx[None, :]
    decay = jnp.exp(jnp.where(incl, gc[..., :, None] - gc[..., None, :], -jnp.inf))
    kb = k * beta[..., None]
    a_low = jnp.where(strict, jnp.einsum('bhnid,bhnjd->bhnij', kb, k) * decay, 0.0)
    m = a_low + jnp.eye(CHUNK, dtype=a_low.dtype)
    u = lax.linalg.triangular_solve(m, v * beta[..., None], left_side=True, lower=True, unit_diagonal=True)
    w = lax.linalg.triangular_solve(m, kb * jnp.exp(gc)[..., None], left_side=True, lower=True, unit_diagonal=True)
    qk = jnp.einsum('bhnid,bhnjd->bhnij', q, k) * decay
    qg = q * jnp.exp(gc)[..., None]
    kg = k * jnp.exp(gc[..., -1:] - gc)[..., None]
    glast = jnp.exp(gc[..., -1])

    def step(s, inp):
        u_n, w_n, qk_n, qg_n, kg_n, gl_n = inp
        v_new = u_n - jnp.einsum('bhcd,bhde->bhce', w_n, s)
        o = jnp.einsum('bhcd,bhde->bhce', qg_n, s) + jnp.einsum('bhij,bhje->bhie', qk_n, v_new)
        s = s * gl_n[..., None, None] + jnp.einsum('bhcd,bhce->bhde', kg_n, v_new)
        return s, o

    xs = (jnp.moveaxis(u, 2, 0), jnp.moveaxis(w, 2, 0), jnp.moveaxis(qk, 2, 0),
          jnp.moveaxis(qg, 2, 0), jnp.moveaxis(kg, 2, 0), jnp.moveaxis(glast, 2, 0))
    s_fin, o = lax.scan(step, s0, xs)
    o = jnp.transpose(o, (1, 0, 3, 2, 4)).reshape(bn, L, H, v.shape[-1])
    return o, s_fin


def delta_mixer(qkv_raw, z, b_raw, a_raw, conv_w, a_log, dt_bias, norm_w, s0):
    bn, L, _ = qkv_raw.shape
    f32 = jnp.float32
    qkv = jax.nn.silu(centred_dwconv(qkv_raw, conv_w).astype(f32))
    q = l2norm(qkv[..., :H_A * DK].reshape(bn, L, H_A, DK)) * (DK ** -0.5)
    k = l2norm(qkv[..., H_A * DK:2 * H_A * DK].reshape(bn, L, H_A, DK))
    v = qkv[..., 2 * H_A * DK:].reshape(bn, L, H_A, DV)
    beta = jax.nn.sigmoid(b_raw.astype(f32))
    g = -jnp.exp(a_log.astype(f32)) * jax.nn.softplus(a_raw.astype(f32) + dt_bias.astype(f32))
    flip = lambda t: t[:, ::-1]
    o_f, s_f = gated_delta_chunked(q, k, v, g[:, :, 0], beta[:, :, 0], s0[:, 0])
    o_b, s_b = gated_delta_chunked(flip(q), flip(k), flip(v), flip(g[:, :, 1]), flip(beta[:, :, 1]), s0[:, 1])
    o = o_f + flip(o_b)
    o = rmsnorm(o, norm_w) * jax.nn.silu(z.astype(f32).reshape(bn, L, H_A, DV))
    return o.reshape(bn, L, W_A), jnp.stack([s_f, s_b], axis=1)


def hyena_filters(L, w1, b1, freq, w2, b2, w3):
    f32 = jnp.float32
    bands = (HY_EMB - 1) // 2
    t = jnp.linspace(0.0, 1.0, L, dtype=f32)[:, None]
    wpos = (2.0 * math.pi / L) * jnp.arange(L, dtype=f32)[:, None]
    fr = jnp.linspace(1e-4, bands - 1, bands, dtype=f32)[None, :]
    zpos = jnp.concatenate([t, jnp.cos(fr * wpos), -jnp.sin(fr * wpos)], axis=-1)
    fq = freq.astype(f32)
    h = jnp.sin(fq * (zpos @ w1.astype(f32) + b1.astype(f32)))
    h = jnp.sin(fq * (h @ w2.astype(f32) + b2.astype(f32)))
    h = h @ w3.astype(f32)
    deltas = jnp.abs(jnp.linspace(math.log(HY_DECAY_TARGET) / HY_SLOW_PCT,
                                  math.log(HY_DECAY_TARGET) / HY_FAST_PCT, W_B, dtype=f32))
    window = jnp.exp(-t * deltas[None, :])
    return h.reshape(L, 2 * HY_ORDER, W_B) * window[:, None, :]


def hyena_mixer(xh, conv_w, w1, b1, freq, w2, b2, w3, bias):
    bn, L, _ = xh.shape
    uc = centred_dwconv(xh, conv_w).astype(jnp.float32)
    x1, x2, v = jnp.split(uc, 3, axis=-1)
    hf = hyena_filters(L, w1, b1, freq, w2, b2, w3)
    hspec = jnp.fft.rfft(hf, n=2 * L, axis=0)
    hk = hspec[:, 0::2] + jnp.conj(hspec[:, 1::2])
    bias = bias.astype(jnp.float32)
    z = v
    for o, gate in enumerate((x1, x2)):
        zs = jnp.fft.rfft(z, n=2 * L, axis=1)
        conv = jnp.fft.irfft(zs * hk[None, :, o], n=2 * L, axis=1)[:, :L]
        z = gate * (conv + bias[o] * z)
    return z


def fourier_mixer(xc):
    bn, L, _ = xc.shape
    y = jnp.fft.fft2(xc.astype(jnp.float32).reshape(bn, L, G_C, DC), axes=(1, 3), norm='ortho').real
    return y.reshape(bn, L, W_C)


def parallel_mixer(h, s0, p):
    bn, L, _ = h.shape
    proj = h @ p['w_in']
    y_a, s_fin = delta_mixer(proj[..., :QKV_W], proj[..., OFF_Z:OFF_B],
                             proj[..., OFF_B:OFF_A].reshape(bn, L, 2, H_A),
                             proj[..., OFF_A:OFF_HY].reshape(bn, L, 2, H_A),
                             p['conv_qkv'], p['a_log'], p['dt_bias'], p['norm_a'], s0)
    y_b = hyena_mixer(proj[..., OFF_HY:OFF_FN], p['conv_hy'], p['hy_w1'], p['hy_b1'], p['hy_freq'],
                      p['hy_w2'], p['hy_b2'], p['hy_w3'], p['hy_bias'])
    y_c = fourier_mixer(proj[..., OFF_FN:OFF_GATE])
    gates = jax.nn.sigmoid(proj[..., OFF_GATE:].astype(jnp.float32)).reshape(bn, L, N_BRANCH, D_MODEL)
    merged = (gates[:, :, 0] * (y_a @ p['w_pa']) + gates[:, :, 1] * (y_b @ p['w_pb'])
              + gates[:, :, 2] * (y_c @ p['w_pc']))
    return merged.astype(h.dtype) @ p['w_o'], s_fin


def swiglu(h, w_gu, w_down):
    gu = h @ w_gu
    gate, up = jnp.split(gu, 2, axis=-1)
    return (jax.nn.silu(gate) * up) @ w_down


def trunk_layer(x, mod, s0, p):
    sh1, sc1, g1, sh2, sc2, g2 = jnp.split(mod, 6, axis=-1)
    h = rmsnorm(x, p['norm1_g']) * (1.0 + sc1) + sh1
    y, s_fin = parallel_mixer(h, s0, p)
    x = x + g1 * y
    h = rmsnorm(x, p['norm2_g']) * (1.0 + sc2) + sh2
    x = x + g2 * swiglu(h, p['w_gu'], p['w_down'])
    return x, s_fin


def setup_inputs(seed: int = 0) -> dict:
    key = jax.random.key(seed)
    keys = list(jax.random.split(key, 32))
    f32 = jnp.float32

    def nrm(shape, scale):
        return jax.random.normal(keys.pop(), shape, f32) * scale

    def gain(shape):
        return 1.0 + nrm(shape, 0.02)

    x_prompt = nrm((BATCH, SEQ, D_MODEL), 1.0)
    x_sample = nrm((DEC_BATCH, DEC_SEQ, D_MODEL), 1.0)
    state_delta = nrm((DEC_BATCH, DEPTH, 2, H_A, DK, DV), 0.1)
    c = nrm((DEC_BATCH, D_MODEL), 1.0)
    c_ctx = nrm((D_MODEL,), 1.0)
    w_mod = nrm((DEPTH, D_MODEL, 6 * D_MODEL), 0.5 * D_MODEL ** -0.5)
    b_mod = nrm((DEPTH, 6 * D_MODEL), 0.01)
    norm1_g = gain((DEPTH, D_MODEL))
    norm2_g = gain((DEPTH, D_MODEL))
    w_in = nrm((DEPTH, D_MODEL, D_IN), D_MODEL ** -0.5)
    conv_qkv = nrm((DEPTH, CONV_K, QKV_W), CONV_K ** -0.5)
    a_log = jnp.log(jax.random.uniform(keys.pop(), (DEPTH, 2, H_A), f32, 1.0, 16.0))
    dt = jnp.exp(jax.random.uniform(keys.pop(), (DEPTH, 2, H_A), f32, math.log(1e-3), math.log(1e-1)))
    dt_bias = dt + jnp.log(-jnp.expm1(-dt))
    norm_a = gain((DEPTH, DV))
    conv_hy = nrm((DEPTH, CONV_K, (HY_ORDER + 1) * W_B), CONV_K ** -0.5)
    hy_w1 = nrm((DEPTH, HY_EMB, HY_HID), HY_EMB ** -0.5)
    hy_b1 = nrm((DEPTH, HY_HID), 0.02)
    hy_freq = gain((DEPTH, HY_HID))
    hy_w2 = nrm((DEPTH, HY_HID, HY_HID), HY_HID ** -0.5)
    hy_b2 = nrm((DEPTH, HY_HID), 0.02)
    hy_w3 = nrm((DEPTH, HY_HID, 2 * HY_ORDER * W_B), 0.02)
    hy_bias = nrm((DEPTH, HY_ORDER, W_B), 0.5)
    w_pa = nrm((DEPTH, W_A, D_MODEL), W_A ** -0.5)
    w_pb = nrm((DEPTH, W_B, D_MODEL), W_B ** -0.5)
    w_pc = nrm((DEPTH, W_C, D_MODEL), W_C ** -0.5)
    w_o = nrm((DEPTH, D_MODEL, D_MODEL), D_MODEL ** -0.5)
    w_gu = nrm((DEPTH, D_MODEL, 2 * D_FF), D_MODEL ** -0.5)
    w_down = nrm((DEPTH, D_FF, D_MODEL), D_FF ** -0.5)
    norm_f = gain((D_MODEL,))
    return {"x_prompt": x_prompt, "x_sample": x_sample, "state_delta": state_delta,
            "c": c, "c_ctx": c_ctx, "w_mod": w_mod, "b_mod": b_mod,
            "norm1_g": norm1_g, "norm2_g": norm2_g, "w_in": w_in, "conv_qkv": conv_qkv,
            "a_log": a_log, "dt_bias": dt_bias, "norm_a": norm_a, "conv_hy": conv_hy,
            "hy_w1": hy_w1, "hy_b1": hy_b1, "hy_freq": hy_freq, "hy_w2": hy_w2,
            "hy_b2": hy_b2, "hy_w3": hy_w3, "hy_bias": hy_bias, "w_pa": w_pa,
            "w_pb": w_pb, "w_pc": w_pc, "w_o": w_o, "w_gu": w_gu, "w_down": w_down,
            "norm_f": norm_f}


def reference(x_prompt, x_sample, state_delta, c, c_ctx, w_mod, b_mod, norm1_g, norm2_g,
              w_in, conv_qkv, a_log, dt_bias, norm_a, conv_hy, hy_w1, hy_b1, hy_freq,
              hy_w2, hy_b2, hy_w3, hy_bias, w_pa, w_pb, w_pc, w_o, w_gu, w_down, norm_f):
    xp = x_prompt
    xs = x_sample + grid_pos_embed(x_sample.shape[1]).astype(x_sample.dtype)[None]
    s_zero = jnp.zeros((x_prompt.shape[0], 2, H_A, DK, DV), jnp.float32)
    ctx_states = []
    for l in range(DEPTH):
        p = {'w_in': w_in[l], 'conv_qkv': conv_qkv[l], 'a_log': a_log[l], 'dt_bias': dt_bias[l],
             'norm_a': norm_a[l], 'conv_hy': conv_hy[l], 'hy_w1': hy_w1[l], 'hy_b1': hy_b1[l],
             'hy_freq': hy_freq[l], 'hy_w2': hy_w2[l], 'hy_b2': hy_b2[l], 'hy_w3': hy_w3[l],
             'hy_bias': hy_bias[l], 'w_pa': w_pa[l], 'w_pb': w_pb[l], 'w_pc': w_pc[l],
             'w_o': w_o[l], 'w_gu': w_gu[l], 'w_down': w_down[l],
             'norm1_g': norm1_g[l], 'norm2_g': norm2_g[l]}
        mod_ctx = (jax.nn.silu(c_ctx) @ w_mod[l] + b_mod[l])[None, None, :]
        mod_lat = (jax.nn.silu(c) @ w_mod[l] + b_mod[l])[:, None, :]
        xp, s_ctx = trunk_layer(xp, mod_ctx, s_zero, p)
        ctx_states.append(s_ctx)
        xs, _ = trunk_layer(xs, mod_lat, state_delta[:, l].astype(jnp.float32), p)
    y_prompt = rmsnorm(xp, norm_f)
    y_sample = rmsnorm(xs, norm_f)
    new_state_delta = jnp.stack(ctx_states, axis=1).astype(x_prompt.dtype)
    return (y_prompt, y_sample, new_state_delta)
```

```python
import math
from contextlib import ExitStack
import numpy as np
import concourse.bass as bass
import concourse.mybir as mybir
from concourse.bass_utils import run_bass_kernel_spmd

F32 = mybir.dt.float32
BF16 = mybir.dt.bfloat16
I32 = mybir.dt.int32
AF = mybir.ActivationFunctionType
ALU = mybir.AluOpType
AX = mybir.AxisListType

SAME_ENG_SYNC = True

D = 1024
DEPTH = 2
H_A = 8
DK = 64
DIN = 7200
DFF = 2816
EPS = 1e-6
CH = 128


class T:
    __slots__ = ("name", "ap", "last_write", "reads", "dsem", "dcount")

    def __init__(self, name, ap):
        self.name = name
        self.ap = ap
        self.last_write = None
        self.reads = []
        self.dsem = None
        self.dcount = 0

    def __getitem__(self, k):
        return self.ap[k]


class TV:
    def __init__(self, base, ap):
        self.base = base
        self.ap = ap
        self.name = base.name

    def __getitem__(self, k):
        return self.ap[k]


class Op:
    __slots__ = ("eng", "fn", "deps", "is_dma", "ndma", "sem_owner", "needed", "sig_sem", "sig_val", "name")

    def __init__(self, eng, fn, name=""):
        self.eng = eng
        self.fn = fn
        self.deps = []
        self.is_dma = False
        self.ndma = 0
        self.sem_owner = None
        self.needed = False
        self.sig_sem = None
        self.sig_val = 0
        self.name = name


ENGS = ("pe", "act", "dve", "pool", "sp")


def _ap(h):
    return h.ap() if callable(getattr(h, "ap", None)) else h


class Prog:
    def __init__(self, nc):
        self.nc = nc
        self.es = ExitStack()
        self.ops = {e: [] for e in ENGS}
        self.all_ops = []
        self.bar_deps = []
        self.bar_id = 0
        self.bar_seen = {e: 0 for e in ENGS}
        self.pending_dma = []
        self.uid = 0
        self.bar_pos = []

    def sb(self, name, shape, dtype, es=None):
        self.uid += 1
        h = (es or self.es).enter_context(self.nc.sbuf_tensor("%s_%d" % (name, self.uid), list(shape), dtype))
        return T(name, _ap(h))

    def ps(self, name, shape, dtype):
        h = self.es.enter_context(self.nc.psum_tensor(name, list(shape), dtype))
        return T(name, _ap(h))

    def tile(self, name, ap):
        return T(name, ap)

    def barrier(self):
        deps = []
        for e in ENGS:
            for o in reversed(self.ops[e]):
                if not o.is_dma:
                    deps.append(o)
                    break
        deps.extend(self.pending_dma)
        self.pending_dma = []
        for d in deps:
            d.needed = True
        self.bar_deps = deps
        self.bar_id += 1
        self.bar_pos.append(len(self.all_ops))

    def _record(self, op, reads, writes):
        reads = [getattr(r, "base", r) for r in reads]
        writes = [getattr(w, "base", w) for w in writes]
        deps = []
        if self.bar_seen[op.eng] != self.bar_id:
            self.bar_seen[op.eng] = self.bar_id
            deps.extend(self.bar_deps)
        for r in reads:
            if r.last_write is not None:
                deps.append(r.last_write)
        for w in writes:
            if w.last_write is not None:
                deps.append(w.last_write)
            deps.extend(w.reads)
        seen = set()
        for d in deps:
            if d is op or id(d) in seen:
                continue
            seen.add(id(d))
            if d.eng == op.eng and not d.is_dma:
                if op.eng == "pe" or not SAME_ENG_SYNC:
                    continue
            op.deps.append(d)
            d.needed = True
        for r in reads:
            r.reads.append(op)
        for w in writes:
            w.last_write = op
            w.reads = []
        self.ops[op.eng].append(op)
        self.all_ops.append(op)
        return op

    def op(self, eng, fn, reads=(), writes=(), name=""):
        return self._record(Op(eng, fn, name), list(reads), list(writes))

    def dma(self, eng, fn, reads=(), writes=(), owner=None, ndma=1, name=""):
        o = Op(eng, fn, name)
        o.is_dma = True
        o.ndma = ndma
        o.sem_owner = owner
        o.needed = True
        self.pending_dma.append(o)
        return self._record(o, list(reads), list(writes))

    def finalize(self):
        nc = self.nc
        es = self.es
        esem = {}
        for e in ("pe", "act", "dve", "pool"):
            esem[e] = es.enter_context(nc.semaphore("s_" + e))
        last_dma = {}
        qtype = {}
        for i, o in enumerate(self.all_ops):
            if o.is_dma:
                k = id(o.sem_owner)
                last_dma[k] = i
                qt = "sw" if o.eng == "pool" else "hw"
                if qtype.get(k, qt) != qt:
                    qtype[k] = "mixed"
                else:
                    qtype[k] = qt
        free = {"sw": [], "hw": [], "mixed": []}
        active = {}
        sem_final = {}
        bpos = list(self.bar_pos)
        bi = 0
        nsem = 0
        for i, o in enumerate(self.all_ops):
            while bi < len(bpos) and bpos[bi] <= i:
                b = bpos[bi]
                bi += 1
                for k in list(active.keys()):
                    ow = active[k]
                    if last_dma[k] < b:
                        if qtype[k] != "mixed":
                            free[qtype[k]].append((ow.dsem, ow.dcount))
                        del active[k]
            if o.is_dma:
                ow = o.sem_owner
                if ow.dsem is None:
                    fl = free[qtype[id(ow)]]
                    if fl and qtype[id(ow)] != "mixed":
                        ow.dsem, ow.dcount = fl.pop()
                    else:
                        ow.dsem = es.enter_context(nc.semaphore("d%d" % nsem))
                        nsem += 1
                    active[id(ow)] = ow
                ow.dcount += 16 * o.ndma
                o.sig_sem = ow.dsem
                o.sig_val = ow.dcount
                sem_final[id(ow.dsem)] = (ow.dsem, ow.dcount)
        dma_final = list(sem_final.values())
        for e in ("pe", "act", "dve", "pool"):
            c = 0
            for o in self.ops[e]:
                if o.is_dma:
                    continue
                if o.needed:
                    c += 1
                    o.sig_sem = esem[e]
                    o.sig_val = c
        self.n_sems = 4 + nsem
        engmap = {"pe": "tensor", "act": "scalar", "dve": "vector", "pool": "gpsimd", "sp": "sync"}
        with nc.Block() as block:
            for e in ENGS:
                ops = self.ops[e]
                final = (e == "sp")

                def body(eh, ops=ops, final=final):
                    seen = {}
                    for o in ops:
                        for d in o.deps:
                            key = id(d.sig_sem)
                            if seen.get(key, 0) >= d.sig_val:
                                continue
                            seen[key] = d.sig_val
                            eh.wait_ge(d.sig_sem, d.sig_val)
                        r = o.fn(eh)
                        if o.is_dma:
                            if not isinstance(r, (list, tuple)):
                                r = [r]
                            assert len(r) == o.ndma, (o.name, len(r), o.ndma)
                            for ins in r:
                                ins.then_inc(o.sig_sem, 16)
                        elif o.needed:
                            r.then_inc(o.sig_sem, 1)
                    if final:
                        for (sm, cnt) in dma_final:
                            eh.wait_ge(sm, cnt)

                getattr(block, engmap[e])(body)
        return self


class Ring:
    def __init__(self, tiles):
        self.tiles = tiles
        self.i = 0

    def next(self):
        t = self.tiles[self.i % len(self.tiles)]
        self.i += 1
        return t


import ml_dtypes
import os as _osx
NPBF = ml_dtypes.bfloat16


def tile_lhsT(w):
    K, N = w.shape
    nt = (N + 127) // 128
    wp = np.zeros((K, nt * 128), np.float32)
    wp[:, :N] = w
    kc = K // 128
    return np.ascontiguousarray(wp.reshape(kc, 128, nt, 128).transpose(2, 1, 0, 3))


def grid_pos_embed_np(n_tokens, grid_w=64):
    rows = n_tokens // grid_w
    r = np.repeat(np.arange(rows), grid_w).astype(np.float32)
    col = np.tile(np.arange(grid_w), rows).astype(np.float32)
    quarter = D // 4
    omega = (1.0 / (10000.0 ** (np.arange(quarter, dtype=np.float32) / quarter))).astype(np.float32)

    def emb(pos):
        a = pos[:, None] * omega[None, :]
        return np.concatenate([np.sin(a), np.cos(a)], axis=-1)

    return np.concatenate([emb(r), emb(col)], axis=-1).astype(np.float32)


def win_tile_cols():
    tiles = []
    for hp in range(4):
        for base in (0, 512, 1024, 1536):
            tiles.append((base + hp * 128, 128))
    tiles.append((2048, 32))
    for i in range(12):
        tiles.append((2080 + i * 128, 128))
    for i in range(4):
        tiles.append((3616 + i * 128, 128))
    for i in range(24):
        tiles.append((4128 + i * 128, 128))
    return tiles


TI_DN = 0
TI_BA = 16
TI_HY = 17
TI_FN = 29
TI_GATE = 33
N_WIN_TILES = 57


def hyena_consts(L):
    bands = 16
    t = np.linspace(0.0, 1.0, L, dtype=np.float32)[:, None]
    wpos = ((2.0 * math.pi / L) * np.arange(L, dtype=np.float32))[:, None].astype(np.float32)
    fr = np.linspace(1e-4, bands - 1, bands, dtype=np.float32)[None, :]
    zpos = np.concatenate([t, np.cos(fr * wpos), -np.sin(fr * wpos)], axis=-1).astype(np.float32)
    deltas = np.abs(np.linspace(math.log(1e-2) / 1.5, math.log(1e-2) / 0.3, 512, dtype=np.float32))
    window = np.exp(-t * deltas[None, :]).astype(np.float32)
    return np.ascontiguousarray(zpos.T), window


def dft_consts(L):
    nfp = (L // 128 + 1) * 128
    s = np.arange(L, dtype=np.float64)[:, None]
    f = np.arange(nfp, dtype=np.float64)[None, :]
    ang = np.pi * np.mod(s * f, 2 * L) / L
    valid = (f <= L)
    Fc = np.where(valid, np.cos(ang), 0.0)
    Fs = np.where(valid, np.sin(ang), 0.0)
    n = 2 * L
    cf = np.where((f == 0) | (f == L), 1.0 / n, 2.0 / n) * valid
    Gc = (Fc * cf).T
    Gs = (-Fs * cf).T
    return (Fc.astype(NPBF), Fs.astype(NPBF), np.ascontiguousarray(Gc).astype(NPBF), np.ascontiguousarray(Gs).astype(NPBF))


def fnet_consts(L):
    t = np.arange(L, dtype=np.float64)
    ang = 2 * np.pi * np.mod(np.outer(t, t), L) / L
    CL = np.cos(ang)
    SLn = -np.sin(ang)
    c = np.arange(64, dtype=np.float64)
    a64 = 2 * np.pi * np.mod(np.outer(c, c), 64) / 64
    sc = 1.0 / math.sqrt(64.0 * L)
    c64 = np.zeros((128, 128))
    s64 = np.zeros((128, 128))
    for g in range(2):
        c64[g * 64:(g + 1) * 64, g * 64:(g + 1) * 64] = np.cos(a64) * sc
        s64[g * 64:(g + 1) * 64, g * 64:(g + 1) * 64] = np.sin(a64) * sc
    return CL.astype(NPBF), SLn.astype(NPBF), c64.astype(NPBF), s64.astype(NPBF)


def mask_consts():
    i = np.arange(128)[:, None]
    j = np.arange(128)[None, :]
    m = np.stack([(i > j), (i >= j), (i < j), (i <= j)]).astype(np.float32)
    return m


M_SL, M_IL, M_SU, M_IU = 0, 1, 2, 3


def level_masks():
    i = np.arange(128)[:, None]
    j = np.arange(128)[None, :]
    ms = []
    for lv in range(7):
        bs = 2 << lv
        ms.append(((i // bs) == (j // bs)) & ((i // (bs // 2)) != (j // (bs // 2))))
    return np.ascontiguousarray(np.stack(ms, axis=1).astype(np.float32))


class Stream:
    def __init__(self, name, nseq, L, cidx, groups, is_sample):
        self.name = name
        self.nseq = nseq
        self.L = L
        self.T = nseq * L
        self.cidx = cidx
        self.nb = self.T // 512
        self.groups = groups
        self.is_sample = is_sample


class Builder:
    def __init__(self, streams, depth=DEPTH, dbg=None, skip=()):
        self.streams = streams
        self.depth = depth
        self.dbg = dbg
        self.skip = skip
        self.nc = bass.Bass("TRN2", target_bir_lowering=False)
        self.P = Prog(self.nc)
        self.dram = {}

    def din(self, name, shape, dtype=F32):
        ap = self.nc.dram_tensor(name, list(shape), dtype, kind="ExternalInput").ap()
        t = T(name, ap)
        self.dram[name] = (t, list(shape), dtype)
        return t

    def dout(self, name, shape, dtype=F32):
        ap = self.nc.dram_tensor(name, list(shape), dtype, kind="ExternalOutput").ap()
        return T(name, ap)

    def dscr(self, name, shape, dtype=F32):
        ap = self.nc.dram_tensor(name, list(shape), dtype, kind="Internal").ap()
        return T(name, ap)

    def mm(self, out_t, out_ap, lhsT_t, lhsT_ap, rhs_t, rhs_ap, start=True, stop=True):
        self.P.op("pe", lambda e: e.matmul(out_ap, lhsT=lhsT_ap, rhs=rhs_ap, start=start, stop=stop),
                  reads=[lhsT_t, rhs_t], writes=[out_t])

    def tr(self, out_t, out_ap, in_t, in_ap, ident_t=None, ident_ap=None):
        if ident_t is None:
            ident_t = self.ident
            n = in_ap.shape[0]
            ident_ap = self.ident[0:n, 0:n]
        self.P.op("pe", lambda e: e.transpose(out_ap, in_ap, ident_ap), reads=[in_t, ident_t], writes=[out_t])

    def act(self, out_t, out_ap, in_t, in_ap, func, bias=None, scale=None, extra_reads=()):
        kw = {}
        if bias is not None:
            kw["bias"] = bias
        if scale is not None:
            kw["scale"] = scale
        self.P.op("act", lambda e: e.activation(out=out_ap, in_=in_ap, func=func, **kw),
                  reads=[in_t] + list(extra_reads), writes=[out_t])

    def tt(self, eng, out_t, out_ap, a_t, a_ap, b_t, b_ap, op):
        self.P.op(eng, lambda e: e.tensor_tensor(out=out_ap, in0=a_ap, in1=b_ap, op=op),
                  reads=[a_t, b_t], writes=[out_t])

    def ts(self, out_t, out_ap, a_t, a_ap, s1, s2, op0, op1=None, extra_reads=()):
        if op1 is None:
            self.P.op("dve", lambda e: e.tensor_scalar(out=out_ap, in0=a_ap, scalar1=s1, scalar2=None, op0=op0),
                      reads=[a_t] + list(extra_reads), writes=[out_t])
        else:
            self.P.op("dve", lambda e: e.tensor_scalar(out=out_ap, in0=a_ap, scalar1=s1, scalar2=s2, op0=op0, op1=op1),
                      reads=[a_t] + list(extra_reads), writes=[out_t])

    def stt(self, out_t, out_ap, a_t, a_ap, scalar, b_t, b_ap, op0, op1, extra_reads=()):
        self.P.op("dve", lambda e: e.scalar_tensor_tensor(out=out_ap, in0=a_ap, scalar=scalar, in1=b_ap, op0=op0, op1=op1),
                  reads=[a_t, b_t] + list(extra_reads), writes=[out_t])

    def copy(self, eng, out_t, out_ap, in_t, in_ap):
        if eng == "act":
            self.P.op("act", lambda e: e.copy(out=out_ap, in_=in_ap), reads=[in_t], writes=[out_t])
        else:
            self.P.op(eng, lambda e: e.tensor_copy(out=out_ap, in_=in_ap), reads=[in_t], writes=[out_t])

    def memset(self, t, ap, val, eng="dve"):
        self.P.op(eng, lambda e: e.memset(ap, val), writes=[t])

    def load(self, out_t, out_ap, in_t, in_ap, eng="sp"):
        self.P.dma(eng, lambda e: e.dma_start(out=out_ap, in_=in_ap), reads=[in_t], writes=[out_t], owner=out_t)

    def store(self, out_t, out_ap, in_t, in_ap, eng="sp"):
        self.P.dma(eng, lambda e: e.dma_start(out=out_ap, in_=in_ap), reads=[in_t], writes=[out_t], owner=in_t)

    def pst(self):
        return self.psr.next()

    def tap(self, name, t, ap):
        if self.dbg and self.dbg.get("tap2") == name and not getattr(self, "_tapped", False):
            self._tapped = True
            self.store(self.dbg_out, self.dbg_out[:], t, ap)

    def build(self):
        P = self.P
        depth = self.depth
        for s in self.streams:
            nfp = (s.L // 128 + 1) * 128
            s.x_in = self.din("x_" + s.name, [s.T, D])
            s.y_out = self.dout("y_" + s.name, [s.T, D])
            s.xres = self.dscr("xres_" + s.name, [128, 8, s.T])
            if s.is_sample:
                s.st_in = self.din("st0_" + s.name, [depth, 2, H_A, DK, DK])
                s.pos = self.din("pos_" + s.name, [s.T, D])
            else:
                s.st_out = self.dout("st_" + s.name, [s.nseq, depth, 2, H_A, DK, DK])
            s.zposT = self.din("zposT_" + s.name, [33, s.L])
            s.window = self.din("win_" + s.name, [s.L, 512])
            s.Fc = self.din("Fc_" + s.name, [s.L, nfp], BF16)
            s.Fs = self.din("Fs_" + s.name, [s.L, nfp], BF16)
            s.Gc = self.din("Gc_" + s.name, [nfp, s.L], BF16)
            s.Gs = self.din("Gs_" + s.name, [nfp, s.L], BF16)
            s.CL = self.din("CL_" + s.name, [s.L, s.L], BF16)
            s.SLn = self.din("SLn_" + s.name, [s.L, s.L], BF16)
            s.c64 = self.din("c64_" + s.name, [128, 128], BF16)
            s.s64 = self.din("s64_" + s.name, [128, 128], BF16)
        d = {}
        d["cvecT"] = self.din("cvecT", [128, 8, 2])
        d["wmod"] = self.din("wmod", [depth, 48, 128, 8, 128])
        d["bmodT"] = self.din("bmodT", [depth, 128, 48])
        d["n1"] = self.din("n1T", [depth, 128, 8])
        d["n2"] = self.din("n2T", [depth, 128, 8])
        d["nf"] = self.din("nfT", [128, 8])
        d["win"] = self.din("win_t", [depth, N_WIN_TILES, 128, 8, 128])
        d["wp"] = self.din("wp_t", [depth, 3, 8, 128, 4, 128])
        d["wo"] = self.din("wo_t", [depth, 8, 128, 8, 128])
        d["wgu"] = self.din("wgu_t", [depth, 44, 128, 8, 128])
        d["wdn"] = self.din("wdn_t", [depth, 8, 128, 22, 128])
        d["convq"] = self.din("convqT", [depth, 64, 3, 8, 3])
        d["convh"] = self.din("convhT", [depth, 128, 12, 3])
        d["alog"] = self.din("alog_bc", [depth, 128, 16])
        d["dtb"] = self.din("dtb_bc", [depth, 128, 16])
        d["norma"] = self.din("norma_bc", [depth, 128, 64])
        d["hyw1"] = self.din("hyw1", [depth, 33, 64])
        d["hyb1"] = self.din("hyb1T", [depth, 64, 1])
        d["hyfq"] = self.din("hyfqT", [depth, 64, 1])
        d["hyw2"] = self.din("hyw2", [depth, 64, 64])
        d["hyb2"] = self.din("hyb2T", [depth, 64, 1])
        d["hyw3"] = self.din("hyw3", [depth, 64, 2048])
        d["hybias"] = self.din("hybiasT", [depth, 128, 2, 4])
        d["masks"] = self.din("masks", [4, 128, 128])
        d["ident"] = self.din("ident", [128, 128])
        d["lmask"] = self.din("lmask", [128, 7, 128])
        self.d = d
        if self.dbg:
            self.dbg_out = self.dout("dbg", self.dbg["shape"], self.dbg.get("dtype", F32))

        self.psr = Ring([P.ps("ps%d" % i, [128, 512], F32) for i in range(8)])
        TMAX = max(s.T for s in self.streams)
        self.hT = P.sb("hT", [128, 8, TMAX], BF16)
        self.merged = P.sb("merged", [128, 8, TMAX], BF16)
        self.w8 = Ring([P.sb("w8_%d" % i, [128, 8, 128], BF16) for i in range(3)])
        self.w4 = Ring([P.sb("w4_%d" % i, [128, 4, 128], BF16) for i in range(2)])
        self.ident = P.sb("ident", [128, 128], F32)
        self.onesb = P.sb("onesb", [128, 128], BF16)
        self.ones32 = P.sb("ones32", [128, 128], F32)
        self.masks = P.sb("masks", [128, 4, 128], F32)
        self.load(self.ident, self.ident[:], d["ident"], d["ident"][:])
        self.memset(self.onesb, self.onesb[:], 1.0 / 1024.0)
        self.identb2 = P.sb("identb2", [128, 2, 128], BF16)
        self.copy("dve", self.identb2, self.identb2[:, 0, :], self.ident, self.ident[:])
        self.copy("dve", self.identb2, self.identb2[:, 1, :], self.ident, self.ident[:])
        self.memset(self.ones32, self.ones32[:], 1.0)
        for i in range(4):
            self.load(self.masks, self.masks[:, i, :], d["masks"], d["masks"][i])
        self.modv = [P.sb("modv%d" % l, [128, 48, 2], F32) for l in range(depth)]
        self.modA = [P.sb("modA%d" % l, [128, 2, 8, 2], F32) for l in range(depth)]
        self.nfT = P.sb("nfT", [128, 8], F32)
        self.load(self.nfT, self.nfT[:], d["nf"], d["nf"][:])

        self.phase_mod()
        self.phase_input()
        for l in range(depth):
            for s in self.streams:
                self.phase_norm1(l, s)
                self.phase_mix(l, s)
                self.phase_out_ffn(l, s)
        self.phase_final()
        if self.dbg:
            self.dbg["fn"](self)
        P.finalize()
        P.es.close()
        return self.nc

    def wload8(self, dram_t, dram_ap):
        w = self.w8.next()
        self.load(w, w[:], dram_t, dram_ap, eng="pool")
        return w

    def phase_mod(self):
        P = self.P
        d = self.d
        with ExitStack() as ph:
            cv = P.sb("cv", [128, 8, 2], F32, ph)
            scv = P.sb("scv", [128, 8, 2], F32, ph)
            wm = Ring([P.sb("wm%d" % i, [128, 8, 128], F32, ph) for i in range(3)])
            bm = P.sb("bm", [128, 48], F32, ph)
            n12 = P.sb("n12", [128, 2, 8], F32, ph)
            self.load(cv, cv[:], d["cvecT"], d["cvecT"][:])
            self.act(scv, scv[:], cv, cv[:], AF.Silu)
            for l in range(self.depth):
                modv = self.modv[l]
                self.load(bm, bm[:], d["bmodT"], d["bmodT"][l])
                self.load(n12, n12[:, 0, :], d["n1"], d["n1"][l])
                self.load(n12, n12[:, 1, :], d["n2"], d["n2"][l])
                for c in range(48):
                    w = wm.next()
                    self.load(w, w[:], d["wmod"], d["wmod"][l, c])
                    ps = self.pst()
                    for kc in range(8):
                        self.mm(ps, ps[:, 0:2], w, w[:, kc, :], scv, scv[:, kc, :], kc == 0, kc == 7)
                    self.ts(modv, modv[:, c, :], ps, ps[:, 0:2], bm[:, c:c + 1], None, ALU.add, extra_reads=[bm])
                mA = self.modA[l]
                for sub in range(2):
                    sc0 = 8 + 24 * sub
                    for j in range(2):
                        self.ts(mA, mA[:, sub, :, j], modv, modv[:, sc0:sc0 + 8, j], 1.0, None, ALU.add)
                        self.tt("dve", mA, mA[:, sub, :, j], mA, mA[:, sub, :, j], n12, n12[:, sub, :], ALU.mult)
        P.barrier()

    def phase_input(self):
        P = self.P
        with ExitStack() as ph:
            xin = Ring([P.sb("xin%d" % i, [128, D], F32, ph) for i in range(2)])
            pin = Ring([P.sb("pin%d" % i, [128, D], F32, ph) for i in range(2)])
            xo = Ring([P.sb("xo%d" % i, [128, 8, 128], F32, ph) for i in range(2)])
            for s in self.streams:
                for tt in range(s.T // 128):
                    xt = xin.next()
                    self.load(xt, xt[:], s.x_in, s.x_in[tt * 128:(tt + 1) * 128, :])
                    if s.is_sample:
                        pt = pin.next()
                        self.load(pt, pt[:], s.pos, s.pos[tt * 128:(tt + 1) * 128, :])
                        self.tt("dve", xt, xt[:], xt, xt[:], pt, pt[:], ALU.add)
                    xot = xo.next()
                    for half in range(2):
                        ps = self.pst()
                        for c4 in range(4):
                            c = half * 4 + c4
                            self.tr(ps, ps[:, c4 * 128:(c4 + 1) * 128], xt, xt[:, c * 128:(c + 1) * 128])
                        self.copy("act" if half else "dve", xot, xot[:, half * 4:half * 4 + 4, :], ps,
                                  ps[:].rearrange("p (c t) -> p c t", c=4))
                    self.store(s.xres, s.xres[:, :, tt * 128:(tt + 1) * 128], xot, xot[:])
        P.barrier()

    def rstd_block(self, xb, xb_ap, sqr, rstd):
        ps = self.pst()
        for c in range(8):
            sq = sqr.next()
            self.act(sq, sq[:], xb, xb_ap[:, c, :], AF.Square)
            self.mm(ps, ps[:], self.onesb, self.onesb[:], sq, sq[:], c == 0, c == 7)
        self.act(rstd, rstd[:], ps, ps[:], AF.Sqrt, bias=EPS, scale=1.0)
        self.P.op("dve", lambda e: e.reciprocal(out=rstd[:], in_=rstd[:]), reads=[rstd], writes=[rstd])

    def norm_block(self, l, s, sub, xb, blk, sqr, rstd, tmpr):
        j = s.cidx
        self.rstd_block(xb, xb[:], sqr, rstd)
        mA = self.modA[l]
        modv = self.modv[l]
        sh0 = 24 * sub
        for c in range(8):
            tmp = tmpr.next()
            self.tt("dve", tmp, tmp[:], xb, xb[:, c, :], rstd, rstd[:], ALU.mult)
            self.act(self.hT, self.hT[:, c, blk * 512:(blk + 1) * 512], tmp, tmp[:], AF.Identity,
                     bias=modv[:, sh0 + c, j:j + 1], scale=mA[:, sub, c, j:j + 1], extra_reads=[mA, modv])

    def phase_norm1(self, l, s):
        P = self.P
        with ExitStack() as ph:
            xbr = Ring([P.sb("xb%d" % i, [128, 8, 512], F32, ph) for i in range(2)])
            sqr = Ring([P.sb("sq%d" % i, [128, 512], BF16, ph) for i in range(2)])
            tmpr = Ring([P.sb("ntmp%d" % i, [128, 512], F32, ph) for i in range(2)])
            rstd = P.sb("rstd", [128, 512], F32, ph)
            for blk in range(s.nb):
                xb = xbr.next()
                self.load(xb, xb[:], s.xres, s.xres[:, :, blk * 512:(blk + 1) * 512])
                self.norm_block(l, s, 0, xb, blk, sqr, rstd, tmpr)
        P.barrier()

    def proj_fm(self, l, ti, s, evac, m0=0, m1=128):
        w = self.wload8(self.d["win"], self.d["win"][l, ti])
        for blk in range(s.nb):
            ps = self.pst()
            for kc in range(8):
                self.mm(ps, ps[0:m1 - m0, :], w, w[:, kc, m0:m1], self.hT, self.hT[:, kc, blk * 512:(blk + 1) * 512], kc == 0, kc == 7)
            evac(ps, blk)

    def merge_branch(self, l, s, br, y_t):
        for o in range(8):
            wg = self.wload8(self.d["win"], self.d["win"][l, TI_GATE + br * 8 + o])
            wp = self.w4.next()
            self.load(wp, wp[:], self.d["wp"], self.d["wp"][l, br, o], eng="pool")
            for blk in range(s.nb):
                sl = slice(blk * 512, (blk + 1) * 512)
                ps1 = self.pst()
                for kc in range(8):
                    self.mm(ps1, ps1[:], wg, wg[:, kc, :], self.hT, self.hT[:, kc, sl], kc == 0, kc == 7)
                ps2 = self.pst()
                for kc in range(4):
                    self.mm(ps2, ps2[:], wp, wp[:, kc, :], y_t, y_t[:, kc, sl], kc == 0, kc == 3)
                sig = self.sigr.next()
                self.act(sig, sig[:], ps1, ps1[:], AF.Sigmoid)
                if br == 0:
                    self.tt("dve", self.merged, self.merged[:, o, sl], sig, sig[:], ps2, ps2[:], ALU.mult)
                else:
                    self.tt("dve", sig, sig[:], sig, sig[:], ps2, ps2[:], ALU.mult)
                    self.tt("pool", self.merged, self.merged[:, o, sl], self.merged, self.merged[:, o, sl], sig, sig[:], ALU.add)

    def phase_mix(self, l, s):
        P = self.P
        with ExitStack() as ph:
            self.sigr = Ring([P.sb("sig%d" % i, [128, 512], F32, ph) for i in range(2)])
            y = P.sb("ybr", [128, 4, s.T], BF16, ph)
            with ExitStack() as ph2:
                if "delta" in self.skip:
                    self.memset(y, y[:], 0.0)
                else:
                    self.mix_delta(l, s, y, ph2)
            P.barrier()
            self.merge_branch(l, s, 0, y)
            P.barrier()
            for (ct0, nct) in s.groups:
                with ExitStack() as ph2:
                    if "hyena" in self.skip:
                        self.memset(y, y[:], 0.0)
                    else:
                        self.mix_hyena(l, s, y, ct0, nct, ph2)
                P.barrier()
            if self.dbg and self.dbg.get("tap") == ("yb", l, s.name):
                self.store(self.dbg_out, self.dbg_out[:], y, y[:])
            self.merge_branch(l, s, 1, y)
            P.barrier()
            for (ct0, nct) in s.groups:
                with ExitStack() as ph2:
                    self.mix_fnet(l, s, y, ct0, nct, ph2)
                P.barrier()
            if self.dbg and self.dbg.get("tap") == ("yc", l, s.name):
                self.store(self.dbg_out, self.dbg_out[:], y, y[:])
            self.merge_branch(l, s, 2, y)
        P.barrier()

    def mix_fnet(self, l, s, y, ct0, nct, ph):
        P = self.P
        L = s.L
        nt = L // 128
        W = nct * 128
        xc = P.sb("xc", [128, nct, s.T], BF16, ph)
        c64 = P.sb("c64", [128, 128], BF16, ph)
        s64 = P.sb("s64", [128, 128], BF16, ph)
        self.load(c64, c64[:], s.c64, s.c64[:])
        self.load(s64, s64[:], s.s64, s.s64[:])
        for ci in range(nct):
            self.proj_fm(l, TI_FN + ct0 + ci, s,
                         lambda ps, blk, ci=ci: self.copy("act", xc, xc[:, ci, blk * 512:(blk + 1) * 512], ps, ps[:]))
        U = P.sb("U", [128, nt, 2, W], BF16, ph)
        NW = 256
        dftc = Ring([P.sb("dftc%d" % i, [128, nt, NW], BF16, ph) for i in range(2)])
        dfts = Ring([P.sb("dfts%d" % i, [128, nt, NW], BF16, ph) for i in range(2)])
        for q in range(s.nseq):
            t0 = q * L
            for tt in range(nt):
                for cs, mat in ((0, c64), (1, s64)):
                    ps = self.pst()
                    for ci in range(nct):
                        self.mm(ps, ps[:, ci * 128:(ci + 1) * 128], xc, xc[:, ci, t0 + tt * 128:t0 + (tt + 1) * 128], mat, mat[:])
                    self.copy("act" if cs else "dve", U, U[:, tt, cs, :], ps, ps[:, 0:W])
            for nbk in range(L // NW):
                cm = dftc.next()
                sm = dfts.next()
                self.load(cm, cm[:], s.CL, s.CL[:, nbk * NW:(nbk + 1) * NW].rearrange("(k p) n -> p k n", p=128))
                self.load(sm, sm[:], s.SLn, s.SLn[:, nbk * NW:(nbk + 1) * NW].rearrange("(k p) n -> p k n", p=128))
                for ci in range(nct):
                    ps = self.pst()
                    for tt in range(nt):
                        self.mm(ps, ps[:, 0:NW], U, U[:, tt, 0, ci * 128:(ci + 1) * 128], cm, cm[:, tt, :], tt == 0, False)
                        self.mm(ps, ps[:, 0:NW], U, U[:, tt, 1, ci * 128:(ci + 1) * 128], sm, sm[:, tt, :], False, tt == nt - 1)
                    self.copy("act" if ci % 2 else "dve", y, y[:, ct0 + ci, t0 + nbk * NW:t0 + (nbk + 1) * NW], ps, ps[:, 0:NW])

    def sin_reduce(self, out_t, out_ap, in_t, in_ap, ti, ti_ap, tf, tf_ap):
        P = self.P
        inv = 1.0 / (2.0 * math.pi)
        P.op("dve", lambda e: e.tensor_scalar(out=ti_ap, in0=in_ap, scalar1=inv, scalar2=None, op0=ALU.mult),
             reads=[in_t], writes=[ti])
        self.copy("dve", tf, tf_ap, ti, ti_ap)
        self.stt(tf, tf_ap, tf, tf_ap, -2.0 * math.pi, in_t, in_ap, ALU.mult, ALU.add)
        self.ts(tf, tf_ap, tf, tf_ap, math.pi, -math.pi, ALU.min, ALU.max)
        self.act(out_t, out_ap, tf, tf_ap, AF.Sin)

    def hy_gate(self, l, s, which, ct, raw, dst, dst_ap, cw):
        L = s.L
        self.proj_fm(l, TI_HY + which * 4 + ct, s,
                     lambda ps, blk: self.copy("act", raw, raw[:, blk * 512:(blk + 1) * 512], ps, ps[:]))
        wi = which * 4 + ct
        for q in range(s.nseq):
            a, b = q * L, (q + 1) * L
            self.ts(dst, dst_ap[:, a:b], raw, raw[:, a:b], cw[:, wi, 1:2], None, ALU.mult, extra_reads=[cw])
            self.stt(dst, dst_ap[:, a + 1:b], raw, raw[:, a:b - 1], cw[:, wi, 0:1], dst, dst_ap[:, a + 1:b], ALU.mult, ALU.add, extra_reads=[cw])
            self.stt(dst, dst_ap[:, a:b - 1], raw, raw[:, a + 1:b], cw[:, wi, 2:3], dst, dst_ap[:, a:b - 1], ALU.mult, ALU.add, extra_reads=[cw])

    def mix_hyena(self, l, s, y, ct0, nct, ph):
        P = self.P
        d = self.d
        L = s.L
        nt = L // 128
        nf = nt + 1
        T_ = s.T
        W = nct * 128
        cw = P.sb("hcw", [128, 12, 3], F32, ph)
        self.load(cw, cw[:], d["convh"], d["convh"][l])
        hb = P.sb("hbias", [128, 2, 4], F32, ph)
        self.load(hb, hb[:], d["hybias"], d["hybias"][l])
        raw = P.sb("hraw", [128, T_], F32, ph)
        gate = P.sb("hgate", [128, nct, T_], BF16, ph)
        z = P.sb("hz", [128, nct, T_], BF16, ph)
        zr = P.sb("hzr", [128, T_], F32, ph)
        for ci in range(nct):
            self.hy_gate(l, s, 2, ct0 + ci, raw, zr, zr[:], cw)
            self.copy("act", z, z[:, ci, :], zr, zr[:])
        hsd = P.sb("hsd", [128, nt, 2, 2, W], BF16, ph)
        with ExitStack() as pf:
            w1 = P.sb("hw1", [33, 64], F32, pf)
            w2 = P.sb("hw2", [64, 64], F32, pf)
            w3 = P.sb("hw3", [64, 2048], F32, pf)
            b1 = P.sb("hb1", [64, 1], F32, pf)
            b2 = P.sb("hb2", [64, 1], F32, pf)
            fq = P.sb("hfq", [64, 1], F32, pf)
            zp = P.sb("hzp", [33, L], F32, pf)
            h1 = P.sb("hh1", [64, L], F32, pf)
            h2 = P.sb("hh2", [64, L], F32, pf)
            ti = P.sb("hti", [64, 512], I32, pf)
            tf = P.sb("htf", [64, 512], F32, pf)
            ta = P.sb("hta", [64, 512], F32, pf)
            win = Ring([P.sb("hwin%d" % i, [128, W], F32, pf) for i in range(2)])
            hf = Ring([P.sb("hf%d" % i, [128, 4, W], F32, pf) for i in range(2)])
            self.load(w1, w1[:], d["hyw1"], d["hyw1"][l])
            self.load(w2, w2[:], d["hyw2"], d["hyw2"][l])
            self.load(w3, w3[:], d["hyw3"], d["hyw3"][l])
            self.load(b1, b1[:], d["hyb1"], d["hyb1"][l])
            self.load(b2, b2[:], d["hyb2"], d["hyb2"][l])
            self.load(fq, fq[:], d["hyfq"], d["hyfq"][l])
            self.load(zp, zp[:], s.zposT, s.zposT[:])
            wb = min(L, 512)
            for (src, wsrc, bsrc, dst, kk) in ((zp, w1, b1, h1, 33), (h1, w2, b2, h2, 64)):
                for blk in range(L // wb):
                    sl = slice(blk * wb, (blk + 1) * wb)
                    ps = self.pst()
                    self.mm(ps, ps[0:64, 0:wb], wsrc, wsrc[0:kk, :], src, src[0:kk, sl])
                    self.ts(ta, ta[:, 0:wb], ps, ps[0:64, 0:wb], bsrc[:, 0:1], fq[:, 0:1], ALU.add, ALU.mult, extra_reads=[bsrc, fq])
                    self.sin_reduce(dst, dst[:, sl], ta, ta[:, 0:wb], ti, ti[:, 0:wb], tf, tf[:, 0:wb])
            for tt in range(nt):
                wn = win.next()
                self.load(wn, wn[:], s.window, s.window[tt * 128:(tt + 1) * 128, ct0 * 128:ct0 * 128 + W])
                h = hf.next()
                for fi in range(4):
                    ps = self.pst()
                    c0 = fi * 512 + ct0 * 128
                    self.mm(ps, ps[:, 0:W], h2, h2[:, tt * 128:(tt + 1) * 128], w3, w3[:, c0:c0 + W])
                    self.tt("dve", h, h[:, fi, :], ps, ps[:, 0:W], wn, wn[:], ALU.mult)
                for o in range(2):
                    self.tt("dve", hsd, hsd[:, tt, o, 0, :], h, h[:, 2 * o, :], h, h[:, 2 * o + 1, :], ALU.add)
                    self.tt("pool", hsd, hsd[:, tt, o, 1, :], h, h[:, 2 * o + 1, :], h, h[:, 2 * o, :], ALU.subtract)
        P.barrier()
        Hre = P.sb("Hre", [128, nf, W], F32, ph)
        Him = P.sb("Him", [128, nf, W], F32, ph)
        ztm = P.sb("ztm", [128, nt, W], BF16, ph)
        Yre = P.sb("Yre", [128, nf, W], BF16, ph)
        Yim = P.sb("Yim", [128, nf, W], BF16, ph)
        fcr = Ring([P.sb("fcr%d" % i, [128, nt, 128], BF16, ph) for i in range(2)])
        fsr = Ring([P.sb("fsr%d" % i, [128, nt, 128], BF16, ph) for i in range(2)])
        NW = 128
        gcr = Ring([P.sb("gcr%d" % i, [128, nf, NW], BF16, ph) for i in range(2)])
        gsr = Ring([P.sb("gsr%d" % i, [128, nf, NW], BF16, ph) for i in range(2)])
        tmpz = Ring([P.sb("tmpz%d" % i, [128, W], F32, ph) for i in range(4)])
        zf = P.sb("hzf", [128, 128], F32, ph)

        def ld_f(ft):
            fcm = fcr.next()
            fsm = fsr.next()
            self.load(fcm, fcm[:], s.Fc, s.Fc[:, ft * 128:(ft + 1) * 128].rearrange("(k p) n -> p k n", p=128))
            self.load(fsm, fsm[:], s.Fs, s.Fs[:, ft * 128:(ft + 1) * 128].rearrange("(k p) n -> p k n", p=128))
            return fcm, fsm

        for o in range(2):
            for ft in range(nf):
                fcm, fsm = ld_f(ft)
                for (mat, sd, dst) in ((fcm, 0, Hre), (fsm, 1, Him)):
                    ps = self.pst()
                    for tt in range(nt):
                        self.mm(ps, ps[:, 0:W], mat, mat[:, tt, :], hsd, hsd[:, tt, o, sd, :], tt == 0, tt == nt - 1)
                    self.copy("act" if sd else "dve", dst, dst[:, ft, :], ps, ps[:, 0:W])
            for ci in range(nct):
                self.hy_gate(l, s, o, ct0 + ci, raw, gate, gate[:, ci, :], cw)
            for q in range(s.nseq):
                t0 = q * L
                for tt in range(nt):
                    ps = self.pst()
                    for ci in range(nct):
                        self.copy("dve", zf, zf[:], z, z[:, ci, t0 + tt * 128:t0 + (tt + 1) * 128])
                        self.tr(ps, ps[:, ci * 128:(ci + 1) * 128], zf, zf[:])
                    self.copy("act" if tt % 2 else "dve", ztm, ztm[:, tt, :], ps, ps[:, 0:W])
                for ft in range(nf):
                    fcm, fsm = ld_f(ft)
                    pc = self.pst()
                    for tt in range(nt):
                        self.mm(pc, pc[:, 0:W], fcm, fcm[:, tt, :], ztm, ztm[:, tt, :], tt == 0, tt == nt - 1)
                    pz = self.pst()
                    for tt in range(nt):
                        self.mm(pz, pz[:, 0:W], fsm, fsm[:, tt, :], ztm, ztm[:, tt, :], tt == 0, tt == nt - 1)
                    a1 = tmpz.next(); a2 = tmpz.next(); a3 = tmpz.next(); a4 = tmpz.next()
                    self.tt("dve", a1, a1[:], pc, pc[:, 0:W], Hre, Hre[:, ft, :], ALU.mult)
                    self.tt("dve", a2, a2[:], pz, pz[:, 0:W], Him, Him[:, ft, :], ALU.mult)
                    self.tt("pool", Yre, Yre[:, ft, :], a1, a1[:], a2, a2[:], ALU.add)
                    self.tt("dve", a3, a3[:], pc, pc[:, 0:W], Him, Him[:, ft, :], ALU.mult)
                    self.tt("dve", a4, a4[:], pz, pz[:, 0:W], Hre, Hre[:, ft, :], ALU.mult)
                    self.tt("pool", Yim, Yim[:, ft, :], a3, a3[:], a4, a4[:], ALU.subtract)
                for nbk in range(L // NW):
                    gc = gcr.next()
                    gs = gsr.next()
                    self.load(gc, gc[:], s.Gc, s.Gc[:, nbk * NW:(nbk + 1) * NW].rearrange("(k p) n -> p k n", p=128))
                    self.load(gs, gs[:], s.Gs, s.Gs[:, nbk * NW:(nbk + 1) * NW].rearrange("(k p) n -> p k n", p=128))
                    sl = slice(t0 + nbk * NW, t0 + (nbk + 1) * NW)
                    for ci in range(nct):
                        ps = self.pst()
                        for ft in range(nf):
                            self.mm(ps, ps[:, 0:NW], Yre, Yre[:, ft, ci * 128:(ci + 1) * 128], gc, gc[:, ft, :], ft == 0, False)
                            self.mm(ps, ps[:, 0:NW], Yim, Yim[:, ft, ci * 128:(ci + 1) * 128], gs, gs[:, ft, :], False, ft == nf - 1)
                        a1 = tmpz.next()
                        self.stt(a1, a1[:, 0:NW], z, z[:, ci, sl], hb[:, o, ct0 + ci:ct0 + ci + 1], ps, ps[:, 0:NW], ALU.mult, ALU.add, extra_reads=[hb])
                        if o == 0:
                            self.tt("dve", z, z[:, ci, sl], a1, a1[:, 0:NW], gate, gate[:, ci, sl], ALU.mult)
                        else:
                            self.tt("dve", y, y[:, ct0 + ci, sl], a1, a1[:, 0:NW], gate, gate[:, ci, sl], ALU.mult)

    def mix_delta(self, l, s, y, ph):
        P = self.P
        d = self.d
        L = s.L
        T_ = s.T
        NT = T_ // 128
        cps = L // 128
        cwq = P.sb("dcw", [64, 3, 8, 3], F32, ph)
        self.load(cwq, cwq[:], d["convq"], d["convq"][l])
        alog = P.sb("dalog", [128, 16], F32, ph)
        dtb = P.sb("ddtb", [128, 16], F32, ph)
        norma = P.sb("dnorma", [128, 64], F32, ph)
        self.load(alog, alog[:], d["alog"], d["alog"][l])
        self.load(dtb, dtb[:], d["dtb"], d["dtb"][l])
        self.load(norma, norma[:], d["norma"], d["norma"][l])
        ba = P.sb("dba", [128, NT, 32], F32, ph)
        beta = P.sb("dbeta", [128, NT, 16], F32, ph)
        nbeta = P.sb("dnbeta", [128, NT, 16], F32, ph)
        g = P.sb("dg", [128, NT, 16], F32, ph)
        wba = self.wload8(d["win"], d["win"][l, TI_BA])
        for tt in range(NT):
            ps = self.pst()
            for kc in range(8):
                self.mm(ps, ps[:, 0:32], self.hT, self.hT[:, kc, tt * 128:(tt + 1) * 128], wba, wba[:, kc, 0:32], kc == 0, kc == 7)
            self.copy("act" if tt % 2 else "dve", ba, ba[:, tt, :], ps, ps[:, 0:32])
        self.act(beta, beta[:], ba, ba[:, :, 0:16], AF.Sigmoid)
        self.ts(nbeta, nbeta[:], beta, beta[:], -1.0, None, ALU.mult)
        self.tt("dve", g, g[:], ba, ba[:, :, 16:32], dtb, dtb[:, None, :].to_broadcast([128, NT, 16]), ALU.add)
        self.act(g, g[:], g, g[:], AF.Exp)
        self.act(g, g[:], g, g[:], AF.Ln, bias=1.0, scale=1.0)
        self.act(alog, alog[:], alog, alog[:], AF.Exp)
        self.stt(g, g[:], g, g[:], -1.0, alog, alog[:, None, :].to_broadcast([128, NT, 16]), ALU.mult, ALU.mult)

        self.tap('g', g, g[:])
        self.tap('beta', beta, beta[:])
        raw = P.sb("draw", [64, T_], F32, ph)
        qf = P.sb("dq", [64, T_], F32, ph)
        kf = P.sb("dk", [64, T_], F32, ph)
        vf = P.sb("dv", [64, T_], F32, ph)
        zf = P.sb("dz", [64, T_], F32, ph)
        qb = P.sb("dqb", [64, T_], BF16, ph)
        kb = P.sb("dkb", [64, T_], BF16, ph)
        osum = P.sb("dosum", [128, NT, 64], F32, ph)
        ytm = P.sb("dytm", [128, NT, 128], F32, ph)
        sqt = P.sb("dsq", [64, 512], F32, ph)
        rn = P.sb("drn", [64, 512], F32, ph)
        KSLOT = int(_osx.environ.get("KSLOT", "2" if s.is_sample else "4"))
        lmask = P.sb("dlmask", [128, 7, 128], F32, ph)
        self.load(lmask, lmask[:], d["lmask"], d["lmask"][:])
        osum2 = P.sb("dosum2", [128, NT, 64], F32, ph)
        r_t1 = Ring([P.sb("dt1%d" % i, [128, 64], F32, ph) for i in range(2)])

        def mkslot(i):
            R = {}
            def a(name, shape, dt):
                R[name] = P.sb("d%s_%d" % (name, i), shape, dt, ph)
            a("S", [64, 64], F32); a("Sb", [64, 64], BF16)
            a("gbc", [128, 128], F32); a("dcol", [128, 4], F32); a("e3", [128, 4], F32)
            a("dabs", [128, 128], F32); a("Dm", [128, 128], F32); a("Ds", [128, 128], F32); a("Di", [128, 128], F32)
            a("P0", [128, 2, 128], F32); a("NTk", [128, 7, 128], BF16); a("qkT", [128, 128], BF16)
            a("kv", [128, 128], F32); a("X", [128, 128], BF16); a("Xf", [128, 128], F32)
            a("kg", [128, 64], BF16); a("wT", [64, 128], BF16); a("vn", [128, 64], BF16); a("t1", [128, 64], F32)
            a("bw", [128, 1], F32); a("tmp", [128, 128], F32)
            nbk = 8 // KSLOT
            bk = self.psr.tiles[nbk * i:nbk * (i + 1)]
            names = ["psd", "psk", "pkv", "pst_", "psw", "ps2", "psx", "pw", "psv", "pso", "pss"]
            if nbk >= 4:
                amap = {"psd": 0, "psk": 1, "pkv": 2, "pst_": 3, "psw": 0, "ps2": 2, "psx": 1, "pw": 3, "psv": 0, "pso": 1, "pss": 2}
            else:
                amap = {"psd": 0, "psk": 1, "pkv": 0, "pst_": 1, "psw": 0, "ps2": 1, "psx": 0, "pw": 1, "psv": 0, "pso": 1, "pss": 0}
            R["ph"] = {n: bk[amap[n] % nbk] for n in names}
            R["TT"] = Ring([P.sb("dTT%d_%d" % (j, i), [128, 2, 128], BF16, ph) for j in range(2)])
            a("WW", [128, 2, 128], BF16)
            return R
        slots = [mkslot(i) for i in range(KSLOT)]
        masks = self.masks
        sq2 = P.sb("dsq2", [128, NT, 64], F32, ph)
        ssq = P.sb("dssq", [128, NT], F32, ph)

        import os as _os
        _NH = int(_os.environ.get('DN_HEADS', '8'))
        _ST = int(_os.environ.get('DN_STAGE', '9'))
        for h in range(_NH):
            hp, half = h // 2, h % 2
            m0, m1 = half * 64, half * 64 + 64
            for which, dst in ((0, qf), (1, kf), (2, vf)):
                self.proj_fm(l, TI_DN + hp * 4 + which, s,
                             lambda ps, blk: self.copy("act", raw, raw[:, blk * 512:(blk + 1) * 512], ps, ps[0:64, :]), m0, m1)
                for q in range(s.nseq):
                    a, b = q * L, (q + 1) * L
                    self.ts(dst, dst[:, a:b], raw, raw[:, a:b], cwq[:, which, h, 1:2], None, ALU.mult, extra_reads=[cwq])
                    self.stt(dst, dst[:, a + 1:b], raw, raw[:, a:b - 1], cwq[:, which, h, 0:1], dst, dst[:, a + 1:b], ALU.mult, ALU.add, extra_reads=[cwq])
                    self.stt(dst, dst[:, a:b - 1], raw, raw[:, a + 1:b], cwq[:, which, h, 2:3], dst, dst[:, a:b - 1], ALU.mult, ALU.add, extra_reads=[cwq])
                self.act(dst, dst[:], dst, dst[:], AF.Silu)
            self.proj_fm(l, TI_DN + hp * 4 + 3, s,
                         lambda ps, blk: self.copy("act", zf, zf[:, blk * 512:(blk + 1) * 512], ps, ps[0:64, :]), m0, m1)
            for (x, xb_, sc) in ((qf, qb, 64.0), (kf, kb, 1.0)):
                for blk in range(s.nb):
                    sl = slice(blk * 512, (blk + 1) * 512)
                    self.tt("dve", sqt, sqt[:], x, x[:, sl], x, x[:, sl], ALU.mult)
                    ps = self.pst()
                    self.mm(ps, ps[0:64, :], self.ones32, self.ones32[0:64, 0:64], sqt, sqt[:])
                    self.act(rn, rn[:], ps, ps[0:64, :], AF.Sqrt, bias=EPS * sc, scale=sc)
                    P.op("dve", lambda e: e.reciprocal(out=rn[:], in_=rn[:]), reads=[rn], writes=[rn])
                    self.tt("dve", x, x[:, sl], x, x[:, sl], rn, rn[:], ALU.mult)
                self.copy("act", xb_, xb_[:], x, x[:])
            self.tap('q', qf, qf[:])
            self.tap('k', kf, kf[:])
            self.tap('v', vf, vf[:])
            def chain(dr, q, R):
                col = dr * 8 + h
                if dr == 0:
                    cm, rm, sm, im = M_IU, M_SL, M_SL, M_IL
                else:
                    cm, rm, sm, im = M_IL, M_SU, M_SU, M_IU
                S, Sbb = R["S"], R["Sb"]
                oacc = osum if dr == 0 else osum2
                if s.is_sample:
                    self.load(S, S[:], s.st_in, s.st_in[l, dr, h])
                else:
                    self.memset(S, S[:], 0.0)
                self.copy("act", Sbb, Sbb[:], S, S[:])
                order = range(cps) if dr == 0 else range(cps - 1, -1, -1)
                for cl in order:
                    c = q * cps + cl
                    tsl = slice(c * 128, (c + 1) * 128)
                    gcol = g[:, c, col:col + 1]
                    bcol = beta[:, c, col:col + 1]
                    nbcol = nbeta[:, c, col:col + 1]
                    gbc, dcol, e3, dabs, Dm, Ds, Di = R["gbc"], R["dcol"], R["e3"], R["dabs"], R["Dm"], R["Ds"], R["Di"]
                    P0, NTk, qkT, kv, X, Xf = R["P0"], R["NTk"], R["qkT"], R["kv"], R["X"], R["Xf"]
                    kg, wT, vn, t1, bw, tmp, WW = R["kg"], R["wT"], R["vn"], R["t1"], R["bw"], R["tmp"], R["WW"]
                    self.copy("pool", gbc, gbc[:], g, gcol.to_broadcast([128, 128]))
                    yield
                    psd = R["ph"]["psd"]
                    self.mm(psd, psd[:, 0:128], gbc, gbc[:], masks, masks[:, cm, :])
                    self.mm(psd, psd[:, 128:129], masks, masks[:, cm, :], g, gcol)
                    self.mm(psd, psd[:, 129:130], masks, masks[:, rm, :], g, gcol)
                    self.mm(psd, psd[:, 130:131], self.ones32, self.ones32[:], g, gcol)
                    yield
                    self.copy("dve", dcol, dcol[:, 0:3], psd, psd[:, 128:131])
                    yield
                    self.act(e3, e3[:, 0:3], dcol, dcol[:, 0:3], AF.Exp)
                    self.ts(dabs, dabs[:], psd, psd[:, 0:128], dcol[:, 0:1], 0.0, ALU.subtract, ALU.max, extra_reads=[dcol])
                    yield
                    self.act(Dm, Dm[:], dabs, dabs[:], AF.Exp, scale=-1.0)
                    yield
                    self.tt("pool", Ds, Ds[:], Dm, Dm[:], masks, masks[:, sm, :], ALU.mult)
                    self.tt("pool", Di, Di[:], Dm, Dm[:], masks, masks[:, im, :], ALU.mult)
                    psk = R["ph"]["psk"]
                    self.mm(psk, psk[:, 0:128], kb, kb[:, tsl], kb, kb[:, tsl])
                    self.mm(psk, psk[:, 128:256], qb, qb[:, tsl], kb, kb[:, tsl])
                    yield
                    self.stt(P0, P0[:, 0, :], psk, psk[:, 0:128], nbcol, Ds, Ds[:], ALU.mult, ALU.mult, extra_reads=[nbeta])
                    self.tt("dve", P0, P0[:, 1, :], psk, psk[:, 128:256], Di, Di[:], ALU.mult)
                    yield
                    pst_ = R["ph"]["pst_"]
                    self.tr(pst_, pst_[:, 0:128], P0, P0[:, 0, :])
                    self.tr(pst_, pst_[:, 128:256], P0, P0[:, 1, :])
                    pkv = R["ph"]["pkv"]
                    self.tr(pkv, pkv[:, 0:64], kf, kf[:, tsl])
                    self.tr(pkv, pkv[:, 64:128], vf, vf[:, tsl])
                    yield
                    self.tt("dve", NTk, NTk[:], pst_, pst_[:, 0:128][:, None, :].to_broadcast([128, 7, 128]), lmask, lmask[:], ALU.mult)
                    self.copy("dve", qkT, qkT[:], pst_, pst_[:, 128:256])
                    self.copy("act", kv, kv[:], pkv, pkv[:, 0:128])
                    self.tt("dve", bw, bw[:], beta, bcol, e3, e3[:, 0:1], ALU.mult)
                    yield
                    self.ts(X, X[:, 0:64], kv, kv[:, 64:128], bcol, None, ALU.mult, extra_reads=[beta])
                    self.ts(X, X[:, 64:128], kv, kv[:, 0:64], bw[:, 0:1], None, ALU.mult, extra_reads=[bw])
                    self.ts(kg, kg[:], kv, kv[:, 0:64], e3[:, 1:2], None, ALU.mult, extra_reads=[e3])
                    yield
                    TT = self.identb2
                    for lev in range(7):
                        psw = R["ph"]["psw"]
                        self.mm(psw, psw[:, 0:128], NTk, NTk[:, lev, :], TT, TT[:, 0, :])
                        self.mm(psw, psw[:, 128:256], TT, TT[:, 0, :], NTk, NTk[:, lev, :])
                        yield
                        self.copy("act", WW, WW[:].rearrange("p a b -> p (a b)"), psw, psw[:, 0:256])
                        yield
                        ps2 = R["ph"]["ps2"]
                        self.mm(ps2, ps2[:, 0:128], TT, TT[:, 1, :], WW, WW[:, 0, :])
                        self.mm(ps2, ps2[:, 128:256], WW, WW[:, 0, :], TT, TT[:, 1, :])
                        yield
                        TTn = R["TT"].next()
                        self.tt("dve", TTn, TTn[:].rearrange("p a b -> p (a b)"), TT, TT[:].rearrange("p a b -> p (a b)"), ps2, ps2[:, 0:256], ALU.add)
                        TT = TTn
                        yield
                    psx = R["ph"]["psx"]
                    self.mm(psx, psx[:, 0:128], TT, TT[:, 1, :], X, X[:])
                    yield
                    self.copy("act", Xf, Xf[:], psx, psx[:, 0:128])
                    yield
                    pw = R["ph"]["pw"]
                    self.tr(pw, pw[0:64, 0:128], Xf, Xf[:, 64:128])
                    yield
                    self.copy("act", wT, wT[:], pw, pw[0:64, 0:128])
                    yield
                    psv = R["ph"]["psv"]
                    self.mm(psv, psv[:, 0:64], wT, wT[:], Sbb, Sbb[:])
                    self.mm(psv, psv[:, 64:128], qb, qb[:, tsl], Sbb, Sbb[:])
                    yield
                    self.tt("dve", vn, vn[:], Xf, Xf[:, 0:64], psv, psv[:, 0:64], ALU.subtract)
                    yield
                    pso = R["ph"]["pso"]
                    self.mm(pso, pso[:, 0:64], qkT, qkT[:], vn, vn[:])
                    self.ts(t1, t1[:], psv, psv[:, 64:128], e3[:, 0:1], None, ALU.mult, extra_reads=[e3])
                    yield
                    self.tt("dve", oacc, oacc[:, c, :], t1, t1[:], pso, pso[:, 0:64], ALU.add)
                    pss = R["ph"]["pss"]
                    self.mm(pss, pss[0:64, 0:64], kg, kg[:], vn, vn[:])
                    yield
                    self.stt(S, S[:], S, S[:], e3[0:64, 2:3], pss, pss[0:64, 0:64], ALU.mult, ALU.add, extra_reads=[e3])
                    yield
                    self.copy("act", Sbb, Sbb[:], S, S[:])
                    yield
                if not s.is_sample:
                    self.store(s.st_out, s.st_out[q, l, dr, h], S, S[:])

            P.barrier()
            pending = [(dr, q) for q in range(s.nseq) for dr in range(2)]
            running = []
            free_slots = list(slots)
            while pending or running:
                while pending and free_slots:
                    dr_, q_ = pending.pop(0)
                    R_ = free_slots.pop(0)
                    running.append((chain(dr_, q_, R_), R_))
                for item in list(running):
                    gen, R_ = item
                    try:
                        next(gen)
                    except StopIteration:
                        running.remove(item)
                        free_slots.append(R_)
            P.barrier()
            self.tt("pool", osum, osum[:], osum, osum[:], osum2, osum2[:], ALU.add)
            self.tap('osum', osum, osum[:])
            self.tt("dve", sq2, sq2[:], osum, osum[:], osum, osum[:], ALU.mult)
            P.op("dve", lambda e, sq2=sq2, ssq=ssq: e.reduce_sum(out=ssq[:], in_=sq2[:], axis=AX.X), reads=[sq2], writes=[ssq])
            self.act(ssq, ssq[:], ssq, ssq[:], AF.Sqrt, bias=EPS, scale=1.0 / 64.0)
            P.op("dve", lambda e, ssq=ssq: e.reciprocal(out=ssq[:], in_=ssq[:]), reads=[ssq], writes=[ssq])
            self.tt("dve", sq2, sq2[:], osum, osum[:], ssq, ssq[:, :, None].to_broadcast([128, NT, 64]), ALU.mult)
            self.tt("dve", sq2, sq2[:], sq2, sq2[:], norma, norma[:, None, :].to_broadcast([128, NT, 64]), ALU.mult)
            for c in range(NT):
                pz = self.pst()
                self.tr(pz, pz[:, 0:64], zf, zf[:, c * 128:(c + 1) * 128])
                t1 = r_t1.next()
                self.act(t1, t1[:], pz, pz[:, 0:64], AF.Silu)
                self.tt("dve", ytm, ytm[:, c, m0:m1], sq2, sq2[:, c, :], t1, t1[:], ALU.mult)
            if half == 1:
                for c in range(NT):
                    py = self.pst()
                    self.tr(py, py[:, 0:128], ytm, ytm[:, c, :])
                    self.copy("act" if c % 2 else "dve", y, y[:, hp, c * 128:(c + 1) * 128], py, py[:, 0:128])
        if self.dbg and self.dbg.get("tap") == ("ya", l, s.name):
            self.store(self.dbg_out, self.dbg_out[:], y, y[:])

    def phase_out_ffn(self, l, s):
        P = self.P
        d = self.d
        j = s.cidx
        modv = self.modv[l]
        with ExitStack() as ph:
            xbr = Ring([P.sb("fxb%d" % i, [128, 8, 512], F32, ph) for i in range(2)])
            sqr = Ring([P.sb("fsq%d" % i, [128, 512], BF16, ph) for i in range(2)])
            tmpr = Ring([P.sb("ftmp%d" % i, [128, 512], F32, ph) for i in range(2)])
            rstd = P.sb("frstd", [128, 512], F32, ph)
            P.barrier()
            for o in range(8):
                w = self.wload8(d["wo"], d["wo"][l, o])
                for blk in range(s.nb):
                    sl = slice(blk * 512, (blk + 1) * 512)
                    ps = self.pst()
                    for kc in range(8):
                        self.mm(ps, ps[:], w, w[:, kc, :], self.merged, self.merged[:, kc, sl], kc == 0, kc == 7)
                    self.copy("act" if blk % 2 else "dve", self.hT, self.hT[:, o, sl], ps, ps[:])
            for blk in range(s.nb):
                sl = slice(blk * 512, (blk + 1) * 512)
                xb = xbr.next()
                self.load(xb, xb[:], s.xres, s.xres[:, :, sl])
                for c in range(8):
                    self.stt(xb, xb[:, c, :], self.hT, self.hT[:, c, sl], modv[:, 16 + c, j:j + 1], xb, xb[:, c, :], ALU.mult, ALU.add, extra_reads=[modv])
                self.store(s.xres, s.xres[:, :, sl], xb, xb[:])
                self.norm_block(l, s, 1, xb, blk, sqr, rstd, tmpr)
            P.barrier()
            MB = min(s.T, 1024)
            nbm = MB // 512
            actb = P.sb("factb", [128, 22, MB], BF16, ph)
            sgr = Ring([P.sb("fsg%d" % i, [128, 512], F32, ph) for i in range(2)])
            w22 = Ring([P.sb("w22_%d" % i, [128, 22, 128], BF16, ph) for i in range(2)])
            for mb in range(s.T // MB):
                for i in range(22):
                    wg = self.wload8(d["wgu"], d["wgu"][l, i])
                    wu = self.wload8(d["wgu"], d["wgu"][l, 22 + i])
                    for b2 in range(nbm):
                        sl = slice(mb * MB + b2 * 512, mb * MB + (b2 + 1) * 512)
                        pg = self.pst()
                        for kc in range(8):
                            self.mm(pg, pg[:], wg, wg[:, kc, :], self.hT, self.hT[:, kc, sl], kc == 0, kc == 7)
                        pu = self.pst()
                        for kc in range(8):
                            self.mm(pu, pu[:], wu, wu[:, kc, :], self.hT, self.hT[:, kc, sl], kc == 0, kc == 7)
                        sg = sgr.next()
                        self.act(sg, sg[:], pg, pg[:], AF.Silu)
                        self.tt("dve", actb, actb[:, i, b2 * 512:(b2 + 1) * 512], sg, sg[:], pu, pu[:], ALU.mult)
                for o in range(8):
                    w = w22.next()
                    self.load(w, w[:], d["wdn"], d["wdn"][l, o], eng="pool")
                    for b2 in range(nbm):
                        sl = slice(mb * MB + b2 * 512, mb * MB + (b2 + 1) * 512)
                        ps = self.pst()
                        for kc in range(22):
                            self.mm(ps, ps[:], w, w[:, kc, :], actb, actb[:, kc, b2 * 512:(b2 + 1) * 512], kc == 0, kc == 21)
                        self.copy("act" if b2 % 2 else "dve", self.merged, self.merged[:, o, sl], ps, ps[:])
            for blk in range(s.nb):
                sl = slice(blk * 512, (blk + 1) * 512)
                xb = xbr.next()
                self.load(xb, xb[:], s.xres, s.xres[:, :, sl])
                for c in range(8):
                    self.stt(xb, xb[:, c, :], self.merged, self.merged[:, c, sl], modv[:, 40 + c, j:j + 1], xb, xb[:, c, :], ALU.mult, ALU.add, extra_reads=[modv])
                self.store(s.xres, s.xres[:, :, sl], xb, xb[:])
        P.barrier()

    def phase_final(self):
        P = self.P
        with ExitStack() as ph:
            xbr = Ring([P.sb("gxb%d" % i, [128, 8, 512], F32, ph) for i in range(2)])
            sqr = Ring([P.sb("gsq%d" % i, [128, 512], BF16, ph) for i in range(2)])
            rstd = P.sb("grstd", [128, 512], F32, ph)
            xn = P.sb("gxn", [128, 8, 512], F32, ph)
            yo = Ring([P.sb("gyo%d" % i, [128, D], F32, ph) for i in range(2)])
            for s in self.streams:
                for blk in range(s.nb):
                    sl = slice(blk * 512, (blk + 1) * 512)
                    xb = xbr.next()
                    self.load(xb, xb[:], s.xres, s.xres[:, :, sl])
                    self.rstd_block(xb, xb[:], sqr, rstd)
                    for c in range(8):
                        self.stt(xn, xn[:, c, :], xb, xb[:, c, :], self.nfT[:, c:c + 1], rstd, rstd[:], ALU.mult, ALU.mult, extra_reads=[self.nfT])
                    for t4 in range(4):
                        yt = yo.next()
                        for half in range(2):
                            ps = self.pst()
                            for c4 in range(4):
                                c = half * 4 + c4
                                self.tr(ps, ps[:, c4 * 128:(c4 + 1) * 128], xn, xn[:, c, t4 * 128:(t4 + 1) * 128])
                            self.copy("act" if half else "dve", yt, yt[:, half * 512:(half + 1) * 512], ps, ps[:])
                        r0 = blk * 512 + t4 * 128
                        self.store(s.y_out, s.y_out[r0:r0 + 128, :], yt, yt[:])
        P.barrier()


N_CORES = 8
_CACHE = {}


def make_streams():
    return [Stream("P", 4, 256, 0, [(0, 4)], False),
            Stream("S", 1, 2048, 1, [(0, 1), (1, 1), (2, 1), (3, 1)], True)]


def shared_inputs(inp, streams, depth):
    f = lambda a: np.ascontiguousarray(np.asarray(a, dtype=np.float32))
    sh = {}
    for s in streams:
        zposT, window = hyena_consts(s.L)
        Fc, Fs, Gc, Gs = dft_consts(s.L)
        CL, SLn, c64, s64 = fnet_consts(s.L)
        sh["zposT_" + s.name] = zposT
        sh["win_" + s.name] = window
        sh["Fc_" + s.name] = Fc
        sh["Fs_" + s.name] = Fs
        sh["Gc_" + s.name] = Gc
        sh["Gs_" + s.name] = Gs
        sh["CL_" + s.name] = CL
        sh["SLn_" + s.name] = SLn
        sh["c64_" + s.name] = c64
        sh["s64_" + s.name] = s64
        if s.is_sample:
            sh["pos_" + s.name] = grid_pos_embed_np(s.T)
    fm = lambda v: np.ascontiguousarray(f(v).reshape(-1, 128).T)
    sh["wmod"] = np.stack([tile_lhsT(f(inp["w_mod"][l])) for l in range(depth)])
    sh["bmodT"] = np.stack([fm(inp["b_mod"][l]) for l in range(depth)])
    sh["n1T"] = np.stack([fm(inp["norm1_g"][l]) for l in range(depth)])
    sh["n2T"] = np.stack([fm(inp["norm2_g"][l]) for l in range(depth)])
    sh["nfT"] = fm(inp["norm_f"])
    cols = win_tile_cols()
    win = np.zeros((depth, N_WIN_TILES, 128, 8, 128), np.float32)
    for l in range(depth):
        w = f(inp["w_in"][l])
        for ti, (c0, wd) in enumerate(cols):
            win[l, ti, :, :, :wd] = w[:, c0:c0 + wd].reshape(8, 128, wd).transpose(1, 0, 2)
    sh["win_t"] = win
    sh["wp_t"] = np.stack([np.stack([tile_lhsT(f(inp[k][l])) for k in ("w_pa", "w_pb", "w_pc")]) for l in range(depth)])
    sh["wo_t"] = np.stack([tile_lhsT(f(inp["w_o"][l])) for l in range(depth)])
    sh["wgu_t"] = np.stack([tile_lhsT(f(inp["w_gu"][l])) for l in range(depth)])
    sh["wdn_t"] = np.stack([tile_lhsT(f(inp["w_down"][l])) for l in range(depth)])
    cq = f(inp["conv_qkv"])[:depth]
    sh["convqT"] = np.ascontiguousarray(cq.reshape(depth, 3, 3, 8, 64).transpose(0, 4, 2, 3, 1))
    chy = f(inp["conv_hy"])[:depth]
    sh["convhT"] = np.ascontiguousarray(chy.reshape(depth, 3, 12, 128).transpose(0, 3, 2, 1))
    sh["alog_bc"] = np.ascontiguousarray(np.broadcast_to(f(inp["a_log"])[:depth].reshape(depth, 1, 16), (depth, 128, 16)))
    sh["dtb_bc"] = np.ascontiguousarray(np.broadcast_to(f(inp["dt_bias"])[:depth].reshape(depth, 1, 16), (depth, 128, 16)))
    sh["norma_bc"] = np.ascontiguousarray(np.broadcast_to(f(inp["norm_a"])[:depth].reshape(depth, 1, 64), (depth, 128, 64)))
    sh["hyw1"] = f(inp["hy_w1"])[:depth]
    sh["hyb1T"] = f(inp["hy_b1"])[:depth].reshape(depth, 64, 1)
    sh["hyfqT"] = f(inp["hy_freq"])[:depth].reshape(depth, 64, 1)
    sh["hyw2"] = f(inp["hy_w2"])[:depth]
    sh["hyb2T"] = f(inp["hy_b2"])[:depth].reshape(depth, 64, 1)
    sh["hyw3"] = f(inp["hy_w3"])[:depth]
    sh["hybiasT"] = np.ascontiguousarray(f(inp["hy_bias"])[:depth].reshape(depth, 2, 4, 128).transpose(0, 3, 1, 2))
    sh["masks"] = mask_consts()
    sh["ident"] = np.eye(128, dtype=np.float32)
    sh["lmask"] = level_masks()
    return sh


def kernel(**inp):
    depth = DEPTH
    streams = make_streams()
    if "nc" not in _CACHE:
        _CACHE["nc"] = Builder(make_streams(), depth=depth).build()
    nc = _CACHE["nc"]
    sh = shared_inputs(inp, streams, depth)
    xp = np.asarray(inp["x_prompt"], np.float32)
    xs = np.asarray(inp["x_sample"], np.float32)
    st = np.asarray(inp["state_delta"], np.float32)
    c = np.asarray(inp["c"], np.float32)
    cctx = np.asarray(inp["c_ctx"], np.float32)
    in_maps = []
    for core in range(N_CORES):
        sidx = core // 4
        m = dict(sh)
        m["x_P"] = np.ascontiguousarray(xp[core * 4:(core + 1) * 4].reshape(1024, D))
        m["x_S"] = np.ascontiguousarray(xs[sidx])
        m["st0_S"] = np.ascontiguousarray(st[sidx][:depth])
        cv = np.stack([cctx, c[sidx]], axis=-1)
        m["cvecT"] = np.ascontiguousarray(cv.reshape(8, 128, 2).transpose(1, 0, 2))
        in_maps.append(m)
    res = run_bass_kernel_spmd(nc, in_maps, core_ids=list(range(N_CORES)))
    r = res.results
    y_prompt = np.concatenate([r[i]["y_P"].reshape(4, 256, D) for i in range(N_CORES)], axis=0).astype(np.float32)
    y_sample = np.stack([r[0]["y_S"], r[4]["y_S"]], axis=0).astype(np.float32)
    new_state = np.concatenate([r[i]["st_P"] for i in range(N_CORES)], axis=0).astype(np.float32)
    return (y_prompt, y_sample, new_state)
```

```python
import math
from contextlib import ExitStack
import numpy as np
import concourse.bass as bass
import concourse.mybir as mybir
from concourse.bass_utils import run_bass_kernel_spmd

F32 = mybir.dt.float32
BF16 = mybir.dt.bfloat16
I32 = mybir.dt.int32
AF = mybir.ActivationFunctionType
ALU = mybir.AluOpType
AX = mybir.AxisListType

SAME_ENG_SYNC = True

D = 1024
DEPTH = 2
H_A = 8
DK = 64
DIN = 7200
DFF = 2816
EPS = 1e-6
CH = 128


class T:
    __slots__ = ("name", "ap", "last_write", "reads", "dsem", "dcount")

    def __init__(self, name, ap):
        self.name = name
        self.ap = ap
        self.last_write = None
        self.reads = []
        self.dsem = None
        self.dcount = 0

    def __getitem__(self, k):
        return self.ap[k]


class TV:
    def __init__(self, base, ap):
        self.base = base
        self.ap = ap
        self.name = base.name

    def __getitem__(self, k):
        return self.ap[k]


class Op:
    __slots__ = ("eng", "fn", "deps", "is_dma", "ndma", "sem_owner", "needed", "sig_sem", "sig_val", "name")

    def __init__(self, eng, fn, name=""):
        self.eng = eng
        self.fn = fn
        self.deps = []
        self.is_dma = False
        self.ndma = 0
        self.sem_owner = None
        self.needed = False
        self.sig_sem = None
        self.sig_val = 0
        self.name = name


ENGS = ("pe", "act", "dve", "pool", "sp")


def _ap(h):
    return h.ap() if callable(getattr(h, "ap", None)) else h


class Prog:
    def __init__(self, nc):
        self.nc = nc
        self.es = ExitStack()
        self.ops = {e: [] for e in ENGS}
        self.all_ops = []
        self.bar_deps = []
        self.bar_id = 0
        self.bar_seen = {e: 0 for e in ENGS}
        self.pending_dma = []
        self.uid = 0
        self.bar_pos = []

    def sb(self, name, shape, dtype, es=None):
        self.uid += 1
        h = (es or self.es).enter_context(self.nc.sbuf_tensor("%s_%d" % (name, self.uid), list(shape), dtype))
        return T(name, _ap(h))

    def ps(self, name, shape, dtype):
        h = self.es.enter_context(self.nc.psum_tensor(name, list(shape), dtype))
        return T(name, _ap(h))

    def tile(self, name, ap):
        return T(name, ap)

    def barrier(self):
        deps = []
        for e in ENGS:
            for o in reversed(self.ops[e]):
                if not o.is_dma:
                    deps.append(o)
                    break
        deps.extend(self.pending_dma)
        self.pending_dma = []
        for d in deps:
            d.needed = True
        self.bar_deps = deps
        self.bar_id += 1
        self.bar_pos.append(len(self.all_ops))

    def _record(self, op, reads, writes):
        reads = [getattr(r, "base", r) for r in reads]
        writes = [getattr(w, "base", w) for w in writes]
        deps = []
        if self.bar_seen[op.eng] != self.bar_id:
            self.bar_seen[op.eng] = self.bar_id
            deps.extend(self.bar_deps)
        for r in reads:
            if r.last_write is not None:
                deps.append(r.last_write)
        for w in writes:
            if w.last_write is not None:
                deps.append(w.last_write)
            deps.extend(w.reads)
        seen = set()
        for d in deps:
            if d is op or id(d) in seen:
                continue
            seen.add(id(d))
            if d.eng == op.eng and not d.is_dma:
                if op.eng == "pe" or not SAME_ENG_SYNC:
                    continue
            op.deps.append(d)
            d.needed = True
        for r in reads:
            r.reads.append(op)
        for w in writes:
            w.last_write = op
            w.reads = []
        self.ops[op.eng].append(op)
        self.all_ops.append(op)
        return op

    def op(self, eng, fn, reads=(), writes=(), name=""):
        return self._record(Op(eng, fn, name), list(reads), list(writes))

    def dma(self, eng, fn, reads=(), writes=(), owner=None, ndma=1, name=""):
        o = Op(eng, fn, name)
        o.is_dma = True
        o.ndma = ndma
        o.sem_owner = owner
        o.needed = True
        self.pending_dma.append(o)
        return self._record(o, list(reads), list(writes))

    def finalize(self):
        nc = self.nc
        es = self.es
        esem = {}
        for e in ("pe", "act", "dve", "pool"):
            esem[e] = es.enter_context(nc.semaphore("s_" + e))
        last_dma = {}
        qtype = {}
        for i, o in enumerate(self.all_ops):
            if o.is_dma:
                k = id(o.sem_owner)
                last_dma[k] = i
                qt = "sw" if o.eng == "pool" else "hw"
                if qtype.get(k, qt) != qt:
                    qtype[k] = "mixed"
                else:
                    qtype[k] = qt
        free = {"sw": [], "hw": [], "mixed": []}
        active = {}
        sem_final = {}
        bpos = list(self.bar_pos)
        bi = 0
        nsem = 0
        for i, o in enumerate(self.all_ops):
            while bi < len(bpos) and bpos[bi] <= i:
                b = bpos[bi]
                bi += 1
                for k in list(active.keys()):
                    ow = active[k]
                    if last_dma[k] < b:
                        if qtype[k] != "mixed":
                            free[qtype[k]].append((ow.dsem, ow.dcount))
                        del active[k]
            if o.is_dma:
                ow = o.sem_owner
                if ow.dsem is None:
                    fl = free[qtype[id(ow)]]
                    if fl and qtype[id(ow)] != "mixed":
                        ow.dsem, ow.dcount = fl.pop()
                    else:
                        ow.dsem = es.enter_context(nc.semaphore("d%d" % nsem))
                        nsem += 1
                    active[id(ow)] = ow
                ow.dcount += 16 * o.ndma
                o.sig_sem = ow.dsem
                o.sig_val = ow.dcount
                sem_final[id(ow.dsem)] = (ow.dsem, ow.dcount)
        dma_final = list(sem_final.values())
        for e in ("pe", "act", "dve", "pool"):
            c = 0
            for o in self.ops[e]:
                if o.is_dma:
                    continue
                if o.needed:
                    c += 1
                    o.sig_sem = esem[e]
                    o.sig_val = c
        self.n_sems = 4 + nsem
        engmap = {"pe": "tensor", "act": "scalar", "dve": "vector", "pool": "gpsimd", "sp": "sync"}
        with nc.Block() as block:
            for e in ENGS:
                ops = self.ops[e]
                final = (e == "sp")

                def body(eh, ops=ops, final=final):
                    seen = {}
                    for o in ops:
                        for d in o.deps:
                            key = id(d.sig_sem)
                            if seen.get(key, 0) >= d.sig_val:
                                continue
                            seen[key] = d.sig_val
                            eh.wait_ge(d.sig_sem, d.sig_val)
                        r = o.fn(eh)
                        if o.is_dma:
                            if not isinstance(r, (list, tuple)):
                                r = [r]
                            assert len(r) == o.ndma, (o.name, len(r), o.ndma)
                            for ins in r:
                                ins.then_inc(o.sig_sem, 16)
                        elif o.needed:
                            r.then_inc(o.sig_sem, 1)
                    if final:
                        for (sm, cnt) in dma_final:
                            eh.wait_ge(sm, cnt)

                getattr(block, engmap[e])(body)
        return self


class Ring:
    def __init__(self, tiles):
        self.tiles = tiles
        self.i = 0

    def next(self):
        t = self.tiles[self.i % len(self.tiles)]
        self.i += 1
        return t


import ml_dtypes
import os as _osx
NPBF = ml_dtypes.bfloat16


def tile_lhsT(w):
    K, N = w.shape
    nt = (N + 127) // 128
    wp = np.zeros((K, nt * 128), np.float32)
    wp[:, :N] = w
    kc = K // 128
    return np.ascontiguousarray(wp.reshape(kc, 128, nt, 128).transpose(2, 1, 0, 3))


def grid_pos_embed_np(n_tokens, grid_w=64):
    rows = n_tokens // grid_w
    r = np.repeat(np.arange(rows), grid_w).astype(np.float32)
    col = np.tile(np.arange(grid_w), rows).astype(np.float32)
    quarter = D // 4
    omega = (1.0 / (10000.0 ** (np.arange(quarter, dtype=np.float32) / quarter))).astype(np.float32)

    def emb(pos):
        a = pos[:, None] * omega[None, :]
        return np.concatenate([np.sin(a), np.cos(a)], axis=-1)

    return np.concatenate([emb(r), emb(col)], axis=-1).astype(np.float32)


def win_tile_cols():
    tiles = []
    for hp in range(4):
        for base in (0, 512, 1024, 1536):
            tiles.append((base + hp * 128, 128))
    tiles.append((2048, 32))
    for i in range(12):
        tiles.append((2080 + i * 128, 128))
    for i in range(4):
        tiles.append((3616 + i * 128, 128))
    for i in range(24):
        tiles.append((4128 + i * 128, 128))
    return tiles


TI_DN = 0
TI_BA = 16
TI_HY = 17
TI_FN = 29
TI_GATE = 33
N_WIN_TILES = 57


def hyena_consts(L):
    bands = 16
    t = np.linspace(0.0, 1.0, L, dtype=np.float32)[:, None]
    wpos = ((2.0 * math.pi / L) * np.arange(L, dtype=np.float32))[:, None].astype(np.float32)
    fr = np.linspace(1e-4, bands - 1, bands, dtype=np.float32)[None, :]
    zpos = np.concatenate([t, np.cos(fr * wpos), -np.sin(fr * wpos)], axis=-1).astype(np.float32)
    deltas = np.abs(np.linspace(math.log(1e-2) / 1.5, math.log(1e-2) / 0.3, 512, dtype=np.float32))
    window = np.exp(-t * deltas[None, :]).astype(np.float32)
    return np.ascontiguousarray(zpos.T), window


def dft_consts(L):
    nfp = (L // 128 + 1) * 128
    s = np.arange(L, dtype=np.float64)[:, None]
    f = np.arange(nfp, dtype=np.float64)[None, :]
    ang = np.pi * np.mod(s * f, 2 * L) / L
    valid = (f <= L)
    Fc = np.where(valid, np.cos(ang), 0.0)
    Fs = np.where(valid, np.sin(ang), 0.0)
    n = 2 * L
    cf = np.where((f == 0) | (f == L), 1.0 / n, 2.0 / n) * valid
    Gc = (Fc * cf).T
    Gs = (-Fs * cf).T
    return (Fc.astype(NPBF), Fs.astype(NPBF), np.ascontiguousarray(Gc).astype(NPBF), np.ascontiguousarray(Gs).astype(NPBF))


def fnet_consts(L):
    t = np.arange(L, dtype=np.float64)
    ang = 2 * np.pi * np.mod(np.outer(t, t), L) / L
    CL = np.cos(ang)
    SLn = -np.sin(ang)
    c = np.arange(64, dtype=np.float64)
    a64 = 2 * np.pi * np.mod(np.outer(c, c), 64) / 64
    sc = 1.0 / math.sqrt(64.0 * L)
    c64 = np.zeros((128, 128))
    s64 = np.zeros((128, 128))
    for g in range(2):
        c64[g * 64:(g + 1) * 64, g * 64:(g + 1) * 64] = np.cos(a64) * sc
        s64[g * 64:(g + 1) * 64, g * 64:(g + 1) * 64] = np.sin(a64) * sc
    return CL.astype(NPBF), SLn.astype(NPBF), c64.astype(NPBF), s64.astype(NPBF)


def mask_consts():
    i = np.arange(128)[:, None]
    j = np.arange(128)[None, :]
    m = np.stack([(i > j), (i >= j), (i < j), (i <= j)]).astype(np.float32)
    return m


M_SL, M_IL, M_SU, M_IU = 0, 1, 2, 3


def level_masks():
    i = np.arange(128)[:, None]
    j = np.arange(128)[None, :]
    ms = []
    for lv in range(7):
        bs = 2 << lv
        ms.append(((i // bs) == (j // bs)) & ((i // (bs // 2)) != (j // (bs // 2))))
    return np.ascontiguousarray(np.stack(ms, axis=1).astype(np.float32))


class Stream:
    def __init__(self, name, nseq, L, cidx, groups, is_sample):
        self.name = name
        self.nseq = nseq
        self.L = L
        self.T = nseq * L
        self.cidx = cidx
        self.nb = self.T // 512
        self.groups = groups
        self.is_sample = is_sample


class Builder:
    def __init__(self, streams, depth=DEPTH, dbg=None, skip=()):
        self.streams = streams
        self.depth = depth
        self.dbg = dbg
        self.skip = skip
        self.nc = bass.Bass("TRN2", target_bir_lowering=False)
        self.P = Prog(self.nc)
        self.dram = {}

    def din(self, name, shape, dtype=F32):
        ap = self.nc.dram_tensor(name, list(shape), dtype, kind="ExternalInput").ap()
        t = T(name, ap)
        self.dram[name] = (t, list(shape), dtype)
        return t

    def dout(self, name, shape, dtype=F32):
        ap = self.nc.dram_tensor(name, list(shape), dtype, kind="ExternalOutput").ap()
        return T(name, ap)

    def dscr(self, name, shape, dtype=F32):
        ap = self.nc.dram_tensor(name, list(shape), dtype, kind="Internal").ap()
        return T(name, ap)

    def mm(self, out_t, out_ap, lhsT_t, lhsT_ap, rhs_t, rhs_ap, start=True, stop=True):
        self.P.op("pe", lambda e: e.matmul(out_ap, lhsT=lhsT_ap, rhs=rhs_ap, start=start, stop=stop),
                  reads=[lhsT_t, rhs_t], writes=[out_t])

    def tr(self, out_t, out_ap, in_t, in_ap, ident_t=None, ident_ap=None):
        if ident_t is None:
            ident_t = self.ident
            n = in_ap.shape[0]
            ident_ap = self.ident[0:n, 0:n]
        self.P.op("pe", lambda e: e.transpose(out_ap, in_ap, ident_ap), reads=[in_t, ident_t], writes=[out_t])

    def act(self, out_t, out_ap, in_t, in_ap, func, bias=None, scale=None, extra_reads=()):
        kw = {}
        if bias is not None:
            kw["bias"] = bias
        if scale is not None:
            kw["scale"] = scale
        self.P.op("act", lambda e: e.activation(out=out_ap, in_=in_ap, func=func, **kw),
                  reads=[in_t] + list(extra_reads), writes=[out_t])

    def tt(self, eng, out_t, out_ap, a_t, a_ap, b_t, b_ap, op):
        self.P.op(eng, lambda e: e.tensor_tensor(out=out_ap, in0=a_ap, in1=b_ap, op=op),
                  reads=[a_t, b_t], writes=[out_t])

    def ts(self, out_t, out_ap, a_t, a_ap, s1, s2, op0, op1=None, extra_reads=()):
        if op1 is None:
            self.P.op("dve", lambda e: e.tensor_scalar(out=out_ap, in0=a_ap, scalar1=s1, scalar2=None, op0=op0),
                      reads=[a_t] + list(extra_reads), writes=[out_t])
        else:
            self.P.op("dve", lambda e: e.tensor_scalar(out=out_ap, in0=a_ap, scalar1=s1, scalar2=s2, op0=op0, op1=op1),
                      reads=[a_t] + list(extra_reads), writes=[out_t])

    def stt(self, out_t, out_ap, a_t, a_ap, scalar, b_t, b_ap, op0, op1, extra_reads=()):
        self.P.op("dve", lambda e: e.scalar_tensor_tensor(out=out_ap, in0=a_ap, scalar=scalar, in1=b_ap, op0=op0, op1=op1),
                  reads=[a_t, b_t] + list(extra_reads), writes=[out_t])

    def copy(self, eng, out_t, out_ap, in_t, in_ap):
        if eng == "act":
            self.P.op("act", lambda e: e.copy(out=out_ap, in_=in_ap), reads=[in_t], writes=[out_t])
        else:
            self.P.op(eng, lambda e: e.tensor_copy(out=out_ap, in_=in_ap), reads=[in_t], writes=[out_t])

    def memset(self, t, ap, val, eng="dve"):
        self.P.op(eng, lambda e: e.memset(ap, val), writes=[t])

    def load(self, out_t, out_ap, in_t, in_ap, eng="sp"):
        self.P.dma(eng, lambda e: e.dma_start(out=out_ap, in_=in_ap), reads=[in_t], writes=[out_t], owner=out_t)

    def store(self, out_t, out_ap, in_t, in_ap, eng="sp"):
        self.P.dma(eng, lambda e: e.dma_start(out=out_ap, in_=in_ap), reads=[in_t], writes=[out_t], owner=in_t)

    def pst(self):
        return self.psr.next()

    def tap(self, name, t, ap):
        if self.dbg and self.dbg.get("tap2") == name and not getattr(self, "_tapped", False):
            self._tapped = True
            self.store(self.dbg_out, self.dbg_out[:], t, ap)

    def build(self):
        P = self.P
        depth = self.depth
        for s in self.streams:
            nfp = (s.L // 128 + 1) * 128
            s.x_in = self.din("x_" + s.name, [s.T, D])
            s.y_out = self.dout("y_" + s.name, [s.T, D])
            s.xres = self.dscr("xres_" + s.name, [128, 8, s.T])
            if s.is_sample:
                s.st_in = self.din("st0_" + s.name, [depth, 2, H_A, DK, DK])
                s.pos = self.din("pos_" + s.name, [s.T, D])
            else:
                s.st_out = self.dout("st_" + s.name, [s.nseq, depth, 2, H_A, DK, DK])
            s.zposT = self.din("zposT_" + s.name, [33, s.L])
            s.window = self.din("win_" + s.name, [s.L, 512])
            s.Fc = self.din("Fc_" + s.name, [s.L, nfp], BF16)
            s.Fs = self.din("Fs_" + s.name, [s.L, nfp], BF16)
            s.Gc = self.din("Gc_" + s.name, [nfp, s.L], BF16)
            s.Gs = self.din("Gs_" + s.name, [nfp, s.L], BF16)
            s.CL = self.din("CL_" + s.name, [s.L, s.L], BF16)
            s.SLn = self.din("SLn_" + s.name, [s.L, s.L], BF16)
            s.c64 = self.din("c64_" + s.name, [128, 128], BF16)
            s.s64 = self.din("s64_" + s.name, [128, 128], BF16)
        d = {}
        d["cvecT"] = self.din("cvecT", [128, 8, 2])
        d["wmod"] = self.din("wmod", [depth, 48, 128, 8, 128])
        d["bmodT"] = self.din("bmodT", [depth, 128, 48])
        d["n1"] = self.din("n1T", [depth, 128, 8])
        d["n2"] = self.din("n2T", [depth, 128, 8])
        d["nf"] = self.din("nfT", [128, 8])
        d["win"] = self.din("win_t", [depth, N_WIN_TILES, 128, 8, 128])
        d["wp"] = self.din("wp_t", [depth, 3, 8, 128, 4, 128])
        d["wo"] = self.din("wo_t", [depth, 8, 128, 8, 128])
        d["wgu"] = self.din("wgu_t", [depth, 44, 128, 8, 128])
        d["wdn"] = self.din("wdn_t", [depth, 8, 128, 22, 128])
        d["convq"] = self.din("convqT", [depth, 64, 3, 8, 3])
        d["convh"] = self.din("convhT", [depth, 128, 12, 3])
        d["alog"] = self.din("alog_bc", [depth, 128, 16])
        d["dtb"] = self.din("dtb_bc", [depth, 128, 16])
        d["norma"] = self.din("norma_bc", [depth, 128, 64])
        d["hyw1"] = self.din("hyw1", [depth, 33, 64])
        d["hyb1"] = self.din("hyb1T", [depth, 64, 1])
        d["hyfq"] = self.din("hyfqT", [depth, 64, 1])
        d["hyw2"] = self.din("hyw2", [depth, 64, 64])
        d["hyb2"] = self.din("hyb2T", [depth, 64, 1])
        d["hyw3"] = self.din("hyw3", [depth, 64, 2048])
        d["hybias"] = self.din("hybiasT", [depth, 128, 2, 4])
        d["masks"] = self.din("masks", [4, 128, 128])
        d["ident"] = self.din("ident", [128, 128])
        d["lmask"] = self.din("lmask", [128, 7, 128])
        self.d = d
        if self.dbg:
            self.dbg_out = self.dout("dbg", self.dbg["shape"], self.dbg.get("dtype", F32))

        self.psr = Ring([P.ps("ps%d" % i, [128, 512], F32) for i in range(8)])
        TMAX = max(s.T for s in self.streams)
        self.hT = P.sb("hT", [128, 8, TMAX], BF16)
        self.merged = P.sb("merged", [128, 8, TMAX], BF16)
        self.w8 = Ring([P.sb("w8_%d" % i, [128, 8, 128], BF16) for i in range(3)])
        self.w4 = Ring([P.sb("w4_%d" % i, [128, 4, 128], BF16) for i in range(2)])
        self.ident = P.sb("ident", [128, 128], F32)
        self.onesb = P.sb("onesb", [128, 128], BF16)
        self.ones32 = P.sb("ones32", [128, 128], F32)
        self.masks = P.sb("masks", [128, 4, 128], F32)
        self.load(self.ident, self.ident[:], d["ident"], d["ident"][:])
        self.memset(self.onesb, self.onesb[:], 1.0 / 1024.0)
        self.identb2 = P.sb("identb2", [128, 2, 128], BF16)
        self.copy("dve", self.identb2, self.identb2[:, 0, :], self.ident, self.ident[:])
        self.copy("dve", self.identb2, self.identb2[:, 1, :], self.ident, self.ident[:])
        self.memset(self.ones32, self.ones32[:], 1.0)
        for i in range(4):
            self.load(self.masks, self.masks[:, i, :], d["masks"], d["masks"][i])
        self.modv = [P.sb("modv%d" % l, [128, 48, 2], F32) for l in range(depth)]
        self.modA = [P.sb("modA%d" % l, [128, 2, 8, 2], F32) for l in range(depth)]
        self.nfT = P.sb("nfT", [128, 8], F32)
        self.load(self.nfT, self.nfT[:], d["nf"], d["nf"][:])

        self.phase_mod()
        self.phase_input()
        for l in range(depth):
            for s in self.streams:
                self.phase_norm1(l, s)
                self.phase_mix(l, s)
                self.phase_out_ffn(l, s)
        self.phase_final()
        if self.dbg:
            self.dbg["fn"](self)
        P.finalize()
        P.es.close()
        return self.nc

    def wload8(self, dram_t, dram_ap):
        w = self.w8.next()
        self.load(w, w[:], dram_t, dram_ap, eng="pool")
        return w

    def phase_mod(self):
        P = self.P
        d = self.d
        with ExitStack() as ph:
            cv = P.sb("cv", [128, 8, 2], F32, ph)
            scv = P.sb("scv", [128, 8, 2], F32, ph)
            wm = Ring([P.sb("wm%d" % i, [128, 8, 128], F32, ph) for i in range(3)])
            bm = P.sb("bm", [128, 48], F32, ph)
            n12 = P.sb("n12", [128, 2, 8], F32, ph)
            self.load(cv, cv[:], d["cvecT"], d["cvecT"][:])
            self.act(scv, scv[:], cv, cv[:], AF.Silu)
            for l in range(self.depth):
                modv = self.modv[l]
                self.load(bm, bm[:], d["bmodT"], d["bmodT"][l])
                self.load(n12, n12[:, 0, :], d["n1"], d["n1"][l])
                self.load(n12, n12[:, 1, :], d["n2"], d["n2"][l])
                for c in range(48):
                    w = wm.next()
                    self.load(w, w[:], d["wmod"], d["wmod"][l, c])
                    ps = self.pst()
                    for kc in range(8):
                        self.mm(ps, ps[:, 0:2], w, w[:, kc, :], scv, scv[:, kc, :], kc == 0, kc == 7)
                    self.ts(modv, modv[:, c, :], ps, ps[:, 0:2], bm[:, c:c + 1], None, ALU.add, extra_reads=[bm])
                mA = self.modA[l]
                for sub in range(2):
                    sc0 = 8 + 24 * sub
                    for j in range(2):
                        self.ts(mA, mA[:, sub, :, j], modv, modv[:, sc0:sc0 + 8, j], 1.0, None, ALU.add)
                        self.tt("dve", mA, mA[:, sub, :, j], mA, mA[:, sub, :, j], n12, n12[:, sub, :], ALU.mult)
        P.barrier()

    def phase_input(self):
        P = self.P
        with ExitStack() as ph:
            xin = Ring([P.sb("xin%d" % i, [128, D], F32, ph) for i in range(2)])
            pin = Ring([P.sb("pin%d" % i, [128, D], F32, ph) for i in range(2)])
            xo = Ring([P.sb("xo%d" % i, [128, 8, 128], F32, ph) for i in range(2)])
            for s in self.streams:
                for tt in range(s.T // 128):
                    xt = xin.next()
                    self.load(xt, xt[:], s.x_in, s.x_in[tt * 128:(tt + 1) * 128, :])
                    if s.is_sample:
                        pt = pin.next()
                        self.load(pt, pt[:], s.pos, s.pos[tt * 128:(tt + 1) * 128, :])
                        self.tt("dve", xt, xt[:], xt, xt[:], pt, pt[:], ALU.add)
                    xot = xo.next()
                    for half in range(2):
                        ps = self.pst()
                        for c4 in range(4):
                            c = half * 4 + c4
                            self.tr(ps, ps[:, c4 * 128:(c4 + 1) * 128], xt, xt[:, c * 128:(c + 1) * 128])
                        self.copy("act" if half else "dve", xot, xot[:, half * 4:half * 4 + 4, :], ps,
                                  ps[:].rearrange("p (c t) -> p c t", c=4))
                    self.store(s.xres, s.xres[:, :, tt * 128:(tt + 1) * 128], xot, xot[:])
        P.barrier()

    def rstd_block(self, xb, xb_ap, sqr, rstd):
        ps = self.pst()
        for c in range(8):
            sq = sqr.next()
            self.act(sq, sq[:], xb, xb_ap[:, c, :], AF.Square)
            self.mm(ps, ps[:], self.onesb, self.onesb[:], sq, sq[:], c == 0, c == 7)
        self.act(rstd, rstd[:], ps, ps[:], AF.Sqrt, bias=EPS, scale=1.0)
        self.P.op("dve", lambda e: e.reciprocal(out=rstd[:], in_=rstd[:]), reads=[rstd], writes=[rstd])

    def norm_block(self, l, s, sub, xb, blk, sqr, rstd, tmpr):
        j = s.cidx
        self.rstd_block(xb, xb[:], sqr, rstd)
        mA = self.modA[l]
        modv = self.modv[l]
        sh0 = 24 * sub
        for c in range(8):
            tmp = tmpr.next()
            self.tt("dve", tmp, tmp[:], xb, xb[:, c, :], rstd, rstd[:], ALU.mult)
            self.act(self.hT, self.hT[:, c, blk * 512:(blk + 1) * 512], tmp, tmp[:], AF.Identity,
                     bias=modv[:, sh0 + c, j:j + 1], scale=mA[:, sub, c, j:j + 1], extra_reads=[mA, modv])

    def phase_norm1(self, l, s):
        P = self.P
        with ExitStack() as ph:
            xbr = Ring([P.sb("xb%d" % i, [128, 8, 512], F32, ph) for i in range(2)])
            sqr = Ring([P.sb("sq%d" % i, [128, 512], BF16, ph) for i in range(2)])
            tmpr = Ring([P.sb("ntmp%d" % i, [128, 512], F32, ph) for i in range(2)])
            rstd = P.sb("rstd", [128, 512], F32, ph)
            for blk in range(s.nb):
                xb = xbr.next()
                self.load(xb, xb[:], s.xres, s.xres[:, :, blk * 512:(blk + 1) * 512])
                self.norm_block(l, s, 0, xb, blk, sqr, rstd, tmpr)
        P.barrier()

    def proj_fm(self, l, ti, s, evac, m0=0, m1=128):
        w = self.wload8(self.d["win"], self.d["win"][l, ti])
        for blk in range(s.nb):
            ps = self.pst()
            for kc in range(8):
                self.mm(ps, ps[0:m1 - m0, :], w, w[:, kc, m0:m1], self.hT, self.hT[:, kc, blk * 512:(blk + 1) * 512], kc == 0, kc == 7)
            evac(ps, blk)

    def merge_branch(self, l, s, br, y_t):
        for o in range(8):
            wg = self.wload8(self.d["win"], self.d["win"][l, TI_GATE + br * 8 + o])
            wp = self.w4.next()
            self.load(wp, wp[:], self.d["wp"], self.d["wp"][l, br, o], eng="pool")
            for blk in range(s.nb):
                sl = slice(blk * 512, (blk + 1) * 512)
                ps1 = self.pst()
                for kc in range(8):
                    self.mm(ps1, ps1[:], wg, wg[:, kc, :], self.hT, self.hT[:, kc, sl], kc == 0, kc == 7)
                ps2 = self.pst()
                for kc in range(4):
                    self.mm(ps2, ps2[:], wp, wp[:, kc, :], y_t, y_t[:, kc, sl], kc == 0, kc == 3)
                sig = self.sigr.next()
                self.act(sig, sig[:], ps1, ps1[:], AF.Sigmoid)
                if br == 0:
                    self.tt("dve", self.merged, self.merged[:, o, sl], sig, sig[:], ps2, ps2[:], ALU.mult)
                else:
                    self.tt("dve", sig, sig[:], sig, sig[:], ps2, ps2[:], ALU.mult)
                    self.tt("pool", self.merged, self.merged[:, o, sl], self.merged, self.merged[:, o, sl], sig, sig[:], ALU.add)

    def phase_mix(self, l, s):
        P = self.P
        with ExitStack() as ph:
            self.sigr = Ring([P.sb("sig%d" % i, [128, 512], F32, ph) for i in range(2)])
            y = P.sb("ybr", [128, 4, s.T], BF16, ph)
            with ExitStack() as ph2:
                if "delta" in self.skip:
                    self.memset(y, y[:], 0.0)
                else:
                    self.mix_delta(l, s, y, ph2)
            P.barrier()
            self.merge_branch(l, s, 0, y)
            P.barrier()
            for (ct0, nct) in s.groups:
                with ExitStack() as ph2:
                    if "hyena" in self.skip:
                        self.memset(y, y[:], 0.0)
                    else:
                        self.mix_hyena(l, s, y, ct0, nct, ph2)
                P.barrier()
            if self.dbg and self.dbg.get("tap") == ("yb", l, s.name):
                self.store(self.dbg_out, self.dbg_out[:], y, y[:])
            self.merge_branch(l, s, 1, y)
            P.barrier()
            for (ct0, nct) in s.groups:
                with ExitStack() as ph2:
                    self.mix_fnet(l, s, y, ct0, nct, ph2)
                P.barrier()
            if self.dbg and self.dbg.get("tap") == ("yc", l, s.name):
                self.store(self.dbg_out, self.dbg_out[:], y, y[:])
            self.merge_branch(l, s, 2, y)
        P.barrier()

    def mix_fnet(self, l, s, y, ct0, nct, ph):
        P = self.P
        L = s.L
        nt = L // 128
        W = nct * 128
        xc = P.sb("xc", [128, nct, s.T], BF16, ph)
        c64 = P.sb("c64", [128, 128], BF16, ph)
        s64 = P.sb("s64", [128, 128], BF16, ph)
        self.load(c64, c64[:], s.c64, s.c64[:])
        self.load(s64, s64[:], s.s64, s.s64[:])
        for ci in range(nct):
            self.proj_fm(l, TI_FN + ct0 + ci, s,
                         lambda ps, blk, ci=ci: self.copy("act", xc, xc[:, ci, blk * 512:(blk + 1) * 512], ps, ps[:]))
        U = P.sb("U", [128, nt, 2, W], BF16, ph)
        NW = 256
        dftc = Ring([P.sb("dftc%d" % i, [128, nt, NW], BF16, ph) for i in range(2)])
        dfts = Ring([P.sb("dfts%d" % i, [128, nt, NW], BF16, ph) for i in range(2)])
        for q in range(s.nseq):
            t0 = q * L
            for tt in range(nt):
                for cs, mat in ((0, c64), (1, s64)):
                    ps = self.pst()
                    for ci in range(nct):
                        self.mm(ps, ps[:, ci * 128:(ci + 1) * 128], xc, xc[:, ci, t0 + tt * 128:t0 + (tt + 1) * 128], mat, mat[:])
                    self.copy("act" if cs else "dve", U, U[:, tt, cs, :], ps, ps[:, 0:W])
            for nbk in range(L // NW):
                cm = dftc.next()
                sm = dfts.next()
                self.load(cm, cm[:], s.CL, s.CL[:, nbk * NW:(nbk + 1) * NW].rearrange("(k p) n -> p k n", p=128))
                self.load(sm, sm[:], s.SLn, s.SLn[:, nbk * NW:(nbk + 1) * NW].rearrange("(k p) n -> p k n", p=128))
                for ci in range(nct):
                    ps = self.pst()
                    for tt in range(nt):
                        self.mm(ps, ps[:, 0:NW], U, U[:, tt, 0, ci * 128:(ci + 1) * 128], cm, cm[:, tt, :], tt == 0, False)
                        self.mm(ps, ps[:, 0:NW], U, U[:, tt, 1, ci * 128:(ci + 1) * 128], sm, sm[:, tt, :], False, tt == nt - 1)
                    self.copy("act" if ci % 2 else "dve", y, y[:, ct0 + ci, t0 + nbk * NW:t0 + (nbk + 1) * NW], ps, ps[:, 0:NW])

    def sin_reduce(self, out_t, out_ap, in_t, in_ap, ti, ti_ap, tf, tf_ap):
        P = self.P
        inv = 1.0 / (2.0 * math.pi)
        P.op("dve", lambda e: e.tensor_scalar(out=ti_ap, in0=in_ap, scalar1=inv, scalar2=None, op0=ALU.mult),
             reads=[in_t], writes=[ti])
        self.copy("dve", tf, tf_ap, ti, ti_ap)
        self.stt(tf, tf_ap, tf, tf_ap, -2.0 * math.pi, in_t, in_ap, ALU.mult, ALU.add)
        self.ts(tf, tf_ap, tf, tf_ap, math.pi, -math.pi, ALU.min, ALU.max)
        self.act(out_t, out_ap, tf, tf_ap, AF.Sin)

    def hy_gate(self, l, s, which, ct, raw, dst, dst_ap, cw):
        L = s.L
        self.proj_fm(l, TI_HY + which * 4 + ct, s,
                     lambda ps, blk: self.copy("act", raw, raw[:, blk * 512:(blk + 1) * 512], ps, ps[:]))
        wi = which * 4 + ct
        for q in range(s.nseq):
            a, b = q * L, (q + 1) * L
            self.ts(dst, dst_ap[:, a:b], raw, raw[:, a:b], cw[:, wi, 1:2], None, ALU.mult, extra_reads=[cw])
            self.stt(dst, dst_ap[:, a + 1:b], raw, raw[:, a:b - 1], cw[:, wi, 0:1], dst, dst_ap[:, a + 1:b], ALU.mult, ALU.add, extra_reads=[cw])
            self.stt(dst, dst_ap[:, a:b - 1], raw, raw[:, a + 1:b], cw[:, wi, 2:3], dst, dst_ap[:, a:b - 1], ALU.mult, ALU.add, extra_reads=[cw])

    def mix_hyena(self, l, s, y, ct0, nct, ph):
        P = self.P
        d = self.d
        L = s.L
        nt = L // 128
        nf = nt + 1
        T_ = s.T
        W = nct * 128
        cw = P.sb("hcw", [128, 12, 3], F32, ph)
        self.load(cw, cw[:], d["convh"], d["convh"][l])
        hb = P.sb("hbias", [128, 2, 4], F32, ph)
        self.load(hb, hb[:], d["hybias"], d["hybias"][l])
        raw = P.sb("hraw", [128, T_], F32, ph)
        gate = P.sb("hgate", [128, nct, T_], BF16, ph)
        z = P.sb("hz", [128, nct, T_], BF16, ph)
        zr = P.sb("hzr", [128, T_], F32, ph)
        for ci in range(nct):
            self.hy_gate(l, s, 2, ct0 + ci, raw, zr, zr[:], cw)
            self.copy("act", z, z[:, ci, :], zr, zr[:])
        hsd = P.sb("hsd", [128, nt, 2, 2, W], BF16, ph)
        with ExitStack() as pf:
            w1 = P.sb("hw1", [33, 64], F32, pf)
            w2 = P.sb("hw2", [64, 64], F32, pf)
            w3 = P.sb("hw3", [64, 2048], F32, pf)
            b1 = P.sb("hb1", [64, 1], F32, pf)
            b2 = P.sb("hb2", [64, 1], F32, pf)
            fq = P.sb("hfq", [64, 1], F32, pf)
            zp = P.sb("hzp", [33, L], F32, pf)
            h1 = P.sb("hh1", [64, L], F32, pf)
            h2 = P.sb("hh2", [64, L], F32, pf)
            ti = P.sb("hti", [64, 512], I32, pf)
            tf = P.sb("htf", [64, 512], F32, pf)
            ta = P.sb("hta", [64, 512], F32, pf)
            win = Ring([P.sb("hwin%d" % i, [128, W], F32, pf) for i in range(2)])
            hf = Ring([P.sb("hf%d" % i, [128, 4, W], F32, pf) for i in range(2)])
            self.load(w1, w1[:], d["hyw1"], d["hyw1"][l])
            self.load(w2, w2[:], d["hyw2"], d["hyw2"][l])
            self.load(w3, w3[:], d["hyw3"], d["hyw3"][l])
            self.load(b1, b1[:], d["hyb1"], d["hyb1"][l])
            self.load(b2, b2[:], d["hyb2"], d["hyb2"][l])
            self.load(fq, fq[:], d["hyfq"], d["hyfq"][l])
            self.load(zp, zp[:], s.zposT, s.zposT[:])
            wb = min(L, 512)
            for (src, wsrc, bsrc, dst, kk) in ((zp, w1, b1, h1, 33), (h1, w2, b2, h2, 64)):
                for blk in range(L // wb):
                    sl = slice(blk * wb, (blk + 1) * wb)
                    ps = self.pst()
                    self.mm(ps, ps[0:64, 0:wb], wsrc, wsrc[0:kk, :], src, src[0:kk, sl])
                    self.ts(ta, ta[:, 0:wb], ps, ps[0:64, 0:wb], bsrc[:, 0:1], fq[:, 0:1], ALU.add, ALU.mult, extra_reads=[bsrc, fq])
                    self.sin_reduce(dst, dst[:, sl], ta, ta[:, 0:wb], ti, ti[:, 0:wb], tf, tf[:, 0:wb])
            for tt in range(nt):
                wn = win.next()
                self.load(wn, wn[:], s.window, s.window[tt * 128:(tt + 1) * 128, ct0 * 128:ct0 * 128 + W])
                h = hf.next()
                for fi in range(4):
                    ps = self.pst()
                    c0 = fi * 512 + ct0 * 128
                    self.mm(ps, ps[:, 0:W], h2, h2[:, tt * 128:(tt + 1) * 128], w3, w3[:, c0:c0 + W])
                    self.tt("dve", h, h[:, fi, :], ps, ps[:, 0:W], wn, wn[:], ALU.mult)
                for o in range(2):
                    self.tt("dve", hsd, hsd[:, tt, o, 0, :], h, h[:, 2 * o, :], h, h[:, 2 * o + 1, :], ALU.add)
                    self.tt("pool", hsd, hsd[:, tt, o, 1, :], h, h[:, 2 * o + 1, :], h, h[:, 2 * o, :], ALU.subtract)
        P.barrier()
        Hre = P.sb("Hre", [128, nf, W], F32, ph)
        Him = P.sb("Him", [128, nf, W], F32, ph)
        ztm = P.sb("ztm", [128, nt, W], BF16, ph)
        Yre = P.sb("Yre", [128, nf, W], BF16, ph)
        Yim = P.sb("Yim", [128, nf, W], BF16, ph)
        fcr = Ring([P.sb("fcr%d" % i, [128, nt, 128], BF16, ph) for i in range(2)])
        fsr = Ring([P.sb("fsr%d" % i, [128, nt, 128], BF16, ph) for i in range(2)])
        NW = 128
        gcr = Ring([P.sb("gcr%d" % i, [128, nf, NW], BF16, ph) for i in range(2)])
        gsr = Ring([P.sb("gsr%d" % i, [128, nf, NW], BF16, ph) for i in range(2)])
        tmpz = Ring([P.sb("tmpz%d" % i, [128, W], F32, ph) for i in range(4)])
        zf = P.sb("hzf", [128, 128], F32, ph)

        def ld_f(ft):
            fcm = fcr.next()
            fsm = fsr.next()
            self.load(fcm, fcm[:], s.Fc, s.Fc[:, ft * 128:(ft + 1) * 128].rearrange("(k p) n -> p k n", p=128))
            self.load(fsm, fsm[:], s.Fs, s.Fs[:, ft * 128:(ft + 1) * 128].rearrange("(k p) n -> p k n", p=128))
            return fcm, fsm

        for o in range(2):
            for ft in range(nf):
                fcm, fsm = ld_f(ft)
                for (mat, sd, dst) in ((fcm, 0, Hre), (fsm, 1, Him)):
                    ps = self.pst()
                    for tt in range(nt):
                        self.mm(ps, ps[:, 0:W], mat, mat[:, tt, :], hsd, hsd[:, tt, o, sd, :], tt == 0, tt == nt - 1)
                    self.copy("act" if sd else "dve", dst, dst[:, ft, :], ps, ps[:, 0:W])
            for ci in range(nct):
                self.hy_gate(l, s, o, ct0 + ci, raw, gate, gate[:, ci, :], cw)
            for q in range(s.nseq):
                t0 = q * L
                for tt in range(nt):
                    ps = self.pst()
                    for ci in range(nct):
                        self.copy("dve", zf, zf[:], z, z[:, ci, t0 + tt * 128:t0 + (tt + 1) * 128])
                        self.tr(ps, ps[:, ci * 128:(ci + 1) * 128], zf, zf[:])
                    self.copy("act" if tt % 2 else "dve", ztm, ztm[:, tt, :], ps, ps[:, 0:W])
                for ft in range(nf):
                    fcm, fsm = ld_f(ft)
                    pc = self.pst()
                    for tt in range(nt):
                        self.mm(pc, pc[:, 0:W], fcm, fcm[:, tt, :], ztm, ztm[:, tt, :], tt == 0, tt == nt - 1)
                    pz = self.pst()
                    for tt in range(nt):
                        self.mm(pz, pz[:, 0:W], fsm, fsm[:, tt, :], ztm, ztm[:, tt, :], tt == 0, tt == nt - 1)
                    a1 = tmpz.next(); a2 = tmpz.next(); a3 = tmpz.next(); a4 = tmpz.next()
                    self.tt("dve", a1, a1[:], pc, pc[:, 0:W], Hre, Hre[:, ft, :], ALU.mult)
                    self.tt("dve", a2, a2[:], pz, pz[:, 0:W], Him, Him[:, ft, :], ALU.mult)
                    self.tt("pool", Yre, Yre[:, ft, :], a1, a1[:], a2, a2[:], ALU.add)
                    self.tt("dve", a3, a3[:], pc, pc[:, 0:W], Him, Him[:, ft, :], ALU.mult)
                    self.tt("dve", a4, a4[:], pz, pz[:, 0:W], Hre, Hre[:, ft, :], ALU.mult)
                    self.tt("pool", Yim, Yim[:, ft, :], a3, a3[:], a4, a4[:], ALU.subtract)
                for nbk in range(L // NW):
                    gc = gcr.next()
                    gs = gsr.next()
                    self.load(gc, gc[:], s.Gc, s.Gc[:, nbk * NW:(nbk + 1) * NW].rearrange("(k p) n -> p k n", p=128))
                    self.load(gs, gs[:], s.Gs, s.Gs[:, nbk * NW:(nbk + 1) * NW].rearrange("(k p) n -> p k n", p=128))
                    sl = slice(t0 + nbk * NW, t0 + (nbk + 1) * NW)
                    for ci in range(nct):
                        ps = self.pst()
                        for ft in range(nf):
                            self.mm(ps, ps[:, 0:NW], Yre, Yre[:, ft, ci * 128:(ci + 1) * 128], gc, gc[:, ft, :], ft == 0, False)
                            self.mm(ps, ps[:, 0:NW], Yim, Yim[:, ft, ci * 128:(ci + 1) * 128], gs, gs[:, ft, :], False, ft == nf - 1)
                        a1 = tmpz.next()
                        self.stt(a1, a1[:, 0:NW], z, z[:, ci, sl], hb[:, o, ct0 + ci:ct0 + ci + 1], ps, ps[:, 0:NW], ALU.mult, ALU.add, extra_reads=[hb])
                        if o == 0:
                            self.tt("dve", z, z[:, ci, sl], a1, a1[:, 0:NW], gate, gate[:, ci, sl], ALU.mult)
                        else:
                            self.tt("dve", y, y[:, ct0 + ci, sl], a1, a1[:, 0:NW], gate, gate[:, ci, sl], ALU.mult)

    def mix_delta(self, l, s, y, ph):
        P = self.P
        d = self.d
        L = s.L
        T_ = s.T
        NT = T_ // 128
        cps = L // 128
        cwq = P.sb("dcw", [64, 3, 8, 3], F32, ph)
        self.load(cwq, cwq[:], d["convq"], d["convq"][l])
        alog = P.sb("dalog", [128, 16], F32, ph)
        dtb = P.sb("ddtb", [128, 16], F32, ph)
        norma = P.sb("dnorma", [128, 64], F32, ph)
        self.load(alog, alog[:], d["alog"], d["alog"][l])
        self.load(dtb, dtb[:], d["dtb"], d["dtb"][l])
        self.load(norma, norma[:], d["norma"], d["norma"][l])
        ba = P.sb("dba", [128, NT, 32], F32, ph)
        beta = P.sb("dbeta", [128, NT, 16], F32, ph)
        nbeta = P.sb("dnbeta", [128, NT, 16], F32, ph)
        g = P.sb("dg", [128, NT, 16], F32, ph)
        wba = self.wload8(d["win"], d["win"][l, TI_BA])
        for tt in range(NT):
            ps = self.pst()
            for kc in range(8):
                self.mm(ps, ps[:, 0:32], self.hT, self.hT[:, kc, tt * 128:(tt + 1) * 128], wba, wba[:, kc, 0:32], kc == 0, kc == 7)
            self.copy("act" if tt % 2 else "dve", ba, ba[:, tt, :], ps, ps[:, 0:32])
        self.act(beta, beta[:], ba, ba[:, :, 0:16], AF.Sigmoid)
        self.ts(nbeta, nbeta[:], beta, beta[:], -1.0, None, ALU.mult)
        self.tt("dve", g, g[:], ba, ba[:, :, 16:32], dtb, dtb[:, None, :].to_broadcast([128, NT, 16]), ALU.add)
        self.act(g, g[:], g, g[:], AF.Exp)
        self.act(g, g[:], g, g[:], AF.Ln, bias=1.0, scale=1.0)
        self.act(alog, alog[:], alog, alog[:], AF.Exp)
        self.stt(g, g[:], g, g[:], -1.0, alog, alog[:, None, :].to_broadcast([128, NT, 16]), ALU.mult, ALU.mult)

        self.tap('g', g, g[:])
        self.tap('beta', beta, beta[:])
        raw = P.sb("draw", [64, T_], F32, ph)
        qf = P.sb("dq", [64, T_], F32, ph)
        kf = P.sb("dk", [64, T_], F32, ph)
        vf = P.sb("dv", [64, T_], F32, ph)
        zf = P.sb("dz", [64, T_], F32, ph)
        qb = P.sb("dqb", [64, T_], BF16, ph)
        kb = P.sb("dkb", [64, T_], BF16, ph)
        osum = P.sb("dosum", [128, NT, 64], F32, ph)
        ytm = P.sb("dytm", [128, NT, 128], F32, ph)
        sqt = P.sb("dsq", [64, 512], F32, ph)
        rn = P.sb("drn", [64, 512], F32, ph)
        KSLOT = int(_osx.environ.get("KSLOT", "3" if s.is_sample else "4"))
        lmask = P.sb("dlmask", [128, 7, 128], F32, ph)
        self.load(lmask, lmask[:], d["lmask"], d["lmask"][:])
        osum2 = P.sb("dosum2", [128, NT, 64], F32, ph)
        r_t1 = Ring([P.sb("dt1%d" % i, [128, 64], F32, ph) for i in range(2)])

        def mkslot(i):
            R = {}
            def a(name, shape, dt):
                R[name] = P.sb("d%s_%d" % (name, i), shape, dt, ph)
            a("S", [64, 64], F32); a("Sb", [64, 64], BF16)
            a("gbc", [128, 128], F32); a("dcol", [128, 4], F32); a("e3", [128, 4], F32)
            a("dabs", [128, 128], F32); a("Dm", [128, 128], F32); a("Ds", [128, 128], F32); a("Di", [128, 128], F32)
            a("P0", [128, 2, 128], F32); a("NTk", [128, 7, 128], BF16); a("qkT", [128, 128], BF16)
            a("kv", [128, 128], F32); a("X", [128, 128], BF16); a("Xf", [128, 128], F32)
            a("kg", [128, 64], BF16); a("wT", [64, 128], BF16); a("vn", [128, 64], BF16); a("t1", [128, 64], F32)
            a("bw", [128, 1], F32)
            nbk = 8 // KSLOT
            bk = self.psr.tiles[nbk * i:nbk * (i + 1)]
            names = ["psd", "psk", "pkv", "pst_", "psw", "ps2", "psx", "pw", "psv", "pso", "pss"]
            if nbk >= 4:
                amap = {"psd": 0, "psk": 1, "pkv": 2, "pst_": 3, "psw": 0, "ps2": 2, "psx": 1, "pw": 3, "psv": 0, "pso": 1, "pss": 2}
            else:
                amap = {"psd": 0, "psk": 1, "pkv": 0, "pst_": 1, "psw": 0, "ps2": 1, "psx": 0, "pw": 1, "psv": 0, "pso": 1, "pss": 0}
            R["ph"] = {n: bk[amap[n] % nbk] for n in names}
            R["TT"] = Ring([P.sb("dTT%d_%d" % (j, i), [128, 2, 128], BF16, ph) for j in range(2)])
            a("WW", [128, 2, 128], BF16)
            return R
        slots = [mkslot(i) for i in range(KSLOT)]
        chS = [(P.sb("dchS%d" % i, [64, 64], F32, ph), P.sb("dchSb%d" % i, [64, 64], BF16, ph)) for i in range(2 * s.nseq)]
        masks = self.masks
        sq2 = P.sb("dsq2", [128, NT, 64], F32, ph)
        ssq = P.sb("dssq", [128, NT], F32, ph)

        import os as _os
        _NH = int(_os.environ.get('DN_HEADS', '8'))
        _ST = int(_os.environ.get('DN_STAGE', '9'))
        for h in range(_NH):
            hp, half = h // 2, h % 2
            m0, m1 = half * 64, half * 64 + 64
            for which, dst in ((0, qf), (1, kf), (2, vf)):
                self.proj_fm(l, TI_DN + hp * 4 + which, s,
                             lambda ps, blk: self.copy("act", raw, raw[:, blk * 512:(blk + 1) * 512], ps, ps[0:64, :]), m0, m1)
                for q in range(s.nseq):
                    a, b = q * L, (q + 1) * L
                    self.ts(dst, dst[:, a:b], raw, raw[:, a:b], cwq[:, which, h, 1:2], None, ALU.mult, extra_reads=[cwq])
                    self.stt(dst, dst[:, a + 1:b], raw, raw[:, a:b - 1], cwq[:, which, h, 0:1], dst, dst[:, a + 1:b], ALU.mult, ALU.add, extra_reads=[cwq])
                    self.stt(dst, dst[:, a:b - 1], raw, raw[:, a + 1:b], cwq[:, which, h, 2:3], dst, dst[:, a:b - 1], ALU.mult, ALU.add, extra_reads=[cwq])
                self.act(dst, dst[:], dst, dst[:], AF.Silu)
            self.proj_fm(l, TI_DN + hp * 4 + 3, s,
                         lambda ps, blk: self.copy("act", zf, zf[:, blk * 512:(blk + 1) * 512], ps, ps[0:64, :]), m0, m1)
            for (x, xb_, sc) in ((qf, qb, 64.0), (kf, kb, 1.0)):
                for blk in range(s.nb):
                    sl = slice(blk * 512, (blk + 1) * 512)
                    self.tt("dve", sqt, sqt[:], x, x[:, sl], x, x[:, sl], ALU.mult)
                    ps = self.pst()
                    self.mm(ps, ps[0:64, :], self.ones32, self.ones32[0:64, 0:64], sqt, sqt[:])
                    self.act(rn, rn[:], ps, ps[0:64, :], AF.Sqrt, bias=EPS * sc, scale=sc)
                    P.op("dve", lambda e: e.reciprocal(out=rn[:], in_=rn[:]), reads=[rn], writes=[rn])
                    self.tt("dve", x, x[:, sl], x, x[:, sl], rn, rn[:], ALU.mult)
                self.copy("act", xb_, xb_[:], x, x[:])
            self.tap('q', qf, qf[:])
            self.tap('k', kf, kf[:])
            self.tap('v', vf, vf[:])
            def unit(dr, q, pos, R):
                col = dr * 8 + h
                if dr == 0:
                    cm, rm, sm, im = M_IU, M_SL, M_SL, M_IL
                else:
                    cm, rm, sm, im = M_IL, M_SU, M_SU, M_IU
                ch = chains[(dr, q)]
                S, Sbb = ch["S"], ch["Sb"]
                oacc = osum if dr == 0 else osum2
                cl = pos if dr == 0 else cps - 1 - pos
                if True:
                    c = q * cps + cl
                    tsl = slice(c * 128, (c + 1) * 128)
                    gcol = g[:, c, col:col + 1]
                    bcol = beta[:, c, col:col + 1]
                    nbcol = nbeta[:, c, col:col + 1]
                    gbc, dcol, e3, dabs, Dm, Ds, Di = R["gbc"], R["dcol"], R["e3"], R["dabs"], R["Dm"], R["Ds"], R["Di"]
                    P0, NTk, qkT, kv, X, Xf = R["P0"], R["NTk"], R["qkT"], R["kv"], R["X"], R["Xf"]
                    kg, wT, vn, t1, bw, WW = R["kg"], R["wT"], R["vn"], R["t1"], R["bw"], R["WW"]
                    self.copy("pool", gbc, gbc[:], g, gcol.to_broadcast([128, 128]))
                    yield
                    psd = R["ph"]["psd"]
                    self.mm(psd, psd[:, 0:128], gbc, gbc[:], masks, masks[:, cm, :])
                    self.mm(psd, psd[:, 128:129], masks, masks[:, cm, :], g, gcol)
                    self.mm(psd, psd[:, 129:130], masks, masks[:, rm, :], g, gcol)
                    self.mm(psd, psd[:, 130:131], self.ones32, self.ones32[:], g, gcol)
                    yield
                    self.copy("dve", dcol, dcol[:, 0:3], psd, psd[:, 128:131])
                    yield
                    self.act(e3, e3[:, 0:3], dcol, dcol[:, 0:3], AF.Exp)
                    self.ts(dabs, dabs[:], psd, psd[:, 0:128], dcol[:, 0:1], 0.0, ALU.subtract, ALU.max, extra_reads=[dcol])
                    yield
                    self.act(Dm, Dm[:], dabs, dabs[:], AF.Exp, scale=-1.0)
                    yield
                    self.tt("pool", Ds, Ds[:], Dm, Dm[:], masks, masks[:, sm, :], ALU.mult)
                    self.tt("pool", Di, Di[:], Dm, Dm[:], masks, masks[:, im, :], ALU.mult)
                    psk = R["ph"]["psk"]
                    self.mm(psk, psk[:, 0:128], kb, kb[:, tsl], kb, kb[:, tsl])
                    self.mm(psk, psk[:, 128:256], qb, qb[:, tsl], kb, kb[:, tsl])
                    yield
                    self.stt(P0, P0[:, 0, :], psk, psk[:, 0:128], nbcol, Ds, Ds[:], ALU.mult, ALU.mult, extra_reads=[nbeta])
                    self.tt("dve", P0, P0[:, 1, :], psk, psk[:, 128:256], Di, Di[:], ALU.mult)
                    yield
                    pst_ = R["ph"]["pst_"]
                    self.tr(pst_, pst_[:, 0:128], P0, P0[:, 0, :])
                    self.tr(pst_, pst_[:, 128:256], P0, P0[:, 1, :])
                    pkv = R["ph"]["pkv"]
                    self.tr(pkv, pkv[:, 0:64], kf, kf[:, tsl])
                    self.tr(pkv, pkv[:, 64:128], vf, vf[:, tsl])
                    yield
                    self.tt("dve", NTk, NTk[:], pst_, pst_[:, 0:128][:, None, :].to_broadcast([128, 7, 128]), lmask, lmask[:], ALU.mult)
                    self.copy("dve", qkT, qkT[:], pst_, pst_[:, 128:256])
                    self.copy("act", kv, kv[:], pkv, pkv[:, 0:128])
                    self.tt("dve", bw, bw[:], beta, bcol, e3, e3[:, 0:1], ALU.mult)
                    yield
                    self.ts(X, X[:, 0:64], kv, kv[:, 64:128], bcol, None, ALU.mult, extra_reads=[beta])
                    self.ts(X, X[:, 64:128], kv, kv[:, 0:64], bw[:, 0:1], None, ALU.mult, extra_reads=[bw])
                    self.ts(kg, kg[:], kv, kv[:, 0:64], e3[:, 1:2], None, ALU.mult, extra_reads=[e3])
                    yield
                    TT = self.identb2
                    for lev in range(7):
                        psw = R["ph"]["psw"]
                        self.mm(psw, psw[:, 0:128], NTk, NTk[:, lev, :], TT, TT[:, 0, :])
                        self.mm(psw, psw[:, 128:256], TT, TT[:, 0, :], NTk, NTk[:, lev, :])
                        yield
                        self.copy("act", WW, WW[:].rearrange("p a b -> p (a b)"), psw, psw[:, 0:256])
                        yield
                        ps2 = R["ph"]["ps2"]
                        self.mm(ps2, ps2[:, 0:128], TT, TT[:, 1, :], WW, WW[:, 0, :])
                        self.mm(ps2, ps2[:, 128:256], WW, WW[:, 0, :], TT, TT[:, 1, :])
                        yield
                        TTn = R["TT"].next()
                        self.tt("dve", TTn, TTn[:].rearrange("p a b -> p (a b)"), TT, TT[:].rearrange("p a b -> p (a b)"), ps2, ps2[:, 0:256], ALU.add)
                        TT = TTn
                        yield
                    psx = R["ph"]["psx"]
                    self.mm(psx, psx[:, 0:128], TT, TT[:, 1, :], X, X[:])
                    yield
                    self.copy("act", Xf, Xf[:], psx, psx[:, 0:128])
                    yield
                    pw = R["ph"]["pw"]
                    self.tr(pw, pw[0:64, 0:128], Xf, Xf[:, 64:128])
                    yield
                    self.copy("act", wT, wT[:], pw, pw[0:64, 0:128])
                    yield
                    while ch["done"] < pos:
                        yield
                    psv = R["ph"]["psv"]
                    self.mm(psv, psv[:, 0:64], wT, wT[:], Sbb, Sbb[:])
                    self.mm(psv, psv[:, 64:128], qb, qb[:, tsl], Sbb, Sbb[:])
                    yield
                    self.tt("dve", vn, vn[:], Xf, Xf[:, 0:64], psv, psv[:, 0:64], ALU.subtract)
                    yield
                    pso = R["ph"]["pso"]
                    self.mm(pso, pso[:, 0:64], qkT, qkT[:], vn, vn[:])
                    self.ts(t1, t1[:], psv, psv[:, 64:128], e3[:, 0:1], None, ALU.mult, extra_reads=[e3])
                    yield
                    self.tt("dve", oacc, oacc[:, c, :], t1, t1[:], pso, pso[:, 0:64], ALU.add)
                    pss = R["ph"]["pss"]
                    self.mm(pss, pss[0:64, 0:64], kg, kg[:], vn, vn[:])
                    yield
                    self.stt(S, S[:], S, S[:], e3[0:64, 2:3], pss, pss[0:64, 0:64], ALU.mult, ALU.add, extra_reads=[e3])
                    yield
                    self.copy("act", Sbb, Sbb[:], S, S[:])
                    ch["done"] += 1
                    yield
                if pos == cps - 1 and not s.is_sample:
                    self.store(s.st_out, s.st_out[q, l, dr, h], S, S[:])

            P.barrier()
            chains = {}
            ci = 0
            for q in range(s.nseq):
                for dr in range(2):
                    S_, Sb_ = chS[ci]
                    ci += 1
                    if s.is_sample:
                        self.load(S_, S_[:], s.st_in, s.st_in[l, dr, h])
                    else:
                        self.memset(S_, S_[:], 0.0)
                    self.copy("act", Sb_, Sb_[:], S_, S_[:])
                    chains[(dr, q)] = {"S": S_, "Sb": Sb_, "done": 0}
            pending = [(dr, q, pos) for pos in range(cps) for q in range(s.nseq) for dr in range(2)]
            running = []
            free_slots = list(slots)
            while pending or running:
                while pending and free_slots:
                    dr_, q_, pos_ = pending.pop(0)
                    R_ = free_slots.pop(0)
                    running.append((unit(dr_, q_, pos_, R_), R_))
                for item in list(running):
                    gen, R_ = item
                    try:
                        next(gen)
                    except StopIteration:
                        running.remove(item)
                        free_slots.append(R_)
            P.barrier()
            self.tt("pool", osum, osum[:], osum, osum[:], osum2, osum2[:], ALU.add)
            self.tap('osum', osum, osum[:])
            self.tt("dve", sq2, sq2[:], osum, osum[:], osum, osum[:], ALU.mult)
            P.op("dve", lambda e, sq2=sq2, ssq=ssq: e.reduce_sum(out=ssq[:], in_=sq2[:], axis=AX.X), reads=[sq2], writes=[ssq])
            self.act(ssq, ssq[:], ssq, ssq[:], AF.Sqrt, bias=EPS, scale=1.0 / 64.0)
            P.op("dve", lambda e, ssq=ssq: e.reciprocal(out=ssq[:], in_=ssq[:]), reads=[ssq], writes=[ssq])
            self.tt("dve", sq2, sq2[:], osum, osum[:], ssq, ssq[:, :, None].to_broadcast([128, NT, 64]), ALU.mult)
            self.tt("dve", sq2, sq2[:], sq2, sq2[:], norma, norma[:, None, :].to_broadcast([128, NT, 64]), ALU.mult)
            for c in range(NT):
                pz = self.pst()
                self.tr(pz, pz[:, 0:64], zf, zf[:, c * 128:(c + 1) * 128])
                t1 = r_t1.next()
                self.act(t1, t1[:], pz, pz[:, 0:64], AF.Silu)
                self.tt("dve", ytm, ytm[:, c, m0:m1], sq2, sq2[:, c, :], t1, t1[:], ALU.mult)
            if half == 1:
                for c in range(NT):
                    py = self.pst()
                    self.tr(py, py[:, 0:128], ytm, ytm[:, c, :])
                    self.copy("act" if c % 2 else "dve", y, y[:, hp, c * 128:(c + 1) * 128], py, py[:, 0:128])
        if self.dbg and self.dbg.get("tap") == ("ya", l, s.name):
            self.store(self.dbg_out, self.dbg_out[:], y, y[:])

    def phase_out_ffn(self, l, s):
        P = self.P
        d = self.d
        j = s.cidx
        modv = self.modv[l]
        with ExitStack() as ph:
            xbr = Ring([P.sb("fxb%d" % i, [128, 8, 512], F32, ph) for i in range(2)])
            sqr = Ring([P.sb("fsq%d" % i, [128, 512], BF16, ph) for i in range(2)])
            tmpr = Ring([P.sb("ftmp%d" % i, [128, 512], F32, ph) for i in range(2)])
            rstd = P.sb("frstd", [128, 512], F32, ph)
            P.barrier()
            for o in range(8):
                w = self.wload8(d["wo"], d["wo"][l, o])
                for blk in range(s.nb):
                    sl = slice(blk * 512, (blk + 1) * 512)
                    ps = self.pst()
                    for kc in range(8):
                        self.mm(ps, ps[:], w, w[:, kc, :], self.merged, self.merged[:, kc, sl], kc == 0, kc == 7)
                    self.copy("act" if blk % 2 else "dve", self.hT, self.hT[:, o, sl], ps, ps[:])
            for blk in range(s.nb):
                sl = slice(blk * 512, (blk + 1) * 512)
                xb = xbr.next()
                self.load(xb, xb[:], s.xres, s.xres[:, :, sl])
                for c in range(8):
                    self.stt(xb, xb[:, c, :], self.hT, self.hT[:, c, sl], modv[:, 16 + c, j:j + 1], xb, xb[:, c, :], ALU.mult, ALU.add, extra_reads=[modv])
                self.store(s.xres, s.xres[:, :, sl], xb, xb[:])
                self.norm_block(l, s, 1, xb, blk, sqr, rstd, tmpr)
            P.barrier()
            MB = min(s.T, 1024)
            nbm = MB // 512
            actb = P.sb("factb", [128, 22, MB], BF16, ph)
            sgr = Ring([P.sb("fsg%d" % i, [128, 512], F32, ph) for i in range(2)])
            w22 = Ring([P.sb("w22_%d" % i, [128, 22, 128], BF16, ph) for i in range(2)])
            for mb in range(s.T // MB):
                for i in range(22):
                    wg = self.wload8(d["wgu"], d["wgu"][l, i])
                    wu = self.wload8(d["wgu"], d["wgu"][l, 22 + i])
                    for b2 in range(nbm):
                        sl = slice(mb * MB + b2 * 512, mb * MB + (b2 + 1) * 512)
                        pg = self.pst()
                        for kc in range(8):
                            self.mm(pg, pg[:], wg, wg[:, kc, :], self.hT, self.hT[:, kc, sl], kc == 0, kc == 7)
                        pu = self.pst()
                        for kc in range(8):
                            self.mm(pu, pu[:], wu, wu[:, kc, :], self.hT, self.hT[:, kc, sl], kc == 0, kc == 7)
                        sg = sgr.next()
                        self.act(sg, sg[:], pg, pg[:], AF.Silu)
                        self.tt("dve", actb, actb[:, i, b2 * 512:(b2 + 1) * 512], sg, sg[:], pu, pu[:], ALU.mult)
                for o in range(8):
                    w = w22.next()
                    self.load(w, w[:], d["wdn"], d["wdn"][l, o], eng="pool")
                    for b2 in range(nbm):
                        sl = slice(mb * MB + b2 * 512, mb * MB + (b2 + 1) * 512)
                        ps = self.pst()
                        for kc in range(22):
                            self.mm(ps, ps[:], w, w[:, kc, :], actb, actb[:, kc, b2 * 512:(b2 + 1) * 512], kc == 0, kc == 21)
                        self.copy("act" if b2 % 2 else "dve", self.merged, self.merged[:, o, sl], ps, ps[:])
            for blk in range(s.nb):
                sl = slice(blk * 512, (blk + 1) * 512)
                xb = xbr.next()
                self.load(xb, xb[:], s.xres, s.xres[:, :, sl])
                for c in range(8):
                    self.stt(xb, xb[:, c, :], self.merged, self.merged[:, c, sl], modv[:, 40 + c, j:j + 1], xb, xb[:, c, :], ALU.mult, ALU.add, extra_reads=[modv])
                self.store(s.xres, s.xres[:, :, sl], xb, xb[:])
        P.barrier()

    def phase_final(self):
        P = self.P
        with ExitStack() as ph:
            xbr = Ring([P.sb("gxb%d" % i, [128, 8, 512], F32, ph) for i in range(2)])
            sqr = Ring([P.sb("gsq%d" % i, [128, 512], BF16, ph) for i in range(2)])
            rstd = P.sb("grstd", [128, 512], F32, ph)
            xn = P.sb("gxn", [128, 8, 512], F32, ph)
            yo = Ring([P.sb("gyo%d" % i, [128, D], F32, ph) for i in range(2)])
            for s in self.streams:
                for blk in range(s.nb):
                    sl = slice(blk * 512, (blk + 1) * 512)
                    xb = xbr.next()
                    self.load(xb, xb[:], s.xres, s.xres[:, :, sl])
                    self.rstd_block(xb, xb[:], sqr, rstd)
                    for c in range(8):
                        self.stt(xn, xn[:, c, :], xb, xb[:, c, :], self.nfT[:, c:c + 1], rstd, rstd[:], ALU.mult, ALU.mult, extra_reads=[self.nfT])
                    for t4 in range(4):
                        yt = yo.next()
                        for half in range(2):
                            ps = self.pst()
                            for c4 in range(4):
                                c = half * 4 + c4
                                self.tr(ps, ps[:, c4 * 128:(c4 + 1) * 128], xn, xn[:, c, t4 * 128:(t4 + 1) * 128])
                            self.copy("act" if half else "dve", yt, yt[:, half * 512:(half + 1) * 512], ps, ps[:])
                        r0 = blk * 512 + t4 * 128
                        self.store(s.y_out, s.y_out[r0:r0 + 128, :], yt, yt[:])
        P.barrier()


N_CORES = 8
_CACHE = {}


def make_streams():
    return [Stream("P", 4, 256, 0, [(0, 4)], False),
            Stream("S", 1, 2048, 1, [(0, 1), (1, 1), (2, 1), (3, 1)], True)]


def shared_inputs(inp, streams, depth):
    f = lambda a: np.ascontiguousarray(np.asarray(a, dtype=np.float32))
    sh = {}
    for s in streams:
        zposT, window = hyena_consts(s.L)
        Fc, Fs, Gc, Gs = dft_consts(s.L)
        CL, SLn, c64, s64 = fnet_consts(s.L)
        sh["zposT_" + s.name] = zposT
        sh["win_" + s.name] = window
        sh["Fc_" + s.name] = Fc
        sh["Fs_" + s.name] = Fs
        sh["Gc_" + s.name] = Gc
        sh["Gs_" + s.name] = Gs
        sh["CL_" + s.name] = CL
        sh["SLn_" + s.name] = SLn
        sh["c64_" + s.name] = c64
        sh["s64_" + s.name] = s64
        if s.is_sample:
            sh["pos_" + s.name] = grid_pos_embed_np(s.T)
    fm = lambda v: np.ascontiguousarray(f(v).reshape(-1, 128).T)
    sh["wmod"] = np.stack([tile_lhsT(f(inp["w_mod"][l])) for l in range(depth)])
    sh["bmodT"] = np.stack([fm(inp["b_mod"][l]) for l in range(depth)])
    sh["n1T"] = np.stack([fm(inp["norm1_g"][l]) for l in range(depth)])
    sh["n2T"] = np.stack([fm(inp["norm2_g"][l]) for l in range(depth)])
    sh["nfT"] = fm(inp["norm_f"])
    cols = win_tile_cols()
    win = np.zeros((depth, N_WIN_TILES, 128, 8, 128), np.float32)
    for l in range(depth):
        w = f(inp["w_in"][l])
        for ti, (c0, wd) in enumerate(cols):
            win[l, ti, :, :, :wd] = w[:, c0:c0 + wd].reshape(8, 128, wd).transpose(1, 0, 2)
    sh["win_t"] = win
    sh["wp_t"] = np.stack([np.stack([tile_lhsT(f(inp[k][l])) for k in ("w_pa", "w_pb", "w_pc")]) for l in range(depth)])
    sh["wo_t"] = np.stack([tile_lhsT(f(inp["w_o"][l])) for l in range(depth)])
    sh["wgu_t"] = np.stack([tile_lhsT(f(inp["w_gu"][l])) for l in range(depth)])
    sh["wdn_t"] = np.stack([tile_lhsT(f(inp["w_down"][l])) for l in range(depth)])
    cq = f(inp["conv_qkv"])[:depth]
    sh["convqT"] = np.ascontiguousarray(cq.reshape(depth, 3, 3, 8, 64).transpose(0, 4, 2, 3, 1))
    chy = f(inp["conv_hy"])[:depth]
    sh["convhT"] = np.ascontiguousarray(chy.reshape(depth, 3, 12, 128).transpose(0, 3, 2, 1))
    sh["alog_bc"] = np.ascontiguousarray(np.broadcast_to(f(inp["a_log"])[:depth].reshape(depth, 1, 16), (depth, 128, 16)))
    sh["dtb_bc"] = np.ascontiguousarray(np.broadcast_to(f(inp["dt_bias"])[:depth].reshape(depth, 1, 16), (depth, 128, 16)))
    sh["norma_bc"] = np.ascontiguousarray(np.broadcast_to(f(inp["norm_a"])[:depth].reshape(depth, 1, 64), (depth, 128, 64)))
    sh["hyw1"] = f(inp["hy_w1"])[:depth]
    sh["hyb1T"] = f(inp["hy_b1"])[:depth].reshape(depth, 64, 1)
    sh["hyfqT"] = f(inp["hy_freq"])[:depth].reshape(depth, 64, 1)
    sh["hyw2"] = f(inp["hy_w2"])[:depth]
    sh["hyb2T"] = f(inp["hy_b2"])[:depth].reshape(depth, 64, 1)
    sh["hyw3"] = f(inp["hy_w3"])[:depth]
    sh["hybiasT"] = np.ascontiguousarray(f(inp["hy_bias"])[:depth].reshape(depth, 2, 4, 128).transpose(0, 3, 1, 2))
    sh["masks"] = mask_consts()
    sh["ident"] = np.eye(128, dtype=np.float32)
    sh["lmask"] = level_masks()
    return sh


def kernel(**inp):
    depth = DEPTH
    streams = make_streams()
    if "nc" not in _CACHE:
        _CACHE["nc"] = Builder(make_streams(), depth=depth).build()
    nc = _CACHE["nc"]
    sh = shared_inputs(inp, streams, depth)
    xp = np.asarray(inp["x_prompt"], np.float32)
    xs = np.asarray(inp["x_sample"], np.float32)
    st = np.asarray(inp["state_delta"], np.float32)
    c = np.asarray(inp["c"], np.float32)
    cctx = np.asarray(inp["c_ctx"], np.float32)
    in_maps = []
    for core in range(N_CORES):
        sidx = core // 4
        m = dict(sh)
        m["x_P"] = np.ascontiguousarray(xp[core * 4:(core + 1) * 4].reshape(1024, D))
        m["x_S"] = np.ascontiguousarray(xs[sidx])
        m["st0_S"] = np.ascontiguousarray(st[sidx][:depth])
        cv = np.stack([cctx, c[sidx]], axis=-1)
        m["cvecT"] = np.ascontiguousarray(cv.reshape(8, 128, 2).transpose(1, 0, 2))
        in_maps.append(m)
    res = run_bass_kernel_spmd(nc, in_maps, core_ids=list(range(N_CORES)))
    r = res.results
    y_prompt = np.concatenate([r[i]["y_P"].reshape(4, 256, D) for i in range(N_CORES)], axis=0).astype(np.float32)
    y_sample = np.stack([r[0]["y_S"], r[4]["y_S"]], axis=0).astype(np.float32)
    new_state = np.concatenate([r[i]["st_P"] for i in range(N_CORES)], axis=0).astype(np.float32)
    return (y_prompt, y_sample, new_state)
```

```python
import math
from contextlib import ExitStack
import numpy as np
import concourse.bass as bass
import concourse.mybir as mybir
from concourse.bass_utils import run_bass_kernel_spmd

F32 = mybir.dt.float32
BF16 = mybir.dt.bfloat16
I32 = mybir.dt.int32
AF = mybir.ActivationFunctionType
ALU = mybir.AluOpType
AX = mybir.AxisListType

SAME_ENG_SYNC = True

D = 1024
DEPTH = 2
H_A = 8
DK = 64
DIN = 7200
DFF = 2816
EPS = 1e-6
CH = 128


class T:
    __slots__ = ("name", "ap", "last_write", "reads", "dsem", "dcount")

    def __init__(self, name, ap):
        self.name = name
        self.ap = ap
        self.last_write = None
        self.reads = []
        self.dsem = None
        self.dcount = 0

    def __getitem__(self, k):
        return self.ap[k]


class TV:
    def __init__(self, base, ap):
        self.base = base
        self.ap = ap
        self.name = base.name

    def __getitem__(self, k):
        return self.ap[k]


class Op:
    __slots__ = ("eng", "fn", "deps", "is_dma", "ndma", "sem_owner", "needed", "sig_sem", "sig_val", "name")

    def __init__(self, eng, fn, name=""):
        self.eng = eng
        self.fn = fn
        self.deps = []
        self.is_dma = False
        self.ndma = 0
        self.sem_owner = None
        self.needed = False
        self.sig_sem = None
        self.sig_val = 0
        self.name = name


ENGS = ("pe", "act", "dve", "pool", "sp")


def _ap(h):
    return h.ap() if callable(getattr(h, "ap", None)) else h


class Prog:
    def __init__(self, nc):
        self.nc = nc
        self.es = ExitStack()
        self.ops = {e: [] for e in ENGS}
        self.all_ops = []
        self.bar_deps = []
        self.bar_id = 0
        self.bar_seen = {e: 0 for e in ENGS}
        self.pending_dma = []
        self.uid = 0
        self.bar_pos = []

    def sb(self, name, shape, dtype, es=None):
        self.uid += 1
        h = (es or self.es).enter_context(self.nc.sbuf_tensor("%s_%d" % (name, self.uid), list(shape), dtype))
        return T(name, _ap(h))

    def ps(self, name, shape, dtype):
        h = self.es.enter_context(self.nc.psum_tensor(name, list(shape), dtype))
        return T(name, _ap(h))

    def tile(self, name, ap):
        return T(name, ap)

    def barrier(self):
        deps = []
        for e in ENGS:
            for o in reversed(self.ops[e]):
                if not o.is_dma:
                    deps.append(o)
                    break
        deps.extend(self.pending_dma)
        self.pending_dma = []
        for d in deps:
            d.needed = True
        self.bar_deps = deps
        self.bar_id += 1
        self.bar_pos.append(len(self.all_ops))

    def _record(self, op, reads, writes):
        reads = [getattr(r, "base", r) for r in reads]
        writes = [getattr(w, "base", w) for w in writes]
        deps = []
        if self.bar_seen[op.eng] != self.bar_id:
            self.bar_seen[op.eng] = self.bar_id
            deps.extend(self.bar_deps)
        for r in reads:
            if r.last_write is not None:
                deps.append(r.last_write)
        for w in writes:
            if w.last_write is not None:
                deps.append(w.last_write)
            deps.extend(w.reads)
        seen = set()
        for d in deps:
            if d is op or id(d) in seen:
                continue
            seen.add(id(d))
            if d.eng == op.eng and not d.is_dma:
                if op.eng == "pe" or not SAME_ENG_SYNC:
                    continue
            op.deps.append(d)
            d.needed = True
        for r in reads:
            r.reads.append(op)
        for w in writes:
            w.last_write = op
            w.reads = []
        self.ops[op.eng].append(op)
        self.all_ops.append(op)
        return op

    def op(self, eng, fn, reads=(), writes=(), name=""):
        return self._record(Op(eng, fn, name), list(reads), list(writes))

    def dma(self, eng, fn, reads=(), writes=(), owner=None, ndma=1, name=""):
        o = Op(eng, fn, name)
        o.is_dma = True
        o.ndma = ndma
        o.sem_owner = owner
        o.needed = True
        self.pending_dma.append(o)
        return self._record(o, list(reads), list(writes))

    def finalize(self):
        nc = self.nc
        es = self.es
        esem = {}
        for e in ("pe", "act", "dve", "pool"):
            esem[e] = es.enter_context(nc.semaphore("s_" + e))
        last_dma = {}
        qtype = {}
        for i, o in enumerate(self.all_ops):
            if o.is_dma:
                k = id(o.sem_owner)
                last_dma[k] = i
                qt = "sw" if o.eng == "pool" else "hw"
                if qtype.get(k, qt) != qt:
                    qtype[k] = "mixed"
                else:
                    qtype[k] = qt
        free = {"sw": [], "hw": [], "mixed": []}
        active = {}
        sem_final = {}
        bpos = list(self.bar_pos)
        bi = 0
        nsem = 0
        for i, o in enumerate(self.all_ops):
            while bi < len(bpos) and bpos[bi] <= i:
                b = bpos[bi]
                bi += 1
                for k in list(active.keys()):
                    ow = active[k]
                    if last_dma[k] < b:
                        if qtype[k] != "mixed":
                            free[qtype[k]].append((ow.dsem, ow.dcount))
                        del active[k]
            if o.is_dma:
                ow = o.sem_owner
                if ow.dsem is None:
                    fl = free[qtype[id(ow)]]
                    if fl and qtype[id(ow)] != "mixed":
                        ow.dsem, ow.dcount = fl.pop()
                    else:
                        ow.dsem = es.enter_context(nc.semaphore("d%d" % nsem))
                        nsem += 1
                    active[id(ow)] = ow
                ow.dcount += 16 * o.ndma
                o.sig_sem = ow.dsem
                o.sig_val = ow.dcount
                sem_final[id(ow.dsem)] = (ow.dsem, ow.dcount)
        dma_final = list(sem_final.values())
        for e in ("pe", "act", "dve", "pool"):
            c = 0
            for o in self.ops[e]:
                if o.is_dma:
                    continue
                if o.needed:
                    c += 1
                    o.sig_sem = esem[e]
                    o.sig_val = c
        self.n_sems = 4 + nsem
        engmap = {"pe": "tensor", "act": "scalar", "dve": "vector", "pool": "gpsimd", "sp": "sync"}
        with nc.Block() as block:
            for e in ENGS:
                ops = self.ops[e]
                final = (e == "sp")

                def body(eh, ops=ops, final=final):
                    seen = {}
                    for o in ops:
                        for d in o.deps:
                            key = id(d.sig_sem)
                            if seen.get(key, 0) >= d.sig_val:
                                continue
                            seen[key] = d.sig_val
                            eh.wait_ge(d.sig_sem, d.sig_val)
                        r = o.fn(eh)
                        if o.is_dma:
                            if not isinstance(r, (list, tuple)):
                                r = [r]
                            assert len(r) == o.ndma, (o.name, len(r), o.ndma)
                            for ins in r:
                                ins.then_inc(o.sig_sem, 16)
                        elif o.needed:
                            r.then_inc(o.sig_sem, 1)
                    if final:
                        for (sm, cnt) in dma_final:
                            eh.wait_ge(sm, cnt)

                getattr(block, engmap[e])(body)
        return self


class Ring:
    def __init__(self, tiles):
        self.tiles = tiles
        self.i = 0

    def next(self):
        t = self.tiles[self.i % len(self.tiles)]
        self.i += 1
        return t


import ml_dtypes
import os as _osx
NPBF = ml_dtypes.bfloat16


def tile_lhsT(w):
    K, N = w.shape
    nt = (N + 127) // 128
    wp = np.zeros((K, nt * 128), np.float32)
    wp[:, :N] = w
    kc = K // 128
    return np.ascontiguousarray(wp.reshape(kc, 128, nt, 128).transpose(2, 1, 0, 3))


def grid_pos_embed_np(n_tokens, grid_w=64):
    rows = n_tokens // grid_w
    r = np.repeat(np.arange(rows), grid_w).astype(np.float32)
    col = np.tile(np.arange(grid_w), rows).astype(np.float32)
    quarter = D // 4
    omega = (1.0 / (10000.0 ** (np.arange(quarter, dtype=np.float32) / quarter))).astype(np.float32)

    def emb(pos):
        a = pos[:, None] * omega[None, :]
        return np.concatenate([np.sin(a), np.cos(a)], axis=-1)

    return np.concatenate([emb(r), emb(col)], axis=-1).astype(np.float32)


def win_tile_cols():
    tiles = []
    for hp in range(4):
        for base in (0, 512, 1024, 1536):
            tiles.append((base + hp * 128, 128))
    tiles.append((2048, 32))
    for i in range(12):
        tiles.append((2080 + i * 128, 128))
    for i in range(4):
        tiles.append((3616 + i * 128, 128))
    for i in range(24):
        tiles.append((4128 + i * 128, 128))
    return tiles


TI_DN = 0
TI_BA = 16
TI_HY = 17
TI_FN = 29
TI_GATE = 33
N_WIN_TILES = 57


def hyena_consts(L):
    bands = 16
    t = np.linspace(0.0, 1.0, L, dtype=np.float32)[:, None]
    wpos = ((2.0 * math.pi / L) * np.arange(L, dtype=np.float32))[:, None].astype(np.float32)
    fr = np.linspace(1e-4, bands - 1, bands, dtype=np.float32)[None, :]
    zpos = np.concatenate([t, np.cos(fr * wpos), -np.sin(fr * wpos)], axis=-1).astype(np.float32)
    deltas = np.abs(np.linspace(math.log(1e-2) / 1.5, math.log(1e-2) / 0.3, 512, dtype=np.float32))
    window = np.exp(-t * deltas[None, :]).astype(np.float32)
    return np.ascontiguousarray(zpos.T), window


def dft_consts(L):
    nfp = (L // 128 + 1) * 128
    s = np.arange(L, dtype=np.float64)[:, None]
    f = np.arange(nfp, dtype=np.float64)[None, :]
    ang = np.pi * np.mod(s * f, 2 * L) / L
    valid = (f <= L)
    Fc = np.where(valid, np.cos(ang), 0.0)
    Fs = np.where(valid, np.sin(ang), 0.0)
    n = 2 * L
    cf = np.where((f == 0) | (f == L), 1.0 / n, 2.0 / n) * valid
    Gc = (Fc * cf).T
    Gs = (-Fs * cf).T
    return (Fc.astype(NPBF), Fs.astype(NPBF), np.ascontiguousarray(Gc).astype(NPBF), np.ascontiguousarray(Gs).astype(NPBF))


def fnet_consts(L):
    t = np.arange(L, dtype=np.float64)
    ang = 2 * np.pi * np.mod(np.outer(t, t), L) / L
    CL = np.cos(ang)
    SLn = -np.sin(ang)
    c = np.arange(64, dtype=np.float64)
    a64 = 2 * np.pi * np.mod(np.outer(c, c), 64) / 64
    sc = 1.0 / math.sqrt(64.0 * L)
    c64 = np.zeros((128, 128))
    s64 = np.zeros((128, 128))
    for g in range(2):
        c64[g * 64:(g + 1) * 64, g * 64:(g + 1) * 64] = np.cos(a64) * sc
        s64[g * 64:(g + 1) * 64, g * 64:(g + 1) * 64] = np.sin(a64) * sc
    return CL.astype(NPBF), SLn.astype(NPBF), c64.astype(NPBF), s64.astype(NPBF)


def mask_consts():
    i = np.arange(128)[:, None]
    j = np.arange(128)[None, :]
    m = np.stack([(i > j), (i >= j), (i < j), (i <= j)]).astype(np.float32)
    return m


M_SL, M_IL, M_SU, M_IU = 0, 1, 2, 3


def level_masks():
    i = np.arange(128)[:, None]
    j = np.arange(128)[None, :]
    ms = []
    for lv in range(7):
        bs = 2 << lv
        ms.append(((i // bs) == (j // bs)) & ((i // (bs // 2)) != (j // (bs // 2))))
    return np.ascontiguousarray(np.stack(ms, axis=1).astype(np.float32))


class Stream:
    def __init__(self, name, nseq, L, cidx, groups, is_sample):
        self.name = name
        self.nseq = nseq
        self.L = L
        self.T = nseq * L
        self.cidx = cidx
        self.nb = self.T // 512
        self.groups = groups
        self.is_sample = is_sample


class Builder:
    def __init__(self, streams, depth=DEPTH, dbg=None, skip=()):
        self.streams = streams
        self.depth = depth
        self.dbg = dbg
        self.skip = skip
        self.nc = bass.Bass("TRN2", target_bir_lowering=False)
        self.P = Prog(self.nc)
        self.dram = {}

    def din(self, name, shape, dtype=F32):
        ap = self.nc.dram_tensor(name, list(shape), dtype, kind="ExternalInput").ap()
        t = T(name, ap)
        self.dram[name] = (t, list(shape), dtype)
        return t

    def dout(self, name, shape, dtype=F32):
        ap = self.nc.dram_tensor(name, list(shape), dtype, kind="ExternalOutput").ap()
        return T(name, ap)

    def dscr(self, name, shape, dtype=F32):
        ap = self.nc.dram_tensor(name, list(shape), dtype, kind="Internal").ap()
        return T(name, ap)

    def mm(self, out_t, out_ap, lhsT_t, lhsT_ap, rhs_t, rhs_ap, start=True, stop=True):
        self.P.op("pe", lambda e: e.matmul(out_ap, lhsT=lhsT_ap, rhs=rhs_ap, start=start, stop=stop),
                  reads=[lhsT_t, rhs_t], writes=[out_t])

    def tr(self, out_t, out_ap, in_t, in_ap, ident_t=None, ident_ap=None):
        if ident_t is None:
            ident_t = self.ident
            n = in_ap.shape[0]
            ident_ap = self.ident[0:n, 0:n]
        self.P.op("pe", lambda e: e.transpose(out_ap, in_ap, ident_ap), reads=[in_t, ident_t], writes=[out_t])

    def act(self, out_t, out_ap, in_t, in_ap, func, bias=None, scale=None, extra_reads=()):
        kw = {}
        if bias is not None:
            kw["bias"] = bias
        if scale is not None:
            kw["scale"] = scale
        self.P.op("act", lambda e: e.activation(out=out_ap, in_=in_ap, func=func, **kw),
                  reads=[in_t] + list(extra_reads), writes=[out_t])

    def tt(self, eng, out_t, out_ap, a_t, a_ap, b_t, b_ap, op):
        self.P.op(eng, lambda e: e.tensor_tensor(out=out_ap, in0=a_ap, in1=b_ap, op=op),
                  reads=[a_t, b_t], writes=[out_t])

    def ts(self, out_t, out_ap, a_t, a_ap, s1, s2, op0, op1=None, extra_reads=()):
        if op1 is None:
            self.P.op("dve", lambda e: e.tensor_scalar(out=out_ap, in0=a_ap, scalar1=s1, scalar2=None, op0=op0),
                      reads=[a_t] + list(extra_reads), writes=[out_t])
        else:
            self.P.op("dve", lambda e: e.tensor_scalar(out=out_ap, in0=a_ap, scalar1=s1, scalar2=s2, op0=op0, op1=op1),
                      reads=[a_t] + list(extra_reads), writes=[out_t])

    def stt(self, out_t, out_ap, a_t, a_ap, scalar, b_t, b_ap, op0, op1, extra_reads=()):
        self.P.op("dve", lambda e: e.scalar_tensor_tensor(out=out_ap, in0=a_ap, scalar=scalar, in1=b_ap, op0=op0, op1=op1),
                  reads=[a_t, b_t] + list(extra_reads), writes=[out_t])

    def copy(self, eng, out_t, out_ap, in_t, in_ap):
        if eng == "act":
            self.P.op("act", lambda e: e.copy(out=out_ap, in_=in_ap), reads=[in_t], writes=[out_t])
        else:
            self.P.op(eng, lambda e: e.tensor_copy(out=out_ap, in_=in_ap), reads=[in_t], writes=[out_t])

    def memset(self, t, ap, val, eng="dve"):
        self.P.op(eng, lambda e: e.memset(ap, val), writes=[t])

    def load(self, out_t, out_ap, in_t, in_ap, eng="sp"):
        self.P.dma(eng, lambda e: e.dma_start(out=out_ap, in_=in_ap), reads=[in_t], writes=[out_t], owner=out_t)

    def store(self, out_t, out_ap, in_t, in_ap, eng="sp"):
        self.P.dma(eng, lambda e: e.dma_start(out=out_ap, in_=in_ap), reads=[in_t], writes=[out_t], owner=in_t)

    def pst(self):
        return self.psr.next()

    def tap(self, name, t, ap):
        if self.dbg and self.dbg.get("tap2") == name and not getattr(self, "_tapped", False):
            self._tapped = True
            self.store(self.dbg_out, self.dbg_out[:], t, ap)

    def build(self):
        P = self.P
        depth = self.depth
        for s in self.streams:
            nfp = (s.L // 128 + 1) * 128
            s.x_in = self.din("x_" + s.name, [s.T, D])
            s.y_out = self.dout("y_" + s.name, [s.T, D])
            s.xres = self.dscr("xres_" + s.name, [128, 8, s.T])
            if s.is_sample:
                s.st_in = self.din("st0_" + s.name, [depth, 2, H_A, DK, DK])
                s.pos = self.din("pos_" + s.name, [s.T, D])
            else:
                s.st_out = self.dout("st_" + s.name, [s.nseq, depth, 2, H_A, DK, DK])
            s.zposT = self.din("zposT_" + s.name, [33, s.L])
            s.window = self.din("win_" + s.name, [s.L, 512])
            s.Fc = self.din("Fc_" + s.name, [s.L, nfp], BF16)
            s.Fs = self.din("Fs_" + s.name, [s.L, nfp], BF16)
            s.Gc = self.din("Gc_" + s.name, [nfp, s.L], BF16)
            s.Gs = self.din("Gs_" + s.name, [nfp, s.L], BF16)
            s.CL = self.din("CL_" + s.name, [s.L, s.L], BF16)
            s.SLn = self.din("SLn_" + s.name, [s.L, s.L], BF16)
            s.c64 = self.din("c64_" + s.name, [128, 128], BF16)
            s.s64 = self.din("s64_" + s.name, [128, 128], BF16)
        d = {}
        d["cvecT"] = self.din("cvecT", [128, 8, 2])
        d["wmod"] = self.din("wmod", [depth, 48, 128, 8, 128])
        d["bmodT"] = self.din("bmodT", [depth, 128, 48])
        d["n1"] = self.din("n1T", [depth, 128, 8])
        d["n2"] = self.din("n2T", [depth, 128, 8])
        d["nf"] = self.din("nfT", [128, 8])
        d["win"] = self.din("win_t", [depth, N_WIN_TILES, 128, 8, 128])
        d["wp"] = self.din("wp_t", [depth, 3, 8, 128, 4, 128])
        d["wo"] = self.din("wo_t", [depth, 8, 128, 8, 128])
        d["wgu"] = self.din("wgu_t", [depth, 44, 128, 8, 128])
        d["wdn"] = self.din("wdn_t", [depth, 8, 128, 22, 128])
        d["convq"] = self.din("convqT", [depth, 64, 3, 8, 3])
        d["convh"] = self.din("convhT", [depth, 128, 12, 3])
        d["alog"] = self.din("alog_bc", [depth, 128, 16])
        d["dtb"] = self.din("dtb_bc", [depth, 128, 16])
        d["norma"] = self.din("norma_bc", [depth, 128, 64])
        d["hyw1"] = self.din("hyw1", [depth, 33, 64])
        d["hyb1"] = self.din("hyb1T", [depth, 64, 1])
        d["hyfq"] = self.din("hyfqT", [depth, 64, 1])
        d["hyw2"] = self.din("hyw2", [depth, 64, 64])
        d["hyb2"] = self.din("hyb2T", [depth, 64, 1])
        d["hyw3"] = self.din("hyw3", [depth, 64, 2048])
        d["hybias"] = self.din("hybiasT", [depth, 128, 2, 4])
        d["masks"] = self.din("masks", [4, 128, 128])
        d["ident"] = self.din("ident", [128, 128])
        d["lmask"] = self.din("lmask", [128, 7, 128])
        self.d = d
        if self.dbg:
            self.dbg_out = self.dout("dbg", self.dbg["shape"], self.dbg.get("dtype", F32))

        self.psr = Ring([P.ps("ps%d" % i, [128, 512], F32) for i in range(8)])
        TMAX = max(s.T for s in self.streams)
        self.hT = P.sb("hT", [128, 8, TMAX], BF16)
        self.merged = P.sb("merged", [128, 8, TMAX], BF16)
        self.w8 = Ring([P.sb("w8_%d" % i, [128, 8, 128], BF16) for i in range(3)])
        self.w4 = Ring([P.sb("w4_%d" % i, [128, 4, 128], BF16) for i in range(2)])
        self.ident = P.sb("ident", [128, 128], F32)
        self.onesb = P.sb("onesb", [128, 128], BF16)
        self.ones32 = P.sb("ones32", [128, 128], F32)
        self.masks = P.sb("masks", [128, 4, 128], F32)
        self.load(self.ident, self.ident[:], d["ident"], d["ident"][:])
        self.memset(self.onesb, self.onesb[:], 1.0 / 1024.0)
        self.identb2 = P.sb("identb2", [128, 2, 128], BF16)
        self.copy("dve", self.identb2, self.identb2[:, 0, :], self.ident, self.ident[:])
        self.copy("dve", self.identb2, self.identb2[:, 1, :], self.ident, self.ident[:])
        self.memset(self.ones32, self.ones32[:], 1.0)
        for i in range(4):
            self.load(self.masks, self.masks[:, i, :], d["masks"], d["masks"][i])
        self.modv = [P.sb("modv%d" % l, [128, 48, 2], F32) for l in range(depth)]
        self.modA = [P.sb("modA%d" % l, [128, 2, 8, 2], F32) for l in range(depth)]
        self.nfT = P.sb("nfT", [128, 8], F32)
        self.load(self.nfT, self.nfT[:], d["nf"], d["nf"][:])

        self.phase_mod()
        self.phase_input()
        for l in range(depth):
            for s in self.streams:
                self.phase_norm1(l, s)
                self.phase_mix(l, s)
                self.phase_out_ffn(l, s)
        self.phase_final()
        if self.dbg:
            self.dbg["fn"](self)
        P.finalize()
        P.es.close()
        return self.nc

    def wload8(self, dram_t, dram_ap):
        w = self.w8.next()
        self.load(w, w[:], dram_t, dram_ap, eng="pool")
        return w

    def phase_mod(self):
        P = self.P
        d = self.d
        with ExitStack() as ph:
            cv = P.sb("cv", [128, 8, 2], F32, ph)
            scv = P.sb("scv", [128, 8, 2], F32, ph)
            wm = Ring([P.sb("wm%d" % i, [128, 8, 128], F32, ph) for i in range(3)])
            bm = P.sb("bm", [128, 48], F32, ph)
            n12 = P.sb("n12", [128, 2, 8], F32, ph)
            self.load(cv, cv[:], d["cvecT"], d["cvecT"][:])
            self.act(scv, scv[:], cv, cv[:], AF.Silu)
            for l in range(self.depth):
                modv = self.modv[l]
                self.load(bm, bm[:], d["bmodT"], d["bmodT"][l])
                self.load(n12, n12[:, 0, :], d["n1"], d["n1"][l])
                self.load(n12, n12[:, 1, :], d["n2"], d["n2"][l])
                for c in range(48):
                    w = wm.next()
                    self.load(w, w[:], d["wmod"], d["wmod"][l, c])
                    ps = self.pst()
                    for kc in range(8):
                        self.mm(ps, ps[:, 0:2], w, w[:, kc, :], scv, scv[:, kc, :], kc == 0, kc == 7)
                    self.ts(modv, modv[:, c, :], ps, ps[:, 0:2], bm[:, c:c + 1], None, ALU.add, extra_reads=[bm])
                mA = self.modA[l]
                for sub in range(2):
                    sc0 = 8 + 24 * sub
                    for j in range(2):
                        self.ts(mA, mA[:, sub, :, j], modv, modv[:, sc0:sc0 + 8, j], 1.0, None, ALU.add)
                        self.tt("dve", mA, mA[:, sub, :, j], mA, mA[:, sub, :, j], n12, n12[:, sub, :], ALU.mult)
        P.barrier()

    def phase_input(self):
        P = self.P
        with ExitStack() as ph:
            xin = Ring([P.sb("xin%d" % i, [128, D], F32, ph) for i in range(2)])
            pin = Ring([P.sb("pin%d" % i, [128, D], F32, ph) for i in range(2)])
            xo = Ring([P.sb("xo%d" % i, [128, 8, 128], F32, ph) for i in range(2)])
            for s in self.streams:
                for tt in range(s.T // 128):
                    xt = xin.next()
                    self.load(xt, xt[:], s.x_in, s.x_in[tt * 128:(tt + 1) * 128, :])
                    if s.is_sample:
                        pt = pin.next()
                        self.load(pt, pt[:], s.pos, s.pos[tt * 128:(tt + 1) * 128, :])
                        self.tt("dve", xt, xt[:], xt, xt[:], pt, pt[:], ALU.add)
                    xot = xo.next()
                    for half in range(2):
                        ps = self.pst()
                        for c4 in range(4):
                            c = half * 4 + c4
                            self.tr(ps, ps[:, c4 * 128:(c4 + 1) * 128], xt, xt[:, c * 128:(c + 1) * 128])
                        self.copy("act" if half else "dve", xot, xot[:, half * 4:half * 4 + 4, :], ps,
                                  ps[:].rearrange("p (c t) -> p c t", c=4))
                    self.store(s.xres, s.xres[:, :, tt * 128:(tt + 1) * 128], xot, xot[:])
        P.barrier()

    def rstd_block(self, xb, xb_ap, sqr, rstd):
        ps = self.pst()
        for c in range(8):
            sq = sqr.next()
            self.act(sq, sq[:], xb, xb_ap[:, c, :], AF.Square)
            self.mm(ps, ps[:], self.onesb, self.onesb[:], sq, sq[:], c == 0, c == 7)
        self.act(rstd, rstd[:], ps, ps[:], AF.Sqrt, bias=EPS, scale=1.0)
        self.P.op("dve", lambda e: e.reciprocal(out=rstd[:], in_=rstd[:]), reads=[rstd], writes=[rstd])

    def norm_block(self, l, s, sub, xb, blk, sqr, rstd, tmpr):
        j = s.cidx
        self.rstd_block(xb, xb[:], sqr, rstd)
        mA = self.modA[l]
        modv = self.modv[l]
        sh0 = 24 * sub
        for c in range(8):
            tmp = tmpr.next()
            self.tt("dve", tmp, tmp[:], xb, xb[:, c, :], rstd, rstd[:], ALU.mult)
            self.act(self.hT, self.hT[:, c, blk * 512:(blk + 1) * 512], tmp, tmp[:], AF.Identity,
                     bias=modv[:, sh0 + c, j:j + 1], scale=mA[:, sub, c, j:j + 1], extra_reads=[mA, modv])

    def phase_norm1(self, l, s):
        P = self.P
        with ExitStack() as ph:
            xbr = Ring([P.sb("xb%d" % i, [128, 8, 512], F32, ph) for i in range(2)])
            sqr = Ring([P.sb("sq%d" % i, [128, 512], BF16, ph) for i in range(2)])
            tmpr = Ring([P.sb("ntmp%d" % i, [128, 512], F32, ph) for i in range(2)])
            rstd = P.sb("rstd", [128, 512], F32, ph)
            for blk in range(s.nb):
                xb = xbr.next()
                self.load(xb, xb[:], s.xres, s.xres[:, :, blk * 512:(blk + 1) * 512])
                self.norm_block(l, s, 0, xb, blk, sqr, rstd, tmpr)
        P.barrier()

    def proj_fm(self, l, ti, s, evac, m0=0, m1=128):
        w = self.wload8(self.d["win"], self.d["win"][l, ti])
        for blk in range(s.nb):
            ps = self.pst()
            for kc in range(8):
                self.mm(ps, ps[0:m1 - m0, :], w, w[:, kc, m0:m1], self.hT, self.hT[:, kc, blk * 512:(blk + 1) * 512], kc == 0, kc == 7)
            evac(ps, blk)

    def merge_branch(self, l, s, br, y_t):
        for o in range(8):
            wg = self.wload8(self.d["win"], self.d["win"][l, TI_GATE + br * 8 + o])
            wp = self.w4.next()
            self.load(wp, wp[:], self.d["wp"], self.d["wp"][l, br, o], eng="pool")
            for blk in range(s.nb):
                sl = slice(blk * 512, (blk + 1) * 512)
                ps1 = self.pst()
                for kc in range(8):
                    self.mm(ps1, ps1[:], wg, wg[:, kc, :], self.hT, self.hT[:, kc, sl], kc == 0, kc == 7)
                ps2 = self.pst()
                for kc in range(4):
                    self.mm(ps2, ps2[:], wp, wp[:, kc, :], y_t, y_t[:, kc, sl], kc == 0, kc == 3)
                sig = self.sigr.next()
                self.act(sig, sig[:], ps1, ps1[:], AF.Sigmoid)
                if br == 0:
                    self.tt("dve", self.merged, self.merged[:, o, sl], sig, sig[:], ps2, ps2[:], ALU.mult)
                else:
                    self.tt("dve", sig, sig[:], sig, sig[:], ps2, ps2[:], ALU.mult)
                    self.tt("pool", self.merged, self.merged[:, o, sl], self.merged, self.merged[:, o, sl], sig, sig[:], ALU.add)

    def phase_mix(self, l, s):
        P = self.P
        with ExitStack() as ph:
            self.sigr = Ring([P.sb("sig%d" % i, [128, 512], F32, ph) for i in range(2)])
            y = P.sb("ybr", [128, 4, s.T], BF16, ph)
            with ExitStack() as ph2:
                if "delta" in self.skip:
                    self.memset(y, y[:], 0.0)
                else:
                    self.mix_delta(l, s, y, ph2)
            P.barrier()
            self.merge_branch(l, s, 0, y)
            P.barrier()
            for (ct0, nct) in s.groups:
                with ExitStack() as ph2:
                    if "hyena" in self.skip:
                        self.memset(y, y[:], 0.0)
                    else:
                        self.mix_hyena(l, s, y, ct0, nct, ph2)
                P.barrier()
            if self.dbg and self.dbg.get("tap") == ("yb", l, s.name):
                self.store(self.dbg_out, self.dbg_out[:], y, y[:])
            self.merge_branch(l, s, 1, y)
            P.barrier()
            for (ct0, nct) in [(0, 4)]:
                with ExitStack() as ph2:
                    self.mix_fnet(l, s, y, ct0, nct, ph2)
                P.barrier()
            if self.dbg and self.dbg.get("tap") == ("yc", l, s.name):
                self.store(self.dbg_out, self.dbg_out[:], y, y[:])
            self.merge_branch(l, s, 2, y)
        P.barrier()

    def mix_fnet(self, l, s, y, ct0, nct, ph):
        P = self.P
        L = s.L
        nt = L // 128
        W = nct * 128
        xc = P.sb("xc", [128, nct, s.T], BF16, ph)
        c64 = P.sb("c64", [128, 128], BF16, ph)
        s64 = P.sb("s64", [128, 128], BF16, ph)
        self.load(c64, c64[:], s.c64, s.c64[:])
        self.load(s64, s64[:], s.s64, s.s64[:])
        for ci in range(nct):
            self.proj_fm(l, TI_FN + ct0 + ci, s,
                         lambda ps, blk, ci=ci: self.copy("act", xc, xc[:, ci, blk * 512:(blk + 1) * 512], ps, ps[:]))
        U = P.sb("U", [128, nt, 2, W], BF16, ph)
        NW = 256
        dftc = Ring([P.sb("dftc%d" % i, [128, nt, NW], BF16, ph) for i in range(2)])
        dfts = Ring([P.sb("dfts%d" % i, [128, nt, NW], BF16, ph) for i in range(2)])
        for q in range(s.nseq):
            t0 = q * L
            for tt in range(nt):
                for cs, mat in ((0, c64), (1, s64)):
                    ps = self.pst()
                    for ci in range(nct):
                        self.mm(ps, ps[:, ci * 128:(ci + 1) * 128], xc, xc[:, ci, t0 + tt * 128:t0 + (tt + 1) * 128], mat, mat[:])
                    self.copy("act" if cs else "dve", U, U[:, tt, cs, :], ps, ps[:, 0:W])
            for nbk in range(L // NW):
                cm = dftc.next()
                sm = dfts.next()
                self.load(cm, cm[:], s.CL, s.CL[:, nbk * NW:(nbk + 1) * NW].rearrange("(k p) n -> p k n", p=128))
                self.load(sm, sm[:], s.SLn, s.SLn[:, nbk * NW:(nbk + 1) * NW].rearrange("(k p) n -> p k n", p=128))
                for ci in range(nct):
                    ps = self.pst()
                    for tt in range(nt):
                        self.mm(ps, ps[:, 0:NW], U, U[:, tt, 0, ci * 128:(ci + 1) * 128], cm, cm[:, tt, :], tt == 0, False)
                        self.mm(ps, ps[:, 0:NW], U, U[:, tt, 1, ci * 128:(ci + 1) * 128], sm, sm[:, tt, :], False, tt == nt - 1)
                    self.copy("act" if ci % 2 else "dve", y, y[:, ct0 + ci, t0 + nbk * NW:t0 + (nbk + 1) * NW], ps, ps[:, 0:NW])

    def sin_reduce(self, out_t, out_ap, in_t, in_ap, ti, ti_ap, tf, tf_ap):
        P = self.P
        inv = 1.0 / (2.0 * math.pi)
        P.op("dve", lambda e: e.tensor_scalar(out=ti_ap, in0=in_ap, scalar1=inv, scalar2=None, op0=ALU.mult),
             reads=[in_t], writes=[ti])
        self.copy("dve", tf, tf_ap, ti, ti_ap)
        self.stt(tf, tf_ap, tf, tf_ap, -2.0 * math.pi, in_t, in_ap, ALU.mult, ALU.add)
        self.ts(tf, tf_ap, tf, tf_ap, math.pi, -math.pi, ALU.min, ALU.max)
        self.act(out_t, out_ap, tf, tf_ap, AF.Sin)

    def hy_gate(self, l, s, which, ct, raw, dst, dst_ap, cw):
        L = s.L
        self.proj_fm(l, TI_HY + which * 4 + ct, s,
                     lambda ps, blk: self.copy("act", raw, raw[:, blk * 512:(blk + 1) * 512], ps, ps[:]))
        wi = which * 4 + ct
        for q in range(s.nseq):
            a, b = q * L, (q + 1) * L
            self.ts(dst, dst_ap[:, a:b], raw, raw[:, a:b], cw[:, wi, 1:2], None, ALU.mult, extra_reads=[cw])
            self.stt(dst, dst_ap[:, a + 1:b], raw, raw[:, a:b - 1], cw[:, wi, 0:1], dst, dst_ap[:, a + 1:b], ALU.mult, ALU.add, extra_reads=[cw])
            self.stt(dst, dst_ap[:, a:b - 1], raw, raw[:, a + 1:b], cw[:, wi, 2:3], dst, dst_ap[:, a:b - 1], ALU.mult, ALU.add, extra_reads=[cw])

    def mix_hyena(self, l, s, y, ct0, nct, ph):
        P = self.P
        d = self.d
        L = s.L
        nt = L // 128
        nf = nt + 1
        T_ = s.T
        W = nct * 128
        cw = P.sb("hcw", [128, 12, 3], F32, ph)
        self.load(cw, cw[:], d["convh"], d["convh"][l])
        hb = P.sb("hbias", [128, 2, 4], F32, ph)
        self.load(hb, hb[:], d["hybias"], d["hybias"][l])
        raw = P.sb("hraw", [128, T_], F32, ph)
        gate = P.sb("hgate", [128, nct, T_], BF16, ph)
        z = P.sb("hz", [128, nct, T_], BF16, ph)
        for ci in range(nct):
            self.hy_gate(l, s, 2, ct0 + ci, raw, z, z[:, ci, :], cw)
        Hre = P.sb("Hre", [128, 2, nf, W], BF16, ph)
        Him = P.sb("Him", [128, 2, nf, W], BF16, ph)
        ztm = P.sb("ztm", [128, nt, W], BF16, ph)
        Yre = P.sb("Yre", [128, nf, W], BF16, ph)
        Yim = P.sb("Yim", [128, nf, W], BF16, ph)
        fcr = Ring([P.sb("fcr%d" % i, [128, nt, 128], BF16, ph) for i in range(2)])
        fsr = Ring([P.sb("fsr%d" % i, [128, nt, 128], BF16, ph) for i in range(2)])
        tmpz = Ring([P.sb("tmpz%d" % i, [128, W], F32, ph) for i in range(4)])
        zf = P.sb("hzf", [128, 128], F32, ph)

        with ExitStack() as pA:
            hsd = P.sb("hsd", [128, nt, 2, 2, W], BF16, pA)
            with ExitStack() as pf:
                w1 = P.sb("hw1", [33, 64], F32, pf)
                w2 = P.sb("hw2", [64, 64], F32, pf)
                w3 = P.sb("hw3", [64, 2048], F32, pf)
                b1 = P.sb("hb1", [64, 1], F32, pf)
                b2 = P.sb("hb2", [64, 1], F32, pf)
                fq = P.sb("hfq", [64, 1], F32, pf)
                zp = P.sb("hzp", [33, min(L, 512)], F32, pf)
                h1 = P.sb("hh1", [64, min(L, 512)], F32, pf)
                h2 = P.sb("hh2", [64, min(L, 512)], F32, pf)
                ti = P.sb("hti", [64, 512], I32, pf)
                tf = P.sb("htf", [64, 512], F32, pf)
                ta = P.sb("hta", [64, 512], F32, pf)
                win = Ring([P.sb("hwin%d" % i, [128, W], F32, pf) for i in range(2)])
                hf = Ring([P.sb("hf%d" % i, [128, 4, W], F32, pf) for i in range(2)])
                self.load(w1, w1[:], d["hyw1"], d["hyw1"][l])
                self.load(w2, w2[:], d["hyw2"], d["hyw2"][l])
                self.load(w3, w3[:], d["hyw3"], d["hyw3"][l])
                self.load(b1, b1[:], d["hyb1"], d["hyb1"][l])
                self.load(b2, b2[:], d["hyb2"], d["hyb2"][l])
                self.load(fq, fq[:], d["hyfq"], d["hyfq"][l])
                wb = min(L, 512)
                for blk in range(L // wb):
                    sl = slice(blk * wb, (blk + 1) * wb)
                    self.load(zp, zp[:], s.zposT, s.zposT[:, sl])
                    for (src, wsrc, bsrc, dst, kk) in ((zp, w1, b1, h1, 33), (h1, w2, b2, h2, 64)):
                        ps = self.pst()
                        self.mm(ps, ps[0:64, 0:wb], wsrc, wsrc[0:kk, :], src, src[0:kk, :])
                        self.ts(ta, ta[:, 0:wb], ps, ps[0:64, 0:wb], bsrc[:, 0:1], fq[:, 0:1], ALU.add, ALU.mult, extra_reads=[bsrc, fq])
                        self.sin_reduce(dst, dst[:], ta, ta[:, 0:wb], ti, ti[:, 0:wb], tf, tf[:, 0:wb])
                    for t4 in range(wb // 128):
                        tt = blk * (wb // 128) + t4
                        wn = win.next()
                        self.load(wn, wn[:], s.window, s.window[tt * 128:(tt + 1) * 128, ct0 * 128:ct0 * 128 + W])
                        h = hf.next()
                        for fi in range(4):
                            ps = self.pst()
                            c0 = fi * 512 + ct0 * 128
                            self.mm(ps, ps[:, 0:W], h2, h2[:, t4 * 128:(t4 + 1) * 128], w3, w3[:, c0:c0 + W])
                            self.tt("dve", h, h[:, fi, :], ps, ps[:, 0:W], wn, wn[:], ALU.mult)
                        for o in range(2):
                            self.tt("dve", hsd, hsd[:, tt, o, 0, :], h, h[:, 2 * o, :], h, h[:, 2 * o + 1, :], ALU.add)
                            self.tt("pool", hsd, hsd[:, tt, o, 1, :], h, h[:, 2 * o + 1, :], h, h[:, 2 * o, :], ALU.subtract)
            P.barrier()
            def ld_f(ft):
                fcm = fcr.next()
                fsm = fsr.next()
                self.load(fcm, fcm[:], s.Fc, s.Fc[:, ft * 128:(ft + 1) * 128].rearrange("(k p) n -> p k n", p=128))
                self.load(fsm, fsm[:], s.Fs, s.Fs[:, ft * 128:(ft + 1) * 128].rearrange("(k p) n -> p k n", p=128))
                return fcm, fsm

            def build_ztm(q):
                t0 = q * L
                for tt in range(nt):
                    ps = self.pst()
                    for ci in range(nct):
                        self.copy("dve", zf, zf[:], z, z[:, ci, t0 + tt * 128:t0 + (tt + 1) * 128])
                        self.tr(ps, ps[:, ci * 128:(ci + 1) * 128], zf, zf[:])
                    self.copy("act" if tt % 2 else "dve", ztm, ztm[:, tt, :], ps, ps[:, 0:W])

            def fwd_product(ft, fcm, fsm, o):
                pc = self.pst()
                for tt in range(nt):
                    self.mm(pc, pc[:, 0:W], fcm, fcm[:, tt, :], ztm, ztm[:, tt, :], tt == 0, tt == nt - 1)
                pz = self.pst()
                for tt in range(nt):
                    self.mm(pz, pz[:, 0:W], fsm, fsm[:, tt, :], ztm, ztm[:, tt, :], tt == 0, tt == nt - 1)
                a1 = tmpz.next(); a2 = tmpz.next(); a3 = tmpz.next(); a4 = tmpz.next()
                self.tt("dve", a1, a1[:], pc, pc[:, 0:W], Hre, Hre[:, o, ft, :], ALU.mult)
                self.tt("dve", a2, a2[:], pz, pz[:, 0:W], Him, Him[:, o, ft, :], ALU.mult)
                self.tt("pool", Yre, Yre[:, ft, :], a1, a1[:], a2, a2[:], ALU.add)
                self.tt("dve", a3, a3[:], pc, pc[:, 0:W], Him, Him[:, o, ft, :], ALU.mult)
                self.tt("dve", a4, a4[:], pz, pz[:, 0:W], Hre, Hre[:, o, ft, :], ALU.mult)
                self.tt("pool", Yim, Yim[:, ft, :], a3, a3[:], a4, a4[:], ALU.subtract)

            fuse = (s.nseq == 1)
            if fuse:
                build_ztm(0)
            for ft in range(nf):
                fcm, fsm = ld_f(ft)
                for o in range(2):
                    for (mat, sd, dst) in ((fcm, 0, Hre), (fsm, 1, Him)):
                        ps = self.pst()
                        for tt in range(nt):
                            self.mm(ps, ps[:, 0:W], mat, mat[:, tt, :], hsd, hsd[:, tt, o, sd, :], tt == 0, tt == nt - 1)
                        self.copy("act" if sd else "dve", dst, dst[:, o, ft, :], ps, ps[:, 0:W])
                if fuse:
                    fwd_product(ft, fcm, fsm, 0)
        P.barrier()
        NW = 128
        gcr = Ring([P.sb("gcr%d" % i, [128, nf, NW], BF16, ph) for i in range(2)])
        gsr = Ring([P.sb("gsr%d" % i, [128, nf, NW], BF16, ph) for i in range(2)])
        for o in range(2):
            for ci in range(nct):
                self.hy_gate(l, s, o, ct0 + ci, raw, gate, gate[:, ci, :], cw)
            for q in range(s.nseq):
                t0 = q * L
                if not (fuse and o == 0):
                    build_ztm(q)
                    for ft in range(nf):
                        fcm, fsm = ld_f(ft)
                        fwd_product(ft, fcm, fsm, o)
                for nbk in range(L // NW):
                    gc = gcr.next()
                    gs = gsr.next()
                    self.load(gc, gc[:], s.Gc, s.Gc[:, nbk * NW:(nbk + 1) * NW].rearrange("(k p) n -> p k n", p=128))
                    self.load(gs, gs[:], s.Gs, s.Gs[:, nbk * NW:(nbk + 1) * NW].rearrange("(k p) n -> p k n", p=128))
                    sl = slice(t0 + nbk * NW, t0 + (nbk + 1) * NW)
                    for ci in range(nct):
                        ps = self.pst()
                        for ft in range(nf):
                            self.mm(ps, ps[:, 0:NW], Yre, Yre[:, ft, ci * 128:(ci + 1) * 128], gc, gc[:, ft, :], ft == 0, False)
                            self.mm(ps, ps[:, 0:NW], Yim, Yim[:, ft, ci * 128:(ci + 1) * 128], gs, gs[:, ft, :], False, ft == nf - 1)
                        a1 = tmpz.next()
                        self.stt(a1, a1[:, 0:NW], z, z[:, ci, sl], hb[:, o, ct0 + ci:ct0 + ci + 1], ps, ps[:, 0:NW], ALU.mult, ALU.add, extra_reads=[hb])
                        if o == 0:
                            self.tt("dve", z, z[:, ci, sl], a1, a1[:, 0:NW], gate, gate[:, ci, sl], ALU.mult)
                        else:
                            self.tt("dve", y, y[:, ct0 + ci, sl], a1, a1[:, 0:NW], gate, gate[:, ci, sl], ALU.mult)

    def mix_delta(self, l, s, y, ph):
        P = self.P
        d = self.d
        L = s.L
        T_ = s.T
        NT = T_ // 128
        cps = L // 128
        cwq = P.sb("dcw", [64, 3, 8, 3], F32, ph)
        self.load(cwq, cwq[:], d["convq"], d["convq"][l])
        alog = P.sb("dalog", [128, 16], F32, ph)
        dtb = P.sb("ddtb", [128, 16], F32, ph)
        norma = P.sb("dnorma", [128, 64], F32, ph)
        self.load(alog, alog[:], d["alog"], d["alog"][l])
        self.load(dtb, dtb[:], d["dtb"], d["dtb"][l])
        self.load(norma, norma[:], d["norma"], d["norma"][l])
        ba = P.sb("dba", [128, NT, 32], F32, ph)
        beta = P.sb("dbeta", [128, NT, 16], F32, ph)
        nbeta = P.sb("dnbeta", [128, NT, 16], F32, ph)
        g = P.sb("dg", [128, NT, 16], F32, ph)
        wba = self.wload8(d["win"], d["win"][l, TI_BA])
        for tt in range(NT):
            ps = self.pst()
            for kc in range(8):
                self.mm(ps, ps[:, 0:32], self.hT, self.hT[:, kc, tt * 128:(tt + 1) * 128], wba, wba[:, kc, 0:32], kc == 0, kc == 7)
            self.copy("act" if tt % 2 else "dve", ba, ba[:, tt, :], ps, ps[:, 0:32])
        self.act(beta, beta[:], ba, ba[:, :, 0:16], AF.Sigmoid)
        self.ts(nbeta, nbeta[:], beta, beta[:], -1.0, None, ALU.mult)
        self.tt("dve", g, g[:], ba, ba[:, :, 16:32], dtb, dtb[:, None, :].to_broadcast([128, NT, 16]), ALU.add)
        self.act(g, g[:], g, g[:], AF.Exp)
        self.act(g, g[:], g, g[:], AF.Ln, bias=1.0, scale=1.0)
        self.act(alog, alog[:], alog, alog[:], AF.Exp)
        self.stt(g, g[:], g, g[:], -1.0, alog, alog[:, None, :].to_broadcast([128, NT, 16]), ALU.mult, ALU.mult)

        self.tap('g', g, g[:])
        self.tap('beta', beta, beta[:])
        raw = P.sb("draw", [64, T_], F32, ph)
        qf = P.sb("dq", [64, T_], F32, ph)
        kf = P.sb("dk", [64, T_], F32, ph)
        vf = P.sb("dv", [64, T_], F32, ph)
        zf = P.sb("dz", [64, T_], F32, ph)
        qb = P.sb("dqb", [64, T_], BF16, ph)
        kb = P.sb("dkb", [64, T_], BF16, ph)
        osum = P.sb("dosum", [128, NT, 64], F32, ph)
        ytm = P.sb("dytm", [128, NT, 128], F32, ph)
        sqt = P.sb("dsq", [64, 512], F32, ph)
        rn = P.sb("drn", [64, 512], F32, ph)
        KSLOT = int(_osx.environ.get("KSLOT", "3" if s.is_sample else "4"))
        lmask = P.sb("dlmask", [128, 7, 128], F32, ph)
        self.load(lmask, lmask[:], d["lmask"], d["lmask"][:])
        osum2 = P.sb("dosum2", [128, NT, 64], F32, ph)
        r_t1 = Ring([P.sb("dt1%d" % i, [128, 64], F32, ph) for i in range(2)])

        def mkslot(i):
            R = {}
            def a(name, shape, dt):
                R[name] = P.sb("d%s_%d" % (name, i), shape, dt, ph)
            a("S", [64, 64], F32); a("Sb", [64, 64], BF16)
            a("gbc", [128, 128], F32); a("dcol", [128, 4], F32); a("e3", [128, 4], F32)
            a("dabs", [128, 128], F32); a("Dm", [128, 128], F32); a("Ds", [128, 128], F32); a("Di", [128, 128], F32)
            a("P0", [128, 2, 128], F32); a("NTk", [128, 7, 128], BF16); a("qkT", [128, 128], BF16)
            a("kv", [128, 128], F32); a("X", [128, 128], BF16); a("Xf", [128, 128], F32)
            a("kg", [128, 64], BF16); a("wT", [64, 128], BF16); a("vn", [128, 64], BF16); a("t1", [128, 64], F32)
            a("bw", [128, 1], F32)
            nbk = 8 // KSLOT
            bk = self.psr.tiles[nbk * i:nbk * (i + 1)]
            names = ["psd", "psk", "pkv", "pst_", "psw", "ps2", "psx", "pw", "psv", "pso", "pss"]
            if nbk >= 4:
                amap = {"psd": 0, "psk": 1, "pkv": 2, "pst_": 3, "psw": 0, "ps2": 2, "psx": 1, "pw": 3, "psv": 0, "pso": 1, "pss": 2}
            else:
                amap = {"psd": 0, "psk": 1, "pkv": 0, "pst_": 1, "psw": 0, "ps2": 1, "psx": 0, "pw": 1, "psv": 0, "pso": 1, "pss": 0}
            R["ph"] = {n: bk[amap[n] % nbk] for n in names}
            R["TT"] = Ring([P.sb("dTT%d_%d" % (j, i), [128, 2, 128], BF16, ph) for j in range(2)])
            a("WW", [128, 2, 128], BF16)
            return R
        slots = [mkslot(i) for i in range(KSLOT)]
        chS = [(P.sb("dchS%d" % i, [64, 64], F32, ph), P.sb("dchSb%d" % i, [64, 64], BF16, ph)) for i in range(2 * s.nseq)]
        masks = self.masks
        sq2 = P.sb("dsq2", [128, NT, 64], F32, ph)
        ssq = P.sb("dssq", [128, NT], F32, ph)

        import os as _os
        _NH = int(_os.environ.get('DN_HEADS', '8'))
        _ST = int(_os.environ.get('DN_STAGE', '9'))
        for h in range(_NH):
            hp, half = h // 2, h % 2
            m0, m1 = half * 64, half * 64 + 64
            for which, dst in ((0, qf), (1, kf), (2, vf)):
                self.proj_fm(l, TI_DN + hp * 4 + which, s,
                             lambda ps, blk: self.copy("act", raw, raw[:, blk * 512:(blk + 1) * 512], ps, ps[0:64, :]), m0, m1)
                for q in range(s.nseq):
                    a, b = q * L, (q + 1) * L
                    self.ts(dst, dst[:, a:b], raw, raw[:, a:b], cwq[:, which, h, 1:2], None, ALU.mult, extra_reads=[cwq])
                    self.stt(dst, dst[:, a + 1:b], raw, raw[:, a:b - 1], cwq[:, which, h, 0:1], dst, dst[:, a + 1:b], ALU.mult, ALU.add, extra_reads=[cwq])
                    self.stt(dst, dst[:, a:b - 1], raw, raw[:, a + 1:b], cwq[:, which, h, 2:3], dst, dst[:, a:b - 1], ALU.mult, ALU.add, extra_reads=[cwq])
                self.act(dst, dst[:], dst, dst[:], AF.Silu)
            self.proj_fm(l, TI_DN + hp * 4 + 3, s,
                         lambda ps, blk: self.copy("act", zf, zf[:, blk * 512:(blk + 1) * 512], ps, ps[0:64, :]), m0, m1)
            for (x, xb_, sc) in ((qf, qb, 64.0), (kf, kb, 1.0)):
                for blk in range(s.nb):
                    sl = slice(blk * 512, (blk + 1) * 512)
                    self.tt("dve", sqt, sqt[:], x, x[:, sl], x, x[:, sl], ALU.mult)
                    ps = self.pst()
                    self.mm(ps, ps[0:64, :], self.ones32, self.ones32[0:64, 0:64], sqt, sqt[:])
                    self.act(rn, rn[:], ps, ps[0:64, :], AF.Sqrt, bias=EPS * sc, scale=sc)
                    P.op("dve", lambda e: e.reciprocal(out=rn[:], in_=rn[:]), reads=[rn], writes=[rn])
                    self.tt("dve", x, x[:, sl], x, x[:, sl], rn, rn[:], ALU.mult)
                self.copy("act", xb_, xb_[:], x, x[:])
            self.tap('q', qf, qf[:])
            self.tap('k', kf, kf[:])
            self.tap('v', vf, vf[:])
            def unit(dr, q, pos, R):
                col = dr * 8 + h
                if dr == 0:
                    cm, rm, sm, im = M_IU, M_SL, M_SL, M_IL
                else:
                    cm, rm, sm, im = M_IL, M_SU, M_SU, M_IU
                ch = chains[(dr, q)]
                S, Sbb = ch["S"], ch["Sb"]
                oacc = osum if dr == 0 else osum2
                cl = pos if dr == 0 else cps - 1 - pos
                if True:
                    c = q * cps + cl
                    tsl = slice(c * 128, (c + 1) * 128)
                    gcol = g[:, c, col:col + 1]
                    bcol = beta[:, c, col:col + 1]
                    nbcol = nbeta[:, c, col:col + 1]
                    gbc, dcol, e3, dabs, Dm, Ds, Di = R["gbc"], R["dcol"], R["e3"], R["dabs"], R["Dm"], R["Ds"], R["Di"]
                    P0, NTk, qkT, kv, X, Xf = R["P0"], R["NTk"], R["qkT"], R["kv"], R["X"], R["Xf"]
                    kg, wT, vn, t1, bw, WW = R["kg"], R["wT"], R["vn"], R["t1"], R["bw"], R["WW"]
                    self.copy("pool", gbc, gbc[:], g, gcol.to_broadcast([128, 128]))
                    yield
                    psd = R["ph"]["psd"]
                    self.mm(psd, psd[:, 0:128], gbc, gbc[:], masks, masks[:, cm, :])
                    self.mm(psd, psd[:, 128:129], masks, masks[:, cm, :], g, gcol)
                    self.mm(psd, psd[:, 129:130], masks, masks[:, rm, :], g, gcol)
                    self.mm(psd, psd[:, 130:131], self.ones32, self.ones32[:], g, gcol)
                    yield
                    self.copy("dve", dcol, dcol[:, 0:3], psd, psd[:, 128:131])
                    yield
                    self.act(e3, e3[:, 0:3], dcol, dcol[:, 0:3], AF.Exp)
                    self.ts(dabs, dabs[:], psd, psd[:, 0:128], dcol[:, 0:1], 0.0, ALU.subtract, ALU.max, extra_reads=[dcol])
                    yield
                    self.act(Dm, Dm[:], dabs, dabs[:], AF.Exp, scale=-1.0)
                    yield
                    self.tt("pool", Ds, Ds[:], Dm, Dm[:], masks, masks[:, sm, :], ALU.mult)
                    self.tt("pool", Di, Di[:], Dm, Dm[:], masks, masks[:, im, :], ALU.mult)
                    psk = R["ph"]["psk"]
                    self.mm(psk, psk[:, 0:128], kb, kb[:, tsl], kb, kb[:, tsl])
                    self.mm(psk, psk[:, 128:256], qb, qb[:, tsl], kb, kb[:, tsl])
                    yield
                    self.stt(P0, P0[:, 0, :], psk, psk[:, 0:128], nbcol, Ds, Ds[:], ALU.mult, ALU.mult, extra_reads=[nbeta])
                    self.tt("dve", P0, P0[:, 1, :], psk, psk[:, 128:256], Di, Di[:], ALU.mult)
                    yield
                    pst_ = R["ph"]["pst_"]
                    self.tr(pst_, pst_[:, 0:128], P0, P0[:, 0, :])
                    self.tr(pst_, pst_[:, 128:256], P0, P0[:, 1, :])
                    pkv = R["ph"]["pkv"]
                    self.tr(pkv, pkv[:, 0:64], kf, kf[:, tsl])
                    self.tr(pkv, pkv[:, 64:128], vf, vf[:, tsl])
                    yield
                    self.tt("dve", NTk, NTk[:], pst_, pst_[:, 0:128][:, None, :].to_broadcast([128, 7, 128]), lmask, lmask[:], ALU.mult)
                    self.copy("dve", qkT, qkT[:], pst_, pst_[:, 128:256])
                    self.copy("act", kv, kv[:], pkv, pkv[:, 0:128])
                    self.tt("dve", bw, bw[:], beta, bcol, e3, e3[:, 0:1], ALU.mult)
                    yield
                    self.ts(X, X[:, 0:64], kv, kv[:, 64:128], bcol, None, ALU.mult, extra_reads=[beta])
                    self.ts(X, X[:, 64:128], kv, kv[:, 0:64], bw[:, 0:1], None, ALU.mult, extra_reads=[bw])
                    self.ts(kg, kg[:], kv, kv[:, 0:64], e3[:, 1:2], None, ALU.mult, extra_reads=[e3])
                    yield
                    TT = self.identb2
                    for lev in range(7):
                        psw = R["ph"]["psw"]
                        self.mm(psw, psw[:, 0:128], NTk, NTk[:, lev, :], TT, TT[:, 0, :])
                        self.mm(psw, psw[:, 128:256], TT, TT[:, 0, :], NTk, NTk[:, lev, :])
                        yield
                        self.copy("act", WW, WW[:].rearrange("p a b -> p (a b)"), psw, psw[:, 0:256])
                        yield
                        ps2 = R["ph"]["ps2"]
                        self.mm(ps2, ps2[:, 0:128], TT, TT[:, 1, :], WW, WW[:, 0, :])
                        self.mm(ps2, ps2[:, 128:256], WW, WW[:, 0, :], TT, TT[:, 1, :])
                        yield
                        TTn = R["TT"].next()
                        self.tt("dve", TTn, TTn[:].rearrange("p a b -> p (a b)"), TT, TT[:].rearrange("p a b -> p (a b)"), ps2, ps2[:, 0:256], ALU.add)
                        TT = TTn
                        yield
                    psx = R["ph"]["psx"]
                    self.mm(psx, psx[:, 0:128], TT, TT[:, 1, :], X, X[:])
                    yield
                    self.copy("act", Xf, Xf[:], psx, psx[:, 0:128])
                    yield
                    pw = R["ph"]["pw"]
                    self.tr(pw, pw[0:64, 0:128], Xf, Xf[:, 64:128])
                    yield
                    self.copy("act", wT, wT[:], pw, pw[0:64, 0:128])
                    yield
                    while ch["done"] < pos:
                        yield
                    psv = R["ph"]["psv"]
                    self.mm(psv, psv[:, 0:64], wT, wT[:], Sbb, Sbb[:])
                    self.mm(psv, psv[:, 64:128], qb, qb[:, tsl], Sbb, Sbb[:])
                    yield
                    self.tt("dve", vn, vn[:], Xf, Xf[:, 0:64], psv, psv[:, 0:64], ALU.subtract)
                    yield
                    pso = R["ph"]["pso"]
                    self.mm(pso, pso[:, 0:64], qkT, qkT[:], vn, vn[:])
                    self.ts(t1, t1[:], psv, psv[:, 64:128], e3[:, 0:1], None, ALU.mult, extra_reads=[e3])
                    yield
                    self.tt("dve", oacc, oacc[:, c, :], t1, t1[:], pso, pso[:, 0:64], ALU.add)
                    pss = R["ph"]["pss"]
                    self.mm(pss, pss[0:64, 0:64], kg, kg[:], vn, vn[:])
                    yield
                    self.stt(S, S[:], S, S[:], e3[0:64, 2:3], pss, pss[0:64, 0:64], ALU.mult, ALU.add, extra_reads=[e3])
                    yield
                    self.copy("act", Sbb, Sbb[:], S, S[:])
                    ch["done"] += 1
                    yield
                if pos == cps - 1 and not s.is_sample:
                    self.store(s.st_out, s.st_out[q, l, dr, h], S, S[:])

            P.barrier()
            chains = {}
            ci = 0
            for q in range(s.nseq):
                for dr in range(2):
                    S_, Sb_ = chS[ci]
                    ci += 1
                    if s.is_sample:
                        self.load(S_, S_[:], s.st_in, s.st_in[l, dr, h])
                    else:
                        self.memset(S_, S_[:], 0.0)
                    self.copy("act", Sb_, Sb_[:], S_, S_[:])
                    chains[(dr, q)] = {"S": S_, "Sb": Sb_, "done": 0}
            pending = [(dr, q, pos) for pos in range(cps) for q in range(s.nseq) for dr in range(2)]
            running = []
            free_slots = list(slots)
            while pending or running:
                while pending and free_slots:
                    dr_, q_, pos_ = pending.pop(0)
                    R_ = free_slots.pop(0)
                    running.append((unit(dr_, q_, pos_, R_), R_))
                for item in list(running):
                    gen, R_ = item
                    try:
                        next(gen)
                    except StopIteration:
                        running.remove(item)
                        free_slots.append(R_)
            P.barrier()
            self.tt("pool", osum, osum[:], osum, osum[:], osum2, osum2[:], ALU.add)
            self.tap('osum', osum, osum[:])
            self.tt("dve", sq2, sq2[:], osum, osum[:], osum, osum[:], ALU.mult)
            P.op("dve", lambda e, sq2=sq2, ssq=ssq: e.reduce_sum(out=ssq[:], in_=sq2[:], axis=AX.X), reads=[sq2], writes=[ssq])
            self.act(ssq, ssq[:], ssq, ssq[:], AF.Sqrt, bias=EPS, scale=1.0 / 64.0)
            P.op("dve", lambda e, ssq=ssq: e.reciprocal(out=ssq[:], in_=ssq[:]), reads=[ssq], writes=[ssq])
            self.tt("dve", sq2, sq2[:], osum, osum[:], ssq, ssq[:, :, None].to_broadcast([128, NT, 64]), ALU.mult)
            self.tt("dve", sq2, sq2[:], sq2, sq2[:], norma, norma[:, None, :].to_broadcast([128, NT, 64]), ALU.mult)
            for c in range(NT):
                pz = self.pst()
                self.tr(pz, pz[:, 0:64], zf, zf[:, c * 128:(c + 1) * 128])
                t1 = r_t1.next()
                self.act(t1, t1[:], pz, pz[:, 0:64], AF.Silu)
                self.tt("dve", ytm, ytm[:, c, m0:m1], sq2, sq2[:, c, :], t1, t1[:], ALU.mult)
            if half == 1:
                for c in range(NT):
                    py = self.pst()
                    self.tr(py, py[:, 0:128], ytm, ytm[:, c, :])
                    self.copy("act" if c % 2 else "dve", y, y[:, hp, c * 128:(c + 1) * 128], py, py[:, 0:128])
        if self.dbg and self.dbg.get("tap") == ("ya", l, s.name):
            self.store(self.dbg_out, self.dbg_out[:], y, y[:])

    def phase_out_ffn(self, l, s):
        P = self.P
        d = self.d
        j = s.cidx
        modv = self.modv[l]
        with ExitStack() as ph:
            xbr = Ring([P.sb("fxb%d" % i, [128, 8, 512], F32, ph) for i in range(2 if s.T <= 1024 else 1)])
            sqr = Ring([P.sb("fsq%d" % i, [128, 512], BF16, ph) for i in range(2)])
            tmpr = Ring([P.sb("ftmp%d" % i, [128, 512], F32, ph) for i in range(2)])
            rstd = P.sb("frstd", [128, 512], F32, ph)
            P.barrier()
            for o in range(8):
                w = self.wload8(d["wo"], d["wo"][l, o])
                for blk in range(s.nb):
                    sl = slice(blk * 512, (blk + 1) * 512)
                    ps = self.pst()
                    for kc in range(8):
                        self.mm(ps, ps[:], w, w[:, kc, :], self.merged, self.merged[:, kc, sl], kc == 0, kc == 7)
                    self.copy("act" if blk % 2 else "dve", self.hT, self.hT[:, o, sl], ps, ps[:])
            for blk in range(s.nb):
                sl = slice(blk * 512, (blk + 1) * 512)
                xb = xbr.next()
                self.load(xb, xb[:], s.xres, s.xres[:, :, sl])
                for c in range(8):
                    self.stt(xb, xb[:, c, :], self.hT, self.hT[:, c, sl], modv[:, 16 + c, j:j + 1], xb, xb[:, c, :], ALU.mult, ALU.add, extra_reads=[modv])
                self.store(s.xres, s.xres[:, :, sl], xb, xb[:])
                self.norm_block(l, s, 1, xb, blk, sqr, rstd, tmpr)
            P.barrier()
            MB = min(s.T, 2048)
            nbm = MB // 512
            actb = P.sb("factb", [128, 22, MB], BF16, ph)
            sgr = Ring([P.sb("fsg%d" % i, [128, 512], F32, ph) for i in range(2)])
            w22 = Ring([P.sb("w22_%d" % i, [128, 22, 128], BF16, ph) for i in range(2)])
            for mb in range(s.T // MB):
                for i in range(22):
                    wg = self.wload8(d["wgu"], d["wgu"][l, i])
                    wu = self.wload8(d["wgu"], d["wgu"][l, 22 + i])
                    for b2 in range(nbm):
                        sl = slice(mb * MB + b2 * 512, mb * MB + (b2 + 1) * 512)
                        pg = self.pst()
                        for kc in range(8):
                            self.mm(pg, pg[:], wg, wg[:, kc, :], self.hT, self.hT[:, kc, sl], kc == 0, kc == 7)
                        pu = self.pst()
                        for kc in range(8):
                            self.mm(pu, pu[:], wu, wu[:, kc, :], self.hT, self.hT[:, kc, sl], kc == 0, kc == 7)
                        sg = sgr.next()
                        self.act(sg, sg[:], pg, pg[:], AF.Silu)
                        self.tt("dve", actb, actb[:, i, b2 * 512:(b2 + 1) * 512], sg, sg[:], pu, pu[:], ALU.mult)
                for o in range(8):
                    w = w22.next()
                    self.load(w, w[:], d["wdn"], d["wdn"][l, o], eng="pool")
                    for b2 in range(nbm):
                        sl = slice(mb * MB + b2 * 512, mb * MB + (b2 + 1) * 512)
                        ps = self.pst()
                        for kc in range(22):
                            self.mm(ps, ps[:], w, w[:, kc, :], actb, actb[:, kc, b2 * 512:(b2 + 1) * 512], kc == 0, kc == 21)
                        self.copy("act" if b2 % 2 else "dve", self.merged, self.merged[:, o, sl], ps, ps[:])
            for blk in range(s.nb):
                sl = slice(blk * 512, (blk + 1) * 512)
                xb = xbr.next()
                self.load(xb, xb[:], s.xres, s.xres[:, :, sl])
                for c in range(8):
                    self.stt(xb, xb[:, c, :], self.merged, self.merged[:, c, sl], modv[:, 40 + c, j:j + 1], xb, xb[:, c, :], ALU.mult, ALU.add, extra_reads=[modv])
                self.store(s.xres, s.xres[:, :, sl], xb, xb[:])
        P.barrier()

    def phase_final(self):
        P = self.P
        with ExitStack() as ph:
            xbr = Ring([P.sb("gxb%d" % i, [128, 8, 512], F32, ph) for i in range(2)])
            sqr = Ring([P.sb("gsq%d" % i, [128, 512], BF16, ph) for i in range(2)])
            rstd = P.sb("grstd", [128, 512], F32, ph)
            xn = P.sb("gxn", [128, 8, 512], F32, ph)
            yo = Ring([P.sb("gyo%d" % i, [128, D], F32, ph) for i in range(2)])
            for s in self.streams:
                for blk in range(s.nb):
                    sl = slice(blk * 512, (blk + 1) * 512)
                    xb = xbr.next()
                    self.load(xb, xb[:], s.xres, s.xres[:, :, sl])
                    self.rstd_block(xb, xb[:], sqr, rstd)
                    for c in range(8):
                        self.stt(xn, xn[:, c, :], xb, xb[:, c, :], self.nfT[:, c:c + 1], rstd, rstd[:], ALU.mult, ALU.mult, extra_reads=[self.nfT])
                    for t4 in range(4):
                        yt = yo.next()
                        for half in range(2):
                            ps = self.pst()
                            for c4 in range(4):
                                c = half * 4 + c4
                                self.tr(ps, ps[:, c4 * 128:(c4 + 1) * 128], xn, xn[:, c, t4 * 128:(t4 + 1) * 128])
                            self.copy("act" if half else "dve", yt, yt[:, half * 512:(half + 1) * 512], ps, ps[:])
                        r0 = blk * 512 + t4 * 128
                        self.store(s.y_out, s.y_out[r0:r0 + 128, :], yt, yt[:])
        P.barrier()


N_CORES = 8
_CACHE = {}


def make_streams():
    return [Stream("P", 4, 256, 0, [(0, 4)], False),
            Stream("S", 1, 2048, 1, [(0, 1), (1, 1), (2, 1), (3, 1)], True)]


def shared_inputs(inp, streams, depth):
    f = lambda a: np.ascontiguousarray(np.asarray(a, dtype=np.float32))
    sh = {}
    for s in streams:
        zposT, window = hyena_consts(s.L)
        Fc, Fs, Gc, Gs = dft_consts(s.L)
        CL, SLn, c64, s64 = fnet_consts(s.L)
        sh["zposT_" + s.name] = zposT
        sh["win_" + s.name] = window
        sh["Fc_" + s.name] = Fc
        sh["Fs_" + s.name] = Fs
        sh["Gc_" + s.name] = Gc
        sh["Gs_" + s.name] = Gs
        sh["CL_" + s.name] = CL
        sh["SLn_" + s.name] = SLn
        sh["c64_" + s.name] = c64
        sh["s64_" + s.name] = s64
        if s.is_sample:
            sh["pos_" + s.name] = grid_pos_embed_np(s.T)
    fm = lambda v: np.ascontiguousarray(f(v).reshape(-1, 128).T)
    sh["wmod"] = np.stack([tile_lhsT(f(inp["w_mod"][l])) for l in range(depth)])
    sh["bmodT"] = np.stack([fm(inp["b_mod"][l]) for l in range(depth)])
    sh["n1T"] = np.stack([fm(inp["norm1_g"][l]) for l in range(depth)])
    sh["n2T"] = np.stack([fm(inp["norm2_g"][l]) for l in range(depth)])
    sh["nfT"] = fm(inp["norm_f"])
    cols = win_tile_cols()
    win = np.zeros((depth, N_WIN_TILES, 128, 8, 128), np.float32)
    for l in range(depth):
        w = f(inp["w_in"][l])
        for ti, (c0, wd) in enumerate(cols):
            win[l, ti, :, :, :wd] = w[:, c0:c0 + wd].reshape(8, 128, wd).transpose(1, 0, 2)
    sh["win_t"] = win
    sh["wp_t"] = np.stack([np.stack([tile_lhsT(f(inp[k][l])) for k in ("w_pa", "w_pb", "w_pc")]) for l in range(depth)])
    sh["wo_t"] = np.stack([tile_lhsT(f(inp["w_o"][l])) for l in range(depth)])
    sh["wgu_t"] = np.stack([tile_lhsT(f(inp["w_gu"][l])) for l in range(depth)])
    sh["wdn_t"] = np.stack([tile_lhsT(f(inp["w_down"][l])) for l in range(depth)])
    cq = f(inp["conv_qkv"])[:depth]
    sh["convqT"] = np.ascontiguousarray(cq.reshape(depth, 3, 3, 8, 64).transpose(0, 4, 2, 3, 1))
    chy = f(inp["conv_hy"])[:depth]
    sh["convhT"] = np.ascontiguousarray(chy.reshape(depth, 3, 12, 128).transpose(0, 3, 2, 1))
    sh["alog_bc"] = np.ascontiguousarray(np.broadcast_to(f(inp["a_log"])[:depth].reshape(depth, 1, 16), (depth, 128, 16)))
    sh["dtb_bc"] = np.ascontiguousarray(np.broadcast_to(f(inp["dt_bias"])[:depth].reshape(depth, 1, 16), (depth, 128, 16)))
    sh["norma_bc"] = np.ascontiguousarray(np.broadcast_to(f(inp["norm_a"])[:depth].reshape(depth, 1, 64), (depth, 128, 64)))
    sh["hyw1"] = f(inp["hy_w1"])[:depth]
    sh["hyb1T"] = f(inp["hy_b1"])[:depth].reshape(depth, 64, 1)
    sh["hyfqT"] = f(inp["hy_freq"])[:depth].reshape(depth, 64, 1)
    sh["hyw2"] = f(inp["hy_w2"])[:depth]
    sh["hyb2T"] = f(inp["hy_b2"])[:depth].reshape(depth, 64, 1)
    sh["hyw3"] = f(inp["hy_w3"])[:depth]
    sh["hybiasT"] = np.ascontiguousarray(f(inp["hy_bias"])[:depth].reshape(depth, 2, 4, 128).transpose(0, 3, 1, 2))
    sh["masks"] = mask_consts()
    sh["ident"] = np.eye(128, dtype=np.float32)
    sh["lmask"] = level_masks()
    return sh


def kernel(**inp):
    depth = DEPTH
    streams = make_streams()
    if "nc" not in _CACHE:
        _CACHE["nc"] = Builder(make_streams(), depth=depth).build()
    nc = _CACHE["nc"]
    sh = shared_inputs(inp, streams, depth)
    xp = np.asarray(inp["x_prompt"], np.float32)
    xs = np.asarray(inp["x_sample"], np.float32)
    st = np.asarray(inp["state_delta"], np.float32)
    c = np.asarray(inp["c"], np.float32)
    cctx = np.asarray(inp["c_ctx"], np.float32)
    in_maps = []
    for core in range(N_CORES):
        sidx = core // 4
        m = dict(sh)
        m["x_P"] = np.ascontiguousarray(xp[core * 4:(core + 1) * 4].reshape(1024, D))
        m["x_S"] = np.ascontiguousarray(xs[sidx])
        m["st0_S"] = np.ascontiguousarray(st[sidx][:depth])
        cv = np.stack([cctx, c[sidx]], axis=-1)
        m["cvecT"] = np.ascontiguousarray(cv.reshape(8, 128, 2).transpose(1, 0, 2))
        in_maps.append(m)
    res = run_bass_kernel_spmd(nc, in_maps, core_ids=list(range(N_CORES)))
    r = res.results
    y_prompt = np.concatenate([r[i]["y_P"].reshape(4, 256, D) for i in range(N_CORES)], axis=0).astype(np.float32)
    y_sample = np.stack([r[0]["y_S"], r[4]["y_S"]], axis=0).astype(np.float32)
    new_state = np.concatenate([r[i]["st_P"] for i in range(N_CORES)], axis=0).astype(np.float32)
    return (y_prompt, y_sample, new_state)
```

```python
import math
from contextlib import ExitStack
import numpy as np
import concourse.bass as bass
import concourse.mybir as mybir
from concourse.bass_utils import run_bass_kernel_spmd

F32 = mybir.dt.float32
BF16 = mybir.dt.bfloat16
I32 = mybir.dt.int32
AF = mybir.ActivationFunctionType
ALU = mybir.AluOpType
AX = mybir.AxisListType

SAME_ENG_SYNC = True

D = 1024
DEPTH = 2
H_A = 8
DK = 64
DIN = 7200
DFF = 2816
EPS = 1e-6
CH = 128


class T:
    __slots__ = ("name", "ap", "last_write", "reads", "dsem", "dcount")

    def __init__(self, name, ap):
        self.name = name
        self.ap = ap
        self.last_write = None
        self.reads = []
        self.dsem = None
        self.dcount = 0

    def __getitem__(self, k):
        return self.ap[k]


class TV:
    def __init__(self, base, ap):
        self.base = base
        self.ap = ap
        self.name = base.name

    def __getitem__(self, k):
        return self.ap[k]


class Op:
    __slots__ = ("eng", "fn", "deps", "is_dma", "ndma", "sem_owner", "needed", "sig_sem", "sig_val", "name")

    def __init__(self, eng, fn, name=""):
        self.eng = eng
        self.fn = fn
        self.deps = []
        self.is_dma = False
        self.ndma = 0
        self.sem_owner = None
        self.needed = False
        self.sig_sem = None
        self.sig_val = 0
        self.name = name


ENGS = ("pe", "act", "dve", "pool", "sp")


def _ap(h):
    return h.ap() if callable(getattr(h, "ap", None)) else h


class Prog:
    def __init__(self, nc):
        self.nc = nc
        self.es = ExitStack()
        self.ops = {e: [] for e in ENGS}
        self.all_ops = []
        self.bar_deps = []
        self.bar_id = 0
        self.bar_seen = {e: 0 for e in ENGS}
        self.pending_dma = []
        self.uid = 0
        self.bar_pos = []

    def sb(self, name, shape, dtype, es=None):
        self.uid += 1
        h = (es or self.es).enter_context(self.nc.sbuf_tensor("%s_%d" % (name, self.uid), list(shape), dtype))
        return T(name, _ap(h))

    def ps(self, name, shape, dtype):
        h = self.es.enter_context(self.nc.psum_tensor(name, list(shape), dtype))
        return T(name, _ap(h))

    def tile(self, name, ap):
        return T(name, ap)

    def barrier(self):
        deps = []
        for e in ENGS:
            for o in reversed(self.ops[e]):
                if not o.is_dma:
                    deps.append(o)
                    break
        deps.extend(self.pending_dma)
        self.pending_dma = []
        for d in deps:
            d.needed = True
        self.bar_deps = deps
        self.bar_id += 1
        self.bar_pos.append(len(self.all_ops))

    def _record(self, op, reads, writes):
        reads = [getattr(r, "base", r) for r in reads]
        writes = [getattr(w, "base", w) for w in writes]
        deps = []
        if self.bar_seen[op.eng] != self.bar_id:
            self.bar_seen[op.eng] = self.bar_id
            deps.extend(self.bar_deps)
        for r in reads:
            if r.last_write is not None:
                deps.append(r.last_write)
        for w in writes:
            if w.last_write is not None:
                deps.append(w.last_write)
            deps.extend(w.reads)
        seen = set()
        for d in deps:
            if d is op or id(d) in seen:
                continue
            seen.add(id(d))
            if d.eng == op.eng and not d.is_dma:
                if op.eng == "pe" or not SAME_ENG_SYNC:
                    continue
            op.deps.append(d)
            d.needed = True
        for r in reads:
            r.reads.append(op)
        for w in writes:
            w.last_write = op
            w.reads = []
        self.ops[op.eng].append(op)
        self.all_ops.append(op)
        return op

    def op(self, eng, fn, reads=(), writes=(), name=""):
        return self._record(Op(eng, fn, name), list(reads), list(writes))

    def dma(self, eng, fn, reads=(), writes=(), owner=None, ndma=1, name=""):
        o = Op(eng, fn, name)
        o.is_dma = True
        o.ndma = ndma
        o.sem_owner = owner
        o.needed = True
        self.pending_dma.append(o)
        return self._record(o, list(reads), list(writes))

    def finalize(self):
        nc = self.nc
        es = self.es
        esem = {}
        for e in ("pe", "act", "dve", "pool"):
            esem[e] = es.enter_context(nc.semaphore("s_" + e))
        last_dma = {}
        qtype = {}
        for i, o in enumerate(self.all_ops):
            if o.is_dma:
                k = id(o.sem_owner)
                last_dma[k] = i
                qt = "sw" if o.eng == "pool" else "hw"
                if qtype.get(k, qt) != qt:
                    qtype[k] = "mixed"
                else:
                    qtype[k] = qt
        free = {"sw": [], "hw": [], "mixed": []}
        active = {}
        sem_final = {}
        bpos = list(self.bar_pos)
        bi = 0
        nsem = 0
        for i, o in enumerate(self.all_ops):
            while bi < len(bpos) and bpos[bi] <= i:
                b = bpos[bi]
                bi += 1
                for k in list(active.keys()):
                    ow = active[k]
                    if last_dma[k] < b:
                        if qtype[k] != "mixed":
                            free[qtype[k]].append((ow.dsem, ow.dcount))
                        del active[k]
            if o.is_dma:
                ow = o.sem_owner
                if ow.dsem is None:
                    fl = free[qtype[id(ow)]]
                    if fl and qtype[id(ow)] != "mixed":
                        ow.dsem, ow.dcount = fl.pop()
                    else:
                        ow.dsem = es.enter_context(nc.semaphore("d%d" % nsem))
                        nsem += 1
                    active[id(ow)] = ow
                ow.dcount += 16 * o.ndma
                o.sig_sem = ow.dsem
                o.sig_val = ow.dcount
                sem_final[id(ow.dsem)] = (ow.dsem, ow.dcount)
        dma_final = list(sem_final.values())
        for e in ("pe", "act", "dve", "pool"):
            c = 0
            for o in self.ops[e]:
                if o.is_dma:
                    continue
                if o.needed:
                    c += 1
                    o.sig_sem = esem[e]
                    o.sig_val = c
        self.n_sems = 4 + nsem
        engmap = {"pe": "tensor", "act": "scalar", "dve": "vector", "pool": "gpsimd", "sp": "sync"}
        with nc.Block() as block:
            for e in ENGS:
                ops = self.ops[e]
                final = (e == "sp")

                def body(eh, ops=ops, final=final):
                    seen = {}
                    for o in ops:
                        for d in o.deps:
                            key = id(d.sig_sem)
                            if seen.get(key, 0) >= d.sig_val:
                                continue
                            seen[key] = d.sig_val
                            eh.wait_ge(d.sig_sem, d.sig_val)
                        r = o.fn(eh)
                        if o.is_dma:
                            if not isinstance(r, (list, tuple)):
                                r = [r]
                            assert len(r) == o.ndma, (o.name, len(r), o.ndma)
                            for ins in r:
                                ins.then_inc(o.sig_sem, 16)
                        elif o.needed:
                            r.then_inc(o.sig_sem, 1)
                    if final:
                        for (sm, cnt) in dma_final:
                            eh.wait_ge(sm, cnt)

                getattr(block, engmap[e])(body)
        return self


class Ring:
    def __init__(self, tiles):
        self.tiles = tiles
        self.i = 0

    def next(self):
        t = self.tiles[self.i % len(self.tiles)]
        self.i += 1
        return t


import ml_dtypes
import os as _osx
NPBF = ml_dtypes.bfloat16


def tile_lhsT(w):
    K, N = w.shape
    nt = (N + 127) // 128
    wp = np.zeros((K, nt * 128), np.float32)
    wp[:, :N] = w
    kc = K // 128
    return np.ascontiguousarray(wp.reshape(kc, 128, nt, 128).transpose(2, 1, 0, 3))


def grid_pos_embed_np(n_tokens, grid_w=64):
    rows = n_tokens // grid_w
    r = np.repeat(np.arange(rows), grid_w).astype(np.float32)
    col = np.tile(np.arange(grid_w), rows).astype(np.float32)
    quarter = D // 4
    omega = (1.0 / (10000.0 ** (np.arange(quarter, dtype=np.float32) / quarter))).astype(np.float32)

    def emb(pos):
        a = pos[:, None] * omega[None, :]
        return np.concatenate([np.sin(a), np.cos(a)], axis=-1)

    return np.concatenate([emb(r), emb(col)], axis=-1).astype(np.float32)


def win_tile_cols():
    tiles = []
    for hp in range(4):
        for base in (0, 512, 1024, 1536):
            tiles.append((base + hp * 128, 128))
    tiles.append((2048, 32))
    for i in range(12):
        tiles.append((2080 + i * 128, 128))
    for i in range(4):
        tiles.append((3616 + i * 128, 128))
    for i in range(24):
        tiles.append((4128 + i * 128, 128))
    return tiles


TI_DN = 0
TI_BA = 16
TI_HY = 17
TI_FN = 29
TI_GATE = 33
N_WIN_TILES = 57


def hyena_consts(L):
    bands = 16
    t = np.linspace(0.0, 1.0, L, dtype=np.float32)[:, None]
    wpos = ((2.0 * math.pi / L) * np.arange(L, dtype=np.float32))[:, None].astype(np.float32)
    fr = np.linspace(1e-4, bands - 1, bands, dtype=np.float32)[None, :]
    zpos = np.concatenate([t, np.cos(fr * wpos), -np.sin(fr * wpos)], axis=-1).astype(np.float32)
    deltas = np.abs(np.linspace(math.log(1e-2) / 1.5, math.log(1e-2) / 0.3, 512, dtype=np.float32))
    window = np.exp(-t * deltas[None, :]).astype(np.float32)
    return np.ascontiguousarray(zpos.T), window


def dft_consts(L):
    nfp = (L // 128 + 1) * 128
    s = np.arange(L, dtype=np.float64)[:, None]
    f = np.arange(nfp, dtype=np.float64)[None, :]
    ang = np.pi * np.mod(s * f, 2 * L) / L
    valid = (f <= L)
    Fc = np.where(valid, np.cos(ang), 0.0)
    Fs = np.where(valid, np.sin(ang), 0.0)
    n = 2 * L
    cf = np.where((f == 0) | (f == L), 1.0 / n, 2.0 / n) * valid
    Gc = (Fc * cf).T
    Gs = (-Fs * cf).T
    nt = L // 128
    nf = nt + 1
    tF = lambda a: np.ascontiguousarray(a.reshape(nt, 128, nf, 128).transpose(2, 1, 0, 3)).astype(NPBF)
    tG = lambda a: np.ascontiguousarray(a.reshape(nf, 128, nt, 128).transpose(2, 1, 0, 3)).astype(NPBF)
    return (tF(Fc), tF(Fs), tG(Gc), tG(Gs))


def fnet_consts(L):
    t = np.arange(L, dtype=np.float64)
    ang = 2 * np.pi * np.mod(np.outer(t, t), L) / L
    CL = np.cos(ang)
    SLn = -np.sin(ang)
    c = np.arange(64, dtype=np.float64)
    a64 = 2 * np.pi * np.mod(np.outer(c, c), 64) / 64
    sc = 1.0 / math.sqrt(64.0 * L)
    c64 = np.zeros((128, 128))
    s64 = np.zeros((128, 128))
    for g in range(2):
        c64[g * 64:(g + 1) * 64, g * 64:(g + 1) * 64] = np.cos(a64) * sc
        s64[g * 64:(g + 1) * 64, g * 64:(g + 1) * 64] = np.sin(a64) * sc
    nt = L // 128
    tC = lambda a: np.ascontiguousarray(a.reshape(nt, 128, L // 256, 256).transpose(2, 1, 0, 3)).astype(NPBF)
    return tC(CL), tC(SLn), c64.astype(NPBF), s64.astype(NPBF)


def mask_consts():
    i = np.arange(128)[:, None]
    j = np.arange(128)[None, :]
    m = np.stack([(i > j), (i >= j), (i < j), (i <= j)]).astype(np.float32)
    return m


M_SL, M_IL, M_SU, M_IU = 0, 1, 2, 3


def level_masks():
    i = np.arange(128)[:, None]
    j = np.arange(128)[None, :]
    ms = []
    for lv in range(7):
        bs = 2 << lv
        ms.append(((i // bs) == (j // bs)) & ((i // (bs // 2)) != (j // (bs // 2))))
    return np.ascontiguousarray(np.stack(ms, axis=1).astype(np.float32))


class Stream:
    def __init__(self, name, nseq, L, cidx, groups, is_sample):
        self.name = name
        self.nseq = nseq
        self.L = L
        self.T = nseq * L
        self.cidx = cidx
        self.nb = self.T // 512
        self.groups = groups
        self.is_sample = is_sample


class Builder:
    def __init__(self, streams, depth=DEPTH, dbg=None, skip=()):
        self.streams = streams
        self.depth = depth
        self.dbg = dbg
        self.skip = skip
        self.nc = bass.Bass("TRN2", target_bir_lowering=False)
        self.P = Prog(self.nc)
        self.dram = {}

    def din(self, name, shape, dtype=F32):
        ap = self.nc.dram_tensor(name, list(shape), dtype, kind="ExternalInput").ap()
        t = T(name, ap)
        self.dram[name] = (t, list(shape), dtype)
        return t

    def dout(self, name, shape, dtype=F32):
        ap = self.nc.dram_tensor(name, list(shape), dtype, kind="ExternalOutput").ap()
        return T(name, ap)

    def dscr(self, name, shape, dtype=F32):
        ap = self.nc.dram_tensor(name, list(shape), dtype, kind="Internal").ap()
        return T(name, ap)

    def mm(self, out_t, out_ap, lhsT_t, lhsT_ap, rhs_t, rhs_ap, start=True, stop=True):
        self.P.op("pe", lambda e: e.matmul(out_ap, lhsT=lhsT_ap, rhs=rhs_ap, start=start, stop=stop),
                  reads=[lhsT_t, rhs_t], writes=[out_t])

    def tr(self, out_t, out_ap, in_t, in_ap, ident_t=None, ident_ap=None):
        if ident_t is None:
            ident_t = self.ident
            n = in_ap.shape[0]
            ident_ap = self.ident[0:n, 0:n]
        self.P.op("pe", lambda e: e.transpose(out_ap, in_ap, ident_ap), reads=[in_t, ident_t], writes=[out_t])

    def act(self, out_t, out_ap, in_t, in_ap, func, bias=None, scale=None, extra_reads=()):
        kw = {}
        if bias is not None:
            kw["bias"] = bias
        if scale is not None:
            kw["scale"] = scale
        self.P.op("act", lambda e: e.activation(out=out_ap, in_=in_ap, func=func, **kw),
                  reads=[in_t] + list(extra_reads), writes=[out_t])

    def tt(self, eng, out_t, out_ap, a_t, a_ap, b_t, b_ap, op):
        self.P.op(eng, lambda e: e.tensor_tensor(out=out_ap, in0=a_ap, in1=b_ap, op=op),
                  reads=[a_t, b_t], writes=[out_t])

    def ts(self, out_t, out_ap, a_t, a_ap, s1, s2, op0, op1=None, extra_reads=()):
        if op1 is None:
            self.P.op("dve", lambda e: e.tensor_scalar(out=out_ap, in0=a_ap, scalar1=s1, scalar2=None, op0=op0),
                      reads=[a_t] + list(extra_reads), writes=[out_t])
        else:
            self.P.op("dve", lambda e: e.tensor_scalar(out=out_ap, in0=a_ap, scalar1=s1, scalar2=s2, op0=op0, op1=op1),
                      reads=[a_t] + list(extra_reads), writes=[out_t])

    def stt(self, out_t, out_ap, a_t, a_ap, scalar, b_t, b_ap, op0, op1, extra_reads=()):
        self.P.op("dve", lambda e: e.scalar_tensor_tensor(out=out_ap, in0=a_ap, scalar=scalar, in1=b_ap, op0=op0, op1=op1),
                  reads=[a_t, b_t] + list(extra_reads), writes=[out_t])

    def copy(self, eng, out_t, out_ap, in_t, in_ap):
        if eng == "act":
            self.P.op("act", lambda e: e.copy(out=out_ap, in_=in_ap), reads=[in_t], writes=[out_t])
        else:
            self.P.op(eng, lambda e: e.tensor_copy(out=out_ap, in_=in_ap), reads=[in_t], writes=[out_t])

    def memset(self, t, ap, val, eng="dve"):
        self.P.op(eng, lambda e: e.memset(ap, val), writes=[t])

    def load(self, out_t, out_ap, in_t, in_ap, eng="sp"):
        self.P.dma(eng, lambda e: e.dma_start(out=out_ap, in_=in_ap), reads=[in_t], writes=[out_t], owner=out_t)

    def store(self, out_t, out_ap, in_t, in_ap, eng="sp"):
        self.P.dma(eng, lambda e: e.dma_start(out=out_ap, in_=in_ap), reads=[in_t], writes=[out_t], owner=in_t)

    def pst(self):
        return self.psr.next()

    def tap(self, name, t, ap):
        if self.dbg and self.dbg.get("tap2") == name and not getattr(self, "_tapped", False):
            self._tapped = True
            self.store(self.dbg_out, self.dbg_out[:], t, ap)

    def build(self):
        P = self.P
        depth = self.depth
        for s in self.streams:
            nfp = (s.L // 128 + 1) * 128
            s.x_in = self.din("x_" + s.name, [s.T, D])
            s.y_out = self.dout("y_" + s.name, [s.T, D])
            s.xres = self.dscr("xres_" + s.name, [128, 8, s.T])
            if s.is_sample:
                s.st_in = self.din("st0_" + s.name, [depth, 2, H_A, DK, DK])
                s.pos = self.din("pos_" + s.name, [s.T, D])
            else:
                s.st_out = self.dout("st_" + s.name, [s.nseq, depth, 2, H_A, DK, DK])
            s.zposT = self.din("zposT_" + s.name, [33, s.L])
            s.window = self.din("win_" + s.name, [s.L, 512])
            nt_ = s.L // 128
            s.Fc = self.din("Fc_" + s.name, [nt_ + 1, 128, nt_, 128], BF16)
            s.Fs = self.din("Fs_" + s.name, [nt_ + 1, 128, nt_, 128], BF16)
            s.Gc = self.din("Gc_" + s.name, [nt_, 128, nt_ + 1, 128], BF16)
            s.Gs = self.din("Gs_" + s.name, [nt_, 128, nt_ + 1, 128], BF16)
            s.CL = self.din("CL_" + s.name, [s.L // 256, 128, nt_, 256], BF16)
            s.SLn = self.din("SLn_" + s.name, [s.L // 256, 128, nt_, 256], BF16)
            s.c64 = self.din("c64_" + s.name, [128, 128], BF16)
            s.s64 = self.din("s64_" + s.name, [128, 128], BF16)
        d = {}
        d["cvecT"] = self.din("cvecT", [128, 8, 2])
        d["wmod"] = self.din("wmod", [depth, 48, 128, 8, 128])
        d["bmodT"] = self.din("bmodT", [depth, 128, 48])
        d["n1"] = self.din("n1T", [depth, 128, 8])
        d["n2"] = self.din("n2T", [depth, 128, 8])
        d["nf"] = self.din("nfT", [128, 8])
        d["win"] = self.din("win_t", [depth, N_WIN_TILES, 128, 8, 128])
        d["wp"] = self.din("wp_t", [depth, 3, 8, 128, 4, 128])
        d["wo"] = self.din("wo_t", [depth, 8, 128, 8, 128])
        d["wgu"] = self.din("wgu_t", [depth, 44, 128, 8, 128])
        d["wdn"] = self.din("wdn_t", [depth, 8, 128, 22, 128])
        d["convq"] = self.din("convqT", [depth, 64, 3, 8, 3])
        d["convh"] = self.din("convhT", [depth, 128, 12, 3])
        d["alog"] = self.din("alog_bc", [depth, 128, 16])
        d["dtb"] = self.din("dtb_bc", [depth, 128, 16])
        d["norma"] = self.din("norma_bc", [depth, 128, 64])
        d["hyw1"] = self.din("hyw1", [depth, 33, 64])
        d["hyb1"] = self.din("hyb1T", [depth, 64, 1])
        d["hyfq"] = self.din("hyfqT", [depth, 64, 1])
        d["hyw2"] = self.din("hyw2", [depth, 64, 64])
        d["hyb2"] = self.din("hyb2T", [depth, 64, 1])
        d["hyw3"] = self.din("hyw3", [depth, 64, 2048])
        d["hybias"] = self.din("hybiasT", [depth, 128, 2, 4])
        d["masks"] = self.din("masks", [4, 128, 128])
        d["ident"] = self.din("ident", [128, 128])
        d["lmask"] = self.din("lmask", [128, 7, 128])
        self.d = d
        if self.dbg:
            self.dbg_out = self.dout("dbg", self.dbg["shape"], self.dbg.get("dtype", F32))

        self.psr = Ring([P.ps("ps%d" % i, [128, 512], F32) for i in range(8)])
        TMAX = max(s.T for s in self.streams)
        self.hT = P.sb("hT", [128, 8, TMAX], BF16)
        self.merged = P.sb("merged", [128, 8, TMAX], BF16)
        self.w8 = Ring([P.sb("w8_%d" % i, [128, 8, 128], BF16) for i in range(3)])
        self.w4 = Ring([P.sb("w4_%d" % i, [128, 4, 128], BF16) for i in range(2)])
        self.ident = P.sb("ident", [128, 128], F32)
        self.onesb = P.sb("onesb", [128, 128], BF16)
        self.ones32 = P.sb("ones32", [128, 128], F32)
        self.masks = P.sb("masks", [128, 4, 128], F32)
        self.load(self.ident, self.ident[:], d["ident"], d["ident"][:])
        self.memset(self.onesb, self.onesb[:], 1.0 / 1024.0)
        self.identb2 = P.sb("identb2", [128, 2, 128], BF16)
        self.copy("dve", self.identb2, self.identb2[:, 0, :], self.ident, self.ident[:])
        self.copy("dve", self.identb2, self.identb2[:, 1, :], self.ident, self.ident[:])
        self.memset(self.ones32, self.ones32[:], 1.0)
        for i in range(4):
            self.load(self.masks, self.masks[:, i, :], d["masks"], d["masks"][i])
        self.modv = [P.sb("modv%d" % l, [128, 48, 2], F32) for l in range(depth)]
        self.modA = [P.sb("modA%d" % l, [128, 2, 8, 2], F32) for l in range(depth)]
        self.nfT = P.sb("nfT", [128, 8], F32)
        self.load(self.nfT, self.nfT[:], d["nf"], d["nf"][:])

        self.phase_mod()
        self.phase_input()
        for l in range(depth):
            for s in self.streams:
                self.phase_norm1(l, s)
                self.phase_mix(l, s)
                self.phase_out_ffn(l, s)
        self.phase_final()
        if self.dbg:
            self.dbg["fn"](self)
        P.finalize()
        P.es.close()
        return self.nc

    def wload8(self, dram_t, dram_ap):
        w = self.w8.next()
        self.load(w, w[:], dram_t, dram_ap, eng="pool")
        return w

    def phase_mod(self):
        P = self.P
        d = self.d
        with ExitStack() as ph:
            cv = P.sb("cv", [128, 8, 2], F32, ph)
            scv = P.sb("scv", [128, 8, 2], F32, ph)
            wm = Ring([P.sb("wm%d" % i, [128, 8, 128], F32, ph) for i in range(3)])
            bm = P.sb("bm", [128, 48], F32, ph)
            n12 = P.sb("n12", [128, 2, 8], F32, ph)
            self.load(cv, cv[:], d["cvecT"], d["cvecT"][:])
            self.act(scv, scv[:], cv, cv[:], AF.Silu)
            for l in range(self.depth):
                modv = self.modv[l]
                self.load(bm, bm[:], d["bmodT"], d["bmodT"][l])
                self.load(n12, n12[:, 0, :], d["n1"], d["n1"][l])
                self.load(n12, n12[:, 1, :], d["n2"], d["n2"][l])
                for c in range(48):
                    w = wm.next()
                    self.load(w, w[:], d["wmod"], d["wmod"][l, c])
                    ps = self.pst()
                    for kc in range(8):
                        self.mm(ps, ps[:, 0:2], w, w[:, kc, :], scv, scv[:, kc, :], kc == 0, kc == 7)
                    self.ts(modv, modv[:, c, :], ps, ps[:, 0:2], bm[:, c:c + 1], None, ALU.add, extra_reads=[bm])
                mA = self.modA[l]
                for sub in range(2):
                    sc0 = 8 + 24 * sub
                    for j in range(2):
                        self.ts(mA, mA[:, sub, :, j], modv, modv[:, sc0:sc0 + 8, j], 1.0, None, ALU.add)
                        self.tt("dve", mA, mA[:, sub, :, j], mA, mA[:, sub, :, j], n12, n12[:, sub, :], ALU.mult)
        P.barrier()

    def phase_input(self):
        P = self.P
        with ExitStack() as ph:
            xin = Ring([P.sb("xin%d" % i, [128, D], F32, ph) for i in range(2)])
            pin = Ring([P.sb("pin%d" % i, [128, D], F32, ph) for i in range(2)])
            xo = Ring([P.sb("xo%d" % i, [128, 8, 128], F32, ph) for i in range(2)])
            for s in self.streams:
                for tt in range(s.T // 128):
                    xt = xin.next()
                    self.load(xt, xt[:], s.x_in, s.x_in[tt * 128:(tt + 1) * 128, :])
                    if s.is_sample:
                        pt = pin.next()
                        self.load(pt, pt[:], s.pos, s.pos[tt * 128:(tt + 1) * 128, :])
                        self.tt("dve", xt, xt[:], xt, xt[:], pt, pt[:], ALU.add)
                    xot = xo.next()
                    for half in range(2):
                        ps = self.pst()
                        for c4 in range(4):
                            c = half * 4 + c4
                            self.tr(ps, ps[:, c4 * 128:(c4 + 1) * 128], xt, xt[:, c * 128:(c + 1) * 128])
                        self.copy("act" if half else "dve", xot, xot[:, half * 4:half * 4 + 4, :], ps,
                                  ps[:].rearrange("p (c t) -> p c t", c=4))
                    self.store(s.xres, s.xres[:, :, tt * 128:(tt + 1) * 128], xot, xot[:])
        P.barrier()

    def rstd_block(self, xb, xb_ap, sqr, rstd):
        ps = self.pst()
        for c in range(8):
            sq = sqr.next()
            self.act(sq, sq[:], xb, xb_ap[:, c, :], AF.Square)
            self.mm(ps, ps[:], self.onesb, self.onesb[:], sq, sq[:], c == 0, c == 7)
        self.act(rstd, rstd[:], ps, ps[:], AF.Sqrt, bias=EPS, scale=1.0)
        self.P.op("dve", lambda e: e.reciprocal(out=rstd[:], in_=rstd[:]), reads=[rstd], writes=[rstd])

    def norm_block(self, l, s, sub, xb, blk, sqr, rstd, tmpr):
        j = s.cidx
        self.rstd_block(xb, xb[:], sqr, rstd)
        mA = self.modA[l]
        modv = self.modv[l]
        sh0 = 24 * sub
        for c in range(8):
            tmp = tmpr.next()
            self.tt("dve", tmp, tmp[:], xb, xb[:, c, :], rstd, rstd[:], ALU.mult)
            self.act(self.hT, self.hT[:, c, blk * 512:(blk + 1) * 512], tmp, tmp[:], AF.Identity,
                     bias=modv[:, sh0 + c, j:j + 1], scale=mA[:, sub, c, j:j + 1], extra_reads=[mA, modv])

    def phase_norm1(self, l, s):
        P = self.P
        with ExitStack() as ph:
            xbr = Ring([P.sb("xb%d" % i, [128, 8, 512], F32, ph) for i in range(2)])
            sqr = Ring([P.sb("sq%d" % i, [128, 512], BF16, ph) for i in range(2)])
            tmpr = Ring([P.sb("ntmp%d" % i, [128, 512], F32, ph) for i in range(2)])
            rstd = P.sb("rstd", [128, 512], F32, ph)
            for blk in range(s.nb):
                xb = xbr.next()
                self.load(xb, xb[:], s.xres, s.xres[:, :, blk * 512:(blk + 1) * 512])
                self.norm_block(l, s, 0, xb, blk, sqr, rstd, tmpr)
        P.barrier()

    def proj_fm(self, l, ti, s, evac, m0=0, m1=128):
        w = self.wload8(self.d["win"], self.d["win"][l, ti])
        for blk in range(s.nb):
            ps = self.pst()
            for kc in range(8):
                self.mm(ps, ps[0:m1 - m0, :], w, w[:, kc, m0:m1], self.hT, self.hT[:, kc, blk * 512:(blk + 1) * 512], kc == 0, kc == 7)
            evac(ps, blk)

    def merge_branch(self, l, s, br, y_t):
        for o in range(8):
            wg = self.wload8(self.d["win"], self.d["win"][l, TI_GATE + br * 8 + o])
            wp = self.w4.next()
            self.load(wp, wp[:], self.d["wp"], self.d["wp"][l, br, o], eng="pool")
            for blk in range(s.nb):
                sl = slice(blk * 512, (blk + 1) * 512)
                ps1 = self.pst()
                for kc in range(8):
                    self.mm(ps1, ps1[:], wg, wg[:, kc, :], self.hT, self.hT[:, kc, sl], kc == 0, kc == 7)
                ps2 = self.pst()
                for kc in range(4):
                    self.mm(ps2, ps2[:], wp, wp[:, kc, :], y_t, y_t[:, kc, sl], kc == 0, kc == 3)
                sig = self.sigr.next()
                self.act(sig, sig[:], ps1, ps1[:], AF.Sigmoid)
                if br == 0:
                    self.tt("dve", self.merged, self.merged[:, o, sl], sig, sig[:], ps2, ps2[:], ALU.mult)
                else:
                    self.tt("dve", sig, sig[:], sig, sig[:], ps2, ps2[:], ALU.mult)
                    self.tt("pool", self.merged, self.merged[:, o, sl], self.merged, self.merged[:, o, sl], sig, sig[:], ALU.add)

    def phase_mix(self, l, s):
        P = self.P
        with ExitStack() as ph:
            self.sigr = Ring([P.sb("sig%d" % i, [128, 512], F32, ph) for i in range(2)])
            y = P.sb("ybr", [128, 4, s.T], BF16, ph)
            with ExitStack() as ph2:
                if "delta" in self.skip:
                    self.memset(y, y[:], 0.0)
                else:
                    self.mix_delta(l, s, y, ph2)
            P.barrier()
            self.merge_branch(l, s, 0, y)
            P.barrier()
            for (ct0, nct) in s.groups:
                with ExitStack() as ph2:
                    if "hyena" in self.skip:
                        self.memset(y, y[:], 0.0)
                    else:
                        self.mix_hyena(l, s, y, ct0, nct, ph2)
                P.barrier()
            if self.dbg and self.dbg.get("tap") == ("yb", l, s.name):
                self.store(self.dbg_out, self.dbg_out[:], y, y[:])
            self.merge_branch(l, s, 1, y)
            P.barrier()
            for (ct0, nct) in [(0, 4)]:
                with ExitStack() as ph2:
                    self.mix_fnet(l, s, y, ct0, nct, ph2)
                P.barrier()
            if self.dbg and self.dbg.get("tap") == ("yc", l, s.name):
                self.store(self.dbg_out, self.dbg_out[:], y, y[:])
            self.merge_branch(l, s, 2, y)
        P.barrier()

    def mix_fnet(self, l, s, y, ct0, nct, ph):
        P = self.P
        L = s.L
        nt = L // 128
        W = nct * 128
        xc = P.sb("xc", [128, nct, s.T], BF16, ph)
        c64 = P.sb("c64", [128, 128], BF16, ph)
        s64 = P.sb("s64", [128, 128], BF16, ph)
        self.load(c64, c64[:], s.c64, s.c64[:])
        self.load(s64, s64[:], s.s64, s.s64[:])
        for ci in range(nct):
            self.proj_fm(l, TI_FN + ct0 + ci, s,
                         lambda ps, blk, ci=ci: self.copy("act", xc, xc[:, ci, blk * 512:(blk + 1) * 512], ps, ps[:]))
        U = P.sb("U", [128, nt, 2, W], BF16, ph)
        NW = 256
        dftc = Ring([P.sb("dftc%d" % i, [128, nt, NW], BF16, ph) for i in range(2)])
        dfts = Ring([P.sb("dfts%d" % i, [128, nt, NW], BF16, ph) for i in range(2)])
        for q in range(s.nseq):
            t0 = q * L
            for tt in range(nt):
                for cs, mat in ((0, c64), (1, s64)):
                    ps = self.pst()
                    for ci in range(nct):
                        self.mm(ps, ps[:, ci * 128:(ci + 1) * 128], xc, xc[:, ci, t0 + tt * 128:t0 + (tt + 1) * 128], mat, mat[:])
                    self.copy("act" if cs else "dve", U, U[:, tt, cs, :], ps, ps[:, 0:W])
            for nbk in range(L // NW):
                cm = dftc.next()
                sm = dfts.next()
                self.load(cm, cm[:], s.CL, s.CL[nbk])
                self.load(sm, sm[:], s.SLn, s.SLn[nbk])
                for ci in range(nct):
                    ps = self.pst()
                    for tt in range(nt):
                        self.mm(ps, ps[:, 0:NW], U, U[:, tt, 0, ci * 128:(ci + 1) * 128], cm, cm[:, tt, :], tt == 0, False)
                        self.mm(ps, ps[:, 0:NW], U, U[:, tt, 1, ci * 128:(ci + 1) * 128], sm, sm[:, tt, :], False, tt == nt - 1)
                    self.copy("act" if ci % 2 else "dve", y, y[:, ct0 + ci, t0 + nbk * NW:t0 + (nbk + 1) * NW], ps, ps[:, 0:NW])

    def sin_reduce(self, out_t, out_ap, in_t, in_ap, ti, ti_ap, tf, tf_ap):
        P = self.P
        inv = 1.0 / (2.0 * math.pi)
        P.op("dve", lambda e: e.tensor_scalar(out=ti_ap, in0=in_ap, scalar1=inv, scalar2=None, op0=ALU.mult),
             reads=[in_t], writes=[ti])
        self.copy("dve", tf, tf_ap, ti, ti_ap)
        self.stt(tf, tf_ap, tf, tf_ap, -2.0 * math.pi, in_t, in_ap, ALU.mult, ALU.add)
        self.ts(tf, tf_ap, tf, tf_ap, math.pi, -math.pi, ALU.min, ALU.max)
        self.act(out_t, out_ap, tf, tf_ap, AF.Sin)

    def hy_gate(self, l, s, which, ct, raw, dst, dst_ap, cw):
        L = s.L
        self.proj_fm(l, TI_HY + which * 4 + ct, s,
                     lambda ps, blk: self.copy("act", raw, raw[:, blk * 512:(blk + 1) * 512], ps, ps[:]))
        wi = which * 4 + ct
        for q in range(s.nseq):
            a, b = q * L, (q + 1) * L
            self.ts(dst, dst_ap[:, a:b], raw, raw[:, a:b], cw[:, wi, 1:2], None, ALU.mult, extra_reads=[cw])
            self.stt(dst, dst_ap[:, a + 1:b], raw, raw[:, a:b - 1], cw[:, wi, 0:1], dst, dst_ap[:, a + 1:b], ALU.mult, ALU.add, extra_reads=[cw])
            self.stt(dst, dst_ap[:, a:b - 1], raw, raw[:, a + 1:b], cw[:, wi, 2:3], dst, dst_ap[:, a:b - 1], ALU.mult, ALU.add, extra_reads=[cw])

    def mix_hyena(self, l, s, y, ct0, nct, ph):
        P = self.P
        d = self.d
        L = s.L
        nt = L // 128
        nf = nt + 1
        T_ = s.T
        W = nct * 128
        cw = P.sb("hcw", [128, 12, 3], F32, ph)
        self.load(cw, cw[:], d["convh"], d["convh"][l])
        hb = P.sb("hbias", [128, 2, 4], F32, ph)
        self.load(hb, hb[:], d["hybias"], d["hybias"][l])
        raw = P.sb("hraw", [128, T_], F32, ph)
        gate = P.sb("hgate", [128, nct, T_], BF16, ph)
        z = P.sb("hz", [128, nct, T_], BF16, ph)
        for ci in range(nct):
            self.hy_gate(l, s, 2, ct0 + ci, raw, z, z[:, ci, :], cw)
        Hre = P.sb("Hre", [128, 2, nf, W], BF16, ph)
        Him = P.sb("Him", [128, 2, nf, W], BF16, ph)
        ztm = P.sb("ztm", [128, nt, W], BF16, ph)
        Yre = P.sb("Yre", [128, nf, W], BF16, ph)
        Yim = P.sb("Yim", [128, nf, W], BF16, ph)
        fcr = Ring([P.sb("fcr%d" % i, [128, nt, 128], BF16, ph) for i in range(2)])
        fsr = Ring([P.sb("fsr%d" % i, [128, nt, 128], BF16, ph) for i in range(2)])
        tmpz = Ring([P.sb("tmpz%d" % i, [128, W], F32, ph) for i in range(4)])
        zf = P.sb("hzf", [128, 128], F32, ph)

        with ExitStack() as pA:
            hsd = P.sb("hsd", [128, nt, 2, 2, W], BF16, pA)
            with ExitStack() as pf:
                w1 = P.sb("hw1", [33, 64], F32, pf)
                w2 = P.sb("hw2", [64, 64], F32, pf)
                w3 = P.sb("hw3", [64, 2048], F32, pf)
                b1 = P.sb("hb1", [64, 1], F32, pf)
                b2 = P.sb("hb2", [64, 1], F32, pf)
                fq = P.sb("hfq", [64, 1], F32, pf)
                zp = P.sb("hzp", [33, min(L, 512)], F32, pf)
                h1 = P.sb("hh1", [64, min(L, 512)], F32, pf)
                h2 = P.sb("hh2", [64, min(L, 512)], F32, pf)
                ti = P.sb("hti", [64, 512], I32, pf)
                tf = P.sb("htf", [64, 512], F32, pf)
                ta = P.sb("hta", [64, 512], F32, pf)
                win = Ring([P.sb("hwin%d" % i, [128, W], F32, pf) for i in range(2)])
                hf = Ring([P.sb("hf%d" % i, [128, 4, W], F32, pf) for i in range(2)])
                self.load(w1, w1[:], d["hyw1"], d["hyw1"][l])
                self.load(w2, w2[:], d["hyw2"], d["hyw2"][l])
                self.load(w3, w3[:], d["hyw3"], d["hyw3"][l])
                self.load(b1, b1[:], d["hyb1"], d["hyb1"][l])
                self.load(b2, b2[:], d["hyb2"], d["hyb2"][l])
                self.load(fq, fq[:], d["hyfq"], d["hyfq"][l])
                wb = min(L, 512)
                for blk in range(L // wb):
                    sl = slice(blk * wb, (blk + 1) * wb)
                    self.load(zp, zp[:], s.zposT, s.zposT[:, sl])
                    for (src, wsrc, bsrc, dst, kk) in ((zp, w1, b1, h1, 33), (h1, w2, b2, h2, 64)):
                        ps = self.pst()
                        self.mm(ps, ps[0:64, 0:wb], wsrc, wsrc[0:kk, :], src, src[0:kk, :])
                        self.ts(ta, ta[:, 0:wb], ps, ps[0:64, 0:wb], bsrc[:, 0:1], fq[:, 0:1], ALU.add, ALU.mult, extra_reads=[bsrc, fq])
                        self.sin_reduce(dst, dst[:], ta, ta[:, 0:wb], ti, ti[:, 0:wb], tf, tf[:, 0:wb])
                    for t4 in range(wb // 128):
                        tt = blk * (wb // 128) + t4
                        wn = win.next()
                        self.load(wn, wn[:], s.window, s.window[tt * 128:(tt + 1) * 128, ct0 * 128:ct0 * 128 + W])
                        h = hf.next()
                        for fi in range(4):
                            ps = self.pst()
                            c0 = fi * 512 + ct0 * 128
                            self.mm(ps, ps[:, 0:W], h2, h2[:, t4 * 128:(t4 + 1) * 128], w3, w3[:, c0:c0 + W])
                            self.tt("dve", h, h[:, fi, :], ps, ps[:, 0:W], wn, wn[:], ALU.mult)
                        for o in range(2):
                            self.tt("dve", hsd, hsd[:, tt, o, 0, :], h, h[:, 2 * o, :], h, h[:, 2 * o + 1, :], ALU.add)
                            self.tt("pool", hsd, hsd[:, tt, o, 1, :], h, h[:, 2 * o + 1, :], h, h[:, 2 * o, :], ALU.subtract)
            P.barrier()
            def ld_f(ft):
                fcm = fcr.next()
                fsm = fsr.next()
                self.load(fcm, fcm[:], s.Fc, s.Fc[ft])
                self.load(fsm, fsm[:], s.Fs, s.Fs[ft])
                return fcm, fsm

            def build_ztm(q):
                t0 = q * L
                for tt in range(nt):
                    ps = self.pst()
                    for ci in range(nct):
                        self.copy("dve", zf, zf[:], z, z[:, ci, t0 + tt * 128:t0 + (tt + 1) * 128])
                        self.tr(ps, ps[:, ci * 128:(ci + 1) * 128], zf, zf[:])
                    self.copy("act" if tt % 2 else "dve", ztm, ztm[:, tt, :], ps, ps[:, 0:W])

            def fwd_product(ft, fcm, fsm, o):
                pc = self.pst()
                for tt in range(nt):
                    self.mm(pc, pc[:, 0:W], fcm, fcm[:, tt, :], ztm, ztm[:, tt, :], tt == 0, tt == nt - 1)
                pz = self.pst()
                for tt in range(nt):
                    self.mm(pz, pz[:, 0:W], fsm, fsm[:, tt, :], ztm, ztm[:, tt, :], tt == 0, tt == nt - 1)
                a1 = tmpz.next(); a2 = tmpz.next(); a3 = tmpz.next(); a4 = tmpz.next()
                self.tt("dve", a1, a1[:], pc, pc[:, 0:W], Hre, Hre[:, o, ft, :], ALU.mult)
                self.tt("dve", a2, a2[:], pz, pz[:, 0:W], Him, Him[:, o, ft, :], ALU.mult)
                self.tt("pool", Yre, Yre[:, ft, :], a1, a1[:], a2, a2[:], ALU.add)
                self.tt("dve", a3, a3[:], pc, pc[:, 0:W], Him, Him[:, o, ft, :], ALU.mult)
                self.tt("dve", a4, a4[:], pz, pz[:, 0:W], Hre, Hre[:, o, ft, :], ALU.mult)
                self.tt("pool", Yim, Yim[:, ft, :], a3, a3[:], a4, a4[:], ALU.subtract)

            fuse = (s.nseq == 1)
            if fuse:
                build_ztm(0)
            for ft in range(nf):
                fcm, fsm = ld_f(ft)
                for o in range(2):
                    for (mat, sd, dst) in ((fcm, 0, Hre), (fsm, 1, Him)):
                        ps = self.pst()
                        for tt in range(nt):
                            self.mm(ps, ps[:, 0:W], mat, mat[:, tt, :], hsd, hsd[:, tt, o, sd, :], tt == 0, tt == nt - 1)
                        self.copy("act" if sd else "dve", dst, dst[:, o, ft, :], ps, ps[:, 0:W])
                if fuse:
                    fwd_product(ft, fcm, fsm, 0)
        P.barrier()
        NW = 128
        gcr = Ring([P.sb("gcr%d" % i, [128, nf, NW], BF16, ph) for i in range(2)])
        gsr = Ring([P.sb("gsr%d" % i, [128, nf, NW], BF16, ph) for i in range(2)])
        for o in range(2):
            for ci in range(nct):
                self.hy_gate(l, s, o, ct0 + ci, raw, gate, gate[:, ci, :], cw)
            for q in range(s.nseq):
                t0 = q * L
                if not (fuse and o == 0):
                    build_ztm(q)
                    for ft in range(nf):
                        fcm, fsm = ld_f(ft)
                        fwd_product(ft, fcm, fsm, o)
                for nbk in range(L // NW):
                    gc = gcr.next()
                    gs = gsr.next()
                    self.load(gc, gc[:], s.Gc, s.Gc[nbk])
                    self.load(gs, gs[:], s.Gs, s.Gs[nbk])
                    sl = slice(t0 + nbk * NW, t0 + (nbk + 1) * NW)
                    for ci in range(nct):
                        ps = self.pst()
                        for ft in range(nf):
                            self.mm(ps, ps[:, 0:NW], Yre, Yre[:, ft, ci * 128:(ci + 1) * 128], gc, gc[:, ft, :], ft == 0, False)
                            self.mm(ps, ps[:, 0:NW], Yim, Yim[:, ft, ci * 128:(ci + 1) * 128], gs, gs[:, ft, :], False, ft == nf - 1)
                        a1 = tmpz.next()
                        self.stt(a1, a1[:, 0:NW], z, z[:, ci, sl], hb[:, o, ct0 + ci:ct0 + ci + 1], ps, ps[:, 0:NW], ALU.mult, ALU.add, extra_reads=[hb])
                        if o == 0:
                            self.tt("dve", z, z[:, ci, sl], a1, a1[:, 0:NW], gate, gate[:, ci, sl], ALU.mult)
                        else:
                            self.tt("dve", y, y[:, ct0 + ci, sl], a1, a1[:, 0:NW], gate, gate[:, ci, sl], ALU.mult)

    def mix_delta(self, l, s, y, ph):
        P = self.P
        d = self.d
        L = s.L
        T_ = s.T
        NT = T_ // 128
        cps = L // 128
        cwq = P.sb("dcw", [64, 3, 8, 3], F32, ph)
        self.load(cwq, cwq[:], d["convq"], d["convq"][l])
        alog = P.sb("dalog", [128, 16], F32, ph)
        dtb = P.sb("ddtb", [128, 16], F32, ph)
        norma = P.sb("dnorma", [128, 64], F32, ph)
        self.load(alog, alog[:], d["alog"], d["alog"][l])
        self.load(dtb, dtb[:], d["dtb"], d["dtb"][l])
        self.load(norma, norma[:], d["norma"], d["norma"][l])
        ba = P.sb("dba", [128, NT, 32], F32, ph)
        beta = P.sb("dbeta", [128, NT, 16], F32, ph)
        nbeta = P.sb("dnbeta", [128, NT, 16], F32, ph)
        g = P.sb("dg", [128, NT, 16], F32, ph)
        wba = self.wload8(d["win"], d["win"][l, TI_BA])
        for tt in range(NT):
            ps = self.pst()
            for kc in range(8):
                self.mm(ps, ps[:, 0:32], self.hT, self.hT[:, kc, tt * 128:(tt + 1) * 128], wba, wba[:, kc, 0:32], kc == 0, kc == 7)
            self.copy("act" if tt % 2 else "dve", ba, ba[:, tt, :], ps, ps[:, 0:32])
        self.act(beta, beta[:], ba, ba[:, :, 0:16], AF.Sigmoid)
        self.ts(nbeta, nbeta[:], beta, beta[:], -1.0, None, ALU.mult)
        self.tt("dve", g, g[:], ba, ba[:, :, 16:32], dtb, dtb[:, None, :].to_broadcast([128, NT, 16]), ALU.add)
        self.act(g, g[:], g, g[:], AF.Exp)
        self.act(g, g[:], g, g[:], AF.Ln, bias=1.0, scale=1.0)
        self.act(alog, alog[:], alog, alog[:], AF.Exp)
        self.stt(g, g[:], g, g[:], -1.0, alog, alog[:, None, :].to_broadcast([128, NT, 16]), ALU.mult, ALU.mult)

        self.tap('g', g, g[:])
        self.tap('beta', beta, beta[:])
        raw = P.sb("draw", [64, T_], F32, ph)
        qf = P.sb("dq", [64, T_], F32, ph)
        kf = P.sb("dk", [64, T_], F32, ph)
        vf = P.sb("dv", [64, T_], F32, ph)
        zf = P.sb("dz", [64, T_], F32, ph)
        qb = P.sb("dqb", [64, T_], BF16, ph)
        kb = P.sb("dkb", [64, T_], BF16, ph)
        osum = P.sb("dosum", [128, NT, 64], F32, ph)
        ytm = P.sb("dytm", [128, NT, 128], F32, ph)
        sqt = P.sb("dsq", [64, 512], F32, ph)
        rn = P.sb("drn", [64, 512], F32, ph)
        KSLOT = int(_osx.environ.get("KSLOT", "3" if s.is_sample else "4"))
        lmask = P.sb("dlmask", [128, 7, 128], F32, ph)
        self.load(lmask, lmask[:], d["lmask"], d["lmask"][:])
        osum2 = P.sb("dosum2", [128, NT, 64], F32, ph)
        r_t1 = Ring([P.sb("dt1%d" % i, [128, 64], F32, ph) for i in range(2)])

        def mkslot(i):
            R = {}
            def a(name, shape, dt):
                R[name] = P.sb("d%s_%d" % (name, i), shape, dt, ph)
            a("S", [64, 64], F32); a("Sb", [64, 64], BF16)
            a("gbc", [128, 128], F32); a("dcol", [128, 4], F32); a("e3", [128, 4], F32)
            a("dabs", [128, 128], F32); a("Dm", [128, 128], F32); a("Ds", [128, 128], F32); a("Di", [128, 128], F32)
            a("P0", [128, 2, 128], F32); a("NTk", [128, 7, 128], BF16); a("qkT", [128, 128], BF16)
            a("kv", [128, 128], F32); a("X", [128, 128], BF16); a("Xf", [128, 128], F32)
            a("kg", [128, 64], BF16); a("wT", [64, 128], BF16); a("vn", [128, 64], BF16); a("t1", [128, 64], F32)
            a("bw", [128, 1], F32)
            nbk = 8 // KSLOT
            bk = self.psr.tiles[nbk * i:nbk * (i + 1)]
            names = ["psd", "psk", "pkv", "pst_", "psw", "ps2", "psx", "pw", "psv", "pso", "pss"]
            if nbk >= 4:
                amap = {"psd": 0, "psk": 1, "pkv": 2, "pst_": 3, "psw": 0, "ps2": 2, "psx": 1, "pw": 3, "psv": 0, "pso": 1, "pss": 2}
            else:
                amap = {"psd": 0, "psk": 1, "pkv": 0, "pst_": 1, "psw": 0, "ps2": 1, "psx": 0, "pw": 1, "psv": 0, "pso": 1, "pss": 0}
            R["ph"] = {n: bk[amap[n] % nbk] for n in names}
            R["TT"] = Ring([P.sb("dTT%d_%d" % (j, i), [128, 2, 128], BF16, ph) for j in range(2)])
            a("WW", [128, 2, 128], BF16)
            return R
        slots = [mkslot(i) for i in range(KSLOT)]
        chS = [(P.sb("dchS%d" % i, [64, 64], F32, ph), P.sb("dchSb%d" % i, [64, 64], BF16, ph)) for i in range(2 * s.nseq)]
        masks = self.masks
        sq2 = P.sb("dsq2", [128, NT, 64], F32, ph)
        ssq = P.sb("dssq", [128, NT], F32, ph)

        import os as _os
        _NH = int(_os.environ.get('DN_HEADS', '8'))
        _ST = int(_os.environ.get('DN_STAGE', '9'))
        for h in range(_NH):
            hp, half = h // 2, h % 2
            m0, m1 = half * 64, half * 64 + 64
            for which, dst in ((0, qf), (1, kf), (2, vf)):
                self.proj_fm(l, TI_DN + hp * 4 + which, s,
                             lambda ps, blk: self.copy("act", raw, raw[:, blk * 512:(blk + 1) * 512], ps, ps[0:64, :]), m0, m1)
                for q in range(s.nseq):
                    a, b = q * L, (q + 1) * L
                    self.ts(dst, dst[:, a:b], raw, raw[:, a:b], cwq[:, which, h, 1:2], None, ALU.mult, extra_reads=[cwq])
                    self.stt(dst, dst[:, a + 1:b], raw, raw[:, a:b - 1], cwq[:, which, h, 0:1], dst, dst[:, a + 1:b], ALU.mult, ALU.add, extra_reads=[cwq])
                    self.stt(dst, dst[:, a:b - 1], raw, raw[:, a + 1:b], cwq[:, which, h, 2:3], dst, dst[:, a:b - 1], ALU.mult, ALU.add, extra_reads=[cwq])
                self.act(dst, dst[:], dst, dst[:], AF.Silu)
            self.proj_fm(l, TI_DN + hp * 4 + 3, s,
                         lambda ps, blk: self.copy("act", zf, zf[:, blk * 512:(blk + 1) * 512], ps, ps[0:64, :]), m0, m1)
            for (x, xb_, sc) in ((qf, qb, 64.0), (kf, kb, 1.0)):
                for blk in range(s.nb):
                    sl = slice(blk * 512, (blk + 1) * 512)
                    self.tt("dve", sqt, sqt[:], x, x[:, sl], x, x[:, sl], ALU.mult)
                    ps = self.pst()
                    self.mm(ps, ps[0:64, :], self.ones32, self.ones32[0:64, 0:64], sqt, sqt[:])
                    self.act(rn, rn[:], ps, ps[0:64, :], AF.Sqrt, bias=EPS * sc, scale=sc)
                    P.op("dve", lambda e: e.reciprocal(out=rn[:], in_=rn[:]), reads=[rn], writes=[rn])
                    self.tt("dve", x, x[:, sl], x, x[:, sl], rn, rn[:], ALU.mult)
                self.copy("act", xb_, xb_[:], x, x[:])
            self.tap('q', qf, qf[:])
            self.tap('k', kf, kf[:])
            self.tap('v', vf, vf[:])
            def unit(dr, q, pos, R):
                col = dr * 8 + h
                if dr == 0:
                    cm, rm, sm, im = M_IU, M_SL, M_SL, M_IL
                else:
                    cm, rm, sm, im = M_IL, M_SU, M_SU, M_IU
                ch = chains[(dr, q)]
                S, Sbb = ch["S"], ch["Sb"]
                oacc = osum if dr == 0 else osum2
                cl = pos if dr == 0 else cps - 1 - pos
                if True:
                    c = q * cps + cl
                    tsl = slice(c * 128, (c + 1) * 128)
                    gcol = g[:, c, col:col + 1]
                    bcol = beta[:, c, col:col + 1]
                    nbcol = nbeta[:, c, col:col + 1]
                    gbc, dcol, e3, dabs, Dm, Ds, Di = R["gbc"], R["dcol"], R["e3"], R["dabs"], R["Dm"], R["Ds"], R["Di"]
                    P0, NTk, qkT, kv, X, Xf = R["P0"], R["NTk"], R["qkT"], R["kv"], R["X"], R["Xf"]
                    kg, wT, vn, t1, bw, WW = R["kg"], R["wT"], R["vn"], R["t1"], R["bw"], R["WW"]
                    self.copy("pool", gbc, gbc[:], g, gcol.to_broadcast([128, 128]))
                    yield
                    psd = R["ph"]["psd"]
                    self.mm(psd, psd[:, 0:128], gbc, gbc[:], masks, masks[:, cm, :])
                    self.mm(psd, psd[:, 128:129], masks, masks[:, cm, :], g, gcol)
                    self.mm(psd, psd[:, 129:130], masks, masks[:, rm, :], g, gcol)
                    self.mm(psd, psd[:, 130:131], self.ones32, self.ones32[:], g, gcol)
                    yield
                    self.copy("dve", dcol, dcol[:, 0:3], psd, psd[:, 128:131])
                    yield
                    self.act(e3, e3[:, 0:3], dcol, dcol[:, 0:3], AF.Exp)
                    self.ts(dabs, dabs[:], psd, psd[:, 0:128], dcol[:, 0:1], 0.0, ALU.subtract, ALU.max, extra_reads=[dcol])
                    yield
                    self.act(Dm, Dm[:], dabs, dabs[:], AF.Exp, scale=-1.0)
                    yield
                    self.tt("pool", Ds, Ds[:], Dm, Dm[:], masks, masks[:, sm, :], ALU.mult)
                    self.tt("pool", Di, Di[:], Dm, Dm[:], masks, masks[:, im, :], ALU.mult)
                    psk = R["ph"]["psk"]
                    self.mm(psk, psk[:, 0:128], kb, kb[:, tsl], kb, kb[:, tsl])
                    self.mm(psk, psk[:, 128:256], qb, qb[:, tsl], kb, kb[:, tsl])
                    yield
                    self.stt(P0, P0[:, 0, :], psk, psk[:, 0:128], nbcol, Ds, Ds[:], ALU.mult, ALU.mult, extra_reads=[nbeta])
                    self.tt("dve", P0, P0[:, 1, :], psk, psk[:, 128:256], Di, Di[:], ALU.mult)
                    yield
                    pst_ = R["ph"]["pst_"]
                    self.tr(pst_, pst_[:, 0:128], P0, P0[:, 0, :])
                    self.tr(pst_, pst_[:, 128:256], P0, P0[:, 1, :])
                    pkv = R["ph"]["pkv"]
                    self.tr(pkv, pkv[:, 0:64], kf, kf[:, tsl])
                    self.tr(pkv, pkv[:, 64:128], vf, vf[:, tsl])
                    yield
                    self.tt("dve", NTk, NTk[:], pst_, pst_[:, 0:128][:, None, :].to_broadcast([128, 7, 128]), lmask, lmask[:], ALU.mult)
                    self.copy("dve", qkT, qkT[:], pst_, pst_[:, 128:256])
                    self.copy("act", kv, kv[:], pkv, pkv[:, 0:128])
                    self.tt("dve", bw, bw[:], beta, bcol, e3, e3[:, 0:1], ALU.mult)
                    yield
                    self.ts(X, X[:, 0:64], kv, kv[:, 64:128], bcol, None, ALU.mult, extra_reads=[beta])
                    self.ts(X, X[:, 64:128], kv, kv[:, 0:64], bw[:, 0:1], None, ALU.mult, extra_reads=[bw])
                    self.ts(kg, kg[:], kv, kv[:, 0:64], e3[:, 1:2], None, ALU.mult, extra_reads=[e3])
                    yield
                    TT = self.identb2
                    for lev in range(7):
                        psw = R["ph"]["psw"]
                        self.mm(psw, psw[:, 0:128], NTk, NTk[:, lev, :], TT, TT[:, 0, :])
                        self.mm(psw, psw[:, 128:256], TT, TT[:, 0, :], NTk, NTk[:, lev, :])
                        yield
                        self.copy("act", WW, WW[:].rearrange("p a b -> p (a b)"), psw, psw[:, 0:256])
                        yield
                        ps2 = R["ph"]["ps2"]
                        self.mm(ps2, ps2[:, 0:128], TT, TT[:, 1, :], WW, WW[:, 0, :])
                        self.mm(ps2, ps2[:, 128:256], WW, WW[:, 0, :], TT, TT[:, 1, :])
                        yield
                        TTn = R["TT"].next()
                        self.tt("dve", TTn, TTn[:].rearrange("p a b -> p (a b)"), TT, TT[:].rearrange("p a b -> p (a b)"), ps2, ps2[:, 0:256], ALU.add)
                        TT = TTn
                        yield
                    psx = R["ph"]["psx"]
                    self.mm(psx, psx[:, 0:128], TT, TT[:, 1, :], X, X[:])
                    yield
                    self.copy("act", Xf, Xf[:], psx, psx[:, 0:128])
                    yield
                    pw = R["ph"]["pw"]
                    self.tr(pw, pw[0:64, 0:128], Xf, Xf[:, 64:128])
                    yield
                    self.copy("act", wT, wT[:], pw, pw[0:64, 0:128])
                    yield
                    while ch["done"] < pos:
                        yield
                    psv = R["ph"]["psv"]
                    self.mm(psv, psv[:, 0:64], wT, wT[:], Sbb, Sbb[:])
                    self.mm(psv, psv[:, 64:128], qb, qb[:, tsl], Sbb, Sbb[:])
                    yield
                    self.tt("dve", vn, vn[:], Xf, Xf[:, 0:64], psv, psv[:, 0:64], ALU.subtract)
                    yield
                    pso = R["ph"]["pso"]
                    self.mm(pso, pso[:, 0:64], qkT, qkT[:], vn, vn[:])
                    self.ts(t1, t1[:], psv, psv[:, 64:128], e3[:, 0:1], None, ALU.mult, extra_reads=[e3])
                    yield
                    self.tt("dve", oacc, oacc[:, c, :], t1, t1[:], pso, pso[:, 0:64], ALU.add)
                    pss = R["ph"]["pss"]
                    self.mm(pss, pss[0:64, 0:64], kg, kg[:], vn, vn[:])
                    yield
                    self.stt(S, S[:], S, S[:], e3[0:64, 2:3], pss, pss[0:64, 0:64], ALU.mult, ALU.add, extra_reads=[e3])
                    yield
                    self.copy("act", Sbb, Sbb[:], S, S[:])
                    ch["done"] += 1
                    yield
                if pos == cps - 1 and not s.is_sample:
                    self.store(s.st_out, s.st_out[q, l, dr, h], S, S[:])

            P.barrier()
            chains = {}
            ci = 0
            for q in range(s.nseq):
                for dr in range(2):
                    S_, Sb_ = chS[ci]
                    ci += 1
                    if s.is_sample:
                        self.load(S_, S_[:], s.st_in, s.st_in[l, dr, h])
                    else:
                        self.memset(S_, S_[:], 0.0)
                    self.copy("act", Sb_, Sb_[:], S_, S_[:])
                    chains[(dr, q)] = {"S": S_, "Sb": Sb_, "done": 0}
            pending = [(dr, q, pos) for pos in range(cps) for q in range(s.nseq) for dr in range(2)]
            running = []
            free_slots = list(slots)
            while pending or running:
                while pending and free_slots:
                    dr_, q_, pos_ = pending.pop(0)
                    R_ = free_slots.pop(0)
                    running.append((unit(dr_, q_, pos_, R_), R_))
                for item in list(running):
                    gen, R_ = item
                    try:
                        next(gen)
                    except StopIteration:
                        running.remove(item)
                        free_slots.append(R_)
            P.barrier()
            self.tt("pool", osum, osum[:], osum, osum[:], osum2, osum2[:], ALU.add)
            self.tap('osum', osum, osum[:])
            self.tt("dve", sq2, sq2[:], osum, osum[:], osum, osum[:], ALU.mult)
            P.op("dve", lambda e, sq2=sq2, ssq=ssq: e.reduce_sum(out=ssq[:], in_=sq2[:], axis=AX.X), reads=[sq2], writes=[ssq])
            self.act(ssq, ssq[:], ssq, ssq[:], AF.Sqrt, bias=EPS, scale=1.0 / 64.0)
            P.op("dve", lambda e, ssq=ssq: e.reciprocal(out=ssq[:], in_=ssq[:]), reads=[ssq], writes=[ssq])
            self.tt("dve", sq2, sq2[:], osum, osum[:], ssq, ssq[:, :, None].to_broadcast([128, NT, 64]), ALU.mult)
            self.tt("dve", sq2, sq2[:], sq2, sq2[:], norma, norma[:, None, :].to_broadcast([128, NT, 64]), ALU.mult)
            for c in range(NT):
                pz = self.pst()
                self.tr(pz, pz[:, 0:64], zf, zf[:, c * 128:(c + 1) * 128])
                t1 = r_t1.next()
                self.act(t1, t1[:], pz, pz[:, 0:64], AF.Silu)
                self.tt("dve", ytm, ytm[:, c, m0:m1], sq2, sq2[:, c, :], t1, t1[:], ALU.mult)
            if half == 1:
                for c in range(NT):
                    py = self.pst()
                    self.tr(py, py[:, 0:128], ytm, ytm[:, c, :])
                    self.copy("act" if c % 2 else "dve", y, y[:, hp, c * 128:(c + 1) * 128], py, py[:, 0:128])
        if self.dbg and self.dbg.get("tap") == ("ya", l, s.name):
            self.store(self.dbg_out, self.dbg_out[:], y, y[:])

    def phase_out_ffn(self, l, s):
        P = self.P
        d = self.d
        j = s.cidx
        modv = self.modv[l]
        with ExitStack() as ph:
            xbr = Ring([P.sb("fxb%d" % i, [128, 8, 512], F32, ph) for i in range(2 if s.T <= 1024 else 1)])
            sqr = Ring([P.sb("fsq%d" % i, [128, 512], BF16, ph) for i in range(2)])
            tmpr = Ring([P.sb("ftmp%d" % i, [128, 512], F32, ph) for i in range(2)])
            rstd = P.sb("frstd", [128, 512], F32, ph)
            P.barrier()
            for o in range(8):
                w = self.wload8(d["wo"], d["wo"][l, o])
                for blk in range(s.nb):
                    sl = slice(blk * 512, (blk + 1) * 512)
                    ps = self.pst()
                    for kc in range(8):
                        self.mm(ps, ps[:], w, w[:, kc, :], self.merged, self.merged[:, kc, sl], kc == 0, kc == 7)
                    self.copy("act" if blk % 2 else "dve", self.hT, self.hT[:, o, sl], ps, ps[:])
            for blk in range(s.nb):
                sl = slice(blk * 512, (blk + 1) * 512)
                xb = xbr.next()
                self.load(xb, xb[:], s.xres, s.xres[:, :, sl])
                for c in range(8):
                    self.stt(xb, xb[:, c, :], self.hT, self.hT[:, c, sl], modv[:, 16 + c, j:j + 1], xb, xb[:, c, :], ALU.mult, ALU.add, extra_reads=[modv])
                self.store(s.xres, s.xres[:, :, sl], xb, xb[:])
                self.norm_block(l, s, 1, xb, blk, sqr, rstd, tmpr)
            P.barrier()
            MB = min(s.T, 2048)
            nbm = MB // 512
            actb = P.sb("factb", [128, 22, MB], BF16, ph)
            sgr = Ring([P.sb("fsg%d" % i, [128, 512], F32, ph) for i in range(2)])
            w22 = Ring([P.sb("w22_%d" % i, [128, 22, 128], BF16, ph) for i in range(2)])
            for mb in range(s.T // MB):
                for i in range(22):
                    wg = self.wload8(d["wgu"], d["wgu"][l, i])
                    wu = self.wload8(d["wgu"], d["wgu"][l, 22 + i])
                    for b2 in range(nbm):
                        sl = slice(mb * MB + b2 * 512, mb * MB + (b2 + 1) * 512)
                        pg = self.pst()
                        for kc in range(8):
                            self.mm(pg, pg[:], wg, wg[:, kc, :], self.hT, self.hT[:, kc, sl], kc == 0, kc == 7)
                        pu = self.pst()
                        for kc in range(8):
                            self.mm(pu, pu[:], wu, wu[:, kc, :], self.hT, self.hT[:, kc, sl], kc == 0, kc == 7)
                        sg = sgr.next()
                        self.act(sg, sg[:], pg, pg[:], AF.Silu)
                        self.tt("dve", actb, actb[:, i, b2 * 512:(b2 + 1) * 512], sg, sg[:], pu, pu[:], ALU.mult)
                for o in range(8):
                    w = w22.next()
                    self.load(w, w[:], d["wdn"], d["wdn"][l, o], eng="pool")
                    for b2 in range(nbm):
                        sl = slice(mb * MB + b2 * 512, mb * MB + (b2 + 1) * 512)
                        ps = self.pst()
                        for kc in range(22):
                            self.mm(ps, ps[:], w, w[:, kc, :], actb, actb[:, kc, b2 * 512:(b2 + 1) * 512], kc == 0, kc == 21)
                        self.copy("act" if b2 % 2 else "dve", self.merged, self.merged[:, o, sl], ps, ps[:])
            for blk in range(s.nb):
                sl = slice(blk * 512, (blk + 1) * 512)
                xb = xbr.next()
                self.load(xb, xb[:], s.xres, s.xres[:, :, sl])
                for c in range(8):
                    self.stt(xb, xb[:, c, :], self.merged, self.merged[:, c, sl], modv[:, 40 + c, j:j + 1], xb, xb[:, c, :], ALU.mult, ALU.add, extra_reads=[modv])
                self.store(s.xres, s.xres[:, :, sl], xb, xb[:])
        P.barrier()

    def phase_final(self):
        P = self.P
        with ExitStack() as ph:
            xbr = Ring([P.sb("gxb%d" % i, [128, 8, 512], F32, ph) for i in range(2)])
            sqr = Ring([P.sb("gsq%d" % i, [128, 512], BF16, ph) for i in range(2)])
            rstd = P.sb("grstd", [128, 512], F32, ph)
            xn = P.sb("gxn", [128, 8, 512], F32, ph)
            yo = Ring([P.sb("gyo%d" % i, [128, D], F32, ph) for i in range(2)])
            for s in self.streams:
                for blk in range(s.nb):
                    sl = slice(blk * 512, (blk + 1) * 512)
                    xb = xbr.next()
                    self.load(xb, xb[:], s.xres, s.xres[:, :, sl])
                    self.rstd_block(xb, xb[:], sqr, rstd)
                    for c in range(8):
                        self.stt(xn, xn[:, c, :], xb, xb[:, c, :], self.nfT[:, c:c + 1], rstd, rstd[:], ALU.mult, ALU.mult, extra_reads=[self.nfT])
                    for t4 in range(4):
                        yt = yo.next()
                        for half in range(2):
                            ps = self.pst()
                            for c4 in range(4):
                                c = half * 4 + c4
                                self.tr(ps, ps[:, c4 * 128:(c4 + 1) * 128], xn, xn[:, c, t4 * 128:(t4 + 1) * 128])
                            self.copy("act" if half else "dve", yt, yt[:, half * 512:(half + 1) * 512], ps, ps[:])
                        r0 = blk * 512 + t4 * 128
                        self.store(s.y_out, s.y_out[r0:r0 + 128, :], yt, yt[:])
        P.barrier()


N_CORES = 8
_CACHE = {}


def make_streams():
    return [Stream("P", 4, 256, 0, [(0, 4)], False),
            Stream("S", 1, 2048, 1, [(0, 1), (1, 1), (2, 1), (3, 1)], True)]


def shared_inputs(inp, streams, depth):
    f = lambda a: np.ascontiguousarray(np.asarray(a, dtype=np.float32))
    sh = {}
    for s in streams:
        zposT, window = hyena_consts(s.L)
        Fc, Fs, Gc, Gs = dft_consts(s.L)
        CL, SLn, c64, s64 = fnet_consts(s.L)
        sh["zposT_" + s.name] = zposT
        sh["win_" + s.name] = window
        sh["Fc_" + s.name] = Fc
        sh["Fs_" + s.name] = Fs
        sh["Gc_" + s.name] = Gc
        sh["Gs_" + s.name] = Gs
        sh["CL_" + s.name] = CL
        sh["SLn_" + s.name] = SLn
        sh["c64_" + s.name] = c64
        sh["s64_" + s.name] = s64
        if s.is_sample:
            sh["pos_" + s.name] = grid_pos_embed_np(s.T)
    fm = lambda v: np.ascontiguousarray(f(v).reshape(-1, 128).T)
    sh["wmod"] = np.stack([tile_lhsT(f(inp["w_mod"][l])) for l in range(depth)])
    sh["bmodT"] = np.stack([fm(inp["b_mod"][l]) for l in range(depth)])
    sh["n1T"] = np.stack([fm(inp["norm1_g"][l]) for l in range(depth)])
    sh["n2T"] = np.stack([fm(inp["norm2_g"][l]) for l in range(depth)])
    sh["nfT"] = fm(inp["norm_f"])
    cols = win_tile_cols()
    win = np.zeros((depth, N_WIN_TILES, 128, 8, 128), np.float32)
    for l in range(depth):
        w = f(inp["w_in"][l])
        for ti, (c0, wd) in enumerate(cols):
            win[l, ti, :, :, :wd] = w[:, c0:c0 + wd].reshape(8, 128, wd).transpose(1, 0, 2)
    sh["win_t"] = win
    sh["wp_t"] = np.stack([np.stack([tile_lhsT(f(inp[k][l])) for k in ("w_pa", "w_pb", "w_pc")]) for l in range(depth)])
    sh["wo_t"] = np.stack([tile_lhsT(f(inp["w_o"][l])) for l in range(depth)])
    sh["wgu_t"] = np.stack([tile_lhsT(f(inp["w_gu"][l])) for l in range(depth)])
    sh["wdn_t"] = np.stack([tile_lhsT(f(inp["w_down"][l])) for l in range(depth)])
    cq = f(inp["conv_qkv"])[:depth]
    sh["convqT"] = np.ascontiguousarray(cq.reshape(depth, 3, 3, 8, 64).transpose(0, 4, 2, 3, 1))
    chy = f(inp["conv_hy"])[:depth]
    sh["convhT"] = np.ascontiguousarray(chy.reshape(depth, 3, 12, 128).transpose(0, 3, 2, 1))
    sh["alog_bc"] = np.ascontiguousarray(np.broadcast_to(f(inp["a_log"])[:depth].reshape(depth, 1, 16), (depth, 128, 16)))
    sh["dtb_bc"] = np.ascontiguousarray(np.broadcast_to(f(inp["dt_bias"])[:depth].reshape(depth, 1, 16), (depth, 128, 16)))
    sh["norma_bc"] = np.ascontiguousarray(np.broadcast_to(f(inp["norm_a"])[:depth].reshape(depth, 1, 64), (depth, 128, 64)))
    sh["hyw1"] = f(inp["hy_w1"])[:depth]
    sh["hyb1T"] = f(inp["hy_b1"])[:depth].reshape(depth, 64, 1)
    sh["hyfqT"] = f(inp["hy_freq"])[:depth].reshape(depth, 64, 1)
    sh["hyw2"] = f(inp["hy_w2"])[:depth]
    sh["hyb2T"] = f(inp["hy_b2"])[:depth].reshape(depth, 64, 1)
    sh["hyw3"] = f(inp["hy_w3"])[:depth]
    sh["hybiasT"] = np.ascontiguousarray(f(inp["hy_bias"])[:depth].reshape(depth, 2, 4, 128).transpose(0, 3, 1, 2))
    sh["masks"] = mask_consts()
    sh["ident"] = np.eye(128, dtype=np.float32)
    sh["lmask"] = level_masks()
    return sh


def kernel(**inp):
    depth = DEPTH
    streams = make_streams()
    if "nc" not in _CACHE:
        _CACHE["nc"] = Builder(make_streams(), depth=depth).build()
    nc = _CACHE["nc"]
    sh = shared_inputs(inp, streams, depth)
    xp = np.asarray(inp["x_prompt"], np.float32)
    xs = np.asarray(inp["x_sample"], np.float32)
    st = np.asarray(inp["state_delta"], np.float32)
    c = np.asarray(inp["c"], np.float32)
    cctx = np.asarray(inp["c_ctx"], np.float32)
    in_maps = []
    for core in range(N_CORES):
        sidx = core // 4
        m = dict(sh)
        m["x_P"] = np.ascontiguousarray(xp[core * 4:(core + 1) * 4].reshape(1024, D))
        m["x_S"] = np.ascontiguousarray(xs[sidx])
        m["st0_S"] = np.ascontiguousarray(st[sidx][:depth])
        cv = np.stack([cctx, c[sidx]], axis=-1)
        m["cvecT"] = np.ascontiguousarray(cv.reshape(8, 128, 2).transpose(1, 0, 2))
        in_maps.append(m)
    res = run_bass_kernel_spmd(nc, in_maps, core_ids=list(range(N_CORES)))
    r = res.results
    y_prompt = np.concatenate([r[i]["y_P"].reshape(4, 256, D) for i in range(N_CORES)], axis=0).astype(np.float32)
    y_sample = np.stack([r[0]["y_S"], r[4]["y_S"]], axis=0).astype(np.float32)
    new_state = np.concatenate([r[i]["st_P"] for i in range(N_CORES)], axis=0).astype(np.float32)
    return (y_prompt, y_sample, new_state)
```

```python
import math
from contextlib import ExitStack
import numpy as np
import concourse.bass as bass
import concourse.mybir as mybir
from concourse.bass_utils import run_bass_kernel_spmd

F32 = mybir.dt.float32
BF16 = mybir.dt.bfloat16
I32 = mybir.dt.int32
AF = mybir.ActivationFunctionType
ALU = mybir.AluOpType
AX = mybir.AxisListType

SAME_ENG_SYNC = True

D = 1024
DEPTH = 2
H_A = 8
DK = 64
DIN = 7200
DFF = 2816
EPS = 1e-6
CH = 128


class T:
    __slots__ = ("name", "ap", "last_write", "reads", "dsem", "dcount")

    def __init__(self, name, ap):
        self.name = name
        self.ap = ap
        self.last_write = None
        self.reads = []
        self.dsem = None
        self.dcount = 0

    def __getitem__(self, k):
        return self.ap[k]


class TV:
    def __init__(self, base, ap):
        self.base = base
        self.ap = ap
        self.name = base.name

    def __getitem__(self, k):
        return self.ap[k]


class Op:
    __slots__ = ("eng", "fn", "deps", "is_dma", "ndma", "sem_owner", "needed", "sig_sem", "sig_val", "name")

    def __init__(self, eng, fn, name=""):
        self.eng = eng
        self.fn = fn
        self.deps = []
        self.is_dma = False
        self.ndma = 0
        self.sem_owner = None
        self.needed = False
        self.sig_sem = None
        self.sig_val = 0
        self.name = name


ENGS = ("pe", "act", "dve", "pool", "sp")


def _ap(h):
    return h.ap() if callable(getattr(h, "ap", None)) else h


class Prog:
    def __init__(self, nc):
        self.nc = nc
        self.es = ExitStack()
        self.ops = {e: [] for e in ENGS}
        self.all_ops = []
        self.bar_deps = []
        self.bar_id = 0
        self.bar_seen = {e: 0 for e in ENGS}
        self.pending_dma = []
        self.uid = 0
        self.bar_pos = []

    def sb(self, name, shape, dtype, es=None):
        self.uid += 1
        h = (es or self.es).enter_context(self.nc.sbuf_tensor("%s_%d" % (name, self.uid), list(shape), dtype))
        return T(name, _ap(h))

    def ps(self, name, shape, dtype):
        h = self.es.enter_context(self.nc.psum_tensor(name, list(shape), dtype))
        return T(name, _ap(h))

    def tile(self, name, ap):
        return T(name, ap)

    def barrier(self):
        deps = []
        for e in ENGS:
            for o in reversed(self.ops[e]):
                if not o.is_dma:
                    deps.append(o)
                    break
        deps.extend(self.pending_dma)
        self.pending_dma = []
        for d in deps:
            d.needed = True
        self.bar_deps = deps
        self.bar_id += 1
        self.bar_pos.append(len(self.all_ops))

    def _record(self, op, reads, writes):
        reads = [getattr(r, "base", r) for r in reads]
        writes = [getattr(w, "base", w) for w in writes]
        deps = []
        if self.bar_seen[op.eng] != self.bar_id:
            self.bar_seen[op.eng] = self.bar_id
            deps.extend(self.bar_deps)
        for r in reads:
            if r.last_write is not None:
                deps.append(r.last_write)
        for w in writes:
            if w.last_write is not None:
                deps.append(w.last_write)
            deps.extend(w.reads)
        seen = set()
        for d in deps:
            if d is op or id(d) in seen:
                continue
            seen.add(id(d))
            if d.eng == op.eng and not d.is_dma:
                if op.eng == "pe" or not SAME_ENG_SYNC:
                    continue
            op.deps.append(d)
            d.needed = True
        for r in reads:
            r.reads.append(op)
        for w in writes:
            w.last_write = op
            w.reads = []
        self.ops[op.eng].append(op)
        self.all_ops.append(op)
        return op

    def op(self, eng, fn, reads=(), writes=(), name=""):
        return self._record(Op(eng, fn, name), list(reads), list(writes))

    def dma(self, eng, fn, reads=(), writes=(), owner=None, ndma=1, name=""):
        o = Op(eng, fn, name)
        o.is_dma = True
        o.ndma = ndma
        o.sem_owner = owner
        o.needed = True
        self.pending_dma.append(o)
        return self._record(o, list(reads), list(writes))

    def finalize(self):
        nc = self.nc
        es = self.es
        esem = {}
        for e in ("pe", "act", "dve", "pool"):
            esem[e] = es.enter_context(nc.semaphore("s_" + e))
        last_dma = {}
        qtype = {}
        for i, o in enumerate(self.all_ops):
            if o.is_dma:
                k = id(o.sem_owner)
                last_dma[k] = i
                qt = "sw" if o.eng == "pool" else "hw"
                if qtype.get(k, qt) != qt:
                    qtype[k] = "mixed"
                else:
                    qtype[k] = qt
        free = {"sw": [], "hw": [], "mixed": []}
        active = {}
        sem_final = {}
        bpos = list(self.bar_pos)
        bi = 0
        nsem = 0
        for i, o in enumerate(self.all_ops):
            while bi < len(bpos) and bpos[bi] <= i:
                b = bpos[bi]
                bi += 1
                for k in list(active.keys()):
                    ow = active[k]
                    if last_dma[k] < b:
                        if qtype[k] != "mixed":
                            free[qtype[k]].append((ow.dsem, ow.dcount))
                        del active[k]
            if o.is_dma:
                ow = o.sem_owner
                if ow.dsem is None:
                    fl = free[qtype[id(ow)]]
                    if fl and qtype[id(ow)] != "mixed":
                        ow.dsem, ow.dcount = fl.pop()
                    else:
                        ow.dsem = es.enter_context(nc.semaphore("d%d" % nsem))
                        nsem += 1
                    active[id(ow)] = ow
                ow.dcount += 16 * o.ndma
                o.sig_sem = ow.dsem
                o.sig_val = ow.dcount
                sem_final[id(ow.dsem)] = (ow.dsem, ow.dcount)
        dma_final = list(sem_final.values())
        for e in ("pe", "act", "dve", "pool"):
            c = 0
            for o in self.ops[e]:
                if o.is_dma:
                    continue
                if o.needed:
                    c += 1
                    o.sig_sem = esem[e]
                    o.sig_val = c
        self.n_sems = 4 + nsem
        engmap = {"pe": "tensor", "act": "scalar", "dve": "vector", "pool": "gpsimd", "sp": "sync"}
        with nc.Block() as block:
            for e in ENGS:
                ops = self.ops[e]
                final = (e == "sp")

                def body(eh, ops=ops, final=final):
                    seen = {}
                    for o in ops:
                        for d in o.deps:
                            key = id(d.sig_sem)
                            if seen.get(key, 0) >= d.sig_val:
                                continue
                            seen[key] = d.sig_val
                            eh.wait_ge(d.sig_sem, d.sig_val)
                        r = o.fn(eh)
                        if o.is_dma:
                            if not isinstance(r, (list, tuple)):
                                r = [r]
                            assert len(r) == o.ndma, (o.name, len(r), o.ndma)
                            for ins in r:
                                ins.then_inc(o.sig_sem, 16)
                        elif o.needed:
                            r.then_inc(o.sig_sem, 1)
                    if final:
                        for (sm, cnt) in dma_final:
                            eh.wait_ge(sm, cnt)

                getattr(block, engmap[e])(body)
        return self


class Ring:
    def __init__(self, tiles):
        self.tiles = tiles
        self.i = 0

    def next(self):
        t = self.tiles[self.i % len(self.tiles)]
        self.i += 1
        return t


import ml_dtypes
import os as _osx
NPBF = ml_dtypes.bfloat16


def tile_lhsT(w):
    K, N = w.shape
    nt = (N + 127) // 128
    wp = np.zeros((K, nt * 128), np.float32)
    wp[:, :N] = w
    kc = K // 128
    return np.ascontiguousarray(wp.reshape(kc, 128, nt, 128).transpose(2, 1, 0, 3))


def grid_pos_embed_np(n_tokens, grid_w=64):
    rows = n_tokens // grid_w
    r = np.repeat(np.arange(rows), grid_w).astype(np.float32)
    col = np.tile(np.arange(grid_w), rows).astype(np.float32)
    quarter = D // 4
    omega = (1.0 / (10000.0 ** (np.arange(quarter, dtype=np.float32) / quarter))).astype(np.float32)

    def emb(pos):
        a = pos[:, None] * omega[None, :]
        return np.concatenate([np.sin(a), np.cos(a)], axis=-1)

    return np.concatenate([emb(r), emb(col)], axis=-1).astype(np.float32)


def win_tile_cols():
    tiles = []
    for hp in range(4):
        for base in (0, 512, 1024, 1536):
            tiles.append((base + hp * 128, 128))
    tiles.append((2048, 32))
    for i in range(12):
        tiles.append((2080 + i * 128, 128))
    for i in range(4):
        tiles.append((3616 + i * 128, 128))
    for i in range(24):
        tiles.append((4128 + i * 128, 128))
    return tiles


TI_DN = 0
TI_BA = 16
TI_HY = 17
TI_FN = 29
TI_GATE = 33
N_WIN_TILES = 57


def hyena_consts(L):
    bands = 16
    t = np.linspace(0.0, 1.0, L, dtype=np.float32)[:, None]
    wpos = ((2.0 * math.pi / L) * np.arange(L, dtype=np.float32))[:, None].astype(np.float32)
    fr = np.linspace(1e-4, bands - 1, bands, dtype=np.float32)[None, :]
    zpos = np.concatenate([t, np.cos(fr * wpos), -np.sin(fr * wpos)], axis=-1).astype(np.float32)
    deltas = np.abs(np.linspace(math.log(1e-2) / 1.5, math.log(1e-2) / 0.3, 512, dtype=np.float32))
    window = np.exp(-t * deltas[None, :]).astype(np.float32)
    return np.ascontiguousarray(zpos.T), window


def dft_consts(L):
    nfp = (L // 128 + 1) * 128
    s = np.arange(L, dtype=np.float64)[:, None]
    f = np.arange(nfp, dtype=np.float64)[None, :]
    ang = np.pi * np.mod(s * f, 2 * L) / L
    valid = (f <= L)
    Fc = np.where(valid, np.cos(ang), 0.0)
    Fs = np.where(valid, np.sin(ang), 0.0)
    n = 2 * L
    cf = np.where((f == 0) | (f == L), 1.0 / n, 2.0 / n) * valid
    Gc = (Fc * cf).T
    Gs = (-Fs * cf).T
    nt = L // 128
    nf = nt + 1
    tF = lambda a: np.ascontiguousarray(a.reshape(nt, 128, nf, 128).transpose(2, 1, 0, 3)).astype(NPBF)
    tG = lambda a: np.ascontiguousarray(a.reshape(nf, 128, L // 256, 256).transpose(2, 1, 0, 3)).astype(NPBF)
    return (tF(Fc), tF(Fs), tG(Gc), tG(Gs))


def fnet_consts(L):
    t = np.arange(L, dtype=np.float64)
    ang = 2 * np.pi * np.mod(np.outer(t, t), L) / L
    CL = np.cos(ang)
    SLn = -np.sin(ang)
    c = np.arange(64, dtype=np.float64)
    a64 = 2 * np.pi * np.mod(np.outer(c, c), 64) / 64
    sc = 1.0 / math.sqrt(64.0 * L)
    c64 = np.zeros((128, 128))
    s64 = np.zeros((128, 128))
    for g in range(2):
        c64[g * 64:(g + 1) * 64, g * 64:(g + 1) * 64] = np.cos(a64) * sc
        s64[g * 64:(g + 1) * 64, g * 64:(g + 1) * 64] = np.sin(a64) * sc
    nt = L // 128
    tC = lambda a: np.ascontiguousarray(a.reshape(nt, 128, L // 256, 256).transpose(2, 1, 0, 3)).astype(NPBF)
    return tC(CL), tC(SLn), c64.astype(NPBF), s64.astype(NPBF)


def mask_consts():
    i = np.arange(128)[:, None]
    j = np.arange(128)[None, :]
    m = np.stack([(i > j), (i >= j), (i < j), (i <= j)]).astype(np.float32)
    return m


M_SL, M_IL, M_SU, M_IU = 0, 1, 2, 3


def level_masks():
    i = np.arange(128)[:, None]
    j = np.arange(128)[None, :]
    ms = []
    for lv in range(7):
        bs = 2 << lv
        ms.append(((i // bs) == (j // bs)) & ((i // (bs // 2)) != (j // (bs // 2))))
    return np.ascontiguousarray(np.stack(ms, axis=1).astype(np.float32))


class Stream:
    def __init__(self, name, nseq, L, cidx, groups, is_sample):
        self.name = name
        self.nseq = nseq
        self.L = L
        self.T = nseq * L
        self.cidx = cidx
        self.nb = self.T // 512
        self.groups = groups
        self.is_sample = is_sample


class Builder:
    def __init__(self, streams, depth=DEPTH, dbg=None, skip=()):
        self.streams = streams
        self.depth = depth
        self.dbg = dbg
        self.skip = skip
        self.nc = bass.Bass("TRN2", target_bir_lowering=False)
        self.P = Prog(self.nc)
        self.dram = {}

    def din(self, name, shape, dtype=F32):
        ap = self.nc.dram_tensor(name, list(shape), dtype, kind="ExternalInput").ap()
        t = T(name, ap)
        self.dram[name] = (t, list(shape), dtype)
        return t

    def dout(self, name, shape, dtype=F32):
        ap = self.nc.dram_tensor(name, list(shape), dtype, kind="ExternalOutput").ap()
        return T(name, ap)

    def dscr(self, name, shape, dtype=F32):
        ap = self.nc.dram_tensor(name, list(shape), dtype, kind="Internal").ap()
        return T(name, ap)

    def mm(self, out_t, out_ap, lhsT_t, lhsT_ap, rhs_t, rhs_ap, start=True, stop=True):
        self.P.op("pe", lambda e: e.matmul(out_ap, lhsT=lhsT_ap, rhs=rhs_ap, start=start, stop=stop),
                  reads=[lhsT_t, rhs_t], writes=[out_t])

    def tr(self, out_t, out_ap, in_t, in_ap, ident_t=None, ident_ap=None):
        if ident_t is None:
            ident_t = self.ident
            n = in_ap.shape[0]
            ident_ap = self.ident[0:n, 0:n]
        self.P.op("pe", lambda e: e.transpose(out_ap, in_ap, ident_ap), reads=[in_t, ident_t], writes=[out_t])

    def act(self, out_t, out_ap, in_t, in_ap, func, bias=None, scale=None, extra_reads=()):
        kw = {}
        if bias is not None:
            kw["bias"] = bias
        if scale is not None:
            kw["scale"] = scale
        self.P.op("act", lambda e: e.activation(out=out_ap, in_=in_ap, func=func, **kw),
                  reads=[in_t] + list(extra_reads), writes=[out_t])

    def tt(self, eng, out_t, out_ap, a_t, a_ap, b_t, b_ap, op):
        self.P.op(eng, lambda e: e.tensor_tensor(out=out_ap, in0=a_ap, in1=b_ap, op=op),
                  reads=[a_t, b_t], writes=[out_t])

    def ts(self, out_t, out_ap, a_t, a_ap, s1, s2, op0, op1=None, extra_reads=()):
        if op1 is None:
            self.P.op("dve", lambda e: e.tensor_scalar(out=out_ap, in0=a_ap, scalar1=s1, scalar2=None, op0=op0),
                      reads=[a_t] + list(extra_reads), writes=[out_t])
        else:
            self.P.op("dve", lambda e: e.tensor_scalar(out=out_ap, in0=a_ap, scalar1=s1, scalar2=s2, op0=op0, op1=op1),
                      reads=[a_t] + list(extra_reads), writes=[out_t])

    def stt(self, out_t, out_ap, a_t, a_ap, scalar, b_t, b_ap, op0, op1, extra_reads=()):
        self.P.op("dve", lambda e: e.scalar_tensor_tensor(out=out_ap, in0=a_ap, scalar=scalar, in1=b_ap, op0=op0, op1=op1),
                  reads=[a_t, b_t] + list(extra_reads), writes=[out_t])

    def copy(self, eng, out_t, out_ap, in_t, in_ap):
        if eng == "act":
            self.P.op("act", lambda e: e.copy(out=out_ap, in_=in_ap), reads=[in_t], writes=[out_t])
        else:
            self.P.op(eng, lambda e: e.tensor_copy(out=out_ap, in_=in_ap), reads=[in_t], writes=[out_t])

    def memset(self, t, ap, val, eng="dve"):
        self.P.op(eng, lambda e: e.memset(ap, val), writes=[t])

    def load(self, out_t, out_ap, in_t, in_ap, eng="sp"):
        self.P.dma(eng, lambda e: e.dma_start(out=out_ap, in_=in_ap), reads=[in_t], writes=[out_t], owner=out_t)

    def store(self, out_t, out_ap, in_t, in_ap, eng="sp"):
        self.P.dma(eng, lambda e: e.dma_start(out=out_ap, in_=in_ap), reads=[in_t], writes=[out_t], owner=in_t)

    def pst(self):
        return self.psr.next()

    def tap(self, name, t, ap):
        if self.dbg and self.dbg.get("tap2") == name and not getattr(self, "_tapped", False):
            self._tapped = True
            self.store(self.dbg_out, self.dbg_out[:], t, ap)

    def build(self):
        P = self.P
        depth = self.depth
        for s in self.streams:
            nfp = (s.L // 128 + 1) * 128
            s.x_in = self.din("x_" + s.name, [s.T, D])
            s.y_out = self.dout("y_" + s.name, [s.T, D])
            s.xres = self.dscr("xres_" + s.name, [128, 8, s.T])
            if s.is_sample:
                s.st_in = self.din("st0_" + s.name, [depth, 2, H_A, DK, DK])
                s.pos = self.din("pos_" + s.name, [s.T, D])
            else:
                s.st_out = self.dout("st_" + s.name, [s.nseq, depth, 2, H_A, DK, DK])
            s.zposT = self.din("zposT_" + s.name, [33, s.L])
            s.window = self.din("win_" + s.name, [s.L, 512])
            nt_ = s.L // 128
            s.Fc = self.din("Fc_" + s.name, [nt_ + 1, 128, nt_, 128], BF16)
            s.Fs = self.din("Fs_" + s.name, [nt_ + 1, 128, nt_, 128], BF16)
            s.Gc = self.din("Gc_" + s.name, [s.L // 256, 128, nt_ + 1, 256], BF16)
            s.Gs = self.din("Gs_" + s.name, [s.L // 256, 128, nt_ + 1, 256], BF16)
            s.CL = self.din("CL_" + s.name, [s.L // 256, 128, nt_, 256], BF16)
            s.SLn = self.din("SLn_" + s.name, [s.L // 256, 128, nt_, 256], BF16)
            s.c64 = self.din("c64_" + s.name, [128, 128], BF16)
            s.s64 = self.din("s64_" + s.name, [128, 128], BF16)
        d = {}
        d["cvecT"] = self.din("cvecT", [128, 8, 2])
        d["wmod"] = self.din("wmod", [depth, 48, 128, 8, 128])
        d["bmodT"] = self.din("bmodT", [depth, 128, 48])
        d["n1"] = self.din("n1T", [depth, 128, 8])
        d["n2"] = self.din("n2T", [depth, 128, 8])
        d["nf"] = self.din("nfT", [128, 8])
        d["win"] = self.din("win_t", [depth, N_WIN_TILES, 128, 8, 128])
        d["wp"] = self.din("wp_t", [depth, 3, 8, 128, 4, 128])
        d["wo"] = self.din("wo_t", [depth, 8, 128, 8, 128])
        d["wgu"] = self.din("wgu_t", [depth, 44, 128, 8, 128])
        d["wdn"] = self.din("wdn_t", [depth, 8, 128, 22, 128])
        d["convq"] = self.din("convqT", [depth, 64, 3, 8, 3])
        d["convh"] = self.din("convhT", [depth, 128, 12, 3])
        d["alog"] = self.din("alog_bc", [depth, 128, 16])
        d["dtb"] = self.din("dtb_bc", [depth, 128, 16])
        d["norma"] = self.din("norma_bc", [depth, 128, 64])
        d["hyw1"] = self.din("hyw1", [depth, 33, 64])
        d["hyb1"] = self.din("hyb1T", [depth, 64, 1])
        d["hyfq"] = self.din("hyfqT", [depth, 64, 1])
        d["hyw2"] = self.din("hyw2", [depth, 64, 64])
        d["hyb2"] = self.din("hyb2T", [depth, 64, 1])
        d["hyw3"] = self.din("hyw3", [depth, 64, 2048])
        d["hybias"] = self.din("hybiasT", [depth, 128, 2, 4])
        d["masks"] = self.din("masks", [4, 128, 128])
        d["ident"] = self.din("ident", [128, 128])
        d["lmask"] = self.din("lmask", [128, 7, 128])
        self.d = d
        if self.dbg:
            self.dbg_out = self.dout("dbg", self.dbg["shape"], self.dbg.get("dtype", F32))

        self.psr = Ring([P.ps("ps%d" % i, [128, 512], F32) for i in range(8)])
        TMAX = max(s.T for s in self.streams)
        self.hT = P.sb("hT", [128, 8, TMAX], BF16)
        self.merged = P.sb("merged", [128, 8, TMAX], BF16)
        self.w8 = Ring([P.sb("w8_%d" % i, [128, 8, 128], BF16) for i in range(3)])
        self.w4 = Ring([P.sb("w4_%d" % i, [128, 4, 128], BF16) for i in range(2)])
        self.ident = P.sb("ident", [128, 128], F32)
        self.onesb = P.sb("onesb", [128, 128], BF16)
        self.ones32 = P.sb("ones32", [128, 128], F32)
        self.masks = P.sb("masks", [128, 4, 128], F32)
        self.load(self.ident, self.ident[:], d["ident"], d["ident"][:])
        self.memset(self.onesb, self.onesb[:], 1.0 / 1024.0)
        self.identb2 = P.sb("identb2", [128, 2, 128], BF16)
        self.copy("dve", self.identb2, self.identb2[:, 0, :], self.ident, self.ident[:])
        self.copy("dve", self.identb2, self.identb2[:, 1, :], self.ident, self.ident[:])
        self.memset(self.ones32, self.ones32[:], 1.0)
        for i in range(4):
            self.load(self.masks, self.masks[:, i, :], d["masks"], d["masks"][i])
        self.modv = [P.sb("modv%d" % l, [128, 48, 2], F32) for l in range(depth)]
        self.modA = [P.sb("modA%d" % l, [128, 2, 8, 2], F32) for l in range(depth)]
        self.nfT = P.sb("nfT", [128, 8], F32)
        self.load(self.nfT, self.nfT[:], d["nf"], d["nf"][:])

        self.phase_mod()
        self.phase_input()
        for l in range(depth):
            for s in self.streams:
                self.phase_norm1(l, s)
                self.phase_mix(l, s)
                self.phase_out_ffn(l, s)
        self.phase_final()
        if self.dbg:
            self.dbg["fn"](self)
        P.finalize()
        P.es.close()
        return self.nc

    def wload8(self, dram_t, dram_ap):
        w = self.w8.next()
        self.load(w, w[:], dram_t, dram_ap, eng="pool")
        return w

    def phase_mod(self):
        P = self.P
        d = self.d
        with ExitStack() as ph:
            cv = P.sb("cv", [128, 8, 2], F32, ph)
            scv = P.sb("scv", [128, 8, 2], F32, ph)
            wm = Ring([P.sb("wm%d" % i, [128, 8, 128], F32, ph) for i in range(3)])
            bm = P.sb("bm", [128, 48], F32, ph)
            n12 = P.sb("n12", [128, 2, 8], F32, ph)
            self.load(cv, cv[:], d["cvecT"], d["cvecT"][:])
            self.act(scv, scv[:], cv, cv[:], AF.Silu)
            for l in range(self.depth):
                modv = self.modv[l]
                self.load(bm, bm[:], d["bmodT"], d["bmodT"][l])
                self.load(n12, n12[:, 0, :], d["n1"], d["n1"][l])
                self.load(n12, n12[:, 1, :], d["n2"], d["n2"][l])
                for c in range(48):
                    w = wm.next()
                    self.load(w, w[:], d["wmod"], d["wmod"][l, c])
                    ps = self.pst()
                    for kc in range(8):
                        self.mm(ps, ps[:, 0:2], w, w[:, kc, :], scv, scv[:, kc, :], kc == 0, kc == 7)
                    self.ts(modv, modv[:, c, :], ps, ps[:, 0:2], bm[:, c:c + 1], None, ALU.add, extra_reads=[bm])
                mA = self.modA[l]
                for sub in range(2):
                    sc0 = 8 + 24 * sub
                    for j in range(2):
                        self.ts(mA, mA[:, sub, :, j], modv, modv[:, sc0:sc0 + 8, j], 1.0, None, ALU.add)
                        self.tt("dve", mA, mA[:, sub, :, j], mA, mA[:, sub, :, j], n12, n12[:, sub, :], ALU.mult)
        P.barrier()

    def phase_input(self):
        P = self.P
        with ExitStack() as ph:
            xin = Ring([P.sb("xin%d" % i, [128, D], F32, ph) for i in range(2)])
            pin = Ring([P.sb("pin%d" % i, [128, D], F32, ph) for i in range(2)])
            xo = Ring([P.sb("xo%d" % i, [128, 8, 128], F32, ph) for i in range(2)])
            for s in self.streams:
                for tt in range(s.T // 128):
                    xt = xin.next()
                    self.load(xt, xt[:], s.x_in, s.x_in[tt * 128:(tt + 1) * 128, :])
                    if s.is_sample:
                        pt = pin.next()
                        self.load(pt, pt[:], s.pos, s.pos[tt * 128:(tt + 1) * 128, :])
                        self.tt("dve", xt, xt[:], xt, xt[:], pt, pt[:], ALU.add)
                    xot = xo.next()
                    for half in range(2):
                        ps = self.pst()
                        for c4 in range(4):
                            c = half * 4 + c4
                            self.tr(ps, ps[:, c4 * 128:(c4 + 1) * 128], xt, xt[:, c * 128:(c + 1) * 128])
                        self.copy("act" if half else "dve", xot, xot[:, half * 4:half * 4 + 4, :], ps,
                                  ps[:].rearrange("p (c t) -> p c t", c=4))
                    self.store(s.xres, s.xres[:, :, tt * 128:(tt + 1) * 128], xot, xot[:])
        P.barrier()

    def rstd_block(self, xb, xb_ap, sqr, rstd):
        ps = self.pst()
        for c in range(8):
            sq = sqr.next()
            self.act(sq, sq[:], xb, xb_ap[:, c, :], AF.Square)
            self.mm(ps, ps[:], self.onesb, self.onesb[:], sq, sq[:], c == 0, c == 7)
        self.act(rstd, rstd[:], ps, ps[:], AF.Sqrt, bias=EPS, scale=1.0)
        self.P.op("dve", lambda e: e.reciprocal(out=rstd[:], in_=rstd[:]), reads=[rstd], writes=[rstd])

    def norm_block(self, l, s, sub, xb, blk, sqr, rstd, tmpr):
        j = s.cidx
        self.rstd_block(xb, xb[:], sqr, rstd)
        mA = self.modA[l]
        modv = self.modv[l]
        sh0 = 24 * sub
        for c in range(8):
            tmp = tmpr.next()
            self.tt("dve", tmp, tmp[:], xb, xb[:, c, :], rstd, rstd[:], ALU.mult)
            self.act(self.hT, self.hT[:, c, blk * 512:(blk + 1) * 512], tmp, tmp[:], AF.Identity,
                     bias=modv[:, sh0 + c, j:j + 1], scale=mA[:, sub, c, j:j + 1], extra_reads=[mA, modv])

    def phase_norm1(self, l, s):
        P = self.P
        with ExitStack() as ph:
            xbr = Ring([P.sb("xb%d" % i, [128, 8, 512], F32, ph) for i in range(2)])
            sqr = Ring([P.sb("sq%d" % i, [128, 512], BF16, ph) for i in range(2)])
            tmpr = Ring([P.sb("ntmp%d" % i, [128, 512], F32, ph) for i in range(2)])
            rstd = P.sb("rstd", [128, 512], F32, ph)
            for blk in range(s.nb):
                xb = xbr.next()
                self.load(xb, xb[:], s.xres, s.xres[:, :, blk * 512:(blk + 1) * 512])
                self.norm_block(l, s, 0, xb, blk, sqr, rstd, tmpr)
        P.barrier()

    def proj_fm(self, l, ti, s, evac, m0=0, m1=128):
        w = self.wload8(self.d["win"], self.d["win"][l, ti])
        for blk in range(s.nb):
            ps = self.pst()
            for kc in range(8):
                self.mm(ps, ps[0:m1 - m0, :], w, w[:, kc, m0:m1], self.hT, self.hT[:, kc, blk * 512:(blk + 1) * 512], kc == 0, kc == 7)
            evac(ps, blk)

    def merge_branch(self, l, s, br, y_t):
        for o in range(8):
            wg = self.wload8(self.d["win"], self.d["win"][l, TI_GATE + br * 8 + o])
            wp = self.w4.next()
            self.load(wp, wp[:], self.d["wp"], self.d["wp"][l, br, o], eng="pool")
            for blk in range(s.nb):
                sl = slice(blk * 512, (blk + 1) * 512)
                ps1 = self.pst()
                for kc in range(8):
                    self.mm(ps1, ps1[:], wg, wg[:, kc, :], self.hT, self.hT[:, kc, sl], kc == 0, kc == 7)
                ps2 = self.pst()
                for kc in range(4):
                    self.mm(ps2, ps2[:], wp, wp[:, kc, :], y_t, y_t[:, kc, sl], kc == 0, kc == 3)
                sig = self.sigr.next()
                self.act(sig, sig[:], ps1, ps1[:], AF.Sigmoid)
                if br == 0:
                    self.tt("dve", self.merged, self.merged[:, o, sl], sig, sig[:], ps2, ps2[:], ALU.mult)
                else:
                    self.tt("dve", sig, sig[:], sig, sig[:], ps2, ps2[:], ALU.mult)
                    self.tt("pool", self.merged, self.merged[:, o, sl], self.merged, self.merged[:, o, sl], sig, sig[:], ALU.add)

    def phase_mix(self, l, s):
        P = self.P
        with ExitStack() as ph:
            self.sigr = Ring([P.sb("sig%d" % i, [128, 512], F32, ph) for i in range(2)])
            y = P.sb("ybr", [128, 4, s.T], BF16, ph)
            with ExitStack() as ph2:
                if "delta" in self.skip:
                    self.memset(y, y[:], 0.0)
                else:
                    self.mix_delta(l, s, y, ph2)
            P.barrier()
            self.merge_branch(l, s, 0, y)
            P.barrier()
            for (ct0, nct) in s.groups:
                with ExitStack() as ph2:
                    if "hyena" in self.skip:
                        self.memset(y, y[:], 0.0)
                    else:
                        self.mix_hyena(l, s, y, ct0, nct, ph2)
                P.barrier()
            if self.dbg and self.dbg.get("tap") == ("yb", l, s.name):
                self.store(self.dbg_out, self.dbg_out[:], y, y[:])
            self.merge_branch(l, s, 1, y)
            P.barrier()
            for (ct0, nct) in [(0, 4)]:
                with ExitStack() as ph2:
                    self.mix_fnet(l, s, y, ct0, nct, ph2)
                P.barrier()
            if self.dbg and self.dbg.get("tap") == ("yc", l, s.name):
                self.store(self.dbg_out, self.dbg_out[:], y, y[:])
            self.merge_branch(l, s, 2, y)
        P.barrier()

    def mix_fnet(self, l, s, y, ct0, nct, ph):
        P = self.P
        L = s.L
        nt = L // 128
        W = nct * 128
        xc = P.sb("xc", [128, nct, s.T], BF16, ph)
        c64 = P.sb("c64", [128, 128], BF16, ph)
        s64 = P.sb("s64", [128, 128], BF16, ph)
        self.load(c64, c64[:], s.c64, s.c64[:])
        self.load(s64, s64[:], s.s64, s.s64[:])
        for ci in range(nct):
            self.proj_fm(l, TI_FN + ct0 + ci, s,
                         lambda ps, blk, ci=ci: self.copy("act", xc, xc[:, ci, blk * 512:(blk + 1) * 512], ps, ps[:]))
        U = P.sb("U", [128, nt, 2, W], BF16, ph)
        NW = 256
        dftc = Ring([P.sb("dftc%d" % i, [128, nt, NW], BF16, ph) for i in range(2)])
        dfts = Ring([P.sb("dfts%d" % i, [128, nt, NW], BF16, ph) for i in range(2)])
        for q in range(s.nseq):
            t0 = q * L
            for tt in range(nt):
                for cs, mat in ((0, c64), (1, s64)):
                    ps = self.pst()
                    for ci in range(nct):
                        self.mm(ps, ps[:, ci * 128:(ci + 1) * 128], xc, xc[:, ci, t0 + tt * 128:t0 + (tt + 1) * 128], mat, mat[:])
                    self.copy("act" if cs else "dve", U, U[:, tt, cs, :], ps, ps[:, 0:W])
            for nbk in range(L // NW):
                cm = dftc.next()
                sm = dfts.next()
                self.load(cm, cm[:], s.CL, s.CL[nbk])
                self.load(sm, sm[:], s.SLn, s.SLn[nbk])
                for ci in range(nct):
                    ps = self.pst()
                    for tt in range(nt):
                        self.mm(ps, ps[:, 0:NW], U, U[:, tt, 0, ci * 128:(ci + 1) * 128], cm, cm[:, tt, :], tt == 0, False)
                        self.mm(ps, ps[:, 0:NW], U, U[:, tt, 1, ci * 128:(ci + 1) * 128], sm, sm[:, tt, :], False, tt == nt - 1)
                    self.copy("act" if ci % 2 else "dve", y, y[:, ct0 + ci, t0 + nbk * NW:t0 + (nbk + 1) * NW], ps, ps[:, 0:NW])

    def sin_reduce(self, out_t, out_ap, in_t, in_ap, ti, ti_ap, tf, tf_ap):
        P = self.P
        inv = 1.0 / (2.0 * math.pi)
        P.op("dve", lambda e: e.tensor_scalar(out=ti_ap, in0=in_ap, scalar1=inv, scalar2=None, op0=ALU.mult),
             reads=[in_t], writes=[ti])
        self.copy("dve", tf, tf_ap, ti, ti_ap)
        self.stt(tf, tf_ap, tf, tf_ap, -2.0 * math.pi, in_t, in_ap, ALU.mult, ALU.add)
        self.ts(tf, tf_ap, tf, tf_ap, math.pi, -math.pi, ALU.min, ALU.max)
        self.act(out_t, out_ap, tf, tf_ap, AF.Sin)

    def hy_gate(self, l, s, which, ct, raw, dst, dst_ap, cw):
        L = s.L
        self.proj_fm(l, TI_HY + which * 4 + ct, s,
                     lambda ps, blk: self.copy("act", raw, raw[:, blk * 512:(blk + 1) * 512], ps, ps[:]))
        wi = which * 4 + ct
        for q in range(s.nseq):
            a, b = q * L, (q + 1) * L
            self.ts(dst, dst_ap[:, a:b], raw, raw[:, a:b], cw[:, wi, 1:2], None, ALU.mult, extra_reads=[cw])
            self.stt(dst, dst_ap[:, a + 1:b], raw, raw[:, a:b - 1], cw[:, wi, 0:1], dst, dst_ap[:, a + 1:b], ALU.mult, ALU.add, extra_reads=[cw])
            self.stt(dst, dst_ap[:, a:b - 1], raw, raw[:, a + 1:b], cw[:, wi, 2:3], dst, dst_ap[:, a:b - 1], ALU.mult, ALU.add, extra_reads=[cw])

    def mix_hyena(self, l, s, y, ct0, nct, ph):
        P = self.P
        d = self.d
        L = s.L
        nt = L // 128
        nf = nt + 1
        T_ = s.T
        W = nct * 128
        cw = P.sb("hcw", [128, 12, 3], F32, ph)
        self.load(cw, cw[:], d["convh"], d["convh"][l])
        hb = P.sb("hbias", [128, 2, 4], F32, ph)
        self.load(hb, hb[:], d["hybias"], d["hybias"][l])
        raw = P.sb("hraw", [128, T_], F32, ph)
        gate = P.sb("hgate", [128, nct, T_], BF16, ph)
        z = P.sb("hz", [128, nct, T_], BF16, ph)
        for ci in range(nct):
            self.hy_gate(l, s, 2, ct0 + ci, raw, z, z[:, ci, :], cw)
        Hre = P.sb("Hre", [128, 2, nf, W], BF16, ph)
        Him = P.sb("Him", [128, 2, nf, W], BF16, ph)
        ztm = P.sb("ztm", [128, nt, W], BF16, ph)
        Yre = P.sb("Yre", [128, nf, W], BF16, ph)
        Yim = P.sb("Yim", [128, nf, W], BF16, ph)
        fcr = Ring([P.sb("fcr%d" % i, [128, nt, 128], BF16, ph) for i in range(2)])
        fsr = Ring([P.sb("fsr%d" % i, [128, nt, 128], BF16, ph) for i in range(2)])
        tmpz = Ring([P.sb("tmpz%d" % i, [128, max(W, 256)], F32, ph) for i in range(4)])
        zf = P.sb("hzf", [128, 128], F32, ph)

        with ExitStack() as pA:
            hsd = P.sb("hsd", [128, nt, 2, 2, W], BF16, pA)
            with ExitStack() as pf:
                w1 = P.sb("hw1", [33, 64], F32, pf)
                w2 = P.sb("hw2", [64, 64], F32, pf)
                w3 = P.sb("hw3", [64, 2048], F32, pf)
                b1 = P.sb("hb1", [64, 1], F32, pf)
                b2 = P.sb("hb2", [64, 1], F32, pf)
                fq = P.sb("hfq", [64, 1], F32, pf)
                zp = P.sb("hzp", [33, min(L, 512)], F32, pf)
                h1 = P.sb("hh1", [64, min(L, 512)], F32, pf)
                h2 = P.sb("hh2", [64, min(L, 512)], F32, pf)
                ti = P.sb("hti", [64, 512], I32, pf)
                tf = P.sb("htf", [64, 512], F32, pf)
                ta = P.sb("hta", [64, 512], F32, pf)
                win = Ring([P.sb("hwin%d" % i, [128, W], F32, pf) for i in range(2)])
                hf = Ring([P.sb("hf%d" % i, [128, 4, W], F32, pf) for i in range(2)])
                self.load(w1, w1[:], d["hyw1"], d["hyw1"][l])
                self.load(w2, w2[:], d["hyw2"], d["hyw2"][l])
                self.load(w3, w3[:], d["hyw3"], d["hyw3"][l])
                self.load(b1, b1[:], d["hyb1"], d["hyb1"][l])
                self.load(b2, b2[:], d["hyb2"], d["hyb2"][l])
                self.load(fq, fq[:], d["hyfq"], d["hyfq"][l])
                wb = min(L, 512)
                for blk in range(L // wb):
                    sl = slice(blk * wb, (blk + 1) * wb)
                    self.load(zp, zp[:], s.zposT, s.zposT[:, sl])
                    for (src, wsrc, bsrc, dst, kk) in ((zp, w1, b1, h1, 33), (h1, w2, b2, h2, 64)):
                        ps = self.pst()
                        self.mm(ps, ps[0:64, 0:wb], wsrc, wsrc[0:kk, :], src, src[0:kk, :])
                        self.ts(ta, ta[:, 0:wb], ps, ps[0:64, 0:wb], bsrc[:, 0:1], fq[:, 0:1], ALU.add, ALU.mult, extra_reads=[bsrc, fq])
                        self.sin_reduce(dst, dst[:], ta, ta[:, 0:wb], ti, ti[:, 0:wb], tf, tf[:, 0:wb])
                    for t4 in range(wb // 128):
                        tt = blk * (wb // 128) + t4
                        wn = win.next()
                        self.load(wn, wn[:], s.window, s.window[tt * 128:(tt + 1) * 128, ct0 * 128:ct0 * 128 + W])
                        h = hf.next()
                        for fi in range(4):
                            ps = self.pst()
                            c0 = fi * 512 + ct0 * 128
                            self.mm(ps, ps[:, 0:W], h2, h2[:, t4 * 128:(t4 + 1) * 128], w3, w3[:, c0:c0 + W])
                            self.tt("dve", h, h[:, fi, :], ps, ps[:, 0:W], wn, wn[:], ALU.mult)
                        for o in range(2):
                            self.tt("dve", hsd, hsd[:, tt, o, 0, :], h, h[:, 2 * o, :], h, h[:, 2 * o + 1, :], ALU.add)
                            self.tt("pool", hsd, hsd[:, tt, o, 1, :], h, h[:, 2 * o + 1, :], h, h[:, 2 * o, :], ALU.subtract)
            P.barrier()
            def ld_f(ft):
                fcm = fcr.next()
                fsm = fsr.next()
                self.load(fcm, fcm[:], s.Fc, s.Fc[ft])
                self.load(fsm, fsm[:], s.Fs, s.Fs[ft])
                return fcm, fsm

            def build_ztm(q):
                t0 = q * L
                for tt in range(nt):
                    ps = self.pst()
                    for ci in range(nct):
                        self.copy("dve", zf, zf[:], z, z[:, ci, t0 + tt * 128:t0 + (tt + 1) * 128])
                        self.tr(ps, ps[:, ci * 128:(ci + 1) * 128], zf, zf[:])
                    self.copy("act" if tt % 2 else "dve", ztm, ztm[:, tt, :], ps, ps[:, 0:W])

            def fwd_product(ft, fcm, fsm, o):
                pc = self.pst()
                for tt in range(nt):
                    self.mm(pc, pc[:, 0:W], fcm, fcm[:, tt, :], ztm, ztm[:, tt, :], tt == 0, tt == nt - 1)
                pz = self.pst()
                for tt in range(nt):
                    self.mm(pz, pz[:, 0:W], fsm, fsm[:, tt, :], ztm, ztm[:, tt, :], tt == 0, tt == nt - 1)
                a1 = tmpz.next(); a2 = tmpz.next(); a3 = tmpz.next(); a4 = tmpz.next()
                self.tt("dve", a1, a1[:, 0:W], pc, pc[:, 0:W], Hre, Hre[:, o, ft, :], ALU.mult)
                self.tt("dve", a2, a2[:, 0:W], pz, pz[:, 0:W], Him, Him[:, o, ft, :], ALU.mult)
                self.tt("pool", Yre, Yre[:, ft, :], a1, a1[:, 0:W], a2, a2[:, 0:W], ALU.add)
                self.tt("dve", a3, a3[:, 0:W], pc, pc[:, 0:W], Him, Him[:, o, ft, :], ALU.mult)
                self.tt("dve", a4, a4[:, 0:W], pz, pz[:, 0:W], Hre, Hre[:, o, ft, :], ALU.mult)
                self.tt("pool", Yim, Yim[:, ft, :], a3, a3[:, 0:W], a4, a4[:, 0:W], ALU.subtract)

            fuse = (s.nseq == 1)
            if fuse:
                build_ztm(0)
            for ft in range(nf):
                fcm, fsm = ld_f(ft)
                for o in range(2):
                    for (mat, sd, dst) in ((fcm, 0, Hre), (fsm, 1, Him)):
                        ps = self.pst()
                        for tt in range(nt):
                            self.mm(ps, ps[:, 0:W], mat, mat[:, tt, :], hsd, hsd[:, tt, o, sd, :], tt == 0, tt == nt - 1)
                        self.copy("act" if sd else "dve", dst, dst[:, o, ft, :], ps, ps[:, 0:W])
                if fuse:
                    fwd_product(ft, fcm, fsm, 0)
        P.barrier()
        NW = 256
        gcr = Ring([P.sb("gcr%d" % i, [128, nf, NW], BF16, ph) for i in range(2)])
        gsr = Ring([P.sb("gsr%d" % i, [128, nf, NW], BF16, ph) for i in range(2)])
        for o in range(2):
            for ci in range(nct):
                self.hy_gate(l, s, o, ct0 + ci, raw, gate, gate[:, ci, :], cw)
            for q in range(s.nseq):
                t0 = q * L
                if not (fuse and o == 0):
                    build_ztm(q)
                    for ft in range(nf):
                        fcm, fsm = ld_f(ft)
                        fwd_product(ft, fcm, fsm, o)
                for nbk in range(L // NW):
                    gc = gcr.next()
                    gs = gsr.next()
                    self.load(gc, gc[:], s.Gc, s.Gc[nbk])
                    self.load(gs, gs[:], s.Gs, s.Gs[nbk])
                    sl = slice(t0 + nbk * NW, t0 + (nbk + 1) * NW)
                    for ci in range(nct):
                        ps = self.pst()
                        for ft in range(nf):
                            self.mm(ps, ps[:, 0:NW], Yre, Yre[:, ft, ci * 128:(ci + 1) * 128], gc, gc[:, ft, :], ft == 0, False)
                            self.mm(ps, ps[:, 0:NW], Yim, Yim[:, ft, ci * 128:(ci + 1) * 128], gs, gs[:, ft, :], False, ft == nf - 1)
                        a1 = tmpz.next()
                        self.stt(a1, a1[:, 0:NW], z, z[:, ci, sl], hb[:, o, ct0 + ci:ct0 + ci + 1], ps, ps[:, 0:NW], ALU.mult, ALU.add, extra_reads=[hb])
                        if o == 0:
                            self.tt("dve", z, z[:, ci, sl], a1, a1[:, 0:NW], gate, gate[:, ci, sl], ALU.mult)
                        else:
                            self.tt("dve", y, y[:, ct0 + ci, sl], a1, a1[:, 0:NW], gate, gate[:, ci, sl], ALU.mult)

    def mix_delta(self, l, s, y, ph):
        P = self.P
        d = self.d
        L = s.L
        T_ = s.T
        NT = T_ // 128
        cps = L // 128
        cwq = P.sb("dcw", [64, 3, 8, 3], F32, ph)
        self.load(cwq, cwq[:], d["convq"], d["convq"][l])
        alog = P.sb("dalog", [128, 16], F32, ph)
        dtb = P.sb("ddtb", [128, 16], F32, ph)
        norma = P.sb("dnorma", [128, 64], F32, ph)
        self.load(alog, alog[:], d["alog"], d["alog"][l])
        self.load(dtb, dtb[:], d["dtb"], d["dtb"][l])
        self.load(norma, norma[:], d["norma"], d["norma"][l])
        ba = P.sb("dba", [128, NT, 32], F32, ph)
        beta = P.sb("dbeta", [128, NT, 16], F32, ph)
        nbeta = P.sb("dnbeta", [128, NT, 16], F32, ph)
        g = P.sb("dg", [128, NT, 16], F32, ph)
        wba = self.wload8(d["win"], d["win"][l, TI_BA])
        for tt in range(NT):
            ps = self.pst()
            for kc in range(8):
                self.mm(ps, ps[:, 0:32], self.hT, self.hT[:, kc, tt * 128:(tt + 1) * 128], wba, wba[:, kc, 0:32], kc == 0, kc == 7)
            self.copy("act" if tt % 2 else "dve", ba, ba[:, tt, :], ps, ps[:, 0:32])
        self.act(beta, beta[:], ba, ba[:, :, 0:16], AF.Sigmoid)
        self.ts(nbeta, nbeta[:], beta, beta[:], -1.0, None, ALU.mult)
        self.tt("dve", g, g[:], ba, ba[:, :, 16:32], dtb, dtb[:, None, :].to_broadcast([128, NT, 16]), ALU.add)
        self.act(g, g[:], g, g[:], AF.Exp)
        self.act(g, g[:], g, g[:], AF.Ln, bias=1.0, scale=1.0)
        self.act(alog, alog[:], alog, alog[:], AF.Exp)
        self.stt(g, g[:], g, g[:], -1.0, alog, alog[:, None, :].to_broadcast([128, NT, 16]), ALU.mult, ALU.mult)

        self.tap('g', g, g[:])
        self.tap('beta', beta, beta[:])
        qf = P.sb("dq", [64, T_], F32, ph)
        kf = P.sb("dk", [64, T_], F32, ph)
        vf = P.sb("dv", [64, T_], F32, ph)
        zf = P.sb("dz", [64, T_], F32, ph)
        qb = P.sb("dqb", [64, T_], BF16, ph)
        kb = P.sb("dkb", [64, T_], BF16, ph)
        osum = P.sb("dosum", [128, NT, 64], F32, ph)
        ytm = P.sb("dytm", [128, NT, 128], F32, ph)
        KSLOT = int(_osx.environ.get("KSLOT", "4"))
        lmask = P.sb("dlmask", [128, 7, 128], F32, ph)
        self.load(lmask, lmask[:], d["lmask"], d["lmask"][:])
        osum2 = P.sb("dosum2", [128, NT, 64], F32, ph)
        r_t1 = Ring([P.sb("dt1%d" % i, [128, 64], F32, ph) for i in range(2)])

        def mkslot(i):
            R = {}
            def a(name, shape, dt):
                R[name] = P.sb("d%s_%d" % (name, i), shape, dt, ph)
            a("S", [64, 64], F32); a("Sb", [64, 64], BF16)
            a("gbc", [128, 128], F32); a("dcol", [128, 4], F32); a("e3", [128, 4], F32)
            a("dabs", [128, 128], F32); a("Dm", [128, 128], F32); a("Ds", [128, 128], F32); a("Di", [128, 128], F32)
            a("P0", [128, 2, 128], F32); a("NTk", [128, 7, 128], BF16); a("qkT", [128, 128], BF16)
            a("kv", [128, 128], F32); a("X", [128, 128], BF16); a("Xf", [128, 128], F32)
            a("kg", [128, 64], BF16); a("wT", [64, 128], BF16); a("vn", [128, 64], BF16); a("t1", [128, 64], F32)
            a("bw", [128, 1], F32)
            nbk = 8 // KSLOT
            bk = self.psr.tiles[nbk * i:nbk * (i + 1)]
            names = ["psd", "psk", "pkv", "pst_", "psw", "ps2", "psx", "pw", "psv", "pso", "pss"]
            if nbk >= 4:
                amap = {"psd": 0, "psk": 1, "pkv": 2, "pst_": 3, "psw": 0, "ps2": 2, "psx": 1, "pw": 3, "psv": 0, "pso": 1, "pss": 2}
            else:
                amap = {"psd": 0, "psk": 1, "pkv": 0, "pst_": 1, "psw": 0, "ps2": 1, "psx": 0, "pw": 1, "psv": 0, "pso": 1, "pss": 0}
            R["ph"] = {n: bk[amap[n] % nbk] for n in names}
            R["TT"] = Ring([P.sb("dTT%d_%d" % (j, i), [128, 2, 128], BF16, ph) for j in range(2)])
            a("WW", [128, 2, 128], BF16)
            return R
        slots = [mkslot(i) for i in range(KSLOT - 1)]
        chS = [(P.sb("dchS%d" % i, [64, 64], F32, ph), P.sb("dchSb%d" % i, [64, 64], BF16, ph)) for i in range(2 * s.nseq)]
        masks = self.masks
        sq2 = P.sb("dsq2", [128, NT, 64], F32, ph)
        ssq = P.sb("dssq", [128, NT], F32, ph)
        with ExitStack() as tmp_es:
            raw = P.sb("draw", [64, T_], F32, tmp_es)
            sqt = P.sb("dsq", [64, 512], F32, tmp_es)
            rn = P.sb("drn", [64, 512], F32, tmp_es)
        slots.append(mkslot(KSLOT - 1))

        import os as _os
        _NH = int(_os.environ.get('DN_HEADS', '8'))
        _ST = int(_os.environ.get('DN_STAGE', '9'))
        for h in range(_NH):
            hp, half = h // 2, h % 2
            m0, m1 = half * 64, half * 64 + 64
            for which, dst in ((0, qf), (1, kf), (2, vf)):
                self.proj_fm(l, TI_DN + hp * 4 + which, s,
                             lambda ps, blk: self.copy("act", raw, raw[:, blk * 512:(blk + 1) * 512], ps, ps[0:64, :]), m0, m1)
                for q in range(s.nseq):
                    a, b = q * L, (q + 1) * L
                    self.ts(dst, dst[:, a:b], raw, raw[:, a:b], cwq[:, which, h, 1:2], None, ALU.mult, extra_reads=[cwq])
                    self.stt(dst, dst[:, a + 1:b], raw, raw[:, a:b - 1], cwq[:, which, h, 0:1], dst, dst[:, a + 1:b], ALU.mult, ALU.add, extra_reads=[cwq])
                    self.stt(dst, dst[:, a:b - 1], raw, raw[:, a + 1:b], cwq[:, which, h, 2:3], dst, dst[:, a:b - 1], ALU.mult, ALU.add, extra_reads=[cwq])
                self.act(dst, dst[:], dst, dst[:], AF.Silu)
            self.proj_fm(l, TI_DN + hp * 4 + 3, s,
                         lambda ps, blk: self.copy("act", zf, zf[:, blk * 512:(blk + 1) * 512], ps, ps[0:64, :]), m0, m1)
            for (x, xb_, sc) in ((qf, qb, 64.0), (kf, kb, 1.0)):
                for blk in range(s.nb):
                    sl = slice(blk * 512, (blk + 1) * 512)
                    self.tt("dve", sqt, sqt[:], x, x[:, sl], x, x[:, sl], ALU.mult)
                    ps = self.pst()
                    self.mm(ps, ps[0:64, :], self.ones32, self.ones32[0:64, 0:64], sqt, sqt[:])
                    self.act(rn, rn[:], ps, ps[0:64, :], AF.Sqrt, bias=EPS * sc, scale=sc)
                    P.op("dve", lambda e: e.reciprocal(out=rn[:], in_=rn[:]), reads=[rn], writes=[rn])
                    self.tt("dve", x, x[:, sl], x, x[:, sl], rn, rn[:], ALU.mult)
                self.copy("act", xb_, xb_[:], x, x[:])
            self.tap('q', qf, qf[:])
            self.tap('k', kf, kf[:])
            self.tap('v', vf, vf[:])
            def unit(dr, q, pos, R):
                col = dr * 8 + h
                if dr == 0:
                    cm, rm, sm, im = M_IU, M_SL, M_SL, M_IL
                else:
                    cm, rm, sm, im = M_IL, M_SU, M_SU, M_IU
                ch = chains[(dr, q)]
                S, Sbb = ch["S"], ch["Sb"]
                oacc = osum if dr == 0 else osum2
                cl = pos if dr == 0 else cps - 1 - pos
                if True:
                    c = q * cps + cl
                    tsl = slice(c * 128, (c + 1) * 128)
                    gcol = g[:, c, col:col + 1]
                    bcol = beta[:, c, col:col + 1]
                    nbcol = nbeta[:, c, col:col + 1]
                    gbc, dcol, e3, dabs, Dm, Ds, Di = R["gbc"], R["dcol"], R["e3"], R["dabs"], R["Dm"], R["Ds"], R["Di"]
                    P0, NTk, qkT, kv, X, Xf = R["P0"], R["NTk"], R["qkT"], R["kv"], R["X"], R["Xf"]
                    kg, wT, vn, t1, bw, WW = R["kg"], R["wT"], R["vn"], R["t1"], R["bw"], R["WW"]
                    self.copy("pool", gbc, gbc[:], g, gcol.to_broadcast([128, 128]))
                    yield
                    psd = R["ph"]["psd"]
                    self.mm(psd, psd[:, 0:128], gbc, gbc[:], masks, masks[:, cm, :])
                    self.mm(psd, psd[:, 128:129], masks, masks[:, cm, :], g, gcol)
                    self.mm(psd, psd[:, 129:130], masks, masks[:, rm, :], g, gcol)
                    self.mm(psd, psd[:, 130:131], self.ones32, self.ones32[:], g, gcol)
                    yield
                    self.copy("dve", dcol, dcol[:, 0:3], psd, psd[:, 128:131])
                    yield
                    self.act(e3, e3[:, 0:3], dcol, dcol[:, 0:3], AF.Exp)
                    self.ts(dabs, dabs[:], psd, psd[:, 0:128], dcol[:, 0:1], 0.0, ALU.subtract, ALU.max, extra_reads=[dcol])
                    yield
                    self.act(Dm, Dm[:], dabs, dabs[:], AF.Exp, scale=-1.0)
                    yield
                    self.tt("pool", Ds, Ds[:], Dm, Dm[:], masks, masks[:, sm, :], ALU.mult)
                    self.tt("pool", Di, Di[:], Dm, Dm[:], masks, masks[:, im, :], ALU.mult)
                    psk = R["ph"]["psk"]
                    self.mm(psk, psk[:, 0:128], kb, kb[:, tsl], kb, kb[:, tsl])
                    self.mm(psk, psk[:, 128:256], qb, qb[:, tsl], kb, kb[:, tsl])
                    yield
                    self.stt(P0, P0[:, 0, :], psk, psk[:, 0:128], nbcol, Ds, Ds[:], ALU.mult, ALU.mult, extra_reads=[nbeta])
                    self.tt("dve", P0, P0[:, 1, :], psk, psk[:, 128:256], Di, Di[:], ALU.mult)
                    yield
                    pst_ = R["ph"]["pst_"]
                    self.tr(pst_, pst_[:, 0:128], P0, P0[:, 0, :])
                    self.tr(pst_, pst_[:, 128:256], P0, P0[:, 1, :])
                    pkv = R["ph"]["pkv"]
                    self.tr(pkv, pkv[:, 0:64], kf, kf[:, tsl])
                    self.tr(pkv, pkv[:, 64:128], vf, vf[:, tsl])
                    yield
                    self.tt("dve", NTk, NTk[:], pst_, pst_[:, 0:128][:, None, :].to_broadcast([128, 7, 128]), lmask, lmask[:], ALU.mult)
                    self.copy("dve", qkT, qkT[:], pst_, pst_[:, 128:256])
                    self.copy("act", kv, kv[:], pkv, pkv[:, 0:128])
                    self.tt("dve", bw, bw[:], beta, bcol, e3, e3[:, 0:1], ALU.mult)
                    yield
                    self.ts(X, X[:, 0:64], kv, kv[:, 64:128], bcol, None, ALU.mult, extra_reads=[beta])
                    self.ts(X, X[:, 64:128], kv, kv[:, 0:64], bw[:, 0:1], None, ALU.mult, extra_reads=[bw])
                    self.ts(kg, kg[:], kv, kv[:, 0:64], e3[:, 1:2], None, ALU.mult, extra_reads=[e3])
                    yield
                    TT = self.identb2
                    for lev in range(7):
                        psw = R["ph"]["psw"]
                        self.mm(psw, psw[:, 0:128], NTk, NTk[:, lev, :], TT, TT[:, 0, :])
                        self.mm(psw, psw[:, 128:256], TT, TT[:, 0, :], NTk, NTk[:, lev, :])
                        yield
                        self.copy("act", WW, WW[:].rearrange("p a b -> p (a b)"), psw, psw[:, 0:256])
                        yield
                        ps2 = R["ph"]["ps2"]
                        self.mm(ps2, ps2[:, 0:128], TT, TT[:, 1, :], WW, WW[:, 0, :])
                        self.mm(ps2, ps2[:, 128:256], WW, WW[:, 0, :], TT, TT[:, 1, :])
                        yield
                        TTn = R["TT"].next()
                        self.tt("dve", TTn, TTn[:].rearrange("p a b -> p (a b)"), TT, TT[:].rearrange("p a b -> p (a b)"), ps2, ps2[:, 0:256], ALU.add)
                        TT = TTn
                        yield
                    psx = R["ph"]["psx"]
                    self.mm(psx, psx[:, 0:128], TT, TT[:, 1, :], X, X[:])
                    yield
                    self.copy("act", Xf, Xf[:], psx, psx[:, 0:128])
                    yield
                    pw = R["ph"]["pw"]
                    self.tr(pw, pw[0:64, 0:128], Xf, Xf[:, 64:128])
                    yield
                    self.copy("act", wT, wT[:], pw, pw[0:64, 0:128])
                    yield
                    while ch["done"] < pos:
                        yield
                    psv = R["ph"]["psv"]
                    self.mm(psv, psv[:, 0:64], wT, wT[:], Sbb, Sbb[:])
                    self.mm(psv, psv[:, 64:128], qb, qb[:, tsl], Sbb, Sbb[:])
                    yield
                    self.tt("dve", vn, vn[:], Xf, Xf[:, 0:64], psv, psv[:, 0:64], ALU.subtract)
                    yield
                    pso = R["ph"]["pso"]
                    self.mm(pso, pso[:, 0:64], qkT, qkT[:], vn, vn[:])
                    self.ts(t1, t1[:], psv, psv[:, 64:128], e3[:, 0:1], None, ALU.mult, extra_reads=[e3])
                    yield
                    self.tt("dve", oacc, oacc[:, c, :], t1, t1[:], pso, pso[:, 0:64], ALU.add)
                    pss = R["ph"]["pss"]
                    self.mm(pss, pss[0:64, 0:64], kg, kg[:], vn, vn[:])
                    yield
                    self.stt(S, S[:], S, S[:], e3[0:64, 2:3], pss, pss[0:64, 0:64], ALU.mult, ALU.add, extra_reads=[e3])
                    yield
                    self.copy("act", Sbb, Sbb[:], S, S[:])
                    ch["done"] += 1
                    yield
                if pos == cps - 1 and not s.is_sample:
                    self.store(s.st_out, s.st_out[q, l, dr, h], S, S[:])

            P.barrier()
            chains = {}
            ci = 0
            for q in range(s.nseq):
                for dr in range(2):
                    S_, Sb_ = chS[ci]
                    ci += 1
                    if s.is_sample:
                        self.load(S_, S_[:], s.st_in, s.st_in[l, dr, h])
                    else:
                        self.memset(S_, S_[:], 0.0)
                    self.copy("act", Sb_, Sb_[:], S_, S_[:])
                    chains[(dr, q)] = {"S": S_, "Sb": Sb_, "done": 0}
            pending = [(dr, q, pos) for pos in range(cps) for q in range(s.nseq) for dr in range(2)]
            running = []
            free_slots = list(slots)
            while pending or running:
                while pending and free_slots:
                    dr_, q_, pos_ = pending.pop(0)
                    R_ = free_slots.pop(0)
                    running.append((unit(dr_, q_, pos_, R_), R_))
                for item in list(running):
                    gen, R_ = item
                    try:
                        next(gen)
                    except StopIteration:
                        running.remove(item)
                        free_slots.append(R_)
            P.barrier()
            self.tt("pool", osum, osum[:], osum, osum[:], osum2, osum2[:], ALU.add)
            self.tap('osum', osum, osum[:])
            self.tt("dve", sq2, sq2[:], osum, osum[:], osum, osum[:], ALU.mult)
            P.op("dve", lambda e, sq2=sq2, ssq=ssq: e.reduce_sum(out=ssq[:], in_=sq2[:], axis=AX.X), reads=[sq2], writes=[ssq])
            self.act(ssq, ssq[:], ssq, ssq[:], AF.Sqrt, bias=EPS, scale=1.0 / 64.0)
            P.op("dve", lambda e, ssq=ssq: e.reciprocal(out=ssq[:], in_=ssq[:]), reads=[ssq], writes=[ssq])
            self.tt("dve", sq2, sq2[:], osum, osum[:], ssq, ssq[:, :, None].to_broadcast([128, NT, 64]), ALU.mult)
            self.tt("dve", sq2, sq2[:], sq2, sq2[:], norma, norma[:, None, :].to_broadcast([128, NT, 64]), ALU.mult)
            for c in range(NT):
                pz = self.pst()
                self.tr(pz, pz[:, 0:64], zf, zf[:, c * 128:(c + 1) * 128])
                t1 = r_t1.next()
                self.act(t1, t1[:], pz, pz[:, 0:64], AF.Silu)
                self.tt("dve", ytm, ytm[:, c, m0:m1], sq2, sq2[:, c, :], t1, t1[:], ALU.mult)
            if half == 1:
                for c in range(NT):
                    py = self.pst()
                    self.tr(py, py[:, 0:128], ytm, ytm[:, c, :])
                    self.copy("act" if c % 2 else "dve", y, y[:, hp, c * 128:(c + 1) * 128], py, py[:, 0:128])
        if self.dbg and self.dbg.get("tap") == ("ya", l, s.name):
            self.store(self.dbg_out, self.dbg_out[:], y, y[:])

    def phase_out_ffn(self, l, s):
        P = self.P
        d = self.d
        j = s.cidx
        modv = self.modv[l]
        with ExitStack() as ph:
            xbr = Ring([P.sb("fxb%d" % i, [128, 8, 512], F32, ph) for i in range(2 if s.T <= 1024 else 1)])
            sqr = Ring([P.sb("fsq%d" % i, [128, 512], BF16, ph) for i in range(2)])
            tmpr = Ring([P.sb("ftmp%d" % i, [128, 512], F32, ph) for i in range(2)])
            rstd = P.sb("frstd", [128, 512], F32, ph)
            P.barrier()
            for o in range(8):
                w = self.wload8(d["wo"], d["wo"][l, o])
                for blk in range(s.nb):
                    sl = slice(blk * 512, (blk + 1) * 512)
                    ps = self.pst()
                    for kc in range(8):
                        self.mm(ps, ps[:], w, w[:, kc, :], self.merged, self.merged[:, kc, sl], kc == 0, kc == 7)
                    self.copy("act" if blk % 2 else "dve", self.hT, self.hT[:, o, sl], ps, ps[:])
            for blk in range(s.nb):
                sl = slice(blk * 512, (blk + 1) * 512)
                xb = xbr.next()
                self.load(xb, xb[:], s.xres, s.xres[:, :, sl])
                for c in range(8):
                    self.stt(xb, xb[:, c, :], self.hT, self.hT[:, c, sl], modv[:, 16 + c, j:j + 1], xb, xb[:, c, :], ALU.mult, ALU.add, extra_reads=[modv])
                self.store(s.xres, s.xres[:, :, sl], xb, xb[:])
                self.norm_block(l, s, 1, xb, blk, sqr, rstd, tmpr)
            P.barrier()
            MB = min(s.T, 2048)
            nbm = MB // 512
            actb = P.sb("factb", [128, 22, MB], BF16, ph)
            sgr = Ring([P.sb("fsg%d" % i, [128, 512], F32, ph) for i in range(2)])
            w22 = Ring([P.sb("w22_%d" % i, [128, 22, 128], BF16, ph) for i in range(2)])
            for mb in range(s.T // MB):
                for i in range(22):
                    wg = self.wload8(d["wgu"], d["wgu"][l, i])
                    wu = self.wload8(d["wgu"], d["wgu"][l, 22 + i])
                    for b2 in range(nbm):
                        sl = slice(mb * MB + b2 * 512, mb * MB + (b2 + 1) * 512)
                        pg = self.pst()
                        for kc in range(8):
                            self.mm(pg, pg[:], wg, wg[:, kc, :], self.hT, self.hT[:, kc, sl], kc == 0, kc == 7)
                        pu = self.pst()
                        for kc in range(8):
                            self.mm(pu, pu[:], wu, wu[:, kc, :], self.hT, self.hT[:, kc, sl], kc == 0, kc == 7)
                        sg = sgr.next()
                        self.act(sg, sg[:], pg, pg[:], AF.Silu)
                        self.tt("dve", actb, actb[:, i, b2 * 512:(b2 + 1) * 512], sg, sg[:], pu, pu[:], ALU.mult)
                for o in range(8):
                    w = w22.next()
                    self.load(w, w[:], d["wdn"], d["wdn"][l, o], eng="pool")
                    for b2 in range(nbm):
                        sl = slice(mb * MB + b2 * 512, mb * MB + (b2 + 1) * 512)
                        ps = self.pst()
                        for kc in range(22):
                            self.mm(ps, ps[:], w, w[:, kc, :], actb, actb[:, kc, b2 * 512:(b2 + 1) * 512], kc == 0, kc == 21)
                        self.copy("act" if b2 % 2 else "dve", self.merged, self.merged[:, o, sl], ps, ps[:])
            for blk in range(s.nb):
                sl = slice(blk * 512, (blk + 1) * 512)
                xb = xbr.next()
                self.load(xb, xb[:], s.xres, s.xres[:, :, sl])
                for c in range(8):
                    self.stt(xb, xb[:, c, :], self.merged, self.merged[:, c, sl], modv[:, 40 + c, j:j + 1], xb, xb[:, c, :], ALU.mult, ALU.add, extra_reads=[modv])
                self.store(s.xres, s.xres[:, :, sl], xb, xb[:])
        P.barrier()

    def phase_final(self):
        P = self.P
        with ExitStack() as ph:
            xbr = Ring([P.sb("gxb%d" % i, [128, 8, 512], F32, ph) for i in range(2)])
            sqr = Ring([P.sb("gsq%d" % i, [128, 512], BF16, ph) for i in range(2)])
            rstd = P.sb("grstd", [128, 512], F32, ph)
            xn = P.sb("gxn", [128, 8, 512], F32, ph)
            yo = Ring([P.sb("gyo%d" % i, [128, D], F32, ph) for i in range(2)])
            for s in self.streams:
                for blk in range(s.nb):
                    sl = slice(blk * 512, (blk + 1) * 512)
                    xb = xbr.next()
                    self.load(xb, xb[:], s.xres, s.xres[:, :, sl])
                    self.rstd_block(xb, xb[:], sqr, rstd)
                    for c in range(8):
                        self.stt(xn, xn[:, c, :], xb, xb[:, c, :], self.nfT[:, c:c + 1], rstd, rstd[:], ALU.mult, ALU.mult, extra_reads=[self.nfT])
                    for t4 in range(4):
                        yt = yo.next()
                        for half in range(2):
                            ps = self.pst()
                            for c4 in range(4):
                                c = half * 4 + c4
                                self.tr(ps, ps[:, c4 * 128:(c4 + 1) * 128], xn, xn[:, c, t4 * 128:(t4 + 1) * 128])
                            self.copy("act" if half else "dve", yt, yt[:, half * 512:(half + 1) * 512], ps, ps[:])
                        r0 = blk * 512 + t4 * 128
                        self.store(s.y_out, s.y_out[r0:r0 + 128, :], yt, yt[:])
        P.barrier()


N_CORES = 8
_CACHE = {}


def make_streams():
    return [Stream("P", 4, 256, 0, [(0, 4)], False),
            Stream("S", 1, 2048, 1, [(0, 1), (1, 1), (2, 1), (3, 1)], True)]


def shared_inputs(inp, streams, depth):
    f = lambda a: np.ascontiguousarray(np.asarray(a, dtype=np.float32))
    sh = {}
    for s in streams:
        zposT, window = hyena_consts(s.L)
        Fc, Fs, Gc, Gs = dft_consts(s.L)
        CL, SLn, c64, s64 = fnet_consts(s.L)
        sh["zposT_" + s.name] = zposT
        sh["win_" + s.name] = window
        sh["Fc_" + s.name] = Fc
        sh["Fs_" + s.name] = Fs
        sh["Gc_" + s.name] = Gc
        sh["Gs_" + s.name] = Gs
        sh["CL_" + s.name] = CL
        sh["SLn_" + s.name] = SLn
        sh["c64_" + s.name] = c64
        sh["s64_" + s.name] = s64
        if s.is_sample:
            sh["pos_" + s.name] = grid_pos_embed_np(s.T)
    fm = lambda v: np.ascontiguousarray(f(v).reshape(-1, 128).T)
    sh["wmod"] = np.stack([tile_lhsT(f(inp["w_mod"][l])) for l in range(depth)])
    sh["bmodT"] = np.stack([fm(inp["b_mod"][l]) for l in range(depth)])
    sh["n1T"] = np.stack([fm(inp["norm1_g"][l]) for l in range(depth)])
    sh["n2T"] = np.stack([fm(inp["norm2_g"][l]) for l in range(depth)])
    sh["nfT"] = fm(inp["norm_f"])
    cols = win_tile_cols()
    win = np.zeros((depth, N_WIN_TILES, 128, 8, 128), np.float32)
    for l in range(depth):
        w = f(inp["w_in"][l])
        for ti, (c0, wd) in enumerate(cols):
            win[l, ti, :, :, :wd] = w[:, c0:c0 + wd].reshape(8, 128, wd).transpose(1, 0, 2)
    sh["win_t"] = win
    sh["wp_t"] = np.stack([np.stack([tile_lhsT(f(inp[k][l])) for k in ("w_pa", "w_pb", "w_pc")]) for l in range(depth)])
    sh["wo_t"] = np.stack([tile_lhsT(f(inp["w_o"][l])) for l in range(depth)])
    sh["wgu_t"] = np.stack([tile_lhsT(f(inp["w_gu"][l])) for l in range(depth)])
    sh["wdn_t"] = np.stack([tile_lhsT(f(inp["w_down"][l])) for l in range(depth)])
    cq = f(inp["conv_qkv"])[:depth]
    sh["convqT"] = np.ascontiguousarray(cq.reshape(depth, 3, 3, 8, 64).transpose(0, 4, 2, 3, 1))
    chy = f(inp["conv_hy"])[:depth]
    sh["convhT"] = np.ascontiguousarray(chy.reshape(depth, 3, 12, 128).transpose(0, 3, 2, 1))
    sh["alog_bc"] = np.ascontiguousarray(np.broadcast_to(f(inp["a_log"])[:depth].reshape(depth, 1, 16), (depth, 128, 16)))
    sh["dtb_bc"] = np.ascontiguousarray(np.broadcast_to(f(inp["dt_bias"])[:depth].reshape(depth, 1, 16), (depth, 128, 16)))
    sh["norma_bc"] = np.ascontiguousarray(np.broadcast_to(f(inp["norm_a"])[:depth].reshape(depth, 1, 64), (depth, 128, 64)))
    sh["hyw1"] = f(inp["hy_w1"])[:depth]
    sh["hyb1T"] = f(inp["hy_b1"])[:depth].reshape(depth, 64, 1)
    sh["hyfqT"] = f(inp["hy_freq"])[:depth].reshape(depth, 64, 1)
    sh["hyw2"] = f(inp["hy_w2"])[:depth]
    sh["hyb2T"] = f(inp["hy_b2"])[:depth].reshape(depth, 64, 1)
    sh["hyw3"] = f(inp["hy_w3"])[:depth]
    sh["hybiasT"] = np.ascontiguousarray(f(inp["hy_bias"])[:depth].reshape(depth, 2, 4, 128).transpose(0, 3, 1, 2))
    sh["masks"] = mask_consts()
    sh["ident"] = np.eye(128, dtype=np.float32)
    sh["lmask"] = level_masks()
    return sh


def kernel(**inp):
    depth = DEPTH
    streams = make_streams()
    if "nc" not in _CACHE:
        _CACHE["nc"] = Builder(make_streams(), depth=depth).build()
    nc = _CACHE["nc"]
    sh = shared_inputs(inp, streams, depth)
    xp = np.asarray(inp["x_prompt"], np.float32)
    xs = np.asarray(inp["x_sample"], np.float32)
    st = np.asarray(inp["state_delta"], np.float32)
    c = np.asarray(inp["c"], np.float32)
    cctx = np.asarray(inp["c_ctx"], np.float32)
    in_maps = []
    for core in range(N_CORES):
        sidx = core // 4
        m = dict(sh)
        m["x_P"] = np.ascontiguousarray(xp[core * 4:(core + 1) * 4].reshape(1024, D))
        m["x_S"] = np.ascontiguousarray(xs[sidx])
        m["st0_S"] = np.ascontiguousarray(st[sidx][:depth])
        cv = np.stack([cctx, c[sidx]], axis=-1)
        m["cvecT"] = np.ascontiguousarray(cv.reshape(8, 128, 2).transpose(1, 0, 2))
        in_maps.append(m)
    res = run_bass_kernel_spmd(nc, in_maps, core_ids=list(range(N_CORES)))
    r = res.results
    y_prompt = np.concatenate([r[i]["y_P"].reshape(4, 256, D) for i in range(N_CORES)], axis=0).astype(np.float32)
    y_sample = np.stack([r[0]["y_S"], r[4]["y_S"]], axis=0).astype(np.float32)
    new_state = np.concatenate([r[i]["st_P"] for i in range(N_CORES)], axis=0).astype(np.float32)
    return (y_prompt, y_sample, new_state)
```

```python
import math
from contextlib import ExitStack
import numpy as np
import concourse.bass as bass
import concourse.mybir as mybir
from concourse.bass_utils import run_bass_kernel_spmd

F32 = mybir.dt.float32
BF16 = mybir.dt.bfloat16
I32 = mybir.dt.int32
AF = mybir.ActivationFunctionType
ALU = mybir.AluOpType
AX = mybir.AxisListType

SAME_ENG_SYNC = True

D = 1024
DEPTH = 2
H_A = 8
DK = 64
DIN = 7200
DFF = 2816
EPS = 1e-6
CH = 128


class T:
    __slots__ = ("name", "ap", "last_write", "reads", "dsem", "dcount")

    def __init__(self, name, ap):
        self.name = name
        self.ap = ap
        self.last_write = None
        self.reads = []
        self.dsem = None
        self.dcount = 0

    def __getitem__(self, k):
        return self.ap[k]


class TV:
    def __init__(self, base, ap):
        self.base = base
        self.ap = ap
        self.name = base.name

    def __getitem__(self, k):
        return self.ap[k]


class Op:
    __slots__ = ("eng", "fn", "deps", "is_dma", "ndma", "sem_owner", "needed", "sig_sem", "sig_val", "name")

    def __init__(self, eng, fn, name=""):
        self.eng = eng
        self.fn = fn
        self.deps = []
        self.is_dma = False
        self.ndma = 0
        self.sem_owner = None
        self.needed = False
        self.sig_sem = None
        self.sig_val = 0
        self.name = name


ENGS = ("pe", "act", "dve", "pool", "sp")


def _ap(h):
    return h.ap() if callable(getattr(h, "ap", None)) else h


class Prog:
    def __init__(self, nc):
        self.nc = nc
        self.es = ExitStack()
        self.ops = {e: [] for e in ENGS}
        self.all_ops = []
        self.bar_deps = []
        self.bar_id = 0
        self.bar_seen = {e: 0 for e in ENGS}
        self.pending_dma = []
        self.uid = 0
        self.bar_pos = []

    def sb(self, name, shape, dtype, es=None):
        self.uid += 1
        h = (es or self.es).enter_context(self.nc.sbuf_tensor("%s_%d" % (name, self.uid), list(shape), dtype))
        return T(name, _ap(h))

    def ps(self, name, shape, dtype):
        h = self.es.enter_context(self.nc.psum_tensor(name, list(shape), dtype))
        return T(name, _ap(h))

    def tile(self, name, ap):
        return T(name, ap)

    def barrier(self):
        deps = []
        for e in ENGS:
            for o in reversed(self.ops[e]):
                if not o.is_dma:
                    deps.append(o)
                    break
        deps.extend(self.pending_dma)
        self.pending_dma = []
        for d in deps:
            d.needed = True
        self.bar_deps = deps
        self.bar_id += 1
        self.bar_pos.append(len(self.all_ops))

    def _record(self, op, reads, writes):
        reads = [getattr(r, "base", r) for r in reads]
        writes = [getattr(w, "base", w) for w in writes]
        deps = []
        if self.bar_seen[op.eng] != self.bar_id:
            self.bar_seen[op.eng] = self.bar_id
            deps.extend(self.bar_deps)
        for r in reads:
            if r.last_write is not None:
                deps.append(r.last_write)
        for w in writes:
            if w.last_write is not None:
                deps.append(w.last_write)
            deps.extend(w.reads)
        seen = set()
        for d in deps:
            if d is op or id(d) in seen:
                continue
            seen.add(id(d))
            if d.eng == op.eng and not d.is_dma:
                if op.eng == "pe" or not SAME_ENG_SYNC:
                    continue
            op.deps.append(d)
            d.needed = True
        for r in reads:
            r.reads.append(op)
        for w in writes:
            w.last_write = op
            w.reads = []
        self.ops[op.eng].append(op)
        self.all_ops.append(op)
        return op

    def op(self, eng, fn, reads=(), writes=(), name=""):
        return self._record(Op(eng, fn, name), list(reads), list(writes))

    def dma(self, eng, fn, reads=(), writes=(), owner=None, ndma=1, name=""):
        o = Op(eng, fn, name)
        o.is_dma = True
        o.ndma = ndma
        o.sem_owner = owner
        o.needed = True
        self.pending_dma.append(o)
        return self._record(o, list(reads), list(writes))

    def finalize(self):
        nc = self.nc
        es = self.es
        esem = {}
        for e in ("pe", "act", "dve", "pool"):
            esem[e] = es.enter_context(nc.semaphore("s_" + e))
        last_dma = {}
        qtype = {}
        for i, o in enumerate(self.all_ops):
            if o.is_dma:
                k = id(o.sem_owner)
                last_dma[k] = i
                qt = "sw" if o.eng == "pool" else "hw"
                if qtype.get(k, qt) != qt:
                    qtype[k] = "mixed"
                else:
                    qtype[k] = qt
        free = {"sw": [], "hw": [], "mixed": []}
        active = {}
        sem_final = {}
        bpos = list(self.bar_pos)
        bi = 0
        nsem = 0
        for i, o in enumerate(self.all_ops):
            while bi < len(bpos) and bpos[bi] <= i:
                b = bpos[bi]
                bi += 1
                for k in list(active.keys()):
                    ow = active[k]
                    if last_dma[k] < b:
                        if qtype[k] != "mixed":
                            free[qtype[k]].append((ow.dsem, ow.dcount))
                        del active[k]
            if o.is_dma:
                ow = o.sem_owner
                if ow.dsem is None:
                    fl = free[qtype[id(ow)]]
                    if fl and qtype[id(ow)] != "mixed":
                        ow.dsem, ow.dcount = fl.pop()
                    else:
                        ow.dsem = es.enter_context(nc.semaphore("d%d" % nsem))
                        nsem += 1
                    active[id(ow)] = ow
                ow.dcount += 16 * o.ndma
                o.sig_sem = ow.dsem
                o.sig_val = ow.dcount
                sem_final[id(ow.dsem)] = (ow.dsem, ow.dcount)
        dma_final = list(sem_final.values())
        for e in ("pe", "act", "dve", "pool"):
            c = 0
            for o in self.ops[e]:
                if o.is_dma:
                    continue
                if o.needed:
                    c += 1
                    o.sig_sem = esem[e]
                    o.sig_val = c
        self.n_sems = 4 + nsem
        engmap = {"pe": "tensor", "act": "scalar", "dve": "vector", "pool": "gpsimd", "sp": "sync"}
        with nc.Block() as block:
            for e in ENGS:
                ops = self.ops[e]
                final = (e == "sp")

                def body(eh, ops=ops, final=final):
                    seen = {}
                    for o in ops:
                        for d in o.deps:
                            key = id(d.sig_sem)
                            if seen.get(key, 0) >= d.sig_val:
                                continue
                            seen[key] = d.sig_val
                            eh.wait_ge(d.sig_sem, d.sig_val)
                        r = o.fn(eh)
                        if o.is_dma:
                            if not isinstance(r, (list, tuple)):
                                r = [r]
                            assert len(r) == o.ndma, (o.name, len(r), o.ndma)
                            for ins in r:
                                ins.then_inc(o.sig_sem, 16)
                        elif o.needed:
                            r.then_inc(o.sig_sem, 1)
                    if final:
                        for (sm, cnt) in dma_final:
                            eh.wait_ge(sm, cnt)

                getattr(block, engmap[e])(body)
        return self


class Ring:
    def __init__(self, tiles):
        self.tiles = tiles
        self.i = 0

    def next(self):
        t = self.tiles[self.i % len(self.tiles)]
        self.i += 1
        return t


import ml_dtypes
import os as _osx
NPBF = ml_dtypes.bfloat16


def tile_lhsT(w):
    K, N = w.shape
    nt = (N + 127) // 128
    wp = np.zeros((K, nt * 128), np.float32)
    wp[:, :N] = w
    kc = K // 128
    return np.ascontiguousarray(wp.reshape(kc, 128, nt, 128).transpose(2, 1, 0, 3))


def grid_pos_embed_np(n_tokens, grid_w=64):
    rows = n_tokens // grid_w
    r = np.repeat(np.arange(rows), grid_w).astype(np.float32)
    col = np.tile(np.arange(grid_w), rows).astype(np.float32)
    quarter = D // 4
    omega = (1.0 / (10000.0 ** (np.arange(quarter, dtype=np.float32) / quarter))).astype(np.float32)

    def emb(pos):
        a = pos[:, None] * omega[None, :]
        return np.concatenate([np.sin(a), np.cos(a)], axis=-1)

    return np.concatenate([emb(r), emb(col)], axis=-1).astype(np.float32)


def win_tile_cols():
    tiles = []
    for hp in range(4):
        for base in (0, 512, 1024, 1536):
            tiles.append((base + hp * 128, 128))
    tiles.append((2048, 32))
    for i in range(12):
        tiles.append((2080 + i * 128, 128))
    for i in range(4):
        tiles.append((3616 + i * 128, 128))
    for i in range(24):
        tiles.append((4128 + i * 128, 128))
    return tiles


TI_DN = 0
TI_BA = 16
TI_HY = 17
TI_FN = 29
TI_GATE = 33
N_WIN_TILES = 57


def hyena_consts(L):
    bands = 16
    t = np.linspace(0.0, 1.0, L, dtype=np.float32)[:, None]
    wpos = ((2.0 * math.pi / L) * np.arange(L, dtype=np.float32))[:, None].astype(np.float32)
    fr = np.linspace(1e-4, bands - 1, bands, dtype=np.float32)[None, :]
    zpos = np.concatenate([t, np.cos(fr * wpos), -np.sin(fr * wpos)], axis=-1).astype(np.float32)
    deltas = np.abs(np.linspace(math.log(1e-2) / 1.5, math.log(1e-2) / 0.3, 512, dtype=np.float32))
    window = np.exp(-t * deltas[None, :]).astype(np.float32)
    return np.ascontiguousarray(zpos.T), window


def dft_consts(L):
    nfp = (L // 128 + 1) * 128
    s = np.arange(L, dtype=np.float64)[:, None]
    f = np.arange(nfp, dtype=np.float64)[None, :]
    ang = np.pi * np.mod(s * f, 2 * L) / L
    valid = (f <= L)
    Fc = np.where(valid, np.cos(ang), 0.0)
    Fs = np.where(valid, np.sin(ang), 0.0)
    n = 2 * L
    cf = np.where((f == 0) | (f == L), 1.0 / n, 2.0 / n) * valid
    Gc = (Fc * cf).T
    Gs = (-Fs * cf).T
    nt = L // 128
    nf = nt + 1
    tF = lambda a: np.ascontiguousarray(a.reshape(nt, 128, nf, 128).transpose(2, 1, 0, 3)).astype(NPBF)
    tG = lambda a: np.ascontiguousarray(a.reshape(nf, 128, L // 256, 256).transpose(2, 1, 0, 3)).astype(NPBF)
    return (tF(Fc), tF(Fs), tG(Gc), tG(Gs))


def fnet_consts(L):
    t = np.arange(L, dtype=np.float64)
    ang = 2 * np.pi * np.mod(np.outer(t, t), L) / L
    CL = np.cos(ang)
    SLn = -np.sin(ang)
    c = np.arange(64, dtype=np.float64)
    a64 = 2 * np.pi * np.mod(np.outer(c, c), 64) / 64
    sc = 1.0 / math.sqrt(64.0 * L)
    c64 = np.zeros((128, 128))
    s64 = np.zeros((128, 128))
    for g in range(2):
        c64[g * 64:(g + 1) * 64, g * 64:(g + 1) * 64] = np.cos(a64) * sc
        s64[g * 64:(g + 1) * 64, g * 64:(g + 1) * 64] = np.sin(a64) * sc
    nt = L // 128
    tC = lambda a: np.ascontiguousarray(a.reshape(nt, 128, L // 256, 256).transpose(2, 1, 0, 3)).astype(NPBF)
    return tC(CL), tC(SLn), c64.astype(NPBF), s64.astype(NPBF)


def mask_consts():
    i = np.arange(128)[:, None]
    j = np.arange(128)[None, :]
    m = np.stack([(i > j), (i >= j), (i < j), (i <= j)]).astype(np.float32)
    return m


M_SL, M_IL, M_SU, M_IU = 0, 1, 2, 3


def level_masks():
    i = np.arange(128)[:, None]
    j = np.arange(128)[None, :]
    ms = []
    for lv in range(7):
        bs = 2 << lv
        ms.append(((i // bs) == (j // bs)) & ((i // (bs // 2)) != (j // (bs // 2))))
    return np.ascontiguousarray(np.stack(ms, axis=1).astype(np.float32))


class Stream:
    def __init__(self, name, nseq, L, cidx, groups, is_sample):
        self.name = name
        self.nseq = nseq
        self.L = L
        self.T = nseq * L
        self.cidx = cidx
        self.nb = self.T // 512
        self.groups = groups
        self.is_sample = is_sample


class Builder:
    def __init__(self, streams, depth=DEPTH, dbg=None, skip=()):
        self.streams = streams
        self.depth = depth
        self.dbg = dbg
        self.skip = skip
        self.nc = bass.Bass("TRN2", target_bir_lowering=False)
        self.P = Prog(self.nc)
        self.dram = {}

    def din(self, name, shape, dtype=F32):
        ap = self.nc.dram_tensor(name, list(shape), dtype, kind="ExternalInput").ap()
        t = T(name, ap)
        self.dram[name] = (t, list(shape), dtype)
        return t

    def dout(self, name, shape, dtype=F32):
        ap = self.nc.dram_tensor(name, list(shape), dtype, kind="ExternalOutput").ap()
        return T(name, ap)

    def dscr(self, name, shape, dtype=F32):
        ap = self.nc.dram_tensor(name, list(shape), dtype, kind="Internal").ap()
        return T(name, ap)

    def mm(self, out_t, out_ap, lhsT_t, lhsT_ap, rhs_t, rhs_ap, start=True, stop=True):
        self.P.op("pe", lambda e: e.matmul(out_ap, lhsT=lhsT_ap, rhs=rhs_ap, start=start, stop=stop),
                  reads=[lhsT_t, rhs_t], writes=[out_t])

    def tr(self, out_t, out_ap, in_t, in_ap, ident_t=None, ident_ap=None):
        if ident_t is None:
            ident_t = self.ident
            n = in_ap.shape[0]
            ident_ap = self.ident[0:n, 0:n]
        self.P.op("pe", lambda e: e.transpose(out_ap, in_ap, ident_ap), reads=[in_t, ident_t], writes=[out_t])

    def act(self, out_t, out_ap, in_t, in_ap, func, bias=None, scale=None, extra_reads=()):
        kw = {}
        if bias is not None:
            kw["bias"] = bias
        if scale is not None:
            kw["scale"] = scale
        self.P.op("act", lambda e: e.activation(out=out_ap, in_=in_ap, func=func, **kw),
                  reads=[in_t] + list(extra_reads), writes=[out_t])

    def tt(self, eng, out_t, out_ap, a_t, a_ap, b_t, b_ap, op):
        self.P.op(eng, lambda e: e.tensor_tensor(out=out_ap, in0=a_ap, in1=b_ap, op=op),
                  reads=[a_t, b_t], writes=[out_t])

    def ts(self, out_t, out_ap, a_t, a_ap, s1, s2, op0, op1=None, extra_reads=()):
        if op1 is None:
            self.P.op("dve", lambda e: e.tensor_scalar(out=out_ap, in0=a_ap, scalar1=s1, scalar2=None, op0=op0),
                      reads=[a_t] + list(extra_reads), writes=[out_t])
        else:
            self.P.op("dve", lambda e: e.tensor_scalar(out=out_ap, in0=a_ap, scalar1=s1, scalar2=s2, op0=op0, op1=op1),
                      reads=[a_t] + list(extra_reads), writes=[out_t])

    def stt(self, out_t, out_ap, a_t, a_ap, scalar, b_t, b_ap, op0, op1, extra_reads=()):
        self.P.op("dve", lambda e: e.scalar_tensor_tensor(out=out_ap, in0=a_ap, scalar=scalar, in1=b_ap, op0=op0, op1=op1),
                  reads=[a_t, b_t] + list(extra_reads), writes=[out_t])

    def copy(self, eng, out_t, out_ap, in_t, in_ap):
        if eng == "act":
            self.P.op("act", lambda e: e.copy(out=out_ap, in_=in_ap), reads=[in_t], writes=[out_t])
        else:
            self.P.op(eng, lambda e: e.tensor_copy(out=out_ap, in_=in_ap), reads=[in_t], writes=[out_t])

    def memset(self, t, ap, val, eng="dve"):
        self.P.op(eng, lambda e: e.memset(ap, val), writes=[t])

    def load(self, out_t, out_ap, in_t, in_ap, eng="sp"):
        self.P.dma(eng, lambda e: e.dma_start(out=out_ap, in_=in_ap), reads=[in_t], writes=[out_t], owner=out_t)

    def store(self, out_t, out_ap, in_t, in_ap, eng="sp"):
        self.P.dma(eng, lambda e: e.dma_start(out=out_ap, in_=in_ap), reads=[in_t], writes=[out_t], owner=in_t)

    def pst(self):
        return self.psr.next()

    def tap(self, name, t, ap):
        if self.dbg and self.dbg.get("tap2") == name and not getattr(self, "_tapped", False):
            self._tapped = True
            self.store(self.dbg_out, self.dbg_out[:], t, ap)

    def build(self):
        P = self.P
        depth = self.depth
        for s in self.streams:
            nfp = (s.L // 128 + 1) * 128
            s.x_in = self.din("x_" + s.name, [s.T, D])
            s.y_out = self.dout("y_" + s.name, [s.T, D])
            s.xres = self.dscr("xres_" + s.name, [128, 8, s.T])
            if s.is_sample:
                s.st_in = self.din("st0_" + s.name, [depth, 2, H_A, DK, DK])
                s.pos = self.din("pos_" + s.name, [s.T, D])
            else:
                s.st_out = self.dout("st_" + s.name, [s.nseq, depth, 2, H_A, DK, DK])
            s.zposT = self.din("zposT_" + s.name, [33, s.L])
            s.window = self.din("win_" + s.name, [s.L, 512])
            nt_ = s.L // 128
            s.Fc = self.din("Fc_" + s.name, [nt_ + 1, 128, nt_, 128], BF16)
            s.Fs = self.din("Fs_" + s.name, [nt_ + 1, 128, nt_, 128], BF16)
            s.Gc = self.din("Gc_" + s.name, [s.L // 256, 128, nt_ + 1, 256], BF16)
            s.Gs = self.din("Gs_" + s.name, [s.L // 256, 128, nt_ + 1, 256], BF16)
            s.CL = self.din("CL_" + s.name, [s.L // 256, 128, nt_, 256], BF16)
            s.SLn = self.din("SLn_" + s.name, [s.L // 256, 128, nt_, 256], BF16)
            s.c64 = self.din("c64_" + s.name, [128, 128], BF16)
            s.s64 = self.din("s64_" + s.name, [128, 128], BF16)
        d = {}
        d["cvecT"] = self.din("cvecT", [128, 8, 2])
        d["wmod"] = self.din("wmod", [depth, 48, 128, 8, 128])
        d["bmodT"] = self.din("bmodT", [depth, 128, 48])
        d["n1"] = self.din("n1T", [depth, 128, 8])
        d["n2"] = self.din("n2T", [depth, 128, 8])
        d["nf"] = self.din("nfT", [128, 8])
        d["win"] = self.din("win_t", [depth, N_WIN_TILES, 128, 8, 128])
        d["wp"] = self.din("wp_t", [depth, 3, 8, 128, 4, 128])
        d["wo"] = self.din("wo_t", [depth, 8, 128, 8, 128])
        d["wgu"] = self.din("wgu_t", [depth, 44, 128, 8, 128])
        d["wdn"] = self.din("wdn_t", [depth, 8, 128, 22, 128])
        d["convq"] = self.din("convqT", [depth, 64, 3, 8, 3])
        d["convh"] = self.din("convhT", [depth, 128, 12, 3])
        d["alog"] = self.din("alog_bc", [depth, 128, 16])
        d["dtb"] = self.din("dtb_bc", [depth, 128, 16])
        d["norma"] = self.din("norma_bc", [depth, 128, 64])
        d["hyw1"] = self.din("hyw1", [depth, 33, 64])
        d["hyb1"] = self.din("hyb1T", [depth, 64, 1])
        d["hyfq"] = self.din("hyfqT", [depth, 64, 1])
        d["hyw2"] = self.din("hyw2", [depth, 64, 64])
        d["hyb2"] = self.din("hyb2T", [depth, 64, 1])
        d["hyw3"] = self.din("hyw3", [depth, 64, 2048])
        d["hybias"] = self.din("hybiasT", [depth, 128, 2, 4])
        d["masks"] = self.din("masks", [4, 128, 128])
        d["ident"] = self.din("ident", [128, 128])
        d["lmask"] = self.din("lmask", [128, 7, 128])
        self.d = d
        if self.dbg:
            self.dbg_out = self.dout("dbg", self.dbg["shape"], self.dbg.get("dtype", F32))

        self.psr = Ring([P.ps("ps%d" % i, [128, 512], F32) for i in range(8)])
        TMAX = max(s.T for s in self.streams)
        self.hT = P.sb("hT", [128, 8, TMAX], BF16)
        self.merged = P.sb("merged", [128, 8, TMAX], BF16)
        self.w8 = Ring([P.sb("w8_%d" % i, [128, 8, 128], BF16) for i in range(3)])
        self.w4 = Ring([P.sb("w4_%d" % i, [128, 4, 128], BF16) for i in range(2)])
        self.ident = P.sb("ident", [128, 128], F32)
        self.onesb = P.sb("onesb", [128, 128], BF16)
        self.ones32 = P.sb("ones32", [128, 128], F32)
        self.masks = P.sb("masks", [128, 4, 128], F32)
        self.load(self.ident, self.ident[:], d["ident"], d["ident"][:])
        self.memset(self.onesb, self.onesb[:], 1.0 / 1024.0)
        self.identb2 = P.sb("identb2", [128, 2, 128], BF16)
        self.copy("dve", self.identb2, self.identb2[:, 0, :], self.ident, self.ident[:])
        self.copy("dve", self.identb2, self.identb2[:, 1, :], self.ident, self.ident[:])
        self.memset(self.ones32, self.ones32[:], 1.0)
        for i in range(4):
            self.load(self.masks, self.masks[:, i, :], d["masks"], d["masks"][i])
        self.modv = [P.sb("modv%d" % l, [128, 48, 2], F32) for l in range(depth)]
        self.modA = [P.sb("modA%d" % l, [128, 2, 8, 2], F32) for l in range(depth)]
        self.nfT = P.sb("nfT", [128, 8], F32)
        self.load(self.nfT, self.nfT[:], d["nf"], d["nf"][:])

        self.phase_mod()
        self.phase_input()
        for l in range(depth):
            for s in self.streams:
                self.phase_norm1(l, s)
                self.phase_mix(l, s)
                self.phase_out_ffn(l, s)
        self.phase_final()
        if self.dbg:
            self.dbg["fn"](self)
        P.finalize()
        P.es.close()
        return self.nc

    def wload8(self, dram_t, dram_ap):
        w = self.w8.next()
        self.load(w, w[:], dram_t, dram_ap, eng="pool")
        return w

    def phase_mod(self):
        P = self.P
        d = self.d
        with ExitStack() as ph:
            cv = P.sb("cv", [128, 8, 2], F32, ph)
            scv = P.sb("scv", [128, 8, 2], F32, ph)
            wm = Ring([P.sb("wm%d" % i, [128, 8, 128], F32, ph) for i in range(3)])
            bm = P.sb("bm", [128, 48], F32, ph)
            n12 = P.sb("n12", [128, 2, 8], F32, ph)
            self.load(cv, cv[:], d["cvecT"], d["cvecT"][:])
            self.act(scv, scv[:], cv, cv[:], AF.Silu)
            for l in range(self.depth):
                modv = self.modv[l]
                self.load(bm, bm[:], d["bmodT"], d["bmodT"][l])
                self.load(n12, n12[:, 0, :], d["n1"], d["n1"][l])
                self.load(n12, n12[:, 1, :], d["n2"], d["n2"][l])
                for c in range(48):
                    w = wm.next()
                    self.load(w, w[:], d["wmod"], d["wmod"][l, c])
                    ps = self.pst()
                    for kc in range(8):
                        self.mm(ps, ps[:, 0:2], w, w[:, kc, :], scv, scv[:, kc, :], kc == 0, kc == 7)
                    self.ts(modv, modv[:, c, :], ps, ps[:, 0:2], bm[:, c:c + 1], None, ALU.add, extra_reads=[bm])
                mA = self.modA[l]
                for sub in range(2):
                    sc0 = 8 + 24 * sub
                    for j in range(2):
                        self.ts(mA, mA[:, sub, :, j], modv, modv[:, sc0:sc0 + 8, j], 1.0, None, ALU.add)
                        self.tt("dve", mA, mA[:, sub, :, j], mA, mA[:, sub, :, j], n12, n12[:, sub, :], ALU.mult)
        P.barrier()

    def phase_input(self):
        P = self.P
        with ExitStack() as ph:
            xin = Ring([P.sb("xin%d" % i, [128, D], F32, ph) for i in range(2)])
            pin = Ring([P.sb("pin%d" % i, [128, D], F32, ph) for i in range(2)])
            xo = Ring([P.sb("xo%d" % i, [128, 8, 128], F32, ph) for i in range(2)])
            for s in self.streams:
                for tt in range(s.T // 128):
                    xt = xin.next()
                    self.load(xt, xt[:], s.x_in, s.x_in[tt * 128:(tt + 1) * 128, :])
                    if s.is_sample:
                        pt = pin.next()
                        self.load(pt, pt[:], s.pos, s.pos[tt * 128:(tt + 1) * 128, :])
                        self.tt("dve", xt, xt[:], xt, xt[:], pt, pt[:], ALU.add)
                    xot = xo.next()
                    for half in range(2):
                        ps = self.pst()
                        for c4 in range(4):
                            c = half * 4 + c4
                            self.tr(ps, ps[:, c4 * 128:(c4 + 1) * 128], xt, xt[:, c * 128:(c + 1) * 128])
                        self.copy("act" if half else "dve", xot, xot[:, half * 4:half * 4 + 4, :], ps,
                                  ps[:].rearrange("p (c t) -> p c t", c=4))
                    self.store(s.xres, s.xres[:, :, tt * 128:(tt + 1) * 128], xot, xot[:])
        P.barrier()

    def rstd_block(self, xb, xb_ap, sqr, rstd):
        ps = self.pst()
        for c in range(8):
            sq = sqr.next()
            self.act(sq, sq[:], xb, xb_ap[:, c, :], AF.Square)
            self.mm(ps, ps[:], self.onesb, self.onesb[:], sq, sq[:], c == 0, c == 7)
        self.act(rstd, rstd[:], ps, ps[:], AF.Sqrt, bias=EPS, scale=1.0)
        self.P.op("dve", lambda e: e.reciprocal(out=rstd[:], in_=rstd[:]), reads=[rstd], writes=[rstd])

    def norm_block(self, l, s, sub, xb, blk, sqr, rstd, tmpr):
        j = s.cidx
        self.rstd_block(xb, xb[:], sqr, rstd)
        mA = self.modA[l]
        modv = self.modv[l]
        sh0 = 24 * sub
        for c in range(8):
            tmp = tmpr.next()
            self.tt("dve", tmp, tmp[:], xb, xb[:, c, :], rstd, rstd[:], ALU.mult)
            self.act(self.hT, self.hT[:, c, blk * 512:(blk + 1) * 512], tmp, tmp[:], AF.Identity,
                     bias=modv[:, sh0 + c, j:j + 1], scale=mA[:, sub, c, j:j + 1], extra_reads=[mA, modv])

    def phase_norm1(self, l, s):
        P = self.P
        with ExitStack() as ph:
            xbr = Ring([P.sb("xb%d" % i, [128, 8, 512], F32, ph) for i in range(2)])
            sqr = Ring([P.sb("sq%d" % i, [128, 512], BF16, ph) for i in range(2)])
            tmpr = Ring([P.sb("ntmp%d" % i, [128, 512], F32, ph) for i in range(2)])
            rstd = P.sb("rstd", [128, 512], F32, ph)
            for blk in range(s.nb):
                xb = xbr.next()
                self.load(xb, xb[:], s.xres, s.xres[:, :, blk * 512:(blk + 1) * 512])
                self.norm_block(l, s, 0, xb, blk, sqr, rstd, tmpr)
        P.barrier()

    def proj_fm(self, l, ti, s, evac, m0=0, m1=128):
        w = self.wload8(self.d["win"], self.d["win"][l, ti])
        for blk in range(s.nb):
            ps = self.pst()
            for kc in range(8):
                self.mm(ps, ps[0:m1 - m0, :], w, w[:, kc, m0:m1], self.hT, self.hT[:, kc, blk * 512:(blk + 1) * 512], kc == 0, kc == 7)
            evac(ps, blk)

    def merge_branch(self, l, s, br, y_t):
        for o in range(8):
            wg = self.wload8(self.d["win"], self.d["win"][l, TI_GATE + br * 8 + o])
            wp = self.w4.next()
            self.load(wp, wp[:], self.d["wp"], self.d["wp"][l, br, o], eng="pool")
            for blk in range(s.nb):
                sl = slice(blk * 512, (blk + 1) * 512)
                ps1 = self.pst()
                for kc in range(8):
                    self.mm(ps1, ps1[:], wg, wg[:, kc, :], self.hT, self.hT[:, kc, sl], kc == 0, kc == 7)
                ps2 = self.pst()
                for kc in range(4):
                    self.mm(ps2, ps2[:], wp, wp[:, kc, :], y_t, y_t[:, kc, sl], kc == 0, kc == 3)
                sig = self.sigr.next()
                self.act(sig, sig[:], ps1, ps1[:], AF.Sigmoid)
                if br == 0:
                    self.tt("dve", self.merged, self.merged[:, o, sl], sig, sig[:], ps2, ps2[:], ALU.mult)
                else:
                    self.tt("dve", sig, sig[:], sig, sig[:], ps2, ps2[:], ALU.mult)
                    self.tt("pool", self.merged, self.merged[:, o, sl], self.merged, self.merged[:, o, sl], sig, sig[:], ALU.add)

    def phase_mix(self, l, s):
        P = self.P
        with ExitStack() as ph:
            self.sigr = Ring([P.sb("sig%d" % i, [128, 512], F32, ph) for i in range(2)])
            y = P.sb("ybr", [128, 4, s.T], BF16, ph)
            with ExitStack() as ph2:
                if "delta" in self.skip:
                    self.memset(y, y[:], 0.0)
                else:
                    self.mix_delta(l, s, y, ph2)
            P.barrier()
            self.merge_branch(l, s, 0, y)
            P.barrier()
            for (ct0, nct) in s.groups:
                with ExitStack() as ph2:
                    if "hyena" in self.skip:
                        self.memset(y, y[:], 0.0)
                    else:
                        self.mix_hyena(l, s, y, ct0, nct, ph2)
                P.barrier()
            if self.dbg and self.dbg.get("tap") == ("yb", l, s.name):
                self.store(self.dbg_out, self.dbg_out[:], y, y[:])
            self.merge_branch(l, s, 1, y)
            P.barrier()
            for (ct0, nct) in [(0, 4)]:
                with ExitStack() as ph2:
                    self.mix_fnet(l, s, y, ct0, nct, ph2)
                P.barrier()
            if self.dbg and self.dbg.get("tap") == ("yc", l, s.name):
                self.store(self.dbg_out, self.dbg_out[:], y, y[:])
            self.merge_branch(l, s, 2, y)
        P.barrier()

    def mix_fnet(self, l, s, y, ct0, nct, ph):
        P = self.P
        L = s.L
        nt = L // 128
        W = nct * 128
        xc = P.sb("xc", [128, nct, s.T], BF16, ph)
        c64 = P.sb("c64", [128, 128], BF16, ph)
        s64 = P.sb("s64", [128, 128], BF16, ph)
        self.load(c64, c64[:], s.c64, s.c64[:])
        self.load(s64, s64[:], s.s64, s.s64[:])
        for ci in range(nct):
            self.proj_fm(l, TI_FN + ct0 + ci, s,
                         lambda ps, blk, ci=ci: self.copy("act", xc, xc[:, ci, blk * 512:(blk + 1) * 512], ps, ps[:]))
        U = P.sb("U", [128, nt, 2, W], BF16, ph)
        NW = 256
        dftc = Ring([P.sb("dftc%d" % i, [128, nt, NW], BF16, ph) for i in range(2)])
        dfts = Ring([P.sb("dfts%d" % i, [128, nt, NW], BF16, ph) for i in range(2)])
        for q in range(s.nseq):
            t0 = q * L
            for tt in range(nt):
                for cs, mat in ((0, c64), (1, s64)):
                    ps = self.pst()
                    for ci in range(nct):
                        self.mm(ps, ps[:, ci * 128:(ci + 1) * 128], xc, xc[:, ci, t0 + tt * 128:t0 + (tt + 1) * 128], mat, mat[:])
                    self.copy("act" if cs else "dve", U, U[:, tt, cs, :], ps, ps[:, 0:W])
            for nbk in range(L // NW):
                cm = dftc.next()
                sm = dfts.next()
                self.load(cm, cm[:], s.CL, s.CL[nbk])
                self.load(sm, sm[:], s.SLn, s.SLn[nbk])
                for ci in range(nct):
                    ps = self.pst()
                    for tt in range(nt):
                        self.mm(ps, ps[:, 0:NW], U, U[:, tt, 0, ci * 128:(ci + 1) * 128], cm, cm[:, tt, :], tt == 0, False)
                        self.mm(ps, ps[:, 0:NW], U, U[:, tt, 1, ci * 128:(ci + 1) * 128], sm, sm[:, tt, :], False, tt == nt - 1)
                    self.copy("act" if ci % 2 else "dve", y, y[:, ct0 + ci, t0 + nbk * NW:t0 + (nbk + 1) * NW], ps, ps[:, 0:NW])

    def sin_reduce(self, out_t, out_ap, in_t, in_ap, ti, ti_ap, tf, tf_ap):
        P = self.P
        inv = 1.0 / (2.0 * math.pi)
        P.op("dve", lambda e: e.tensor_scalar(out=ti_ap, in0=in_ap, scalar1=inv, scalar2=None, op0=ALU.mult),
             reads=[in_t], writes=[ti])
        self.copy("dve", tf, tf_ap, ti, ti_ap)
        self.stt(tf, tf_ap, tf, tf_ap, -2.0 * math.pi, in_t, in_ap, ALU.mult, ALU.add)
        self.ts(tf, tf_ap, tf, tf_ap, math.pi, -math.pi, ALU.min, ALU.max)
        self.act(out_t, out_ap, tf, tf_ap, AF.Sin)

    def hy_gate(self, l, s, which, ct, raw, dst, dst_ap, cw):
        L = s.L
        self.proj_fm(l, TI_HY + which * 4 + ct, s,
                     lambda ps, blk: self.copy("act", raw, raw[:, blk * 512:(blk + 1) * 512], ps, ps[:]))
        wi = which * 4 + ct
        for q in range(s.nseq):
            a, b = q * L, (q + 1) * L
            self.ts(dst, dst_ap[:, a:b], raw, raw[:, a:b], cw[:, wi, 1:2], None, ALU.mult, extra_reads=[cw])
            self.stt(dst, dst_ap[:, a + 1:b], raw, raw[:, a:b - 1], cw[:, wi, 0:1], dst, dst_ap[:, a + 1:b], ALU.mult, ALU.add, extra_reads=[cw])
            self.stt(dst, dst_ap[:, a:b - 1], raw, raw[:, a + 1:b], cw[:, wi, 2:3], dst, dst_ap[:, a:b - 1], ALU.mult, ALU.add, extra_reads=[cw])

    def mix_hyena(self, l, s, y, ct0, nct, ph):
        P = self.P
        d = self.d
        L = s.L
        nt = L // 128
        nf = nt + 1
        T_ = s.T
        W = nct * 128
        cw = P.sb("hcw", [128, 12, 3], F32, ph)
        self.load(cw, cw[:], d["convh"], d["convh"][l])
        hb = P.sb("hbias", [128, 2, 4], F32, ph)
        self.load(hb, hb[:], d["hybias"], d["hybias"][l])
        raw = P.sb("hraw", [128, T_], F32, ph)
        gate = P.sb("hgate", [128, nct, T_], BF16, ph)
        z = P.sb("hz", [128, nct, T_], BF16, ph)
        for ci in range(nct):
            self.hy_gate(l, s, 2, ct0 + ci, raw, z, z[:, ci, :], cw)
        Hre = P.sb("Hre", [128, 2, nf, W], BF16, ph)
        Him = P.sb("Him", [128, 2, nf, W], BF16, ph)
        ztm = P.sb("ztm", [128, nt, W], BF16, ph)
        Yre = P.sb("Yre", [128, nf, W], BF16, ph)
        Yim = P.sb("Yim", [128, nf, W], BF16, ph)
        fcr = Ring([P.sb("fcr%d" % i, [128, nt, 128], BF16, ph) for i in range(2)])
        fsr = Ring([P.sb("fsr%d" % i, [128, nt, 128], BF16, ph) for i in range(2)])
        tmpz = Ring([P.sb("tmpz%d" % i, [128, max(W, 256)], F32, ph) for i in range(4)])
        zf = P.sb("hzf", [128, 128], F32, ph)

        with ExitStack() as pA:
            hsd = P.sb("hsd", [128, nt, 2, 2, W], BF16, pA)
            with ExitStack() as pf:
                w1 = P.sb("hw1", [33, 64], F32, pf)
                w2 = P.sb("hw2", [64, 64], F32, pf)
                w3 = P.sb("hw3", [64, 2048], F32, pf)
                b1 = P.sb("hb1", [64, 1], F32, pf)
                b2 = P.sb("hb2", [64, 1], F32, pf)
                fq = P.sb("hfq", [64, 1], F32, pf)
                zp = P.sb("hzp", [33, min(L, 512)], F32, pf)
                h1 = P.sb("hh1", [64, min(L, 512)], F32, pf)
                h2 = P.sb("hh2", [64, min(L, 512)], F32, pf)
                ti = P.sb("hti", [64, 512], I32, pf)
                tf = P.sb("htf", [64, 512], F32, pf)
                ta = P.sb("hta", [64, 512], F32, pf)
                win = Ring([P.sb("hwin%d" % i, [128, W], F32, pf) for i in range(2)])
                hf = Ring([P.sb("hf%d" % i, [128, 4, W], F32, pf) for i in range(2)])
                self.load(w1, w1[:], d["hyw1"], d["hyw1"][l])
                self.load(w2, w2[:], d["hyw2"], d["hyw2"][l])
                self.load(w3, w3[:], d["hyw3"], d["hyw3"][l])
                self.load(b1, b1[:], d["hyb1"], d["hyb1"][l])
                self.load(b2, b2[:], d["hyb2"], d["hyb2"][l])
                self.load(fq, fq[:], d["hyfq"], d["hyfq"][l])
                wb = min(L, 512)
                for blk in range(L // wb):
                    sl = slice(blk * wb, (blk + 1) * wb)
                    self.load(zp, zp[:], s.zposT, s.zposT[:, sl])
                    for (src, wsrc, bsrc, dst, kk) in ((zp, w1, b1, h1, 33), (h1, w2, b2, h2, 64)):
                        ps = self.pst()
                        self.mm(ps, ps[0:64, 0:wb], wsrc, wsrc[0:kk, :], src, src[0:kk, :])
                        self.ts(ta, ta[:, 0:wb], ps, ps[0:64, 0:wb], bsrc[:, 0:1], fq[:, 0:1], ALU.add, ALU.mult, extra_reads=[bsrc, fq])
                        self.sin_reduce(dst, dst[:], ta, ta[:, 0:wb], ti, ti[:, 0:wb], tf, tf[:, 0:wb])
                    for t4 in range(wb // 128):
                        tt = blk * (wb // 128) + t4
                        wn = win.next()
                        self.load(wn, wn[:], s.window, s.window[tt * 128:(tt + 1) * 128, ct0 * 128:ct0 * 128 + W])
                        h = hf.next()
                        for fi in range(4):
                            ps = self.pst()
                            c0 = fi * 512 + ct0 * 128
                            self.mm(ps, ps[:, 0:W], h2, h2[:, t4 * 128:(t4 + 1) * 128], w3, w3[:, c0:c0 + W])
                            self.tt("dve", h, h[:, fi, :], ps, ps[:, 0:W], wn, wn[:], ALU.mult)
                        for o in range(2):
                            self.tt("dve", hsd, hsd[:, tt, o, 0, :], h, h[:, 2 * o, :], h, h[:, 2 * o + 1, :], ALU.add)
                            self.tt("pool", hsd, hsd[:, tt, o, 1, :], h, h[:, 2 * o + 1, :], h, h[:, 2 * o, :], ALU.subtract)
            P.barrier()
            def ld_f(ft):
                fcm = fcr.next()
                fsm = fsr.next()
                self.load(fcm, fcm[:], s.Fc, s.Fc[ft])
                self.load(fsm, fsm[:], s.Fs, s.Fs[ft])
                return fcm, fsm

            def build_ztm(q):
                t0 = q * L
                for tt in range(nt):
                    ps = self.pst()
                    for ci in range(nct):
                        self.copy("dve", zf, zf[:], z, z[:, ci, t0 + tt * 128:t0 + (tt + 1) * 128])
                        self.tr(ps, ps[:, ci * 128:(ci + 1) * 128], zf, zf[:])
                    self.copy("act" if tt % 2 else "dve", ztm, ztm[:, tt, :], ps, ps[:, 0:W])

            def fwd_product(ft, fcm, fsm, o):
                pc = self.pst()
                for tt in range(nt):
                    self.mm(pc, pc[:, 0:W], fcm, fcm[:, tt, :], ztm, ztm[:, tt, :], tt == 0, tt == nt - 1)
                pz = self.pst()
                for tt in range(nt):
                    self.mm(pz, pz[:, 0:W], fsm, fsm[:, tt, :], ztm, ztm[:, tt, :], tt == 0, tt == nt - 1)
                a1 = tmpz.next(); a2 = tmpz.next(); a3 = tmpz.next(); a4 = tmpz.next()
                self.tt("dve", a1, a1[:, 0:W], pc, pc[:, 0:W], Hre, Hre[:, o, ft, :], ALU.mult)
                self.tt("dve", a2, a2[:, 0:W], pz, pz[:, 0:W], Him, Him[:, o, ft, :], ALU.mult)
                self.tt("pool", Yre, Yre[:, ft, :], a1, a1[:, 0:W], a2, a2[:, 0:W], ALU.add)
                self.tt("dve", a3, a3[:, 0:W], pc, pc[:, 0:W], Him, Him[:, o, ft, :], ALU.mult)
                self.tt("dve", a4, a4[:, 0:W], pz, pz[:, 0:W], Hre, Hre[:, o, ft, :], ALU.mult)
                self.tt("pool", Yim, Yim[:, ft, :], a3, a3[:, 0:W], a4, a4[:, 0:W], ALU.subtract)

            fuse = (s.nseq == 1)
            if fuse:
                build_ztm(0)
            for ft in range(nf):
                fcm, fsm = ld_f(ft)
                for o in range(2):
                    for (mat, sd, dst) in ((fcm, 0, Hre), (fsm, 1, Him)):
                        ps = self.pst()
                        for tt in range(nt):
                            self.mm(ps, ps[:, 0:W], mat, mat[:, tt, :], hsd, hsd[:, tt, o, sd, :], tt == 0, tt == nt - 1)
                        self.copy("act" if sd else "dve", dst, dst[:, o, ft, :], ps, ps[:, 0:W])
                if fuse:
                    fwd_product(ft, fcm, fsm, 0)
        P.barrier()
        NW = 256
        gcr = Ring([P.sb("gcr%d" % i, [128, nf, NW], BF16, ph) for i in range(2)])
        gsr = Ring([P.sb("gsr%d" % i, [128, nf, NW], BF16, ph) for i in range(2)])
        for o in range(2):
            for ci in range(nct):
                self.hy_gate(l, s, o, ct0 + ci, raw, gate, gate[:, ci, :], cw)
            for q in range(s.nseq):
                t0 = q * L
                if not (fuse and o == 0):
                    build_ztm(q)
                    for ft in range(nf):
                        fcm, fsm = ld_f(ft)
                        fwd_product(ft, fcm, fsm, o)
                for nbk in range(L // NW):
                    gc = gcr.next()
                    gs = gsr.next()
                    self.load(gc, gc[:], s.Gc, s.Gc[nbk])
                    self.load(gs, gs[:], s.Gs, s.Gs[nbk])
                    sl = slice(t0 + nbk * NW, t0 + (nbk + 1) * NW)
                    for ci in range(nct):
                        ps = self.pst()
                        for ft in range(nf):
                            self.mm(ps, ps[:, 0:NW], Yre, Yre[:, ft, ci * 128:(ci + 1) * 128], gc, gc[:, ft, :], ft == 0, False)
                            self.mm(ps, ps[:, 0:NW], Yim, Yim[:, ft, ci * 128:(ci + 1) * 128], gs, gs[:, ft, :], False, ft == nf - 1)
                        a1 = tmpz.next()
                        self.stt(a1, a1[:, 0:NW], z, z[:, ci, sl], hb[:, o, ct0 + ci:ct0 + ci + 1], ps, ps[:, 0:NW], ALU.mult, ALU.add, extra_reads=[hb])
                        if o == 0:
                            self.tt("dve", z, z[:, ci, sl], a1, a1[:, 0:NW], gate, gate[:, ci, sl], ALU.mult)
                        else:
                            self.tt("dve", y, y[:, ct0 + ci, sl], a1, a1[:, 0:NW], gate, gate[:, ci, sl], ALU.mult)

    def mix_delta(self, l, s, y, ph):
        P = self.P
        d = self.d
        L = s.L
        T_ = s.T
        NT = T_ // 128
        cps = L // 128
        cwq = P.sb("dcw", [64, 3, 8, 3], F32, ph)
        self.load(cwq, cwq[:], d["convq"], d["convq"][l])
        alog = P.sb("dalog", [128, 16], F32, ph)
        dtb = P.sb("ddtb", [128, 16], F32, ph)
        norma = P.sb("dnorma", [128, 64], F32, ph)
        self.load(alog, alog[:], d["alog"], d["alog"][l])
        self.load(dtb, dtb[:], d["dtb"], d["dtb"][l])
        self.load(norma, norma[:], d["norma"], d["norma"][l])
        ba = P.sb("dba", [128, NT, 32], F32, ph)
        beta = P.sb("dbeta", [128, NT, 16], F32, ph)
        nbeta = P.sb("dnbeta", [128, NT, 16], F32, ph)
        g = P.sb("dg", [128, NT, 16], F32, ph)
        wba = self.wload8(d["win"], d["win"][l, TI_BA])
        for tt in range(NT):
            ps = self.pst()
            for kc in range(8):
                self.mm(ps, ps[:, 0:32], self.hT, self.hT[:, kc, tt * 128:(tt + 1) * 128], wba, wba[:, kc, 0:32], kc == 0, kc == 7)
            self.copy("act" if tt % 2 else "dve", ba, ba[:, tt, :], ps, ps[:, 0:32])
        self.act(beta, beta[:], ba, ba[:, :, 0:16], AF.Sigmoid)
        self.ts(nbeta, nbeta[:], beta, beta[:], -1.0, None, ALU.mult)
        self.tt("dve", g, g[:], ba, ba[:, :, 16:32], dtb, dtb[:, None, :].to_broadcast([128, NT, 16]), ALU.add)
        self.act(g, g[:], g, g[:], AF.Exp)
        self.act(g, g[:], g, g[:], AF.Ln, bias=1.0, scale=1.0)
        self.act(alog, alog[:], alog, alog[:], AF.Exp)
        self.stt(g, g[:], g, g[:], -1.0, alog, alog[:, None, :].to_broadcast([128, NT, 16]), ALU.mult, ALU.mult)

        self.tap('g', g, g[:])
        self.tap('beta', beta, beta[:])
        qf = P.sb("dq", [64, T_], F32, ph)
        kf = P.sb("dk", [64, T_], F32, ph)
        vf = P.sb("dv", [64, T_], F32, ph)
        zf = P.sb("dz", [64, T_], F32, ph)
        qb = P.sb("dqb", [64, T_], BF16, ph)
        kb = P.sb("dkb", [64, T_], BF16, ph)
        osum = P.sb("dosum", [128, NT, 64], F32, ph)
        ytm = P.sb("dytm", [128, NT, 128], F32, ph)
        KSLOT = int(_osx.environ.get("KSLOT", "4"))
        lmask = P.sb("dlmask", [128, 7, 128], F32, ph)
        self.load(lmask, lmask[:], d["lmask"], d["lmask"][:])
        osum2 = P.sb("dosum2", [128, NT, 64], F32, ph)
        r_t1 = Ring([P.sb("dt1%d" % i, [128, 64], F32, ph) for i in range(2)])

        def mkslot(i):
            R = {}
            def a(name, shape, dt):
                R[name] = P.sb("d%s_%d" % (name, i), shape, dt, ph)
            a("S", [64, 64], F32); a("Sb", [64, 64], BF16)
            a("gbc", [128, 128], F32); a("dcol", [128, 4], F32); a("e3", [128, 4], F32)
            a("dabs", [128, 128], F32); a("Dm", [128, 128], F32); a("Ds", [128, 128], F32); a("Di", [128, 128], F32)
            a("P0", [128, 2, 128], F32); a("NTk", [128, 7, 128], BF16); a("qkT", [128, 128], BF16)
            a("kv", [128, 128], F32); a("X", [128, 128], BF16); a("Xf", [128, 128], F32)
            a("kg", [128, 64], BF16); a("wT", [64, 128], BF16); a("vn", [128, 64], BF16); a("t1", [128, 64], F32)
            a("bw", [128, 1], F32)
            nbk = 8 // KSLOT
            bk = self.psr.tiles[nbk * i:nbk * (i + 1)]
            names = ["psd", "psk", "pkv", "pst_", "psw", "ps2", "psx", "pw", "psv", "pso", "pss"]
            if nbk >= 4:
                amap = {"psd": 0, "psk": 1, "pkv": 2, "pst_": 3, "psw": 0, "ps2": 2, "psx": 1, "pw": 3, "psv": 0, "pso": 1, "pss": 2}
            else:
                amap = {"psd": 0, "psk": 1, "pkv": 0, "pst_": 1, "psw": 0, "ps2": 1, "psx": 0, "pw": 1, "psv": 0, "pso": 1, "pss": 0}
            R["ph"] = {n: bk[amap[n] % nbk] for n in names}
            R["TT"] = Ring([P.sb("dTT%d_%d" % (j, i), [128, 2, 128], BF16, ph) for j in range(2)])
            a("WW", [128, 2, 128], BF16)
            return R
        slots = [mkslot(i) for i in range(KSLOT - 1)]
        chS = [(P.sb("dchS%d" % i, [64, 64], F32, ph), P.sb("dchSb%d" % i, [64, 64], BF16, ph)) for i in range(2 * s.nseq)]
        masks = self.masks
        sq2 = P.sb("dsq2", [128, NT, 64], F32, ph)
        ssq = P.sb("dssq", [128, NT], F32, ph)
        with ExitStack() as tmp_es:
            raw = P.sb("draw", [64, T_], F32, tmp_es)
            sqt = P.sb("dsq", [64, 512], BF16, tmp_es)
            rn = P.sb("drn", [64, 512], F32, tmp_es)
        slots.append(mkslot(KSLOT - 1))

        import os as _os
        _NH = int(_os.environ.get('DN_HEADS', '8'))
        _ST = int(_os.environ.get('DN_STAGE', '9'))
        for h in range(_NH):
            hp, half = h // 2, h % 2
            m0, m1 = half * 64, half * 64 + 64
            for which, dst in ((0, qf), (1, kf), (2, vf)):
                self.proj_fm(l, TI_DN + hp * 4 + which, s,
                             lambda ps, blk: self.copy("act", raw, raw[:, blk * 512:(blk + 1) * 512], ps, ps[0:64, :]), m0, m1)
                for q in range(s.nseq):
                    a, b = q * L, (q + 1) * L
                    self.ts(dst, dst[:, a:b], raw, raw[:, a:b], cwq[:, which, h, 1:2], None, ALU.mult, extra_reads=[cwq])
                    self.stt(dst, dst[:, a + 1:b], raw, raw[:, a:b - 1], cwq[:, which, h, 0:1], dst, dst[:, a + 1:b], ALU.mult, ALU.add, extra_reads=[cwq])
                    self.stt(dst, dst[:, a:b - 1], raw, raw[:, a + 1:b], cwq[:, which, h, 2:3], dst, dst[:, a:b - 1], ALU.mult, ALU.add, extra_reads=[cwq])
                self.act(dst, dst[:], dst, dst[:], AF.Silu)
            self.proj_fm(l, TI_DN + hp * 4 + 3, s,
                         lambda ps, blk: self.copy("act", zf, zf[:, blk * 512:(blk + 1) * 512], ps, ps[0:64, :]), m0, m1)
            for (x, xb_, sc) in ((qf, qb, 64.0), (kf, kb, 1.0)):
                for blk in range(s.nb):
                    sl = slice(blk * 512, (blk + 1) * 512)
                    self.tt("dve", sqt, sqt[:], x, x[:, sl], x, x[:, sl], ALU.mult)
                    ps = self.pst()
                    self.mm(ps, ps[0:64, :], self.onesb, self.onesb[0:64, 0:64], sqt, sqt[:])
                    self.act(rn, rn[:], ps, ps[0:64, :], AF.Sqrt, bias=EPS * sc, scale=sc * 1024.0)
                    P.op("dve", lambda e: e.reciprocal(out=rn[:], in_=rn[:]), reads=[rn], writes=[rn])
                    self.tt("dve", x, x[:, sl], x, x[:, sl], rn, rn[:], ALU.mult)
                self.copy("act", xb_, xb_[:], x, x[:])
            self.tap('q', qf, qf[:])
            self.tap('k', kf, kf[:])
            self.tap('v', vf, vf[:])
            def unit(dr, q, pos, R):
                col = dr * 8 + h
                if dr == 0:
                    cm, rm, sm, im = M_IU, M_SL, M_SL, M_IL
                else:
                    cm, rm, sm, im = M_IL, M_SU, M_SU, M_IU
                ch = chains[(dr, q)]
                S, Sbb = ch["S"], ch["Sb"]
                oacc = osum if dr == 0 else osum2
                cl = pos if dr == 0 else cps - 1 - pos
                if True:
                    c = q * cps + cl
                    tsl = slice(c * 128, (c + 1) * 128)
                    gcol = g[:, c, col:col + 1]
                    bcol = beta[:, c, col:col + 1]
                    nbcol = nbeta[:, c, col:col + 1]
                    gbc, dcol, e3, dabs, Dm, Ds, Di = R["gbc"], R["dcol"], R["e3"], R["dabs"], R["Dm"], R["Ds"], R["Di"]
                    P0, NTk, qkT, kv, X, Xf = R["P0"], R["NTk"], R["qkT"], R["kv"], R["X"], R["Xf"]
                    kg, wT, vn, t1, bw, WW = R["kg"], R["wT"], R["vn"], R["t1"], R["bw"], R["WW"]
                    self.copy("pool", gbc, gbc[:], g, gcol.to_broadcast([128, 128]))
                    yield
                    psd = R["ph"]["psd"]
                    self.mm(psd, psd[:, 0:128], gbc, gbc[:], masks, masks[:, cm, :])
                    self.mm(psd, psd[:, 128:129], masks, masks[:, cm, :], g, gcol)
                    self.mm(psd, psd[:, 129:130], masks, masks[:, rm, :], g, gcol)
                    self.mm(psd, psd[:, 130:131], self.ones32, self.ones32[:], g, gcol)
                    yield
                    self.copy("dve", dcol, dcol[:, 0:3], psd, psd[:, 128:131])
                    yield
                    self.act(e3, e3[:, 0:3], dcol, dcol[:, 0:3], AF.Exp)
                    self.ts(dabs, dabs[:], psd, psd[:, 0:128], dcol[:, 0:1], 0.0, ALU.subtract, ALU.max, extra_reads=[dcol])
                    yield
                    self.act(Dm, Dm[:], dabs, dabs[:], AF.Exp, scale=-1.0)
                    yield
                    self.tt("pool", Ds, Ds[:], Dm, Dm[:], masks, masks[:, sm, :], ALU.mult)
                    self.tt("pool", Di, Di[:], Dm, Dm[:], masks, masks[:, im, :], ALU.mult)
                    psk = R["ph"]["psk"]
                    self.mm(psk, psk[:, 0:128], kb, kb[:, tsl], kb, kb[:, tsl])
                    self.mm(psk, psk[:, 128:256], qb, qb[:, tsl], kb, kb[:, tsl])
                    yield
                    self.stt(P0, P0[:, 0, :], psk, psk[:, 0:128], nbcol, Ds, Ds[:], ALU.mult, ALU.mult, extra_reads=[nbeta])
                    self.tt("dve", P0, P0[:, 1, :], psk, psk[:, 128:256], Di, Di[:], ALU.mult)
                    yield
                    pst_ = R["ph"]["pst_"]
                    self.tr(pst_, pst_[:, 0:128], P0, P0[:, 0, :])
                    self.tr(pst_, pst_[:, 128:256], P0, P0[:, 1, :])
                    pkv = R["ph"]["pkv"]
                    self.tr(pkv, pkv[:, 0:64], kf, kf[:, tsl])
                    self.tr(pkv, pkv[:, 64:128], vf, vf[:, tsl])
                    yield
                    self.tt("dve", NTk, NTk[:], pst_, pst_[:, 0:128][:, None, :].to_broadcast([128, 7, 128]), lmask, lmask[:], ALU.mult)
                    self.copy("dve", qkT, qkT[:], pst_, pst_[:, 128:256])
                    self.copy("act", kv, kv[:], pkv, pkv[:, 0:128])
                    self.tt("dve", bw, bw[:], beta, bcol, e3, e3[:, 0:1], ALU.mult)
                    yield
                    self.ts(X, X[:, 0:64], kv, kv[:, 64:128], bcol, None, ALU.mult, extra_reads=[beta])
                    self.ts(X, X[:, 64:128], kv, kv[:, 0:64], bw[:, 0:1], None, ALU.mult, extra_reads=[bw])
                    self.ts(kg, kg[:], kv, kv[:, 0:64], e3[:, 1:2], None, ALU.mult, extra_reads=[e3])
                    yield
                    TT = self.identb2
                    for lev in range(7):
                        psw = R["ph"]["psw"]
                        self.mm(psw, psw[:, 0:128], NTk, NTk[:, lev, :], TT, TT[:, 0, :])
                        self.mm(psw, psw[:, 128:256], TT, TT[:, 0, :], NTk, NTk[:, lev, :])
                        yield
                        self.copy("act", WW, WW[:].rearrange("p a b -> p (a b)"), psw, psw[:, 0:256])
                        yield
                        ps2 = R["ph"]["ps2"]
                        self.mm(ps2, ps2[:, 0:128], TT, TT[:, 1, :], WW, WW[:, 0, :])
                        self.mm(ps2, ps2[:, 128:256], WW, WW[:, 0, :], TT, TT[:, 1, :])
                        yield
                        TTn = R["TT"].next()
                        self.tt("dve", TTn, TTn[:].rearrange("p a b -> p (a b)"), TT, TT[:].rearrange("p a b -> p (a b)"), ps2, ps2[:, 0:256], ALU.add)
                        TT = TTn
                        yield
                    psx = R["ph"]["psx"]
                    self.mm(psx, psx[:, 0:128], TT, TT[:, 1, :], X, X[:])
                    yield
                    self.copy("act", Xf, Xf[:], psx, psx[:, 0:128])
                    yield
                    pw = R["ph"]["pw"]
                    self.tr(pw, pw[0:64, 0:128], Xf, Xf[:, 64:128])
                    yield
                    self.copy("act", wT, wT[:], pw, pw[0:64, 0:128])
                    yield
                    while ch["done"] < pos:
                        yield
                    psv = R["ph"]["psv"]
                    self.mm(psv, psv[:, 0:64], wT, wT[:], Sbb, Sbb[:])
                    self.mm(psv, psv[:, 64:128], qb, qb[:, tsl], Sbb, Sbb[:])
                    yield
                    self.tt("dve", vn, vn[:], Xf, Xf[:, 0:64], psv, psv[:, 0:64], ALU.subtract)
                    yield
                    pso = R["ph"]["pso"]
                    self.mm(pso, pso[:, 0:64], qkT, qkT[:], vn, vn[:])
                    self.ts(t1, t1[:], psv, psv[:, 64:128], e3[:, 0:1], None, ALU.mult, extra_reads=[e3])
                    yield
                    self.tt("dve", oacc, oacc[:, c, :], t1, t1[:], pso, pso[:, 0:64], ALU.add)
                    pss = R["ph"]["pss"]
                    self.mm(pss, pss[0:64, 0:64], kg, kg[:], vn, vn[:])
                    yield
                    self.stt(S, S[:], S, S[:], e3[0:64, 2:3], pss, pss[0:64, 0:64], ALU.mult, ALU.add, extra_reads=[e3])
                    yield
                    self.copy("act", Sbb, Sbb[:], S, S[:])
                    ch["done"] += 1
                    yield
                if pos == cps - 1 and not s.is_sample:
                    self.store(s.st_out, s.st_out[q, l, dr, h], S, S[:])

            P.barrier()
            chains = {}
            ci = 0
            for q in range(s.nseq):
                for dr in range(2):
                    S_, Sb_ = chS[ci]
                    ci += 1
                    if s.is_sample:
                        self.load(S_, S_[:], s.st_in, s.st_in[l, dr, h])
                    else:
                        self.memset(S_, S_[:], 0.0)
                    self.copy("act", Sb_, Sb_[:], S_, S_[:])
                    chains[(dr, q)] = {"S": S_, "Sb": Sb_, "done": 0}
            pending = [(dr, q, pos) for pos in range(cps) for q in range(s.nseq) for dr in range(2)]
            running = []
            free_slots = list(slots)
            while pending or running:
                while pending and free_slots:
                    dr_, q_, pos_ = pending.pop(0)
                    R_ = free_slots.pop(0)
                    running.append((unit(dr_, q_, pos_, R_), R_))
                for item in list(running):
                    gen, R_ = item
                    try:
                        next(gen)
                    except StopIteration:
                        running.remove(item)
                        free_slots.append(R_)
            P.barrier()
            self.tt("pool", osum, osum[:], osum, osum[:], osum2, osum2[:], ALU.add)
            self.tap('osum', osum, osum[:])
            self.tt("dve", sq2, sq2[:], osum, osum[:], osum, osum[:], ALU.mult)
            P.op("dve", lambda e, sq2=sq2, ssq=ssq: e.reduce_sum(out=ssq[:], in_=sq2[:], axis=AX.X), reads=[sq2], writes=[ssq])
            self.act(ssq, ssq[:], ssq, ssq[:], AF.Sqrt, bias=EPS, scale=1.0 / 64.0)
            P.op("dve", lambda e, ssq=ssq: e.reciprocal(out=ssq[:], in_=ssq[:]), reads=[ssq], writes=[ssq])
            self.tt("dve", sq2, sq2[:], osum, osum[:], ssq, ssq[:, :, None].to_broadcast([128, NT, 64]), ALU.mult)
            self.tt("dve", sq2, sq2[:], sq2, sq2[:], norma, norma[:, None, :].to_broadcast([128, NT, 64]), ALU.mult)
            for c in range(NT):
                pz = self.pst()
                self.tr(pz, pz[:, 0:64], zf, zf[:, c * 128:(c + 1) * 128])
                t1 = r_t1.next()
                self.act(t1, t1[:], pz, pz[:, 0:64], AF.Silu)
                self.tt("dve", ytm, ytm[:, c, m0:m1], sq2, sq2[:, c, :], t1, t1[:], ALU.mult)
            if half == 1:
                for c in range(NT):
                    py = self.pst()
                    self.tr(py, py[:, 0:128], ytm, ytm[:, c, :])
                    self.copy("act" if c % 2 else "dve", y, y[:, hp, c * 128:(c + 1) * 128], py, py[:, 0:128])
        if self.dbg and self.dbg.get("tap") == ("ya", l, s.name):
            self.store(self.dbg_out, self.dbg_out[:], y, y[:])

    def phase_out_ffn(self, l, s):
        P = self.P
        d = self.d
        j = s.cidx
        modv = self.modv[l]
        with ExitStack() as ph:
            xbr = Ring([P.sb("fxb%d" % i, [128, 8, 512], F32, ph) for i in range(2 if s.T <= 1024 else 1)])
            sqr = Ring([P.sb("fsq%d" % i, [128, 512], BF16, ph) for i in range(2)])
            tmpr = Ring([P.sb("ftmp%d" % i, [128, 512], F32, ph) for i in range(2)])
            rstd = P.sb("frstd", [128, 512], F32, ph)
            P.barrier()
            for o in range(8):
                w = self.wload8(d["wo"], d["wo"][l, o])
                for blk in range(s.nb):
                    sl = slice(blk * 512, (blk + 1) * 512)
                    ps = self.pst()
                    for kc in range(8):
                        self.mm(ps, ps[:], w, w[:, kc, :], self.merged, self.merged[:, kc, sl], kc == 0, kc == 7)
                    self.copy("act" if blk % 2 else "dve", self.hT, self.hT[:, o, sl], ps, ps[:])
            for blk in range(s.nb):
                sl = slice(blk * 512, (blk + 1) * 512)
                xb = xbr.next()
                self.load(xb, xb[:], s.xres, s.xres[:, :, sl])
                for c in range(8):
                    self.stt(xb, xb[:, c, :], self.hT, self.hT[:, c, sl], modv[:, 16 + c, j:j + 1], xb, xb[:, c, :], ALU.mult, ALU.add, extra_reads=[modv])
                self.store(s.xres, s.xres[:, :, sl], xb, xb[:])
                self.norm_block(l, s, 1, xb, blk, sqr, rstd, tmpr)
            P.barrier()
            MB = min(s.T, 2048)
            nbm = MB // 512
            actb = P.sb("factb", [128, 22, MB], BF16, ph)
            sgr = Ring([P.sb("fsg%d" % i, [128, 512], F32, ph) for i in range(2)])
            w22 = Ring([P.sb("w22_%d" % i, [128, 22, 128], BF16, ph) for i in range(2)])
            for mb in range(s.T // MB):
                for i in range(22):
                    wg = self.wload8(d["wgu"], d["wgu"][l, i])
                    wu = self.wload8(d["wgu"], d["wgu"][l, 22 + i])
                    for b2 in range(nbm):
                        sl = slice(mb * MB + b2 * 512, mb * MB + (b2 + 1) * 512)
                        pg = self.pst()
                        for kc in range(8):
                            self.mm(pg, pg[:], wg, wg[:, kc, :], self.hT, self.hT[:, kc, sl], kc == 0, kc == 7)
                        pu = self.pst()
                        for kc in range(8):
                            self.mm(pu, pu[:], wu, wu[:, kc, :], self.hT, self.hT[:, kc, sl], kc == 0, kc == 7)
                        sg = sgr.next()
                        self.act(sg, sg[:], pg, pg[:], AF.Silu)
                        self.tt("dve", actb, actb[:, i, b2 * 512:(b2 + 1) * 512], sg, sg[:], pu, pu[:], ALU.mult)
                for o in range(8):
                    w = w22.next()
                    self.load(w, w[:], d["wdn"], d["wdn"][l, o], eng="pool")
                    for b2 in range(nbm):
                        sl = slice(mb * MB + b2 * 512, mb * MB + (b2 + 1) * 512)
                        ps = self.pst()
                        for kc in range(22):
                            self.mm(ps, ps[:], w, w[:, kc, :], actb, actb[:, kc, b2 * 512:(b2 + 1) * 512], kc == 0, kc == 21)
                        self.copy("act" if b2 % 2 else "dve", self.merged, self.merged[:, o, sl], ps, ps[:])
            for blk in range(s.nb):
                sl = slice(blk * 512, (blk + 1) * 512)
                xb = xbr.next()
                self.load(xb, xb[:], s.xres, s.xres[:, :, sl])
                for c in range(8):
                    self.stt(xb, xb[:, c, :], self.merged, self.merged[:, c, sl], modv[:, 40 + c, j:j + 1], xb, xb[:, c, :], ALU.mult, ALU.add, extra_reads=[modv])
                self.store(s.xres, s.xres[:, :, sl], xb, xb[:])
        P.barrier()

    def phase_final(self):
        P = self.P
        with ExitStack() as ph:
            xbr = Ring([P.sb("gxb%d" % i, [128, 8, 512], F32, ph) for i in range(2)])
            sqr = Ring([P.sb("gsq%d" % i, [128, 512], BF16, ph) for i in range(2)])
            rstd = P.sb("grstd", [128, 512], F32, ph)
            xn = P.sb("gxn", [128, 8, 512], F32, ph)
            yo = Ring([P.sb("gyo%d" % i, [128, D], F32, ph) for i in range(2)])
            for s in self.streams:
                for blk in range(s.nb):
                    sl = slice(blk * 512, (blk + 1) * 512)
                    xb = xbr.next()
                    self.load(xb, xb[:], s.xres, s.xres[:, :, sl])
                    self.rstd_block(xb, xb[:], sqr, rstd)
                    for c in range(8):
                        self.stt(xn, xn[:, c, :], xb, xb[:, c, :], self.nfT[:, c:c + 1], rstd, rstd[:], ALU.mult, ALU.mult, extra_reads=[self.nfT])
                    for t4 in range(4):
                        yt = yo.next()
                        for half in range(2):
                            ps = self.pst()
                            for c4 in range(4):
                                c = half * 4 + c4
                                self.tr(ps, ps[:, c4 * 128:(c4 + 1) * 128], xn, xn[:, c, t4 * 128:(t4 + 1) * 128])
                            self.copy("act" if half else "dve", yt, yt[:, half * 512:(half + 1) * 512], ps, ps[:])
                        r0 = blk * 512 + t4 * 128
                        self.store(s.y_out, s.y_out[r0:r0 + 128, :], yt, yt[:])
        P.barrier()


N_CORES = 8
_CACHE = {}


def make_streams():
    return [Stream("P", 4, 256, 0, [(0, 4)], False),
            Stream("S", 1, 2048, 1, [(0, 1), (1, 1), (2, 1), (3, 1)], True)]


def shared_inputs(inp, streams, depth):
    f = lambda a: np.ascontiguousarray(np.asarray(a, dtype=np.float32))
    sh = {}
    for s in streams:
        zposT, window = hyena_consts(s.L)
        Fc, Fs, Gc, Gs = dft_consts(s.L)
        CL, SLn, c64, s64 = fnet_consts(s.L)
        sh["zposT_" + s.name] = zposT
        sh["win_" + s.name] = window
        sh["Fc_" + s.name] = Fc
        sh["Fs_" + s.name] = Fs
        sh["Gc_" + s.name] = Gc
        sh["Gs_" + s.name] = Gs
        sh["CL_" + s.name] = CL
        sh["SLn_" + s.name] = SLn
        sh["c64_" + s.name] = c64
        sh["s64_" + s.name] = s64
        if s.is_sample:
            sh["pos_" + s.name] = grid_pos_embed_np(s.T)
    fm = lambda v: np.ascontiguousarray(f(v).reshape(-1, 128).T)
    sh["wmod"] = np.stack([tile_lhsT(f(inp["w_mod"][l])) for l in range(depth)])
    sh["bmodT"] = np.stack([fm(inp["b_mod"][l]) for l in range(depth)])
    sh["n1T"] = np.stack([fm(inp["norm1_g"][l]) for l in range(depth)])
    sh["n2T"] = np.stack([fm(inp["norm2_g"][l]) for l in range(depth)])
    sh["nfT"] = fm(inp["norm_f"])
    cols = win_tile_cols()
    win = np.zeros((depth, N_WIN_TILES, 128, 8, 128), np.float32)
    for l in range(depth):
        w = f(inp["w_in"][l])
        for ti, (c0, wd) in enumerate(cols):
            win[l, ti, :, :, :wd] = w[:, c0:c0 + wd].reshape(8, 128, wd).transpose(1, 0, 2)
    sh["win_t"] = win
    sh["wp_t"] = np.stack([np.stack([tile_lhsT(f(inp[k][l])) for k in ("w_pa", "w_pb", "w_pc")]) for l in range(depth)])
    sh["wo_t"] = np.stack([tile_lhsT(f(inp["w_o"][l])) for l in range(depth)])
    sh["wgu_t"] = np.stack([tile_lhsT(f(inp["w_gu"][l])) for l in range(depth)])
    sh["wdn_t"] = np.stack([tile_lhsT(f(inp["w_down"][l])) for l in range(depth)])
    cq = f(inp["conv_qkv"])[:depth]
    sh["convqT"] = np.ascontiguousarray(cq.reshape(depth, 3, 3, 8, 64).transpose(0, 4, 2, 3, 1))
    chy = f(inp["conv_hy"])[:depth]
    sh["convhT"] = np.ascontiguousarray(chy.reshape(depth, 3, 12, 128).transpose(0, 3, 2, 1))
    sh["alog_bc"] = np.ascontiguousarray(np.broadcast_to(f(inp["a_log"])[:depth].reshape(depth, 1, 16), (depth, 128, 16)))
    sh["dtb_bc"] = np.ascontiguousarray(np.broadcast_to(f(inp["dt_bias"])[:depth].reshape(depth, 1, 16), (depth, 128, 16)))
    sh["norma_bc"] = np.ascontiguousarray(np.broadcast_to(f(inp["norm_a"])[:depth].reshape(depth, 1, 64), (depth, 128, 64)))
    sh["hyw1"] = f(inp["hy_w1"])[:depth]
    sh["hyb1T"] = f(inp["hy_b1"])[:depth].reshape(depth, 64, 1)
    sh["hyfqT"] = f(inp["hy_freq"])[:depth].reshape(depth, 64, 1)
    sh["hyw2"] = f(inp["hy_w2"])[:depth]
    sh["hyb2T"] = f(inp["hy_b2"])[:depth].reshape(depth, 64, 1)
    sh["hyw3"] = f(inp["hy_w3"])[:depth]
    sh["hybiasT"] = np.ascontiguousarray(f(inp["hy_bias"])[:depth].reshape(depth, 2, 4, 128).transpose(0, 3, 1, 2))
    sh["masks"] = mask_consts()
    sh["ident"] = np.eye(128, dtype=np.float32)
    sh["lmask"] = level_masks()
    return sh


def kernel(**inp):
    depth = DEPTH
    streams = make_streams()
    if "nc" not in _CACHE:
        _CACHE["nc"] = Builder(make_streams(), depth=depth).build()
    nc = _CACHE["nc"]
    sh = shared_inputs(inp, streams, depth)
    xp = np.asarray(inp["x_prompt"], np.float32)
    xs = np.asarray(inp["x_sample"], np.float32)
    st = np.asarray(inp["state_delta"], np.float32)
    c = np.asarray(inp["c"], np.float32)
    cctx = np.asarray(inp["c_ctx"], np.float32)
    in_maps = []
    for core in range(N_CORES):
        sidx = core // 4
        m = dict(sh)
        m["x_P"] = np.ascontiguousarray(xp[core * 4:(core + 1) * 4].reshape(1024, D))
        m["x_S"] = np.ascontiguousarray(xs[sidx])
        m["st0_S"] = np.ascontiguousarray(st[sidx][:depth])
        cv = np.stack([cctx, c[sidx]], axis=-1)
        m["cvecT"] = np.ascontiguousarray(cv.reshape(8, 128, 2).transpose(1, 0, 2))
        in_maps.append(m)
    res = run_bass_kernel_spmd(nc, in_maps, core_ids=list(range(N_CORES)))
    r = res.results
    y_prompt = np.concatenate([r[i]["y_P"].reshape(4, 256, D) for i in range(N_CORES)], axis=0).astype(np.float32)
    y_sample = np.stack([r[0]["y_S"], r[4]["y_S"]], axis=0).astype(np.float32)
    new_state = np.concatenate([r[i]["st_P"] for i in range(N_CORES)], axis=0).astype(np.float32)
    return (y_prompt, y_sample, new_state)
```
